# Optimizing a Trainium2 kernel written in Bass

```python
import jax, jax.numpy as jnp
from jax import lax
import numpy as np

D_MODEL = 1024
BATCH = 4
SEQ = 4096
DEPTH = 2

HEAD_DIM = 64
N_Q_HEADS = 8
N_KV_HEADS = 2
Q_PER_KV = N_Q_HEADS // N_KV_HEADS
ATTN_WIDTH = N_Q_HEADS * HEAD_DIM
KV_WIDTH = N_KV_HEADS * HEAD_DIM
WINDOW = 128
ATTN_BLOCK = 128
ROPE_DIM = HEAD_DIM // 4
ROPE_THETA = 500000.0
N_SG_GROUPS = 8
SG_GROUP_DIM = 64
SG_WIDTH = N_SG_GROUPS * SG_GROUP_DIM
SG_CHUNK = 128
EVEN_GATE_WIDTH = ATTN_WIDTH + SG_WIDTH
EVEN_SPLITS = tuple(int(s) for s in np.cumsum([ATTN_WIDTH, KV_WIDTH, KV_WIDTH, SG_WIDTH, SG_WIDTH]))
EVEN_IN_WIDTH = EVEN_SPLITS[-1] + EVEN_GATE_WIDTH
RNN_WIDTH = D_MODEL
RNN_HEADS = 8
RNN_HEAD_DIM = RNN_WIDTH // RNN_HEADS
CONV_WIDTH = 4
CONV_PAD = (2, 1)
RG_LRU_C = 8.0
ODD_IN_WIDTH = 2 * RNN_WIDTH
N_EVEN = (DEPTH + 1) // 2
N_ODD = DEPTH // 2
DEEPNORM_ALPHA = (2 * DEPTH) ** 0.25
DEEPNORM_BETA = (8 * DEPTH) ** -0.25
LN_EPS = 1e-5
NEG_INF = -1e30

kernel_name = "hybrid_swa_gmlp_rglru_deepnorm_encoder"


def layer_norm(x, g, b):
    xf = x.astype(jnp.float32)
    mu = xf.mean(-1, keepdims=True)
    var = jnp.square(xf - mu).mean(-1, keepdims=True)
    y = (xf - mu) * lax.rsqrt(var + LN_EPS)
    return (y * g.astype(jnp.float32) + b.astype(jnp.float32)).astype(x.dtype)


def partial_rotary(t, positions):
    half = ROPE_DIM // 2
    inv_freq = jnp.power(jnp.float32(ROPE_THETA), -jnp.arange(half, dtype=jnp.float32) / half)
    ang = positions.astype(jnp.float32)[:, :, None, None] * inv_freq
    cos, sin = jnp.cos(ang), jnp.sin(ang)
    tr = t[..., :ROPE_DIM].astype(jnp.float32)
    t1, t2 = tr[..., :half], tr[..., half:]
    rot = jnp.concatenate([t1 * cos - t2 * sin, t2 * cos + t1 * sin], axis=-1)
    return jnp.concatenate([rot.astype(t.dtype), t[..., ROPE_DIM:]], axis=-1)


def windowed_gqa_sink(q, k, v, sink):
    B, S = q.shape[0], q.shape[1]
    nb = S // ATTN_BLOCK
    qb = q.reshape(B, nb, ATTN_BLOCK, N_KV_HEADS, Q_PER_KV, HEAD_DIM)

    def band(t):
        tp = jnp.pad(t, ((0, 0), (ATTN_BLOCK, ATTN_BLOCK), (0, 0), (0, 0)))
        parts = [tp[:, o * ATTN_BLOCK:o * ATTN_BLOCK + S].reshape(B, nb, ATTN_BLOCK, N_KV_HEADS, HEAD_DIM)
                 for o in range(3)]
        return jnp.concatenate(parts, axis=2)

    kb, vb = band(k), band(v)
    s = jnp.einsum('bnqhgd,bnkhd->bnhgqk', qb, kb).astype(jnp.float32) * (HEAD_DIM ** -0.5)
    qi = jnp.arange(ATTN_BLOCK)[:, None]
    kj = jnp.arange(3 * ATTN_BLOCK)[None, :]
    blk = jnp.arange(nb)[:, None, None]
    k_abs = blk * ATTN_BLOCK - ATTN_BLOCK + kj
    valid = (jnp.abs(kj - ATTN_BLOCK - qi) <= WINDOW) & (k_abs >= 0) & (k_abs < S)
    s = jnp.where(valid[None, :, None, None], s, NEG_INF)
    sink_l = sink.astype(jnp.float32).reshape(N_KV_HEADS, Q_PER_KV)[None, None, :, :, None, None]
    m = jnp.maximum(s.max(-1, keepdims=True), sink_l)
    p = jnp.exp(s - m)
    denom = p.sum(-1, keepdims=True) + jnp.exp(sink_l - m)
    o = jnp.einsum('bnhgqk,bnkhd->bnqhgd', (p / denom).astype(v.dtype), vb)
    return o.reshape(B, S, ATTN_WIDTH)


def chunked_spatial_gating(u, v, ln_g, ln_b, w_s, b_s):
    B, S = u.shape[0], u.shape[1]
    nc = S // SG_CHUNK
    vg = v.reshape(B, S, N_SG_GROUPS, SG_GROUP_DIM)
    vg = layer_norm(vg, ln_g.reshape(N_SG_GROUPS, SG_GROUP_DIM), ln_b.reshape(N_SG_GROUPS, SG_GROUP_DIM))
    vc = vg.reshape(B, nc, SG_CHUNK, N_SG_GROUPS, SG_GROUP_DIM)
    sv = jnp.einsum('gpq,bcqgd->bcpgd', w_s, vc) + b_s.T[None, None, :, :, None]
    return u * sv.reshape(B, S, SG_WIDTH)


def centred_depthwise_conv(x, w, b):
    y = lax.conv_general_dilated(x, w[:, None, :], window_strides=(1,), padding=[CONV_PAD],
                                 dimension_numbers=('NWC', 'WIO', 'NWC'),
                                 feature_group_count=x.shape[-1])
    return y + b


def rg_lru(x, w_a, b_a, w_x, b_x, lam, reverse):
    B, S = x.shape[0], x.shape[1]
    xh = x.reshape(B, S, RNN_HEADS, RNN_HEAD_DIM)
    pre_r = jnp.einsum('bshi,hij->bshj', xh, w_a).reshape(B, S, RNN_WIDTH) + b_a
    pre_i = jnp.einsum('bshi,hij->bshj', xh, w_x).reshape(B, S, RNN_WIDTH) + b_x
    rec_gate = jax.nn.sigmoid(pre_r.astype(jnp.float32))
    in_gate = jax.nn.sigmoid(pre_i.astype(jnp.float32))
    log_a = -RG_LRU_C * rec_gate * jax.nn.softplus(-lam.astype(jnp.float32))
    a = jnp.exp(log_a)
    bterm = jnp.sqrt(-jnp.expm1(2.0 * log_a)) * in_gate * x.astype(jnp.float32)

    def combine(lhs, rhs):
        a1, b1 = lhs
        a2, b2 = rhs
        return a1 * a2, a2 * b1 + b2

    _, h = lax.associative_scan(combine, (a, bterm), axis=1, reverse=reverse)
    return h


def even_mixer(h, positions, w_in, w_out, sink, sg_ln_g, sg_ln_b, sg_w, sg_b):
    B, S = h.shape[0], h.shape[1]
    q, k, v, su, sv, g = jnp.split(h @ w_in, EVEN_SPLITS, axis=-1)
    q = partial_rotary(q.reshape(B, S, N_Q_HEADS, HEAD_DIM), positions)
    k = partial_rotary(k.reshape(B, S, N_KV_HEADS, HEAD_DIM), positions)
    v = v.reshape(B, S, N_KV_HEADS, HEAD_DIM)
    y_attn = windowed_gqa_sink(q, k, v, sink)
    y_sg = chunked_spatial_gating(su, sv, sg_ln_g, sg_ln_b, sg_w, sg_b)
    y = jnp.concatenate([y_attn, y_sg], axis=-1) * jax.nn.silu(g)
    return y @ w_out


def odd_mixer(h, w_in, conv_w, conv_b, w_a, b_a, w_x, b_x, lam, w_out):
    xr, g = jnp.split(h @ w_in, 2, axis=-1)
    xr = centred_depthwise_conv(xr, conv_w, conv_b)
    y = (rg_lru(xr, w_a[0], b_a[0], w_x[0], b_x[0], lam[0], reverse=False)
         + rg_lru(xr, w_a[1], b_a[1], w_x[1], b_x[1], lam[1], reverse=True))
    y = y.astype(h.dtype) * jax.nn.silu(g)
    return y @ w_out


def setup_inputs(seed: int = 0) -> dict:
    key = jax.random.key(seed)
    ks = jax.random.split(key, 24)
    f32 = jnp.float32
    nrm = lambda k, shape, s: jax.random.normal(k, shape, f32) * s
    a_c = jax.random.uniform(ks[22], (N_ODD, 2, RNN_WIDTH), f32, minval=0.9, maxval=0.999)
    p = a_c ** (1.0 / RG_LRU_C)
    lam = jnp.log(p) - jnp.log1p(-p)
    return {
        "x": nrm(ks[0], (BATCH, SEQ, D_MODEL), 1.0),
        "c": nrm(ks[1], (BATCH, D_MODEL), 1.0),
        "positions": jnp.broadcast_to(jnp.arange(SEQ, dtype=jnp.int32), (BATCH, SEQ)),
        "ada_w": nrm(ks[2], (DEPTH, D_MODEL, 3 * D_MODEL), D_MODEL ** -0.5),
        "ada_b": nrm(ks[3], (DEPTH, 3 * D_MODEL), 0.01),
        "ln_g": 1.0 + nrm(ks[4], (DEPTH, D_MODEL), 0.02),
        "ln_b": nrm(ks[5], (DEPTH, D_MODEL), 0.02),
        "ev_w_in": nrm(ks[6], (N_EVEN, D_MODEL, EVEN_IN_WIDTH), D_MODEL ** -0.5),
        "ev_w_out": nrm(ks[7], (N_EVEN, EVEN_GATE_WIDTH, D_MODEL), DEEPNORM_BETA * EVEN_GATE_WIDTH ** -0.5),
        "ev_sink": nrm(ks[8], (N_EVEN, N_Q_HEADS), 1.0),
        "ev_sg_ln_g": 1.0 + nrm(ks[9], (N_EVEN, SG_WIDTH), 0.02),
        "ev_sg_ln_b": nrm(ks[10], (N_EVEN, SG_WIDTH), 0.02),
        "ev_sg_w": nrm(ks[11], (N_EVEN, N_SG_GROUPS, SG_CHUNK, SG_CHUNK), SG_CHUNK ** -0.5),
        "ev_sg_b": 1.0 + nrm(ks[12], (N_EVEN, N_SG_GROUPS, SG_CHUNK), 0.1),
        "od_w_in": nrm(ks[13], (N_ODD, D_MODEL, ODD_IN_WIDTH), D_MODEL ** -0.5),
        "od_conv_w": nrm(ks[14], (N_ODD, CONV_WIDTH, RNN_WIDTH), CONV_WIDTH ** -0.5),
        "od_conv_b": nrm(ks[15], (N_ODD, RNN_WIDTH), 0.01),
        "od_w_a": nrm(ks[16], (N_ODD, 2, RNN_HEADS, RNN_HEAD_DIM, RNN_HEAD_DIM), RNN_HEAD_DIM ** -0.5),
        "od_b_a": nrm(ks[17], (N_ODD, 2, RNN_WIDTH), 0.01),
        "od_w_x": nrm(ks[18], (N_ODD, 2, RNN_HEADS, RNN_HEAD_DIM, RNN_HEAD_DIM), RNN_HEAD_DIM ** -0.5),
        "od_b_x": nrm(ks[19], (N_ODD, 2, RNN_WIDTH), 0.01),
        "od_lam": lam,
        "od_w_out": nrm(ks[20], (N_ODD, RNN_WIDTH, D_MODEL), DEEPNORM_BETA * RNN_WIDTH ** -0.5),
    }


def reference(x, c, positions, ada_w, ada_b, ln_g, ln_b, ev_w_in, ev_w_out, ev_sink, ev_sg_ln_g, ev_sg_ln_b,
              ev_sg_w, ev_sg_b, od_w_in, od_conv_w, od_conv_b, od_w_a, od_b_a, od_w_x, od_b_x, od_lam, od_w_out):
    cond = jax.nn.silu(c)
    for layer in range(DEPTH):
        mod = cond @ ada_w[layer] + ada_b[layer]
        shift, scale, gate = jnp.split(mod, 3, axis=-1)
        h = x * (1.0 + scale[:, None, :]) + shift[:, None, :]
        j = layer // 2
        if layer % 2 == 0:
            y = even_mixer(h, positions, ev_w_in[j], ev_w_out[j], ev_sink[j], ev_sg_ln_g[j], ev_sg_ln_b[j],
                           ev_sg_w[j], ev_sg_b[j])
        else:
            y = odd_mixer(h, od_w_in[j], od_conv_w[j], od_conv_b[j], od_w_a[j], od_b_a[j], od_w_x[j],
                          od_b_x[j], od_lam[j], od_w_out[j])
        x = layer_norm(DEEPNORM_ALPHA * x + gate[:, None, :] * y, ln_g[layer], ln_b[layer])
    return x
```

```python
import contextlib
import numpy as np
import concourse.bass as bass
import concourse.mybir as mybir
from concourse.bass_utils import run_bass_kernel_spmd

F32 = mybir.dt.float32
BF16 = mybir.dt.bfloat16
I32 = mybir.dt.int32
ALU = mybir.AluOpType
AF = mybir.ActivationFunctionType
AX = mybir.AxisListType

PE, ACT, DVE, POOL, SP = "tensor", "scalar", "vector", "gpsimd", "sync"
ENGS = (PE, ACT, DVE, POOL, SP)

D = 1024
T = 2048
NT = 16
ALPHA = 4.0 ** 0.25
LN_EPS = 1e-5
PERM = [0, 4, 1, 5, 2, 6, 3, 7]
INV_FREQ = [float(np.float32(500000.0) ** np.float32(-i / 8.0)) for i in range(8)]
NEG = -30000.0
ARENA_WORDS = 34880


class Buf:
    __slots__ = ("name", "w", "r", "excl")

    def __init__(self, name):
        self.name = name
        self.w = {}
        self.r = {}
        self.excl = False


class Op:
    __slots__ = ("eng", "fn", "deps", "sig", "dma", "idx")

    def __init__(self, eng, fn):
        self.eng = eng
        self.fn = fn
        self.deps = {}
        self.sig = None
        self.dma = None
        self.idx = None


class Sched:
    def __init__(self, nc, n_dma_sems=8, n_cc_sems=2):
        self.nc = nc
        self.ops = {e: [] for e in ENGS}
        self.n_dma_sems = n_dma_sems
        self.n_cc = n_cc_sems
        nt = n_dma_sems + n_cc_sems
        self.sem_inc = [16] * n_dma_sems + [1] * n_cc_sems
        self.dma_val = [0] * nt
        self.dma_last = [None] * nt
        self.dma_rr = 0
        self.cc_rr = 0
        self.pending = {e: [] for e in ENGS}

    def barrier(self):
        toks = []
        for e in ENGS:
            if self.ops[e]:
                last = [o for o in self.ops[e] if o.dma is None]
                if last:
                    toks.append(("op", last[-1]))
        for t in self.dma_last[:self.n_dma_sems]:
            if t is not None:
                toks.append(t)
        for e in ENGS:
            self.pending[e] = list(toks)

    @staticmethod
    def _key(tok):
        return tok[1].eng if tok[0] == "op" else ("dma", tok[1])

    @staticmethod
    def _newer(a, b):
        if a is None:
            return b
        if a[0] == "op":
            return b if b[1].idx > a[1].idx else a
        return b if b[2] > a[2] else a

    def _add_dep(self, op, tok):
        if tok[0] == "op" and tok[1] is op:
            return
        k = self._key(tok)
        op.deps[k] = self._newer(op.deps.get(k), tok)

    def op(self, eng, fn, reads=(), writes=(), partial=(), dma=False, cc=False):
        o = Op(eng, fn)
        o.idx = len(self.ops[eng])
        had_pending = bool(self.pending[eng])
        if had_pending:
            for t in self.pending[eng]:
                self._add_dep(o, t)
            self.pending[eng] = []
        if cc:
            dma = True
            si = self.n_dma_sems + self.cc_rr
            self.cc_rr = (self.cc_rr + 1) % self.n_cc
        elif dma:
            si = self.dma_rr
            self.dma_rr = (self.dma_rr + 1) % self.n_dma_sems
        if dma:
            prev = self.dma_last[si]
            if prev is not None:
                self._add_dep(o, prev)
            self.dma_val[si] += self.sem_inc[si]
            o.dma = (si, self.dma_val[si])
            tok = ("dma", si, self.dma_val[si])
            self.dma_last[si] = tok
        else:
            tok = ("op", o)
        for b in reads:
            for t in b.w.values():
                self._add_dep(o, t)
            if b.excl:
                for kk, t in b.r.items():
                    if kk != eng:
                        self._add_dep(o, t)
        for b in list(writes) + list(partial):
            for t in b.w.values():
                self._add_dep(o, t)
            for t in b.r.values():
                self._add_dep(o, t)
        if not dma and eng == PE and not had_pending:
            raw = -1
            for b in reads:
                t = b.w.get(eng)
                if t is not None and t[0] == "op":
                    raw = max(raw, t[1].idx)
            if eng in o.deps:
                if raw < 0:
                    del o.deps[eng]
                else:
                    o.deps[eng] = ("op", self.ops[eng][raw])
        k = self._key(tok)
        for b in reads:
            b.r[k] = self._newer(b.r.get(k), tok)
        for b in writes:
            b.w = {k: tok}
            b.r = {}
        for b in partial:
            b.w[k] = self._newer(b.w.get(k), tok)
        self.ops[eng].append(o)
        return o

    def emit(self, final_eng=SP):
        nc = self.nc
        needed = set()
        for e in ENGS:
            for o in self.ops[e]:
                for t in o.deps.values():
                    if t[0] == "op":
                        needed.add(id(t[1]))
        for e in ENGS:
            c = 0
            for o in self.ops[e]:
                if o.dma is None and id(o) in needed:
                    c += 1
                    o.sig = c
        with contextlib.ExitStack() as st:
            esem = {e: st.enter_context(nc.semaphore("s_" + e)) for e in ENGS}
            dsem = [st.enter_context(nc.semaphore("d_%d" % i)) for i in range(self.n_dma_sems + self.n_cc)]
            block = st.enter_context(nc.Block())

            def run(eng_name, engine):
                known = {}
                for o in self.ops[eng_name]:
                    for k, t in o.deps.items():
                        if t[0] == "op":
                            sem, val = esem[t[1].eng], t[1].sig
                        else:
                            sem, val = dsem[t[1]], t[2]
                        if known.get(k, 0) >= val:
                            continue
                        known[k] = val
                        engine.wait_ge(sem, val)
                    ins = o.fn(engine)
                    if o.dma is not None:
                        ins.then_inc(dsem[o.dma[0]], self.sem_inc[o.dma[0]])
                    elif o.sig is not None:
                        ins.then_inc(esem[eng_name], 1)
                if eng_name == final_eng:
                    for i in range(self.n_dma_sems + self.n_cc):
                        if self.dma_val[i] > 0:
                            engine.wait_ge(dsem[i], self.dma_val[i])

            @block.tensor
            def _(e):
                run(PE, e)

            @block.scalar
            def _(e):
                run(ACT, e)

            @block.vector
            def _(e):
                run(DVE, e)

            @block.gpsimd
            def _(e):
                run(POOL, e)

            @block.sync
            def _(e):
                run(SP, e)


class Ctx:
    def __init__(self, nc, st):
        self.nc = nc
        self.st = st
        self.S = Sched(nc)
        self.bufs = {}
        self.arena = None
        self.aoff = 0
        self.asize = 0
        self.apeak = 0

    def make_arena(self, words):
        self.arena = self.sb("arena", [128, words])
        self.asize = words
        self.aoff = 0

    def areset(self):
        self.aoff = 0

    def asb(self, name, shape, dt=F32):
        if self.arena is None:
            return self.sb(name, shape, dt)[:]
        esz = {F32: 4, I32: 4, BF16: 2}[dt]
        n = 1
        for d_ in shape[1:]:
            n *= d_
        words = (n * esz + 3) // 4
        off = self.aoff
        self.aoff += words
        self.apeak = max(self.apeak, self.aoff)
        assert self.aoff <= self.asize, ("arena overflow", name, self.aoff, self.asize)
        ap = self.arena[:, off:off + words]
        if dt != F32:
            ap = ap.bitcast(dt)[:, 0:n]
        if shape[0] != 128:
            ap = ap[0:shape[0], :]
        if len(shape) == 3:
            ap = ap.rearrange("p (a b) -> p a b", b=shape[2])
        elif len(shape) == 4:
            ap = ap.rearrange("p (a b c) -> p a b c", b=shape[2], c=shape[3])
        return ap

    def sb(self, name, shape, dt=F32):
        return self.st.enter_context(self.nc.sbuf_tensor("sb_" + name, shape, dt))

    def ps(self, name, shape, dt=F32):
        return self.st.enter_context(self.nc.psum_tensor("ps_" + name, shape, dt))

    def B(self, name):
        b = self.bufs.get(name)
        if b is None:
            b = self.bufs[name] = Buf(name)
        return b

    def dram(self, name, shape, dt, kind):
        if not hasattr(self, "_dram"):
            self._dram = {}
        if name not in self._dram:
            self._dram[name] = self.nc.dram_tensor(name, list(shape), dt, kind=kind).ap()
        return self._dram[name]


def v3(ap, d):
    return ap.rearrange("p (h d) -> p h d", d=d)


def bc_mid(ap2, n):
    return ap2.unsqueeze(1).broadcast_to([ap2.shape[0], n, ap2.shape[1]])


def bc_last(ap2, n):
    return ap2.unsqueeze(2).broadcast_to([ap2.shape[0], ap2.shape[1], n])


def emit_consts(C):
    S = C.S
    C.id32 = C.sb("id32", [128, 128])
    C.idb = C.sb("idb", [128, 128], BF16)
    C.ones32 = C.sb("ones32", [128, 128])
    d_id = C.dram("ident", [128, 128], F32, "ExternalInput")
    S.op(SP, lambda e: e.dma_start(out=C.id32[:], in_=d_id[:, :]), writes=[C.B("id32")], dma=True)
    S.op(DVE, lambda e: e.tensor_copy(C.idb[:], C.id32[:]), reads=[C.B("id32")], writes=[C.B("idb")])
    S.op(DVE, lambda e: e.memset(C.ones32[:], 1.0), writes=[C.B("ones32")])


def emit_adaln(C, lname, pbanks, rows_out, crep, bcrep, brow_ap=None, cgs=range(6), tag="", ring=None, bring=None):
    S = C.S
    d_c = C.dram(lname + "_cT", [128, 8], F32, "ExternalInput")
    d_w = C.dram(lname + "_ada_w", [1024, 3072], F32, "ExternalInput")
    d_b = C.dram(lname + "_ada_b", [1, 3072], F32, "ExternalInput")
    cT = C.asb(lname + tag + "cT", [128, 8])
    brow = brow_ap if brow_ap is not None else C.asb(lname + tag + "brow", [1, 1024])
    bcT = C.B(lname + tag + "cT")
    bbrows = [C.B(lname + "brow0"), C.B(lname + "brow1")]
    S.op(SP, lambda e: e.dma_start(out=cT[:], in_=d_c[:, :]), writes=[bcT], dma=True)
    S.op(ACT, lambda e: e.activation(out=cT[:], in_=cT[:], func=AF.Silu), reads=[bcT], writes=[bcT])
    S.op(DVE, lambda e: e.tensor_copy(crep, bc_last(cT[:], 128)), reads=[bcT], writes=[bcrep])
    if ring is None:
        ring = C.ring3
    i = 0
    for cg in cgs:
        pb, bpb = pbanks[cg % 2]
        bbrow = bbrows[cg % 2]
        bsl = slice((cg % 2) * 512, (cg % 2 + 1) * 512)
        S.op(SP, lambda e, cg=cg, bsl=bsl: e.dma_start(out=brow[0:1, bsl], in_=d_b[0:1, cg * 512:(cg + 1) * 512]), writes=[bbrow], dma=True)
        for k in range(8):
            sap, sbufs = ring[i % len(ring)]
            i += 1
            S.op(SP, lambda e, k=k, cg=cg, sap=sap: e.dma_start(
                out=sap, in_=d_w[k * 128:(k + 1) * 128, cg * 512:(cg + 1) * 512]),
                writes=sbufs, dma=True)
            if bring is not None:
                bap, bbufs = bring[(i - 1) % len(bring)]
                ceng = (ACT, DVE, ACT, POOL)[i % 4]
                if ceng == ACT:
                    S.op(ACT, lambda e, sap=sap, bap=bap: e.activation(out=bap, in_=sap, func=AF.Copy), reads=sbufs, writes=bbufs)
                else:
                    S.op(ceng, lambda e, sap=sap, bap=bap: e.tensor_copy(bap, sap), reads=sbufs, writes=bbufs)
                S.op(PE, lambda e, k=k, bap=bap, pb=pb: e.matmul(pb[:, 0:512], crep[:, k, :], bap, start=(k == 0), stop=False),
                     reads=[bcrep] + bbufs, partial=[bpb] if k else (), writes=[bpb] if k == 0 else ())
                continue
            S.op(PE, lambda e, k=k, sap=sap, pb=pb: e.matmul(pb[:, 0:512], crep[:, k, :], sap,
                                                            start=(k == 0), stop=False),
                 reads=[bcrep] + sbufs, partial=[bpb] if k else (), writes=[bpb] if k == 0 else ())
        S.op(PE, lambda e, cg=cg, pb=pb, bsl=bsl: e.matmul(pb[:, 0:512], C.ones32[0:1, :], brow[0:1, bsl],
                                                  start=False, stop=True),
             reads=[C.B("ones32"), bbrow], partial=[bpb])
        dst, bdst, add_one = rows_out[cg // 2]
        half = cg % 2
        if add_one:
            S.op(ACT, lambda e, dst=dst, half=half, pb=pb: e.activation(
                out=dst[:, half * 512:(half + 1) * 512], in_=pb[:, 0:512], func=AF.Identity, bias=C.one_col[:, 0:1], scale=1.0),
                reads=[bpb, C.B("one_col")], partial=[bdst])
        else:
            S.op(ACT, lambda e, dst=dst, half=half, pb=pb: e.activation(
                out=dst[:, half * 512:(half + 1) * 512], in_=pb[:, 0:512], func=AF.Copy),
                reads=[bpb], partial=[bdst])


def emit_adaln2(C, lname, pbanks, rows_out, crep, bcrep, brow, rows, ring4k, bring2k, tag=""):
    S = C.S
    d_c = C.dram(lname + "_cT", [128, 8], F32, "ExternalInput")
    d_w = C.dram(lname + "_ada_w", [1024, 3072], F32, "ExternalInput")
    d_b = C.dram(lname + "_ada_b", [1, 3072], F32, "ExternalInput")
    cT = C.asb(lname + tag + "cT", [128, 8])
    bcT = C.B(lname + tag + "cT")
    bbrow = C.B(lname + tag + "brow")
    S.op(SP, lambda e: e.dma_start(out=cT[:], in_=d_c[:, :]), writes=[bcT], dma=True)
    S.op(ACT, lambda e: e.activation(out=cT[:], in_=cT[:], func=AF.Silu), reads=[bcT], writes=[bcT])
    S.op(DVE, lambda e: e.tensor_copy(crep, bc_last(cT[:], 128)), reads=[bcT], writes=[bcrep])
    i = 0
    for r in rows:
        S.op(SP, lambda e, r=r: e.dma_start(out=brow[0:1, 0:1024], in_=d_b[0:1, r * 1024:(r + 1) * 1024]), writes=[bbrow], dma=True)
        for k in range(8):
            sap, sbufs = ring4k[i % len(ring4k)]
            bap, bbufs = bring2k[i % len(bring2k)]
            S.op(SP, lambda e, k=k, r=r, sap=sap: e.dma_start(out=sap, in_=d_w[k * 128:(k + 1) * 128, r * 1024:(r + 1) * 1024]),
                 writes=sbufs, dma=True)
            if i % 2:
                S.op(ACT, lambda e, sap=sap, bap=bap: e.activation(out=bap, in_=sap, func=AF.Copy), reads=sbufs, writes=bbufs)
            else:
                S.op(DVE, lambda e, sap=sap, bap=bap: e.tensor_copy(bap, sap), reads=sbufs, writes=bbufs)
            i += 1
            for hh in range(2):
                pb, bpb = pbanks[hh]
                S.op(PE, lambda e, k=k, bap=bap, pb=pb, hh=hh: e.matmul(pb[:, 0:512], crep[:, k, :], bap[:, hh * 512:(hh + 1) * 512],
                                                                      start=(k == 0), stop=False),
                     reads=[bcrep] + bbufs, partial=[bpb] if k else (), writes=[bpb] if k == 0 else ())
        dst, bdst, add_one = rows_out[r]
        for hh in range(2):
            pb, bpb = pbanks[hh]
            S.op(PE, lambda e, pb=pb, hh=hh: e.matmul(pb[:, 0:512], C.ones32[0:1, :], brow[0:1, hh * 512:(hh + 1) * 512], start=False, stop=True),
                 reads=[C.B("ones32"), bbrow], partial=[bpb])
            if add_one:
                S.op(ACT, lambda e, dst=dst, hh=hh, pb=pb: e.activation(out=dst[:, hh * 512:(hh + 1) * 512], in_=pb[:, 0:512], func=AF.Identity,
                                                                      bias=C.one_col[:, 0:1], scale=1.0),
                     reads=[bpb, C.B("one_col")], partial=[bdst])
            else:
                S.op(ACT, lambda e, dst=dst, hh=hh, pb=pb: e.activation(out=dst[:, hh * 512:(hh + 1) * 512], in_=pb[:, 0:512], func=AF.Copy),
                     reads=[bpb], partial=[bdst])


def emit_weight_bf16(C, dname, dst, bdst, nk, ncols, ring, scale_row=None, bscale=None, chunk=512):
    S = C.S
    d_w = C.dram(dname, [nk * 128, ncols], F32, "ExternalInput")
    i = 0
    engs = [DVE, ACT] if scale_row is None else [DVE]
    for k in range(nk):
        for c0 in range(0, ncols, chunk):
            n = min(chunk, ncols - c0)
            sap, sbufs = ring[i % len(ring)]
            S.op(SP, lambda e, k=k, c0=c0, n=n, sap=sap: e.dma_start(
                out=sap[:, 0:n], in_=d_w[k * 128:(k + 1) * 128, c0:c0 + n]), writes=sbufs, dma=True)
            eng = engs[i % len(engs)]
            if scale_row is not None:
                S.op(eng, lambda e, k=k, c0=c0, n=n, sap=sap: e.tensor_tensor(
                    dst[:, k, c0:c0 + n], sap[:, 0:n], scale_row[:, c0:c0 + n], ALU.mult),
                    reads=sbufs + [bscale], partial=[bdst])
            elif eng == ACT:
                S.op(ACT, lambda e, k=k, c0=c0, n=n, sap=sap: e.activation(
                    out=dst[:, k, c0:c0 + n], in_=sap[:, 0:n], func=AF.Copy), reads=sbufs, partial=[bdst])
            else:
                S.op(eng, lambda e, k=k, c0=c0, n=n, sap=sap: e.tensor_copy(
                    dst[:, k, c0:c0 + n], sap[:, 0:n]), reads=sbufs, partial=[bdst])
            i += 1


def emit_ln_tile(C, pfx, zt, bz, xdst, bxdst, lng, lnb, brows, small=None):
    S = C.S
    if small is None:
        st6, mv, sd = C.ln_st6, C.ln_mv, C.ln_sd
        bst6, bmv, bsd = C.B("ln_st6"), C.B("ln_mv"), C.B("ln_sd")
    else:
        st6, mv, sd, sfx = small
        bst6, bmv, bsd = C.B("ln_st6" + sfx), C.B("ln_mv" + sfx), C.B("ln_sd" + sfx)
    for hh in range(2):
        S.op(DVE, lambda e, hh=hh: e.bn_stats(st6[:, hh * 6:(hh + 1) * 6], zt[:, hh * 512:(hh + 1) * 512]), reads=[bz],
             writes=[bst6] if hh == 0 else (), partial=[bst6] if hh else ())
    S.op(DVE, lambda e: e.bn_aggr(mv[:, 0:2], st6[:]), reads=[bst6], writes=[bmv])
    S.op(ACT, lambda e: e.activation(out=sd[:, 0:1], in_=mv[:, 1:2], func=AF.Sqrt, bias=C.eps_col[:, 0:1], scale=1.0),
         reads=[bmv, C.B("eps_col")], writes=[bsd])
    S.op(DVE, lambda e: e.reciprocal(sd[:, 1:2], sd[:, 0:1]), reads=[bsd], partial=[bsd])
    S.op(DVE, lambda e: e.scalar_tensor_tensor(sd[:, 2:3], mv[:, 0:1], -1.0, sd[:, 1:2], ALU.mult, ALU.mult),
         reads=[bmv, bsd], partial=[bsd])
    S.op(ACT, lambda e: e.activation(out=zt[:], in_=zt[:], func=AF.Identity, scale=sd[:, 1:2], bias=sd[:, 2:3]),
         reads=[bz, bsd], writes=[bz])
    S.op(POOL, lambda e: e.tensor_tensor(zt[:], zt[:], lng[:], ALU.mult), reads=[bz, brows], writes=[bz])
    S.op(POOL, lambda e: e.tensor_tensor(xdst, zt[:], lnb[:], ALU.add), reads=[bz, brows], writes=[bxdst])


def emit_l0(C, x_dram, out_dram):
    S, nc = C.S, C.nc
    B = C.B
    xres = C.xres
    d_pos = C.dram("l0_pos", [128, 17], I32, "ExternalInput")
    d_sink = C.dram("l0_sink", [128, 8], F32, "ExternalInput")
    d_sgg = C.dram("l0_sg_ln_g", [128, 512], F32, "ExternalInput")
    d_sgb = C.dram("l0_sg_ln_b", [128, 512], F32, "ExternalInput")
    d_sgw = C.dram("l0_sg_wT", [128, 1024], F32, "ExternalInput")
    d_sgbias = C.dram("l0_sg_b", [128, 8], F32, "ExternalInput")
    d_lng = C.dram("l0_ln_g", [128, 1024], F32, "ExternalInput")
    d_lnb = C.dram("l0_ln_b", [128, 1024], F32, "ExternalInput")
    d_mask = C.dram("l0_mask", [128, 768], F32, "ExternalInput")
    w_in = C.asb("l0_w_in", [128, 8, 2816], BF16)
    w_out = C.asb("l0_w_out", [128, 8, 1024], BF16)
    sgw = C.asb("l0_sgw", [128, 8, 128], BF16)
    modrow = C.asb("l0_modrow", [128, 2, 1024])
    lnrows = C.asb("l0_lnrows", [128, 2, 1024])
    sgrows = C.asb("l0_sgrows", [128, 2, 512])
    sgbias = C.asb("l0_sgbias", [128, 8])
    nsink = C.asb("l0_nsink", [128, 8])
    sink = C.asb("l0_sink", [128, 8])
    mask = C.asb("l0_mask", [128, 768], BF16)
    kT = C.asb("l0_kT", [128, 18, 128], BF16)
    V = C.asb("l0_V", [128, 18, 128], BF16)
    posi = C.asb("l0_posi", [128, 17], I32)
    posf = C.asb("l0_posf", [128, 17])
    invf = C.asb("l0_invf", [128, 8])
    ang = C.asb("l0_ang", [128, 17, 8])
    angk = C.asb("l0_angk", [128, 17, 8])
    angi = C.asb("l0_angi", [128, 17, 8], I32)
    C2 = C.asb("l0_C2", [128, 17, 16])
    S2 = C.asb("l0_S2", [128, 17, 16])
    h32 = C.asb("l0_h32", [128, 1024])
    mask32 = h32[:, 0:768]
    hbf = C.asb("l0_hbf", [128, 1024], BF16)
    hT = C.asb("l0_hT", [128, 1024], BF16)
    qkr = C.asb("l0_qkr", [128, 640], BF16)
    qT = C.asb("l0_qT", [128, 2, 512], BF16)
    su = C.asb("l0_su", [128, 512], BF16)
    vn = C.asb("l0_vn", [128, 512])
    vnb = C.asb("l0_vnb", [128, 512], BF16)
    st8 = C.asb("l0_st8", [128, 8, 8])
    sgl = C.asb("l0_sgl", [128, 2, 1024], BF16)
    y = C.asb("l0_y", [128, 2, 1024], BF16)
    yT = C.asb("l0_yT", [128, 1024], BF16)
    P = C.asb("l0_P", [128, 2, 384], BF16)
    PT = C.asb("l0_PT", [128, 2, 384], BF16)
    att = C.asb("l0_att", [128, 6, 8])
    z = C.asb("l0_z", [128, 1024])
    gaterow = z
    yat = vn
    rot = C.asb("l0_rot", [128, 2, 160])
    sq = z[:, 512:1024]
    ptr = C.ps_ptr
    pb = C.ps_f32[0:2]
    sbk = C.ps_f32[2:4]
    ob = C.ps_f32[4]
    svb = C.ps_f32[5]
    bptr = [B("ptr0"), B("ptr1")]
    bpb = [B("pb0"), B("pb1")]
    bsbk = [B("pb2"), B("pb3")]
    cnt = {"ptr": 0, "pb": 0}
    for bn in ("ptr0", "ptr1", "pb0", "pb1", "pb2", "pb3", "ob", "svb"):
        B(bn).excl = True

    def next_ptr():
        i = cnt["ptr"] % 2
        cnt["ptr"] += 1
        return ptr[i], bptr[i]

    def next_pb():
        i = cnt["pb"] % 2
        cnt["pb"] += 1
        return pb[i], bpb[i]

    w4k = (C.wstage[:].rearrange("p a b -> p (a b)")[:, 0:1024], [C.bwstage[0], C.bwstage[1]])
    ring0 = [w4k,
             (y.rearrange("p a b -> p (a b)").bitcast(F32), [B("y_a0"), B("y_s0"), B("y_a1"), B("y_s1")]),
             (sgl.rearrange("p a b -> p (a b)").bitcast(F32), [B("sgl_a0"), B("sgl_s0"), B("sgl_a1"), B("sgl_s1")])]
    bring0 = [(hT[:], [B("hT")]), (hbf[:], [B("hbf")]), (yT[:], [B("yT")]), (qT.rearrange("p a b -> p (a b)"), [B("qT0"), B("qT1")])]
    S.op(SP, lambda e: e.dma_start(out=xres[:, 0, :], in_=x_dram[0:128, :]), writes=[B("x0")], dma=True)
    S.op(SP, lambda e: e.dma_start(out=posi[:], in_=d_pos[:, :]), writes=[B("posi")], dma=True)
    S.op(SP, lambda e: e.dma_start(out=sink[:], in_=d_sink[:, :]), writes=[B("sink")], dma=True)
    S.op(SP, lambda e: e.dma_start(out=mask32, in_=d_mask[:, :]), writes=[B("h32")], dma=True)
    S.op(POOL, lambda e: e.tensor_copy(mask[:], mask32), reads=[B("h32")], writes=[B("mask")])
    S.op(SP, lambda e: e.dma_start(out=sgbias[:], in_=d_sgbias[:, :]), writes=[B("sgbias")], dma=True)
    crep_b = h32[:, 0:512].bitcast(BF16).rearrange("p (k m) -> p k m", m=128)
    brow0 = C.asb("l0_brow", [1, 1024])
    emit_adaln2(C, "l0", [(pb[0], bpb[0]), (pb[1], bpb[1])],
                [(modrow[:, 0, :], B("l0shift"), False), (modrow[:, 1, :], B("l0scale"), True), (gaterow[:], B("z"), False)],
                crep_b, B("h32"), brow0, (0, 1, 2), ring0, bring0)
    emit_weight_bf16(C, "l0_w_in", w_in, B("l0w_in"), 8, 2816, ring0, chunk=1024)
    emit_weight_bf16(C, "l0_w_out", w_out, B("l0w_out"), 8, 1024, ring0, scale_row=gaterow, bscale=B("z"), chunk=1024)
    S.op(SP, lambda e: e.dma_start(out=sgrows[:, 0, :], in_=d_sgg[:, :]), partial=[B("sgrows")], dma=True)
    S.op(SP, lambda e: e.dma_start(out=sgrows[:, 1, :], in_=d_sgb[:, :]), partial=[B("sgrows")], dma=True)
    S.op(SP, lambda e: e.dma_start(out=lnrows[:, 0, :], in_=d_lng[:, :]), partial=[B("l0lnrows")], dma=True)
    S.op(SP, lambda e: e.dma_start(out=lnrows[:, 1, :], in_=d_lnb[:, :]), partial=[B("l0lnrows")], dma=True)
    S.op(SP, lambda e: e.dma_start(out=xres[:, 1, :], in_=x_dram[128:256, :]), writes=[B("x1")], dma=True)
    for hh in range(2):
        slot = hh
        S.op(SP, lambda e, hh=hh, slot=slot: e.dma_start(out=C.wstage[:, slot, 0:512], in_=d_sgw[:, hh * 512:(hh + 1) * 512]),
             writes=[C.bwstage[slot]], dma=True)
        S.op(POOL, lambda e, hh=hh, slot=slot: e.tensor_copy(sgw[:, hh * 4:(hh + 1) * 4, :].rearrange("p g q -> p (g q)"), C.wstage[:, slot, 0:512]),
             reads=[C.bwstage[slot]], partial=[B("sgw")])
    S.op(POOL, lambda e: e.tensor_scalar(nsink[:], sink[:], -1.0, None, ALU.mult), reads=[B("sink")], writes=[B("nsink")])
    S.op(POOL, lambda e: e.memset(kT[:, 0, :], 0.0), partial=[B("kT0")])
    S.op(POOL, lambda e: e.memset(V[:, 0, :], 0.0), partial=[B("V0")])

    for f in range(8):
        S.op(POOL, lambda e, f=f: e.memset(invf[:, f:f + 1], INV_FREQ[f]), partial=[B("invf")])
    S.op(DVE, lambda e: e.tensor_copy(posf[:], posi[:]), reads=[B("posi")], writes=[B("posf")])
    S.op(DVE, lambda e: e.tensor_tensor(ang[:], bc_last(posf[:], 8), bc_mid(invf[:], 17), ALU.mult),
         reads=[B("posf"), B("invf")], writes=[B("ang")])
    TWO_PI = float(2 * np.pi)

    def sin_into(dst_ap, shift, negate, tag):
        bk, bi_, br = B("angk"), B("angi"), B("angr")
        S.op(DVE, lambda e: e.tensor_scalar(angk[:], ang[:], shift, 1.0 / TWO_PI, ALU.add, ALU.mult), reads=[B("ang")], writes=[bk])
        S.op(DVE, lambda e: e.tensor_copy(angi[:], angk[:]), reads=[bk], writes=[bi_])
        S.op(DVE, lambda e: e.tensor_copy(angk[:], angi[:]), reads=[bi_], writes=[bk])
        S.op(DVE, lambda e: e.scalar_tensor_tensor(angk[:], angk[:], -TWO_PI, ang[:], ALU.mult, ALU.add), reads=[bk, B("ang")], writes=[bk])
        S.op(DVE, lambda e: e.tensor_scalar(angk[:], angk[:], shift, float(np.pi), ALU.add, ALU.min), reads=[bk], writes=[bk])
        S.op(DVE, lambda e: e.tensor_scalar(angk[:], angk[:], float(-np.pi), None, ALU.max), reads=[bk], writes=[bk])
        S.op(ACT, lambda e: e.activation(out=dst_ap, in_=angk[:], func=AF.Sin, scale=(-1.0 if negate else 1.0)),
             reads=[bk], partial=[B(tag)])

    sin_into(C2[:, :, 0:8], float(np.pi / 2), False, "C2")
    sin_into(C2[:, :, 8:16], float(np.pi / 2), False, "C2")
    sin_into(S2[:, :, 0:8], 0.0, True, "S2")
    sin_into(S2[:, :, 8:16], 0.0, False, "S2")

    GA, GB, GC, GD, GE, GF = (0, 512), (512, 256), (768, 512), (1280, 512), (1792, 512), (2304, 512)

    def rotary(src_bank, nh, dst3, t, slot, bdst):
        s3 = v3(src_bank[:, 0:nh * 64], 64)
        ta = v3(rot[:, 0, 0:nh * 16], 16)
        tb = v3(rot[:, 1, 0:nh * 16], 16)
        br_ = B("rot")
        S.op(DVE, lambda e: e.tensor_tensor(ta, s3[:, :, 0:16], bc_mid(C2[:, t, :], nh), ALU.mult),
             reads=[slot, B("C2")], writes=[br_])
        S.op(DVE, lambda e: e.tensor_tensor(tb[:, :, 0:8], s3[:, :, 8:16], bc_mid(S2[:, t, 0:8], nh), ALU.mult),
             reads=[slot, B("S2")], partial=[br_])
        S.op(DVE, lambda e: e.tensor_tensor(tb[:, :, 8:16], s3[:, :, 0:8], bc_mid(S2[:, t, 8:16], nh), ALU.mult),
             reads=[slot, B("S2")], partial=[br_])
        S.op(DVE, lambda e: e.tensor_tensor(dst3[:, :, 0:16], ta, tb, ALU.add), reads=[br_], partial=[bdst])

    def modulate(t):
        if t == NT:
            S.op(SP, lambda e: e.dma_start(out=h32[:], in_=x_dram[2048:2176, :]), writes=[B("h32")], dma=True)
            xsrc, bxs = h32[:], B("h32")
        else:
            xsrc, bxs = xres[:, t, :], B("x%d" % t)
        S.op(POOL, lambda e: e.tensor_tensor(h32[:], xsrc, modrow[:, 1, :], ALU.mult), reads=[bxs, B("l0scale")], writes=[B("h32")])
        S.op(POOL, lambda e: e.tensor_tensor(hbf[:], h32[:], modrow[:, 0, :], ALU.add), reads=[B("h32"), B("l0shift")], writes=[B("hbf")])

    def front(t):
        par = t % 2
        last = (t == NT)
        pt, bpt = next_ptr()
        for k in range(8):
            S.op(PE, lambda e, k=k, pt=pt: e.transpose(pt[:, k * 128:(k + 1) * 128], hbf[:, k * 128:(k + 1) * 128], C.idb[:]),
                 reads=[B("hbf"), B("idb")], writes=[bpt] if k == 0 else (), partial=[bpt] if k else ())
        if t < NT:
            modulate(t + 1)
        S.op(ACT, lambda e, pt=pt: e.activation(out=hT[:], in_=pt[:], func=AF.Copy), reads=[bpt], writes=[B("hT")])
        yield "a"

        def proj(g):
            c0, n = g
            bank, bbank = next_pb()
            for k in range(8):
                S.op(PE, lambda e, k=k: e.matmul(bank[:, 0:n], hT[:, k * 128:(k + 1) * 128], w_in[:, k, c0:c0 + n],
                                                 start=(k == 0), stop=(k == 7)),
                     reads=[B("hT"), B("l0w_in")], writes=[bbank] if k == 0 else (), partial=[bbank] if k else ())
            return bank, bbank

        if not last:
            bank, bb = proj(GA)
            S.op(ACT, lambda e, bank=bank: e.activation(out=qkr[:, 0:512], in_=bank[:, 0:512], func=AF.Copy), reads=[bb], writes=[B("qkr_q")])
            rotary(bank, 8, v3(qkr[:, 0:512], 64), t, bb, B("qkr_q"))
        bank, bb = proj(GB)
        S.op(ACT, lambda e, bank=bank: e.activation(out=qkr[:, 512:640], in_=bank[:, 0:128], func=AF.Copy), reads=[bb], writes=[B("qkr_k")])
        S.op(DVE, lambda e, bank=bank: e.tensor_copy(V[:, t + 1, :], bank[:, 128:256]), reads=[bb], writes=[B("V%d" % (t + 1))])
        rotary(bank, 2, v3(qkr[:, 512:640], 64), t, bb, B("qkr_k"))
        pt, bpt = next_ptr()
        first = True
        if not last:
            for j in range(4):
                S.op(PE, lambda e, j=j, pt=pt: e.transpose(pt[:, j * 128:(j + 1) * 128], qkr[:, j * 128:(j + 1) * 128], C.idb[:]),
                     reads=[B("qkr_q"), B("idb")], writes=[bpt] if first else (), partial=() if first else [bpt])
                first = False
        S.op(PE, lambda e, pt=pt: e.transpose(pt[:, 512:640], qkr[:, 512:640], C.idb[:]),
             reads=[B("qkr_k"), B("idb")], writes=[bpt] if first else (), partial=() if first else [bpt])
        if not last:
            S.op(DVE, lambda e, pt=pt: e.tensor_copy(qT[:, par, :], pt[:, 0:512]), reads=[bpt], writes=[B("qT%d" % par)])
        S.op(DVE, lambda e, pt=pt: e.tensor_copy(kT[:, t + 1, :], pt[:, 512:640]), reads=[bpt], writes=[B("kT%d" % (t + 1))])
        if last:
            return
        bank, bb = proj(GC)
        S.op(DVE, lambda e, bank=bank: e.tensor_copy(su[:], bank[:, 0:512]), reads=[bb], writes=[B("su")])
        bank, bb = proj(GD)
        bst = B("st8")
        S.op(ACT, lambda e, bank=bank: e.activation(out=sq, in_=bank[:, 0:512], func=AF.Square), reads=[bb], writes=[B("z")])
        S.op(DVE, lambda e, bank=bank: e.tensor_reduce(st8[:, 0, :], v3(bank[:, 0:512], 64), AX.X, ALU.add), reads=[bb], writes=[bst])
        S.op(DVE, lambda e: e.tensor_reduce(st8[:, 1, :], v3(sq, 64), AX.X, ALU.add), reads=[B("z")], partial=[bst])
        S.op(DVE, lambda e: e.tensor_scalar(st8[:, 2, :], st8[:, 0, :], 1.0 / 64, None, ALU.mult), reads=[bst], partial=[bst])
        S.op(DVE, lambda e: e.tensor_tensor(st8[:, 3, :], st8[:, 2, :], st8[:, 2, :], ALU.mult), reads=[bst], partial=[bst])
        S.op(DVE, lambda e: e.scalar_tensor_tensor(st8[:, 4, :], st8[:, 1, :], 1.0 / 64, st8[:, 3, :], ALU.mult, ALU.subtract),
             reads=[bst], partial=[bst])
        S.op(ACT, lambda e: e.activation(out=st8[:, 5, :], in_=st8[:, 4, :], func=AF.Sqrt, bias=C.eps_col[:, 0:1], scale=1.0),
             reads=[bst, B("eps_col")], partial=[bst])
        S.op(DVE, lambda e: e.reciprocal(st8[:, 6, :], st8[:, 5, :]), reads=[bst], partial=[bst])
        S.op(DVE, lambda e, bank=bank: e.tensor_tensor(v3(vn[:], 64), v3(bank[:, 0:512], 64), bc_last(st8[:, 2, :], 64), ALU.subtract),
             reads=[bb, bst], writes=[B("vn")])
        S.op(DVE, lambda e: e.tensor_tensor(v3(vn[:], 64), v3(vn[:], 64), bc_last(st8[:, 6, :], 64), ALU.mult),
             reads=[B("vn"), bst], writes=[B("vn")])
        S.op(POOL, lambda e: e.tensor_tensor(vn[:], vn[:], sgrows[:, 0, :], ALU.mult), reads=[B("vn"), B("sgrows")], writes=[B("vn")])
        S.op(POOL, lambda e: e.tensor_tensor(vnb[:], vn[:], sgrows[:, 1, :], ALU.add), reads=[B("vn"), B("sgrows")], writes=[B("vnb")])
        yield "b1"
        bankE, bbE = proj(GE)
        bankF, bbF = proj(GF)
        yield "gm"
        S.op(ACT, lambda e, bank=bankE: e.activation(out=sgl[:, par, 0:512], in_=bank[:, 0:512], func=AF.Silu), reads=[bbE], writes=[B("sgl_a%d" % par)])
        S.op(ACT, lambda e, bank=bankF: e.activation(out=sgl[:, par, 512:1024], in_=bank[:, 0:512], func=AF.Silu), reads=[bbF], writes=[B("sgl_s%d" % par)])
        for g in range(8):
            S.op(PE, lambda e, g=g: e.matmul(svb[:, g * 64:(g + 1) * 64], sgw[:, g, :], vnb[:, g * 64:(g + 1) * 64], start=True, stop=True),
                 reads=[B("sgw"), B("vnb")], writes=[B("svb")] if g == 0 else (), partial=[B("svb")] if g else ())
        S.op(POOL, lambda e: e.tensor_tensor(su[:], su[:], sgl[:, par, 512:1024], ALU.mult), reads=[B("su"), B("sgl_s%d" % par)], writes=[B("su")])
        S.op(DVE, lambda e: e.tensor_tensor(v3(sq, 64), v3(svb[:, 0:512], 64), bc_last(sgbias[:], 64), ALU.add),
             reads=[B("svb"), B("sgbias")], writes=[B("z")])
        S.op(DVE, lambda e: e.tensor_tensor(y[:, par, 512:1024], sq, su[:], ALU.mult), reads=[B("z"), B("su")], writes=[B("y_s%d" % par)])

    def attn(t, mid=None):
        par = t % 2
        mrow = 0 if t == 0 else 1
        batt = B("att")

        def stage1(hp):
            j, half = hp // 2, hp % 2
            sb_, bsb = sbk[hp % 2], bsbk[hp % 2]
            S.op(PE, lambda e: e.matmul(sb_[:, 0:384], qT[half * 64:(half + 1) * 64, par, j * 128:(j + 1) * 128],
                                        kT[half * 64:(half + 1) * 64, t:t + 3, :].rearrange("p a b -> p (a b)"), start=True, stop=False),
                 reads=[B("qT%d" % par), B("kT%d" % t), B("kT%d" % (t + 1)), B("kT%d" % (t + 2))], writes=[bsb])
            S.op(PE, lambda e: e.matmul(sb_[:, 0:384], C.idb[:], mask[:, mrow * 384:(mrow + 1) * 384], start=False, stop=True),
                 reads=[B("idb"), B("mask")], partial=[bsb])
            S.op(DVE, lambda e: e.tensor_reduce(att[:, 0, hp:hp + 1], sb_[:, 0:384], AX.X, ALU.max), reads=[bsb], writes=[B("amx%d" % hp)])
            S.op(DVE, lambda e: e.tensor_scalar(att[:, 1, hp:hp + 1], att[:, 0, hp:hp + 1], C.m0125_col[:, 0:1], nsink[:, hp:hp + 1], ALU.mult, ALU.min),
                 reads=[B("amx%d" % hp), B("nsink"), B("m0125_col")], writes=[B("anm%d" % hp)])
            S.op(ACT, lambda e: e.activation(out=P[:, hp % 2, :], in_=sb_[:, 0:384], func=AF.Exp, bias=att[:, 1, hp:hp + 1], scale=0.125,
                                             accum_out=att[:, 2, hp:hp + 1]),
                 reads=[bsb, B("anm%d" % hp), B("ars")], writes=[B("P%d" % (hp % 2)), B("ars%d" % hp)])

        def stage2a(hp):
            pt, bpt = next_ptr()
            for kt in range(3):
                S.op(PE, lambda e, kt=kt: e.transpose(pt[:, kt * 128:(kt + 1) * 128], P[:, hp % 2, kt * 128:(kt + 1) * 128], C.idb[:]),
                     reads=[B("P%d" % (hp % 2)), B("idb")], writes=[bpt] if kt == 0 else (), partial=[bpt] if kt else ())
            S.op(DVE, lambda e: e.tensor_copy(PT[:, hp % 2, :], pt[:, 0:384]), reads=[bpt], writes=[B("PT%d" % (hp % 2))])

        def stage2b(hp):
            half = hp % 2
            for kt in range(3):
                S.op(PE, lambda e, kt=kt: e.matmul(ob[:, hp * 64:(hp + 1) * 64], PT[:, hp % 2, kt * 128:(kt + 1) * 128],
                                                   V[:, t + kt, half * 64:(half + 1) * 64], start=(kt == 0), stop=(kt == 2)),
                     reads=[B("PT%d" % (hp % 2)), B("V%d" % (t + kt))],
                     writes=[B("ob")] if (hp == 0 and kt == 0) else (), partial=() if (hp == 0 and kt == 0) else [B("ob")])

        S.op(DVE, lambda e: e.memset(att[:, 2, :], 0.0), writes=[B("ars")] + [B("ars%d" % h) for h in range(8)])
        stage1(0)
        stage1(1)
        if mid is not None:
            mid()
        for hp in range(8):
            stage2a(hp)
            if hp + 2 < 8:
                stage1(hp + 2)
            stage2b(hp)
        anm = [B("anm%d" % h) for h in range(8)]
        ars = [B("ars%d" % h) for h in range(8)]
        S.op(DVE, lambda e: e.tensor_tensor(att[:, 3, :], att[:, 1, :], sink[:], ALU.add), reads=anm + [B("sink")], writes=[batt])
        S.op(ACT, lambda e: e.activation(out=att[:, 3, :], in_=att[:, 3, :], func=AF.Exp), reads=[batt], writes=[batt])
        S.op(DVE, lambda e: e.tensor_tensor(att[:, 4, :], att[:, 3, :], att[:, 2, :], ALU.add), reads=[batt] + ars, writes=[batt])
        S.op(DVE, lambda e: e.reciprocal(att[:, 5, :], att[:, 4, :]), reads=[batt], writes=[batt])
        S.op(DVE, lambda e: e.tensor_tensor(v3(yat[:], 64), v3(ob[:, 0:512], 64), bc_last(att[:, 5, :], 64), ALU.mult),
             reads=[B("ob"), batt], writes=[B("vn")])
        S.op(DVE, lambda e: e.tensor_tensor(y[:, par, 0:512], yat[:], sgl[:, par, 0:512], ALU.mult),
             reads=[B("vn"), B("sgl_a%d" % par)], writes=[B("y_a%d" % par)])

    def back(t):
        par = t % 2
        pt, bpt = next_ptr()
        for k in range(8):
            S.op(PE, lambda e, k=k: e.transpose(pt[:, k * 128:(k + 1) * 128], y[:, par, k * 128:(k + 1) * 128], C.idb[:]),
                 reads=[B("y_a%d" % par), B("y_s%d" % par), B("idb")], writes=[bpt] if k == 0 else (), partial=[bpt] if k else ())
        S.op(ACT, lambda e: e.activation(out=yT[:], in_=pt[:], func=AF.Copy), reads=[bpt], writes=[B("yT")])
        for cg in range(2):
            bank, bb = next_pb()
            for k in range(8):
                S.op(PE, lambda e, k=k, cg=cg, bank=bank: e.matmul(bank[:, 0:512], yT[:, k * 128:(k + 1) * 128], w_out[:, k, cg * 512:(cg + 1) * 512],
                                                        start=(k == 0), stop=(k == 7)),
                     reads=[B("yT"), B("l0w_out")], writes=[bb] if k == 0 else (), partial=[bb] if k else ())
            S.op(DVE, lambda e, cg=cg, bank=bank: e.scalar_tensor_tensor(z[:, cg * 512:(cg + 1) * 512], xres[:, t, cg * 512:(cg + 1) * 512], ALPHA,
                                                              bank[:, 0:512], ALU.mult, ALU.add),
                 reads=[B("x%d" % t), bb], writes=[B("z")] if cg == 0 else (), partial=[B("z")] if cg else ())
        yield "a"
        emit_ln_tile(C, "l0", z, B("z"), xres[:, t, :], B("x%d" % t), lnrows[:, 0, :], lnrows[:, 1, :], B("l0lnrows"))
        if out_dram is not None:
            S.op(SP, lambda e: e.dma_start(out=out_dram[t * 128:(t + 1) * 128, :], in_=xres[:, t, :]), reads=[B("x%d" % t)], dma=True)

    import os
    dbg = os.environ.get("L0_DBG", "")
    if dbg == "a":
        for t in range(NT):
            S.op(SP, lambda e, t=t: e.dma_start(out=out_dram[t * 128:(t + 1) * 128, :], in_=xres[:, t, :]), reads=[B("x%d" % t)], dma=True)
        return
    def step(g):
        try:
            next(g)
        except StopIteration:
            pass

    def finish(g):
        for _ in g:
            pass

    modulate(0)
    pend_back = None
    for t in range(NT + 1):
        if t + 2 < NT:
            S.op(SP, lambda e, t=t: e.dma_start(out=xres[:, t + 2, :], in_=x_dram[(t + 2) * 128:(t + 3) * 128, :]),
                 writes=[B("x%d" % (t + 2))], dma=True)
        f = front(t)
        step(f)
        if pend_back is not None:
            finish(pend_back)
            pend_back = None
        step(f)
        if t >= 1:
            attn(t - 1, mid=lambda f=f: step(f))
            finish(f)
            b = back(t - 1)
            step(b)
            pend_back = b
        else:
            finish(f)
    if pend_back is not None:
        finish(pend_back)
    if C.send1 is not None:
        S.op(SP, lambda e: e.dma_start(out=C.send1[0:1, :], in_=xres[127:128, 15, :]), reads=[B("x15")], partial=[B("send1")], dma=True)
        S.op(SP, lambda e: e.dma_start(out=C.send1[1:2, :], in_=xres[126:127, 15, :]), reads=[B("x15")], partial=[B("send1")], dma=True)
        S.op(POOL, lambda e: e.collective_compute("AllGather", ALU.bypass, replica_groups=[[0, 1], [2, 3], [4, 5], [6, 7]],
                                                 ins=[C.send1], outs=[C.recv1]), reads=[B("send1")], writes=[B("recv1")], cc=True)
    if dbg in ("b", "c"):
        for t in range(NT):
            S.op(SP, lambda e, t=t: e.dma_start(out=out_dram[t * 128:(t + 1) * 128, :], in_=xres[:, t, :]), reads=[B("x%d" % t)], dma=True)


def alloc_common(C):
    C.xres = C.sb("xres", [128, NT, 1024])
    C.wstage = C.sb("wstage", [128, 3, 512])
    C.ps_ptr = [C.ps("ptr%d" % i, [128, 1024], BF16) for i in range(2)]
    C.ps_f32 = [C.ps("pf%d" % i, [128, 512]) for i in range(6)]
    if not hasattr(C, "send1"):
        C.send1 = None
    C.bwstage = [C.B("wstage%d" % i) for i in range(3)]
    C.ring3 = [(C.wstage[:, i, :], [C.bwstage[i]]) for i in range(3)]
    C.ln_st6 = C.sb("ln_st6", [128, 12])
    C.m0125_col = C.sb("m0125_col", [128, 1])
    C.S.op(DVE, lambda e: e.memset(C.m0125_col[:], -0.125), writes=[C.B("m0125_col")])
    C.ln_mv = C.sb("ln_mv", [128, 2])
    C.ln_sd = C.sb("ln_sd", [128, 4])
    C.eps_col = C.sb("eps_col", [128, 1])
    C.one_col = C.sb("one_col", [128, 1])
    C.S.op(DVE, lambda e: e.memset(C.eps_col[:], LN_EPS), writes=[C.B("eps_col")])
    C.S.op(DVE, lambda e: e.memset(C.one_col[:], 1.0), writes=[C.B("one_col")])


def build_l0():
    nc = bass.Bass("TRN2", target_bir_lowering=False)
    with contextlib.ExitStack() as st:
        C = Ctx(nc, st)
        x_d = C.dram("x", [2176, 1024], F32, "ExternalInput")
        o_d = C.dram("out", [2048, 1024], F32, "ExternalOutput")
        alloc_common(C)
        emit_consts(C)
        emit_l0(C, x_d, o_d)
        print("sbuf remaining after l0 alloc:", nc.sbuf_bytes_remaining)
        C.S.emit()
    return nc


def rev_ap(ap2):
    n = ap2.shape[1]
    return bass.AP(ap2.tensor, ap2.offset + (n - 1), [list(ap2.ap[0]), [-1, n]])


def emit_l1(C, out_dram):
    S, nc = C.S, C.nc
    B = C.B
    xres = C.xres
    S.barrier()
    C.areset()
    RG = [[0, 1], [2, 3], [4, 5], [6, 7]]
    d_win = C.dram("l1_w_in", [1024, 2048], F32, "ExternalInput")
    d_w5 = C.dram("l1_w5", [128, 40], F32, "ExternalInput")
    d_cb = C.dram("l1_cb", [128, 8], F32, "ExternalInput")
    d_wg = C.dram("l1_wg", [128, 4096], F32, "ExternalInput")
    d_bg = C.dram("l1_bg", [128, 32], F32, "ExternalInput")
    d_lam = C.dram("l1_lam", [128, 16], F32, "ExternalInput")
    d_lng = C.dram("l1_ln_g", [128, 1024], F32, "ExternalInput")
    d_lnb = C.dram("l1_ln_b", [128, 1024], F32, "ExternalInput")
    d_sel = C.dram("sel", [128, 2], F32, "ExternalInput")
    hT = C.asb("l1_hT", [128, 8, 2056], BF16)
    w_out = hT.rearrange("p k t -> p (k t)")[:, 0:8192].rearrange("p (k c) -> p k c", c=1024)
    yT = C.asb("l1_yT", [128, 8, 2048], BF16)
    xc = C.asb("l1_xc", [128, 8, 2048], BF16)
    wg = C.asb("l1_wg", [128, 4, 8, 128], BF16)
    wt = C.asb("l1_wt", [128, 8, 128], BF16)
    xr = C.asb("l1_xr", [128, 2052], BF16)
    sg = xr[:, 0:2048]
    lnrows = xc[:, 4:6, :].rearrange("p a b -> p (a b)").bitcast(F32).rearrange("p (a b) -> p a b", a=2)
    dg = C.asb("l1_dg", [128, 10, 128], BF16)
    modrow = xc[:, 0:2, :].rearrange("p a b -> p (a b)").bitcast(F32).rearrange("p (a b) -> p a b", a=2)
    wk = C.asb("l1_wk", [128, 5120])
    WA = wk[:, 0:2048]
    WTI = wk[:, 2048:3072].bitcast(BF16)
    WVB = wk[:, 3072:4096].bitcast(BF16)
    WH = wk[:, 4096:5120].rearrange("p (a b) -> p a b", a=2)
    h32 = wk[:, 0:1024]
    hbf = wk[:, 1024:1536].bitcast(BF16)
    hb2 = wk[:, 2048:3072]
    z = h32
    gaterow = wk[:, 1024:2048]
    crep = wk[:, 0:1024].rearrange("p (k m) -> p k m", m=128)
    brow = wk[0:1, 4096:5120]
    hbg = C.asb("l1_hbg", [128, 32])
    hn = C.asb("l1_hn", [128, 16])
    w5 = C.asb("l1_w5", [128, 40])
    cb = C.asb("l1_cb", [128, 8])
    bg = C.asb("l1_bg", [128, 32])
    lam = C.asb("l1_lam", [128, 16])
    nsp8 = C.asb("l1_nsp8", [128, 16])
    sel = C.asb("l1_sel", [128, 2])
    carry = C.asb("l1_carry", [128, 8])
    cga = C.asb("l1_cga", [128, 8])
    cgb = C.asb("l1_cgb", [128, 8])
    stA = C.asb("l1_stA", [128, 8])
    stB = C.asb("l1_stB", [128, 8])
    ptr = C.ps_ptr
    pb = C.ps_f32[0:4]
    bptr = [B("ptr0"), B("ptr1")]
    bpb = [B("pb%d" % i) for i in range(4)]
    for b_ in bptr + bpb:
        b_.excl = True
    cnt = {"ptr": 0, "pb": 0}

    def next_ptr():
        i = cnt["ptr"] % 2
        cnt["ptr"] += 1
        return ptr[i], bptr[i]

    def next_pb():
        i = cnt["pb"] % 4
        cnt["pb"] += 1
        return pb[i], bpb[i]

    w4k = (C.wstage[:].rearrange("p a b -> p (a b)")[:, 0:1024], [C.bwstage[0], C.bwstage[1]])
    ring1 = [w4k] + [(yT[:, k, :].bitcast(F32), [B("yT%d" % k)]) for k in range(8)]
    bring1 = [(xc[:, 3, 0:1024], [B("xc3")]), (xc[:, 3, 1024:2048], [B("xc3")]), (xc[:, 2, 0:1024], [B("xc2")]), (xc[:, 2, 1024:2048], [B("xc2")])]
    crep_b = wk[:, 0:512].bitcast(BF16).rearrange("p (k m) -> p k m", m=128)
    for nm, dst, src in (("w5", w5, d_w5), ("cb", cb, d_cb), ("bg", bg, d_bg), ("lam", lam, d_lam), ("sel", sel, d_sel)):
        S.op(SP, lambda e, dst=dst, src=src: e.dma_start(out=dst, in_=src[:, :]), writes=[B("l1" + nm)], dma=True)
    emit_adaln2(C, "l1", [(pb[0], bpb[0]), (pb[1], bpb[1])],
                [(modrow[:, 0, :], B("modrow"), False), (modrow[:, 1, :], B("modrow"), True), (gaterow, B("t23"), False)],
                crep_b, B("t01"), brow, (0, 1), ring1, bring1, tag="a")
    wg2 = wg.rearrange("p m h j -> p (m h j)")
    for i in range(8):
        slot = i % 3
        sap, sbufs = ring1[(i + 5) % len(ring1)]
        S.op(SP, lambda e, i=i, sap=sap: e.dma_start(out=sap[:, 0:512], in_=d_wg[:, i * 512:(i + 1) * 512]), writes=sbufs, dma=True)
        S.op(DVE if i % 2 else ACT, (lambda e, i=i, sap=sap: e.tensor_copy(wg2[:, i * 512:(i + 1) * 512], sap[:, 0:512])) if i % 2 else
             (lambda e, i=i, sap=sap: e.activation(out=wg2[:, i * 512:(i + 1) * 512], in_=sap[:, 0:512], func=AF.Copy)),
             reads=sbufs, partial=[B("l1wg")])
    S.op(ACT, lambda e: e.activation(out=nsp8, in_=lam, func=AF.Exp, scale=-1.0), reads=[B("l1lam")], writes=[B("nsp8")])
    S.op(ACT, lambda e: e.activation(out=nsp8, in_=nsp8, func=AF.Ln, bias=C.one_col[:, 0:1], scale=1.0), reads=[B("nsp8"), B("one_col")], writes=[B("nsp8")])
    S.op(DVE, lambda e: e.tensor_scalar(nsp8, nsp8, -8.0, None, ALU.mult), reads=[B("nsp8")], writes=[B("nsp8")])
    S.op(DVE, lambda e: e.tensor_scalar(hn, nsp8, 0.5, None, ALU.mult), reads=[B("nsp8")], writes=[B("hn")])
    S.op(DVE, lambda e: e.tensor_scalar(hbg, bg, 0.5, None, ALU.mult), reads=[B("l1bg")], writes=[B("hbg")])
    S.op(DVE, lambda e: e.memset(xr[:, 0:2], 0.0), partial=[B("xr")])

    for t in range(NT + 1):
        if t < NT:
            xsrc, bxs = xres[:, t, :], [B("x%d" % t)]
        else:
            S.op(DVE, lambda e: e.memset(h32, 0.0), writes=[B("t01")])
            S.op(SP, lambda e: e.dma_start(out=h32[0:2, :], in_=C.recv1[0:2, :]), reads=[B("recv1")], partial=[B("t01")], dma=True)
            S.op(SP, lambda e: e.dma_start(out=hb2[0:2, :], in_=C.recv1[2:4, :]), reads=[B("recv1")], writes=[B("t34")], dma=True)
            S.op(DVE, lambda e: e.tensor_scalar(h32[0:2, :], h32[0:2, :], sel[0:2, 0:1], None, ALU.mult), reads=[B("t01"), B("l1sel")], partial=[B("t01")])
            S.op(DVE, lambda e: e.scalar_tensor_tensor(h32[0:2, :], hb2[0:2, :], sel[0:2, 1:2], h32[0:2, :], ALU.mult, ALU.add),
                 reads=[B("t01"), B("t34"), B("l1sel")], partial=[B("t01")])
            xsrc, bxs = h32, [B("t01")]
        meng = POOL if t % 3 == 0 else DVE
        S.op(meng, lambda e, xsrc=xsrc: e.tensor_tensor(h32, xsrc, modrow[:, 1, :], ALU.mult), reads=bxs + [B("modrow")], writes=[B("t01")])
        S.op(meng, lambda e: e.tensor_tensor(hbf, h32, modrow[:, 0, :], ALU.add), reads=[B("t01"), B("modrow")], writes=[B("t23")])
        pt, bpt = next_ptr()
        for k in range(8):
            S.op(PE, lambda e, k=k, pt=pt: e.transpose(pt[:, k * 128:(k + 1) * 128], hbf[:, k * 128:(k + 1) * 128], C.idb[:]),
                 reads=[B("t23"), B("idb")], writes=[bpt] if k == 0 else (), partial=[bpt] if k else ())
        ncol = 128 if t < NT else 8
        S.op(ACT, lambda e, t=t, pt=pt, ncol=ncol: e.activation(out=hT[:, :, t * 128:t * 128 + ncol],
                                                               in_=pt[:].rearrange("p (k m) -> p k m", m=128)[:, :, 0:ncol], func=AF.Copy),
             reads=[bpt], partial=[B("l1hT")])
    S.barrier()

    def load_wt(c0):
        wst = C.wstage[:].rearrange("p a b -> p (a b)")[:, 0:1024].rearrange("p (k m) -> p k m", m=128)
        S.op(SP, lambda e: e.dma_start(out=wst, in_=d_win.rearrange("(k p) c -> p k c", p=128)[:, :, c0:c0 + 128]),
             writes=[C.bwstage[0], C.bwstage[1]], dma=True)
        S.op(DVE, lambda e: e.tensor_copy(wt, wst), reads=[C.bwstage[0], C.bwstage[1]], writes=[B("l1wt")])

    def batch(d, ct, finish_chunk):
        qs = [0, 1, 2, 3] if d == 0 else [3, 2, 1, 0]
        ca = (2 * d) * 8 + ct
        ci = (2 * d + 1) * 8 + ct
        cl = d * 8 + ct
        bxc = B("xc%d" % ct)
        for n_, q in enumerate(qs):
            sl = slice(q * 512, (q + 1) * 512)
            bA, bI, bV = B("wA%d" % q), B("wTI%d" % q), B("wVB%d" % q)
            Ht, bHt = WH[:, n_ % 2, :], B("wH%d" % (n_ % 2))
            bk_r, bb_r = next_pb()
            bk_i, bb_i = next_pb()
            S.op(PE, lambda e, sl=sl, bk_r=bk_r: e.matmul(bk_r[:, 0:512], wg[:, 2 * d, ct, :], xc[:, ct, sl], start=True, stop=True),
                 reads=[B("l1wg"), bxc], writes=[bb_r])
            S.op(PE, lambda e, sl=sl, bk_i=bk_i: e.matmul(bk_i[:, 0:512], wg[:, 2 * d + 1, ct, :], xc[:, ct, sl], start=True, stop=True),
                 reads=[B("l1wg"), bxc], writes=[bb_i])
            S.op(ACT, lambda e, sl=sl, bk_r=bk_r: e.activation(out=WA[:, sl], in_=bk_r[:, 0:512], func=AF.Tanh, bias=hbg[:, ca:ca + 1], scale=0.5),
                 reads=[bb_r, B("hbg")], writes=[bA])
            S.op(ACT, lambda e, sl=sl, bk_i=bk_i: e.activation(out=WTI[:, sl], in_=bk_i[:, 0:512], func=AF.Tanh, bias=hbg[:, ci:ci + 1], scale=0.5),
                 reads=[bb_i, B("hbg")], writes=[bI])
            S.op(ACT, lambda e, sl=sl: e.activation(out=WA[:, sl], in_=WA[:, sl], func=AF.Exp, bias=hn[:, cl:cl + 1], scale=hn[:, cl:cl + 1]),
                 reads=[bA, B("hn")], writes=[bA])
            S.op(DVE, lambda e, sl=sl, Ht=Ht: e.tensor_tensor(Ht, WA[:, sl], WA[:, sl], ALU.mult), reads=[bA], writes=[bHt])
            S.op(DVE, lambda e, sl=sl, Ht=Ht: e.tensor_scalar(WVB[:, sl], Ht, -1.0, 1.0, ALU.mult, ALU.add), reads=[bHt], writes=[bV])
        yield "G"
        allV = [B("wVB%d" % q) for q in range(4)]
        allI = [B("wTI%d" % q) for q in range(4)]
        S.op(ACT, lambda e: e.activation(out=WVB, in_=WVB, func=AF.Sqrt, scale=0.25), reads=allV, writes=allV)
        S.op(DVE, lambda e: e.scalar_tensor_tensor(WVB, WTI, 1.0, WVB, ALU.add, ALU.mult), reads=allV + allI, writes=allV)
        S.op(DVE, lambda e: e.tensor_tensor(WVB, WVB, xc[:, ct, :], ALU.mult), reads=allV + [bxc], writes=allV)
        st = stA if d == 0 else stB
        bst = B("stA") if d == 0 else B("stB")
        for n_, q in enumerate(qs):
            sl = slice(q * 512, (q + 1) * 512)
            bA, bV = B("wA%d" % q), B("wVB%d" % q)
            H, bH = WH[:, n_ % 2, :], B("wH%d" % (n_ % 2))
            if d == 0:
                init = 0.0 if n_ == 0 else st[:, ct:ct + 1]
                S.op(DVE, lambda e, sl=sl, H=H, init=init: e.tensor_tensor_scan(H, WA[:, sl], WVB[:, sl], init, ALU.mult, ALU.add),
                     reads=[bA, bV, bst], writes=[bH])
                S.op(DVE, lambda e, H=H: e.tensor_copy(st[:, ct:ct + 1], H[:, 511:512]), reads=[bH], writes=[bst])
            else:
                init = carry[:, ct:ct + 1] if n_ == 0 else st[:, ct:ct + 1]
                S.op(DVE, lambda e, sl=sl, H=H, init=init: e.tensor_tensor_scan(rev_ap(H), rev_ap(WA[:, sl]), rev_ap(WVB[:, sl]), init, ALU.mult, ALU.add),
                     reads=[bA, bV, bst, B("l1carry")], writes=[bH])
                S.op(DVE, lambda e, H=H: e.tensor_copy(st[:, ct:ct + 1], H[:, 0:1]), reads=[bH], writes=[bst])
            finish_chunk(q, H, bH)

    def conv_stage(ct):
        for tg in range(5):
            n = 512 if tg < 4 else 8
            bank, bb = next_pb()
            for k in range(8):
                S.op(PE, lambda e, k=k, tg=tg, n=n, bank=bank: e.matmul(bank[:, 0:n], wt[:, k, :], hT[:, k, tg * 512:tg * 512 + n],
                                                                      start=(k == 0), stop=(k == 7)),
                     reads=[B("l1wt"), B("l1hT")], writes=[bb] if k == 0 else (), partial=[bb] if k else ())
            m = n if tg < 4 else 2
            S.op(DVE, lambda e, tg=tg, m=m, bank=bank: e.tensor_copy(xr[:, 2 + tg * 512:2 + tg * 512 + m], bank[:, 0:m]),
                 reads=[bb], partial=[B("xr")])
        if ct + 1 < 8:
            load_wt((ct + 1) * 128)
        par = ct % 2
        for j in range(5):
            S.op(ACT, lambda e, ct=ct, j=j, par=par: e.activation(out=dg[:, par * 5 + j, :], in_=C.idb[:], func=AF.Copy, scale=w5[:, ct * 5 + j:ct * 5 + j + 1]),
                 reads=[B("idb"), B("l1w5")], writes=[B("dg%d_%d" % (par, j))])
        for q in range(4):
            sl = slice(q * 512, (q + 1) * 512)
            bank, bb = next_pb()
            for j in range(5):
                S.op(PE, lambda e, j=j, q=q, par=par, bank=bank: e.matmul(bank[:, 0:512], dg[:, par * 5 + j, :], xr[:, q * 512 + j:q * 512 + j + 512],
                                                                        start=(j == 0), stop=(j == 4)),
                     reads=[B("dg%d_%d" % (par, j)), B("xr")], writes=[bb] if j == 0 else (), partial=[bb] if j else ())
            S.op(DVE, lambda e, ct=ct, sl=sl, bank=bank: e.tensor_scalar(xc[:, ct, sl], bank[:, 0:512], cb[:, ct:ct + 1], None, ALU.add),
                 reads=[bb, B("l1cb")], partial=[B("xc%d" % ct)])

    def drain(g):
        for _ in g:
            pass

    load_wt(0)
    conv_stage(0)
    for ct in range(8):
        def fin_a(q, H, bH, ct=ct):
            sl = slice(q * 512, (q + 1) * 512)
            S.op(DVE, lambda e: e.tensor_copy(yT[:, ct, sl], H), reads=[bH], partial=[B("yT%d" % ct)])
        g = batch(0, ct, fin_a)
        next(g)
        if ct + 1 < 8:
            conv_stage(ct + 1)
        drain(g)

    S.op(SP, lambda e: e.dma_start(out=C.send2[:, :], in_=stA), reads=[B("stA")], writes=[B("send2")], dma=True)
    S.op(POOL, lambda e: e.collective_compute("AllGather", ALU.bypass, replica_groups=RG, ins=[C.send2], outs=[C.recv2]),
         reads=[B("send2")], writes=[B("recv2")], cc=True)
    S.op(SP, lambda e: e.dma_start(out=cga, in_=C.recv2[0:128, :]), reads=[B("recv2")], writes=[B("cga")], dma=True)
    S.op(SP, lambda e: e.dma_start(out=cgb, in_=C.recv2[128:256, :]), reads=[B("recv2")], writes=[B("cgb")], dma=True)
    S.op(DVE, lambda e: e.tensor_scalar(carry, cga, sel[:, 0:1], None, ALU.mult), reads=[B("cga"), B("l1sel")], writes=[B("l1carry")])
    S.op(DVE, lambda e: e.scalar_tensor_tensor(carry, cgb, sel[:, 1:2], carry, ALU.mult, ALU.add), reads=[B("cgb"), B("l1carry"), B("l1sel")], writes=[B("l1carry")])

    def gate_chunk(ct, q):
        bank, bb = next_pb()
        for k in range(8):
            S.op(PE, lambda e, k=k, bank=bank: e.matmul(bank[:, 0:512], wt[:, k, :], hT[:, k, q * 512:(q + 1) * 512],
                                                       start=(k == 0), stop=(k == 7)),
                 reads=[B("l1wt"), B("l1hT")], writes=[bb] if k == 0 else (), partial=[bb] if k else ())
        S.op(ACT, lambda e, bank=bank: e.activation(out=sg[:, q * 512:(q + 1) * 512], in_=bank[:, 0:512], func=AF.Silu),
             reads=[bb], writes=[B("sg%d" % q)], partial=[B("xr")] if ct == 0 else ())

    load_wt(1024)
    for q in (3, 2, 1, 0):
        gate_chunk(0, q)
    for ct in range(8):
        if ct + 1 < 8:
            load_wt(1024 + (ct + 1) * 128)

        def fin_b(q, H, bH, ct=ct):
            sl = slice(q * 512, (q + 1) * 512)
            S.op(DVE, lambda e: e.tensor_tensor(H, H, yT[:, ct, sl], ALU.add), reads=[bH, B("yT%d" % ct)], writes=[bH])
            S.op(DVE, lambda e: e.tensor_tensor(yT[:, ct, sl], H, sg[:, sl], ALU.mult), reads=[bH, B("sg%d" % q)], partial=[B("yT%d" % ct)])
            if ct + 1 < 8:
                gate_chunk(ct + 1, q)
        drain(batch(1, ct, fin_b))

    S.barrier()
    ring2 = [w4k, (xc[:, 6, :].bitcast(F32), [B("xc6")]), (xc[:, 7, :].bitcast(F32), [B("xc7")])]
    emit_adaln2(C, "l1", [(pb[0], bpb[0]), (pb[1], bpb[1])],
                [None, None, (gaterow, B("t23"), False)], crep_b, B("t01"), brow, (2,), ring2, bring1, tag="b")
    emit_weight_bf16(C, "l1_w_out", w_out, B("l1hT"), 8, 1024, ring2, scale_row=gaterow, bscale=B("t23"), chunk=1024)
    S.op(SP, lambda e: e.dma_start(out=lnrows[:, 0, :], in_=d_lng[:, :]), writes=[B("xc4")], partial=[B("l1lnrows")], dma=True)
    S.op(SP, lambda e: e.dma_start(out=lnrows[:, 1, :], in_=d_lnb[:, :]), writes=[B("xc5")], partial=[B("l1lnrows")], dma=True)
    yTb = [B("yT%d" % k) for k in range(8)]
    S.barrier()
    zall = xc.rearrange("p a b -> p (a b)").bitcast(F32)
    lnsm = [(C.asb("l1_st6_%d" % i, [128, 12]), C.asb("l1_mv_%d" % i, [128, 2]), C.asb("l1_sd_%d" % i, [128, 4]), "_%d" % i) for i in range(3)]
    for t in range(NT):
        z = zall[:, (t % 3) * 1024:(t % 3 + 1) * 1024]
        bz = B("l1z%d" % (t % 3))
        for cg in range(2):
            bank, bb = next_pb()
            for k in range(8):
                S.op(PE, lambda e, k=k, cg=cg, t=t, bank=bank: e.matmul(bank[:, 0:512], yT[:, k, t * 128:(t + 1) * 128], w_out[:, k, cg * 512:(cg + 1) * 512],
                                                                       start=(k == 0), stop=(k == 7)),
                     reads=[yTb[k], B("l1hT")], writes=[bb] if k == 0 else (), partial=[bb] if k else ())
            S.op(DVE, lambda e, cg=cg, t=t, bank=bank, z=z: e.scalar_tensor_tensor(z[:, cg * 512:(cg + 1) * 512], xres[:, t, cg * 512:(cg + 1) * 512], ALPHA,
                                                                             bank[:, 0:512], ALU.mult, ALU.add),
                 reads=[B("x%d" % t), bb], writes=[bz] if cg == 0 else (), partial=[bz] if cg else ())
        emit_ln_tile(C, "l1", z, bz, xres[:, t, :], B("x%d" % t), lnrows[:, 0, :], lnrows[:, 1, :], B("l1lnrows"), small=lnsm[t % 3])
        S.op(SP, lambda e, t=t: e.dma_start(out=out_dram[t * 128:(t + 1) * 128, :], in_=xres[:, t, :]), reads=[B("x%d" % t)], dma=True)


def l1_inputs(c, inp):
    b, half = c // 2, c % 2
    dA, dB = (0, 1) if half == 0 else (1, 0)
    cw = inp["od_conv_w"][0]
    zero = np.zeros((1, 1024), np.float32)
    w5 = np.concatenate([cw, zero], 0) if half == 0 else np.concatenate([zero, cw[::-1]], 0)
    w5 = np.ascontiguousarray(w5.reshape(5, 8, 128).transpose(2, 1, 0)).reshape(128, 40)
    wa, wx = inp["od_w_a"][0], inp["od_w_x"][0]
    wg = np.stack([wa[dA], wx[dA], wa[dB], wx[dB]], 0)
    wg = np.ascontiguousarray(wg.transpose(2, 0, 1, 3)).reshape(128, 4096)
    ba, bx = inp["od_b_a"][0], inp["od_b_x"][0]
    bgm = np.stack([ba[dA], bx[dA], ba[dB], bx[dB]], 0).reshape(4, 8, 128)
    bgm = np.ascontiguousarray(bgm.transpose(2, 0, 1)).reshape(128, 32)
    lam = inp["od_lam"][0]
    lamm = np.stack([lam[dA], lam[dB]], 0).reshape(2, 8, 128)
    lamm = np.ascontiguousarray(lamm.transpose(2, 0, 1)).reshape(128, 16)
    return {
        "sel": rep(np.array([0.0, 1.0], np.float32) if half == 0 else np.array([1.0, 0.0], np.float32)),
        "l1_cT": np.ascontiguousarray(inp["c"][b].reshape(8, 128).T),
        "l1_ada_w": np.ascontiguousarray(inp["ada_w"][1]),
        "l1_ada_b": np.ascontiguousarray(inp["ada_b"][1][None, :]),
        "l1_w_in": np.ascontiguousarray(inp["od_w_in"][0]),
        "l1_w5": w5,
        "l1_cb": np.ascontiguousarray(inp["od_conv_b"][0].reshape(8, 128).T),
        "l1_wg": wg,
        "l1_bg": bgm,
        "l1_lam": lamm,
        "l1_w_out": np.ascontiguousarray(inp["od_w_out"][0]),
        "l1_ln_g": rep(inp["ln_g"][1]),
        "l1_ln_b": rep(inp["ln_b"][1]),
    }


def build_fused():
    nc = bass.Bass("TRN2", target_bir_lowering=False)
    with contextlib.ExitStack() as st:
        C = Ctx(nc, st)
        x_d = C.dram("x", [2176, 1024], F32, "ExternalInput")
        o_d = C.dram("out", [2048, 1024], F32, "ExternalOutput")
        C.send1 = nc.dram_tensor("send1", [2, 1024], F32).ap()
        C.recv1 = nc.dram_tensor("recv1", [4, 1024], F32).ap()
        C.send2 = nc.dram_tensor("send2", [128, 8], F32).ap()
        C.recv2 = nc.dram_tensor("recv2", [256, 8], F32).ap()
        alloc_common(C)
        emit_consts(C)
        C.make_arena(ARENA_WORDS)
        emit_l0(C, x_d, None)
        p0 = C.apeak
        emit_l1(C, o_d)
        print("arena words: l0 peak", p0, "overall peak", C.apeak, "of", C.asize, "sbuf remaining", nc.sbuf_bytes_remaining)
        C.S.emit()
    return nc


def rep(v, n=128):
    return np.ascontiguousarray(np.broadcast_to(np.asarray(v, np.float32)[None, :], (n, v.shape[0])))


def core_tokens(half):
    if half == 0:
        return np.arange(0, 2048), np.arange(2048, 2176)
    return np.arange(4095, 2047, -1), np.arange(2047, 1919, -1)


def l0_masks():
    qi = np.arange(128)[:, None]
    kj = np.arange(384)[None, :]
    valid = np.abs(kj - 128 - qi) <= 128
    m1 = np.where(valid, 0.0, NEG).astype(np.float32)
    m0 = np.where(valid & (kj >= 128), 0.0, NEG).astype(np.float32)
    return np.ascontiguousarray(np.concatenate([m0, m1], axis=1))


def l0_inputs(c, inp, x_override=None):
    b, half = c // 2, c % 2
    own, halo = core_tokens(half)
    idx = np.concatenate([own, halo])
    x = inp["x"][b][idx]
    pos = inp["positions"][b][idx].astype(np.int32).reshape(17, 128).T
    w_in = inp["ev_w_in"][0]
    qcols = np.concatenate([np.arange(h * 64, (h + 1) * 64) for h in PERM])
    cols = np.concatenate([qcols, np.arange(512, 1792), 1792 + qcols, np.arange(2304, 2816)])
    w_out = inp["ev_w_out"][0]
    rows = np.concatenate([qcols, np.arange(512, 1024)])
    sgw = inp["ev_sg_w"][0]
    sgb = inp["ev_sg_b"][0]
    if half == 1:
        sgw = sgw[:, ::-1, ::-1]
        sgb = sgb[:, ::-1]
    sgwT = np.ascontiguousarray(np.transpose(sgw, (2, 0, 1))).reshape(128, 1024)
    return {
        "x": np.ascontiguousarray(x),
        "ident": np.eye(128, dtype=np.float32),
        "l0_pos": np.ascontiguousarray(pos),
        "l0_cT": np.ascontiguousarray(inp["c"][b].reshape(8, 128).T),
        "l0_ada_w": np.ascontiguousarray(inp["ada_w"][0]),
        "l0_ada_b": np.ascontiguousarray(inp["ada_b"][0][None, :]),
        "l0_w_in": np.ascontiguousarray(w_in[:, cols]),
        "l0_w_out": np.ascontiguousarray(w_out[rows, :]),
        "l0_sink": rep(inp["ev_sink"][0][PERM]),
        "l0_sg_ln_g": rep(inp["ev_sg_ln_g"][0]),
        "l0_sg_ln_b": rep(inp["ev_sg_ln_b"][0]),
        "l0_sg_wT": sgwT,
        "l0_sg_b": np.ascontiguousarray(sgb.T),
        "l0_ln_g": rep(inp["ln_g"][0]),
        "l0_ln_b": rep(inp["ln_b"][0]),
        "l0_mask": l0_masks(),
    }


def run_l0(inp):
    nc = build_l0()
    in_maps = [l0_inputs(c, inp) for c in range(8)]
    res = run_bass_kernel_spmd(nc, in_maps, core_ids=list(range(8)))
    return [r["out"] for r in res.results]


def assemble(outs):
    full = np.zeros((4, 4096, 1024), np.float32)
    for c in range(8):
        b, half = c // 2, c % 2
        own, _ = core_tokens(half)
        full[b, own] = outs[c]
    return full


def fused_inputs(c, inp):
    d = l0_inputs(c, inp)
    d.update(l1_inputs(c, inp))
    return d


def kernel(**inputs):
    inp = {k: np.asarray(v) for k, v in inputs.items()}
    nc = build_fused()
    in_maps = [fused_inputs(c, inp) for c in range(8)]
    res = run_bass_kernel_spmd(nc, in_maps, core_ids=list(range(8)))
    return assemble([r["out"] for r in res.results])
```

```python
import contextlib
import numpy as np
import concourse.bass as bass
import concourse.mybir as mybir
from concourse.bass_utils import run_bass_kernel_spmd

F32 = mybir.dt.float32
BF16 = mybir.dt.bfloat16
I32 = mybir.dt.int32
ALU = mybir.AluOpType
AF = mybir.ActivationFunctionType
AX = mybir.AxisListType

PE, ACT, DVE, POOL, SP = "tensor", "scalar", "vector", "gpsimd", "sync"
ENGS = (PE, ACT, DVE, POOL, SP)

D = 1024
T = 2048
NT = 16
ALPHA = 4.0 ** 0.25
LN_EPS = 1e-5
PERM = [0, 4, 1, 5, 2, 6, 3, 7]
INV_FREQ = [float(np.float32(500000.0) ** np.float32(-i / 8.0)) for i in range(8)]
NEG = -30000.0
ARENA_WORDS = 34880


class Buf:
    __slots__ = ("name", "w", "r", "excl")

    def __init__(self, name):
        self.name = name
        self.w = {}
        self.r = {}
        self.excl = False


class Op:
    __slots__ = ("eng", "fn", "deps", "sig", "dma", "idx")

    def __init__(self, eng, fn):
        self.eng = eng
        self.fn = fn
        self.deps = {}
        self.sig = None
        self.dma = None
        self.idx = None


class Sched:
    def __init__(self, nc, n_dma_sems=8, n_cc_sems=2):
        self.nc = nc
        self.ops = {e: [] for e in ENGS}
        self.n_dma_sems = n_dma_sems
        self.n_cc = n_cc_sems
        nt = n_dma_sems + n_cc_sems
        self.sem_inc = [16] * n_dma_sems + [1] * n_cc_sems
        self.dma_val = [0] * nt
        self.dma_last = [None] * nt
        self.dma_rr = 0
        self.cc_rr = 0
        self.pending = {e: [] for e in ENGS}

    def barrier(self):
        toks = []
        for e in ENGS:
            if self.ops[e]:
                last = [o for o in self.ops[e] if o.dma is None]
                if last:
                    toks.append(("op", last[-1]))
        for t in self.dma_last[:self.n_dma_sems]:
            if t is not None:
                toks.append(t)
        for e in ENGS:
            self.pending[e] = list(toks)

    @staticmethod
    def _key(tok):
        return tok[1].eng if tok[0] == "op" else ("dma", tok[1])

    @staticmethod
    def _newer(a, b):
        if a is None:
            return b
        if a[0] == "op":
            return b if b[1].idx > a[1].idx else a
        return b if b[2] > a[2] else a

    def _add_dep(self, op, tok):
        if tok[0] == "op" and tok[1] is op:
            return
        k = self._key(tok)
        op.deps[k] = self._newer(op.deps.get(k), tok)

    def op(self, eng, fn, reads=(), writes=(), partial=(), dma=False, cc=False):
        o = Op(eng, fn)
        o.idx = len(self.ops[eng])
        had_pending = bool(self.pending[eng])
        if had_pending:
            for t in self.pending[eng]:
                self._add_dep(o, t)
            self.pending[eng] = []
        if cc:
            dma = True
            si = self.n_dma_sems + self.cc_rr
            self.cc_rr = (self.cc_rr + 1) % self.n_cc
        elif dma:
            si = self.dma_rr
            self.dma_rr = (self.dma_rr + 1) % self.n_dma_sems
        if dma:
            prev = self.dma_last[si]
            if prev is not None:
                self._add_dep(o, prev)
            self.dma_val[si] += self.sem_inc[si]
            o.dma = (si, self.dma_val[si])
            tok = ("dma", si, self.dma_val[si])
            self.dma_last[si] = tok
        else:
            tok = ("op", o)
        for b in reads:
            for t in b.w.values():
                self._add_dep(o, t)
            if b.excl:
                for kk, t in b.r.items():
                    if kk != eng:
                        self._add_dep(o, t)
        for b in list(writes) + list(partial):
            for t in b.w.values():
                self._add_dep(o, t)
            for t in b.r.values():
                self._add_dep(o, t)
        if not dma and eng == PE and not had_pending:
            raw = -1
            for b in reads:
                t = b.w.get(eng)
                if t is not None and t[0] == "op":
                    raw = max(raw, t[1].idx)
            if eng in o.deps:
                if raw < 0:
                    del o.deps[eng]
                else:
                    o.deps[eng] = ("op", self.ops[eng][raw])
        k = self._key(tok)
        for b in reads:
            b.r[k] = self._newer(b.r.get(k), tok)
        for b in writes:
            b.w = {k: tok}
            b.r = {}
        for b in partial:
            b.w[k] = self._newer(b.w.get(k), tok)
        self.ops[eng].append(o)
        return o

    def emit(self, final_eng=SP):
        nc = self.nc
        needed = set()
        for e in ENGS:
            for o in self.ops[e]:
                for t in o.deps.values():
                    if t[0] == "op":
                        needed.add(id(t[1]))
        for e in ENGS:
            c = 0
            for o in self.ops[e]:
                if o.dma is None and id(o) in needed:
                    c += 1
                    o.sig = c
        with contextlib.ExitStack() as st:
            esem = {e: st.enter_context(nc.semaphore("s_" + e)) for e in ENGS}
            dsem = [st.enter_context(nc.semaphore("d_%d" % i)) for i in range(self.n_dma_sems + self.n_cc)]
            block = st.enter_context(nc.Block())

            def run(eng_name, engine):
                known = {}
                for o in self.ops[eng_name]:
                    for k, t in o.deps.items():
                        if t[0] == "op":
                            sem, val = esem[t[1].eng], t[1].sig
                        else:
                            sem, val = dsem[t[1]], t[2]
                        if known.get(k, 0) >= val:
                            continue
                        known[k] = val
                        engine.wait_ge(sem, val)
                    ins = o.fn(engine)
                    if o.dma is not None:
                        ins.then_inc(dsem[o.dma[0]], self.sem_inc[o.dma[0]])
                    elif o.sig is not None:
                        ins.then_inc(esem[eng_name], 1)
                if eng_name == final_eng:
                    for i in range(self.n_dma_sems + self.n_cc):
                        if self.dma_val[i] > 0:
                            engine.wait_ge(dsem[i], self.dma_val[i])

            @block.tensor
            def _(e):
                run(PE, e)

            @block.scalar
            def _(e):
                run(ACT, e)

            @block.vector
            def _(e):
                run(DVE, e)

            @block.gpsimd
            def _(e):
                run(POOL, e)

            @block.sync
            def _(e):
                run(SP, e)


class Ctx:
    def __init__(self, nc, st):
        self.nc = nc
        self.st = st
        self.S = Sched(nc)
        self.bufs = {}
        self.arena = None
        self.aoff = 0
        self.asize = 0
        self.apeak = 0

    def make_arena(self, words):
        self.arena = self.sb("arena", [128, words])
        self.asize = words
        self.aoff = 0

    def areset(self):
        self.aoff = 0

    def asb(self, name, shape, dt=F32):
        if self.arena is None:
            return self.sb(name, shape, dt)[:]
        esz = {F32: 4, I32: 4, BF16: 2}[dt]
        n = 1
        for d_ in shape[1:]:
            n *= d_
        words = (n * esz + 3) // 4
        off = self.aoff
        self.aoff += words
        self.apeak = max(self.apeak, self.aoff)
        assert self.aoff <= self.asize, ("arena overflow", name, self.aoff, self.asize)
        ap = self.arena[:, off:off + words]
        if dt != F32:
            ap = ap.bitcast(dt)[:, 0:n]
        if shape[0] != 128:
            ap = ap[0:shape[0], :]
        if len(shape) == 3:
            ap = ap.rearrange("p (a b) -> p a b", b=shape[2])
        elif len(shape) == 4:
            ap = ap.rearrange("p (a b c) -> p a b c", b=shape[2], c=shape[3])
        return ap

    def sb(self, name, shape, dt=F32):
        return self.st.enter_context(self.nc.sbuf_tensor("sb_" + name, shape, dt))

    def ps(self, name, shape, dt=F32):
        return self.st.enter_context(self.nc.psum_tensor("ps_" + name, shape, dt))

    def B(self, name):
        b = self.bufs.get(name)
        if b is None:
            b = self.bufs[name] = Buf(name)
        return b

    def dram(self, name, shape, dt, kind):
        if not hasattr(self, "_dram"):
            self._dram = {}
        if name not in self._dram:
            self._dram[name] = self.nc.dram_tensor(name, list(shape), dt, kind=kind).ap()
        return self._dram[name]


def v3(ap, d):
    return ap.rearrange("p (h d) -> p h d", d=d)


def bc_mid(ap2, n):
    return ap2.unsqueeze(1).broadcast_to([ap2.shape[0], n, ap2.shape[1]])


def bc_last(ap2, n):
    return ap2.unsqueeze(2).broadcast_to([ap2.shape[0], ap2.shape[1], n])


def emit_consts(C):
    S = C.S
    C.id32 = C.sb("id32", [128, 128])
    C.idb = C.sb("idb", [128, 128], BF16)
    C.ones32 = C.sb("ones32", [128, 128])
    d_id = C.dram("ident", [128, 128], F32, "ExternalInput")
    S.op(SP, lambda e: e.dma_start(out=C.id32[:], in_=d_id[:, :]), writes=[C.B("id32")], dma=True)
    S.op(DVE, lambda e: e.tensor_copy(C.idb[:], C.id32[:]), reads=[C.B("id32")], writes=[C.B("idb")])
    S.op(DVE, lambda e: e.memset(C.ones32[:], 1.0), writes=[C.B("ones32")])


def emit_adaln(C, lname, pbanks, rows_out, crep, bcrep, brow_ap=None, cgs=range(6), tag="", ring=None, bring=None):
    S = C.S
    d_c = C.dram(lname + "_cT", [128, 8], F32, "ExternalInput")
    d_w = C.dram(lname + "_ada_w", [1024, 3072], F32, "ExternalInput")
    d_b = C.dram(lname + "_ada_b", [1, 3072], F32, "ExternalInput")
    cT = C.asb(lname + tag + "cT", [128, 8])
    brow = brow_ap if brow_ap is not None else C.asb(lname + tag + "brow", [1, 1024])
    bcT = C.B(lname + tag + "cT")
    bbrows = [C.B(lname + "brow0"), C.B(lname + "brow1")]
    S.op(SP, lambda e: e.dma_start(out=cT[:], in_=d_c[:, :]), writes=[bcT], dma=True)
    S.op(ACT, lambda e: e.activation(out=cT[:], in_=cT[:], func=AF.Silu), reads=[bcT], writes=[bcT])
    S.op(DVE, lambda e: e.tensor_copy(crep, bc_last(cT[:], 128)), reads=[bcT], writes=[bcrep])
    if ring is None:
        ring = C.ring3
    i = 0
    for cg in cgs:
        pb, bpb = pbanks[cg % 2]
        bbrow = bbrows[cg % 2]
        bsl = slice((cg % 2) * 512, (cg % 2 + 1) * 512)
        S.op(SP, lambda e, cg=cg, bsl=bsl: e.dma_start(out=brow[0:1, bsl], in_=d_b[0:1, cg * 512:(cg + 1) * 512]), writes=[bbrow], dma=True)
        for k in range(8):
            sap, sbufs = ring[i % len(ring)]
            i += 1
            S.op(SP, lambda e, k=k, cg=cg, sap=sap: e.dma_start(
                out=sap, in_=d_w[k * 128:(k + 1) * 128, cg * 512:(cg + 1) * 512]),
                writes=sbufs, dma=True)
            if bring is not None:
                bap, bbufs = bring[(i - 1) % len(bring)]
                ceng = (ACT, DVE, ACT, POOL)[i % 4]
                if ceng == ACT:
                    S.op(ACT, lambda e, sap=sap, bap=bap: e.activation(out=bap, in_=sap, func=AF.Copy), reads=sbufs, writes=bbufs)
                else:
                    S.op(ceng, lambda e, sap=sap, bap=bap: e.tensor_copy(bap, sap), reads=sbufs, writes=bbufs)
                S.op(PE, lambda e, k=k, bap=bap, pb=pb: e.matmul(pb[:, 0:512], crep[:, k, :], bap, start=(k == 0), stop=False),
                     reads=[bcrep] + bbufs, partial=[bpb] if k else (), writes=[bpb] if k == 0 else ())
                continue
            S.op(PE, lambda e, k=k, sap=sap, pb=pb: e.matmul(pb[:, 0:512], crep[:, k, :], sap,
                                                            start=(k == 0), stop=False),
                 reads=[bcrep] + sbufs, partial=[bpb] if k else (), writes=[bpb] if k == 0 else ())
        S.op(PE, lambda e, cg=cg, pb=pb, bsl=bsl: e.matmul(pb[:, 0:512], C.ones32[0:1, :], brow[0:1, bsl],
                                                  start=False, stop=True),
             reads=[C.B("ones32"), bbrow], partial=[bpb])
        dst, bdst, add_one = rows_out[cg // 2]
        half = cg % 2
        if add_one:
            S.op(ACT, lambda e, dst=dst, half=half, pb=pb: e.activation(
                out=dst[:, half * 512:(half + 1) * 512], in_=pb[:, 0:512], func=AF.Identity, bias=C.one_col[:, 0:1], scale=1.0),
                reads=[bpb, C.B("one_col")], partial=[bdst])
        else:
            S.op(ACT, lambda e, dst=dst, half=half, pb=pb: e.activation(
                out=dst[:, half * 512:(half + 1) * 512], in_=pb[:, 0:512], func=AF.Copy),
                reads=[bpb], partial=[bdst])


def emit_adaln2(C, lname, pbanks, rows_out, crep, bcrep, brow, rows, ring4k, bring2k, tag=""):
    S = C.S
    d_c = C.dram(lname + "_cT", [128, 8], F32, "ExternalInput")
    d_w = C.dram(lname + "_ada_w", [1024, 3072], F32, "ExternalInput")
    d_b = C.dram(lname + "_ada_b", [1, 3072], F32, "ExternalInput")
    cT = C.asb(lname + tag + "cT", [128, 8])
    bcT = C.B(lname + tag + "cT")
    bbrow = C.B(lname + tag + "brow")
    S.op(SP, lambda e: e.dma_start(out=cT[:], in_=d_c[:, :]), writes=[bcT], dma=True)
    S.op(ACT, lambda e: e.activation(out=cT[:], in_=cT[:], func=AF.Silu), reads=[bcT], writes=[bcT])
    S.op(DVE, lambda e: e.tensor_copy(crep, bc_last(cT[:], 128)), reads=[bcT], writes=[bcrep])
    i = 0
    for r in rows:
        S.op(SP, lambda e, r=r: e.dma_start(out=brow[0:1, 0:1024], in_=d_b[0:1, r * 1024:(r + 1) * 1024]), writes=[bbrow], dma=True)
        for k in range(8):
            sap, sbufs = ring4k[i % len(ring4k)]
            bap, bbufs = bring2k[i % len(bring2k)]
            S.op(SP, lambda e, k=k, r=r, sap=sap: e.dma_start(out=sap, in_=d_w[k * 128:(k + 1) * 128, r * 1024:(r + 1) * 1024]),
                 writes=sbufs, dma=True)
            if i % 2:
                S.op(ACT, lambda e, sap=sap, bap=bap: e.activation(out=bap, in_=sap, func=AF.Copy), reads=sbufs, writes=bbufs)
            else:
                S.op(DVE, lambda e, sap=sap, bap=bap: e.tensor_copy(bap, sap), reads=sbufs, writes=bbufs)
            i += 1
            for hh in range(2):
                pb, bpb = pbanks[hh]
                S.op(PE, lambda e, k=k, bap=bap, pb=pb, hh=hh: e.matmul(pb[:, 0:512], crep[:, k, :], bap[:, hh * 512:(hh + 1) * 512],
                                                                      start=(k == 0), stop=False),
                     reads=[bcrep] + bbufs, partial=[bpb] if k else (), writes=[bpb] if k == 0 else ())
        dst, bdst, add_one = rows_out[r]
        for hh in range(2):
            pb, bpb = pbanks[hh]
            S.op(PE, lambda e, pb=pb, hh=hh: e.matmul(pb[:, 0:512], C.ones32[0:1, :], brow[0:1, hh * 512:(hh + 1) * 512], start=False, stop=True),
                 reads=[C.B("ones32"), bbrow], partial=[bpb])
            if add_one:
                S.op(ACT, lambda e, dst=dst, hh=hh, pb=pb: e.activation(out=dst[:, hh * 512:(hh + 1) * 512], in_=pb[:, 0:512], func=AF.Identity,
                                                                      bias=C.one_col[:, 0:1], scale=1.0),
                     reads=[bpb, C.B("one_col")], partial=[bdst])
            else:
                S.op(ACT, lambda e, dst=dst, hh=hh, pb=pb: e.activation(out=dst[:, hh * 512:(hh + 1) * 512], in_=pb[:, 0:512], func=AF.Copy),
                     reads=[bpb], partial=[bdst])


def emit_weight_bf16(C, dname, dst, bdst, nk, ncols, ring, scale_row=None, bscale=None, chunk=512):
    S = C.S
    d_w = C.dram(dname, [nk * 128, ncols], F32, "ExternalInput")
    i = 0
    engs = [DVE, ACT] if scale_row is None else [DVE]
    for k in range(nk):
        for c0 in range(0, ncols, chunk):
            n = min(chunk, ncols - c0)
            sap, sbufs = ring[i % len(ring)]
            S.op(SP, lambda e, k=k, c0=c0, n=n, sap=sap: e.dma_start(
                out=sap[:, 0:n], in_=d_w[k * 128:(k + 1) * 128, c0:c0 + n]), writes=sbufs, dma=True)
            eng = engs[i % len(engs)]
            if scale_row is not None:
                S.op(eng, lambda e, k=k, c0=c0, n=n, sap=sap: e.tensor_tensor(
                    dst[:, k, c0:c0 + n], sap[:, 0:n], scale_row[:, c0:c0 + n], ALU.mult),
                    reads=sbufs + [bscale], partial=[bdst])
            elif eng == ACT:
                S.op(ACT, lambda e, k=k, c0=c0, n=n, sap=sap: e.activation(
                    out=dst[:, k, c0:c0 + n], in_=sap[:, 0:n], func=AF.Copy), reads=sbufs, partial=[bdst])
            else:
                S.op(eng, lambda e, k=k, c0=c0, n=n, sap=sap: e.tensor_copy(
                    dst[:, k, c0:c0 + n], sap[:, 0:n]), reads=sbufs, partial=[bdst])
            i += 1


def emit_ln_tile(C, pfx, zt, bz, xdst, bxdst, lng, lnb, brows, small=None):
    S = C.S
    if small is None:
        st6, mv, sd = C.ln_st6, C.ln_mv, C.ln_sd
        bst6, bmv, bsd = C.B("ln_st6"), C.B("ln_mv"), C.B("ln_sd")
    else:
        st6, mv, sd, sfx = small
        bst6, bmv, bsd = C.B("ln_st6" + sfx), C.B("ln_mv" + sfx), C.B("ln_sd" + sfx)
    for hh in range(2):
        S.op(DVE, lambda e, hh=hh: e.bn_stats(st6[:, hh * 6:(hh + 1) * 6], zt[:, hh * 512:(hh + 1) * 512]), reads=[bz],
             writes=[bst6] if hh == 0 else (), partial=[bst6] if hh else ())
    S.op(DVE, lambda e: e.bn_aggr(mv[:, 0:2], st6[:]), reads=[bst6], writes=[bmv])
    S.op(ACT, lambda e: e.activation(out=sd[:, 0:1], in_=mv[:, 1:2], func=AF.Sqrt, bias=C.eps_col[:, 0:1], scale=1.0),
         reads=[bmv, C.B("eps_col")], writes=[bsd])
    S.op(DVE, lambda e: e.reciprocal(sd[:, 1:2], sd[:, 0:1]), reads=[bsd], partial=[bsd])
    S.op(DVE, lambda e: e.scalar_tensor_tensor(sd[:, 2:3], mv[:, 0:1], -1.0, sd[:, 1:2], ALU.mult, ALU.mult),
         reads=[bmv, bsd], partial=[bsd])
    S.op(ACT, lambda e: e.activation(out=zt[:], in_=zt[:], func=AF.Identity, scale=sd[:, 1:2], bias=sd[:, 2:3]),
         reads=[bz, bsd], writes=[bz])
    S.op(POOL, lambda e: e.tensor_tensor(zt[:], zt[:], lng[:], ALU.mult), reads=[bz, brows], writes=[bz])
    S.op(POOL, lambda e: e.tensor_tensor(xdst, zt[:], lnb[:], ALU.add), reads=[bz, brows], writes=[bxdst])


def emit_l0(C, x_dram, out_dram):
    S, nc = C.S, C.nc
    B = C.B
    xres = C.xres
    d_pos = C.dram("l0_pos", [128, 17], I32, "ExternalInput")
    d_sink = C.dram("l0_sink", [128, 8], F32, "ExternalInput")
    d_sgg = C.dram("l0_sg_ln_g", [128, 512], F32, "ExternalInput")
    d_sgb = C.dram("l0_sg_ln_b", [128, 512], F32, "ExternalInput")
    d_sgw = C.dram("l0_sg_wT", [128, 1024], F32, "ExternalInput")
    d_sgbias = C.dram("l0_sg_b", [128, 8], F32, "ExternalInput")
    d_lng = C.dram("l0_ln_g", [128, 1024], F32, "ExternalInput")
    d_lnb = C.dram("l0_ln_b", [128, 1024], F32, "ExternalInput")
    d_mask = C.dram("l0_mask", [128, 768], F32, "ExternalInput")
    w_in = C.asb("l0_w_in", [128, 8, 2816], BF16)
    w_out = C.asb("l0_w_out", [128, 8, 1024], BF16)
    sgw = C.asb("l0_sgw", [128, 8, 128], BF16)
    modrow = C.asb("l0_modrow", [128, 2, 1024])
    lnrows = C.asb("l0_lnrows", [128, 2, 1024])
    sgrows = C.asb("l0_sgrows", [128, 2, 512])
    sgbias = C.asb("l0_sgbias", [128, 8])
    nsink = C.asb("l0_nsink", [128, 8])
    sink = C.asb("l0_sink", [128, 8])
    mask = C.asb("l0_mask", [128, 768], BF16)
    kT = C.asb("l0_kT", [128, 18, 128], BF16)
    V = C.asb("l0_V", [128, 18, 128], BF16)
    posi = C.asb("l0_posi", [128, 17], I32)
    posf = C.asb("l0_posf", [128, 17])
    invf = C.asb("l0_invf", [128, 8])
    ang = C.asb("l0_ang", [128, 17, 8])
    angk = C.asb("l0_angk", [128, 17, 8])
    angi = C.asb("l0_angi", [128, 17, 8], I32)
    C2 = C.asb("l0_C2", [128, 17, 16])
    S2 = C.asb("l0_S2", [128, 17, 16])
    h32 = C.asb("l0_h32", [128, 1024])
    mask32 = h32[:, 0:768]
    hbf = C.asb("l0_hbf", [128, 1024], BF16)
    hT = C.asb("l0_hT", [128, 1024], BF16)
    qkr = C.asb("l0_qkr", [128, 640], BF16)
    qT = C.asb("l0_qT", [128, 2, 512], BF16)
    su = C.asb("l0_su", [128, 512], BF16)
    vn = C.asb("l0_vn", [128, 512])
    vnb = C.asb("l0_vnb", [128, 512], BF16)
    st8 = C.asb("l0_st8", [128, 8, 8])
    sgl = C.asb("l0_sgl", [128, 2, 1024], BF16)
    y = C.asb("l0_y", [128, 2, 1024], BF16)
    yT = C.asb("l0_yT", [128, 1024], BF16)
    P = C.asb("l0_P", [128, 2, 384], BF16)
    PT = C.asb("l0_PT", [128, 2, 384], BF16)
    att = C.asb("l0_att", [128, 6, 8])
    z = C.asb("l0_z", [128, 1024])
    gaterow = z
    yat = vn
    rot = C.asb("l0_rot", [128, 2, 160])
    sq = z[:, 512:1024]
    ptr = C.ps_ptr
    pb = C.ps_f32[0:2]
    sbk = C.ps_f32[2:4]
    ob = C.ps_f32[4]
    svb = C.ps_f32[5]
    bptr = [B("ptr0"), B("ptr1")]
    bpb = [B("pb0"), B("pb1")]
    bsbk = [B("pb2"), B("pb3")]
    cnt = {"ptr": 0, "pb": 0}
    for bn in ("ptr0", "ptr1", "pb0", "pb1", "pb2", "pb3", "ob", "svb"):
        B(bn).excl = True

    def next_ptr():
        i = cnt["ptr"] % 2
        cnt["ptr"] += 1
        return ptr[i], bptr[i]

    def next_pb():
        i = cnt["pb"] % 2
        cnt["pb"] += 1
        return pb[i], bpb[i]

    w4k = (C.wstage[:].rearrange("p a b -> p (a b)")[:, 0:1024], [C.bwstage[0], C.bwstage[1]])
    ring0 = [w4k,
             (y.rearrange("p a b -> p (a b)").bitcast(F32), [B("y_a0"), B("y_s0"), B("y_a1"), B("y_s1")]),
             (sgl.rearrange("p a b -> p (a b)").bitcast(F32), [B("sgl_a0"), B("sgl_s0"), B("sgl_a1"), B("sgl_s1")])]
    bring0 = [(hT[:], [B("hT")]), (hbf[:], [B("hbf")]), (yT[:], [B("yT")]), (qT.rearrange("p a b -> p (a b)"), [B("qT0"), B("qT1")])]
    S.op(SP, lambda e: e.dma_start(out=xres[:, 0, :], in_=x_dram[0:128, :]), writes=[B("x0")], dma=True)
    S.op(SP, lambda e: e.dma_start(out=posi[:], in_=d_pos[:, :]), writes=[B("posi")], dma=True)
    S.op(SP, lambda e: e.dma_start(out=sink[:], in_=d_sink[:, :]), writes=[B("sink")], dma=True)
    S.op(SP, lambda e: e.dma_start(out=mask32, in_=d_mask[:, :]), writes=[B("h32")], dma=True)
    S.op(POOL, lambda e: e.tensor_copy(mask[:], mask32), reads=[B("h32")], writes=[B("mask")])
    S.op(SP, lambda e: e.dma_start(out=sgbias[:], in_=d_sgbias[:, :]), writes=[B("sgbias")], dma=True)
    crep_b = h32[:, 0:512].bitcast(BF16).rearrange("p (k m) -> p k m", m=128)
    brow0 = C.asb("l0_brow", [1, 1024])
    emit_adaln2(C, "l0", [(pb[0], bpb[0]), (pb[1], bpb[1])],
                [(modrow[:, 0, :], B("l0shift"), False), (modrow[:, 1, :], B("l0scale"), True), (gaterow[:], B("z"), False)],
                crep_b, B("h32"), brow0, (0, 1, 2), ring0, bring0)
    emit_weight_bf16(C, "l0_w_in", w_in, B("l0w_in"), 8, 2816, ring0, chunk=1024)
    emit_weight_bf16(C, "l0_w_out", w_out, B("l0w_out"), 8, 1024, ring0, scale_row=gaterow, bscale=B("z"), chunk=1024)
    S.op(SP, lambda e: e.dma_start(out=sgrows[:, 0, :], in_=d_sgg[:, :]), partial=[B("sgrows")], dma=True)
    S.op(SP, lambda e: e.dma_start(out=sgrows[:, 1, :], in_=d_sgb[:, :]), partial=[B("sgrows")], dma=True)
    S.op(SP, lambda e: e.dma_start(out=lnrows[:, 0, :], in_=d_lng[:, :]), partial=[B("l0lnrows")], dma=True)
    S.op(SP, lambda e: e.dma_start(out=lnrows[:, 1, :], in_=d_lnb[:, :]), partial=[B("l0lnrows")], dma=True)
    S.op(SP, lambda e: e.dma_start(out=xres[:, 1, :], in_=x_dram[128:256, :]), writes=[B("x1")], dma=True)
    for hh in range(2):
        slot = hh
        S.op(SP, lambda e, hh=hh, slot=slot: e.dma_start(out=C.wstage[:, slot, 0:512], in_=d_sgw[:, hh * 512:(hh + 1) * 512]),
             writes=[C.bwstage[slot]], dma=True)
        S.op(POOL, lambda e, hh=hh, slot=slot: e.tensor_copy(sgw[:, hh * 4:(hh + 1) * 4, :].rearrange("p g q -> p (g q)"), C.wstage[:, slot, 0:512]),
             reads=[C.bwstage[slot]], partial=[B("sgw")])
    S.op(POOL, lambda e: e.tensor_scalar(nsink[:], sink[:], -1.0, None, ALU.mult), reads=[B("sink")], writes=[B("nsink")])
    S.op(POOL, lambda e: e.memset(kT[:, 0, :], 0.0), partial=[B("kT0")])
    S.op(POOL, lambda e: e.memset(V[:, 0, :], 0.0), partial=[B("V0")])

    for f in range(8):
        S.op(POOL, lambda e, f=f: e.memset(invf[:, f:f + 1], INV_FREQ[f]), partial=[B("invf")])
    S.op(DVE, lambda e: e.tensor_copy(posf[:], posi[:]), reads=[B("posi")], writes=[B("posf")])
    S.op(DVE, lambda e: e.tensor_tensor(ang[:], bc_last(posf[:], 8), bc_mid(invf[:], 17), ALU.mult),
         reads=[B("posf"), B("invf")], writes=[B("ang")])
    TWO_PI = float(2 * np.pi)

    def sin_into(dst_ap, shift, negate, tag):
        bk, bi_, br = B("angk"), B("angi"), B("angr")
        S.op(DVE, lambda e: e.tensor_scalar(angk[:], ang[:], shift, 1.0 / TWO_PI, ALU.add, ALU.mult), reads=[B("ang")], writes=[bk])
        S.op(DVE, lambda e: e.tensor_copy(angi[:], angk[:]), reads=[bk], writes=[bi_])
        S.op(DVE, lambda e: e.tensor_copy(angk[:], angi[:]), reads=[bi_], writes=[bk])
        S.op(DVE, lambda e: e.scalar_tensor_tensor(angk[:], angk[:], -TWO_PI, ang[:], ALU.mult, ALU.add), reads=[bk, B("ang")], writes=[bk])
        S.op(DVE, lambda e: e.tensor_scalar(angk[:], angk[:], shift, float(np.pi), ALU.add, ALU.min), reads=[bk], writes=[bk])
        S.op(DVE, lambda e: e.tensor_scalar(angk[:], angk[:], float(-np.pi), None, ALU.max), reads=[bk], writes=[bk])
        S.op(ACT, lambda e: e.activation(out=dst_ap, in_=angk[:], func=AF.Sin, scale=(-1.0 if negate else 1.0)),
             reads=[bk], partial=[B(tag)])

    sin_into(C2[:, :, 0:8], float(np.pi / 2), False, "C2")
    sin_into(C2[:, :, 8:16], float(np.pi / 2), False, "C2")
    sin_into(S2[:, :, 0:8], 0.0, True, "S2")
    sin_into(S2[:, :, 8:16], 0.0, False, "S2")

    GA, GB, GC, GD, GE, GF = (0, 512), (512, 256), (768, 512), (1280, 512), (1792, 512), (2304, 512)

    def rotary(src_bank, nh, dst3, t, slot, bdst):
        s3 = v3(src_bank[:, 0:nh * 64], 64)
        ta = v3(rot[:, 0, 0:nh * 16], 16)
        tb = v3(rot[:, 1, 0:nh * 16], 16)
        br_ = B("rot")
        S.op(DVE, lambda e: e.tensor_tensor(ta, s3[:, :, 0:16], bc_mid(C2[:, t, :], nh), ALU.mult),
             reads=[slot, B("C2")], writes=[br_])
        S.op(DVE, lambda e: e.tensor_tensor(tb[:, :, 0:8], s3[:, :, 8:16], bc_mid(S2[:, t, 0:8], nh), ALU.mult),
             reads=[slot, B("S2")], partial=[br_])
        S.op(DVE, lambda e: e.tensor_tensor(tb[:, :, 8:16], s3[:, :, 0:8], bc_mid(S2[:, t, 8:16], nh), ALU.mult),
             reads=[slot, B("S2")], partial=[br_])
        S.op(DVE, lambda e: e.tensor_tensor(dst3[:, :, 0:16], ta, tb, ALU.add), reads=[br_], partial=[bdst])

    def modulate(t):
        if t == NT:
            S.op(SP, lambda e: e.dma_start(out=h32[:], in_=x_dram[2048:2176, :]), writes=[B("h32")], dma=True)
            xsrc, bxs = h32[:], B("h32")
        else:
            xsrc, bxs = xres[:, t, :], B("x%d" % t)
        S.op(POOL, lambda e: e.tensor_tensor(h32[:], xsrc, modrow[:, 1, :], ALU.mult), reads=[bxs, B("l0scale")], writes=[B("h32")])
        S.op(POOL, lambda e: e.tensor_tensor(hbf[:], h32[:], modrow[:, 0, :], ALU.add), reads=[B("h32"), B("l0shift")], writes=[B("hbf")])

    def front(t):
        par = t % 2
        last = (t == NT)
        pt, bpt = next_ptr()
        for k in range(8):
            S.op(PE, lambda e, k=k, pt=pt: e.transpose(pt[:, k * 128:(k + 1) * 128], hbf[:, k * 128:(k + 1) * 128], C.idb[:]),
                 reads=[B("hbf"), B("idb")], writes=[bpt] if k == 0 else (), partial=[bpt] if k else ())
        if t < NT:
            modulate(t + 1)
        S.op(ACT, lambda e, pt=pt: e.activation(out=hT[:], in_=pt[:], func=AF.Copy), reads=[bpt], writes=[B("hT")])
        yield "a"

        def proj(g):
            c0, n = g
            bank, bbank = next_pb()
            for k in range(8):
                S.op(PE, lambda e, k=k: e.matmul(bank[:, 0:n], hT[:, k * 128:(k + 1) * 128], w_in[:, k, c0:c0 + n],
                                                 start=(k == 0), stop=(k == 7)),
                     reads=[B("hT"), B("l0w_in")], writes=[bbank] if k == 0 else (), partial=[bbank] if k else ())
            return bank, bbank

        if not last:
            bank, bb = proj(GA)
            S.op(ACT, lambda e, bank=bank: e.activation(out=qkr[:, 0:512], in_=bank[:, 0:512], func=AF.Copy), reads=[bb], writes=[B("qkr_q")])
            rotary(bank, 8, v3(qkr[:, 0:512], 64), t, bb, B("qkr_q"))
        bank, bb = proj(GB)
        S.op(ACT, lambda e, bank=bank: e.activation(out=qkr[:, 512:640], in_=bank[:, 0:128], func=AF.Copy), reads=[bb], writes=[B("qkr_k")])
        S.op(ACT, lambda e, bank=bank: e.activation(out=V[:, t + 1, :], in_=bank[:, 128:256], func=AF.Copy), reads=[bb], writes=[B("V%d" % (t + 1))])
        rotary(bank, 2, v3(qkr[:, 512:640], 64), t, bb, B("qkr_k"))
        pt, bpt = next_ptr()
        first = True
        if not last:
            for j in range(4):
                S.op(PE, lambda e, j=j, pt=pt: e.transpose(pt[:, j * 128:(j + 1) * 128], qkr[:, j * 128:(j + 1) * 128], C.idb[:]),
                     reads=[B("qkr_q"), B("idb")], writes=[bpt] if first else (), partial=() if first else [bpt])
                first = False
        S.op(PE, lambda e, pt=pt: e.transpose(pt[:, 512:640], qkr[:, 512:640], C.idb[:]),
             reads=[B("qkr_k"), B("idb")], writes=[bpt] if first else (), partial=() if first else [bpt])
        if not last:
            S.op(DVE, lambda e, pt=pt: e.tensor_copy(qT[:, par, :], pt[:, 0:512]), reads=[bpt], writes=[B("qT%d" % par)])
        S.op(DVE, lambda e, pt=pt: e.tensor_copy(kT[:, t + 1, :], pt[:, 512:640]), reads=[bpt], writes=[B("kT%d" % (t + 1))])
        if last:
            return
        bank, bb = proj(GC)
        S.op(ACT, lambda e, bank=bank: e.activation(out=su[:], in_=bank[:, 0:512], func=AF.Copy), reads=[bb], writes=[B("su")])
        bank, bb = proj(GD)
        bst = B("st8")
        S.op(ACT, lambda e, bank=bank: e.activation(out=sq, in_=bank[:, 0:512], func=AF.Square), reads=[bb], writes=[B("z")])
        S.op(DVE, lambda e, bank=bank: e.tensor_reduce(st8[:, 0, :], v3(bank[:, 0:512], 64), AX.X, ALU.add), reads=[bb], writes=[bst])
        S.op(DVE, lambda e: e.tensor_reduce(st8[:, 1, :], v3(sq, 64), AX.X, ALU.add), reads=[B("z")], partial=[bst])
        S.op(DVE, lambda e: e.tensor_scalar(st8[:, 2, :], st8[:, 0, :], 1.0 / 64, None, ALU.mult), reads=[bst], partial=[bst])
        S.op(DVE, lambda e: e.tensor_tensor(st8[:, 3, :], st8[:, 2, :], st8[:, 2, :], ALU.mult), reads=[bst], partial=[bst])
        S.op(DVE, lambda e: e.scalar_tensor_tensor(st8[:, 4, :], st8[:, 1, :], 1.0 / 64, st8[:, 3, :], ALU.mult, ALU.subtract),
             reads=[bst], partial=[bst])
        S.op(ACT, lambda e: e.activation(out=st8[:, 5, :], in_=st8[:, 4, :], func=AF.Sqrt, bias=C.eps_col[:, 0:1], scale=1.0),
             reads=[bst, B("eps_col")], partial=[bst])
        S.op(DVE, lambda e: e.reciprocal(st8[:, 6, :], st8[:, 5, :]), reads=[bst], partial=[bst])
        S.op(DVE, lambda e, bank=bank: e.tensor_tensor(v3(vn[:], 64), v3(bank[:, 0:512], 64), bc_last(st8[:, 2, :], 64), ALU.subtract),
             reads=[bb, bst], writes=[B("vn")])
        S.op(DVE, lambda e: e.tensor_tensor(v3(vn[:], 64), v3(vn[:], 64), bc_last(st8[:, 6, :], 64), ALU.mult),
             reads=[B("vn"), bst], writes=[B("vn")])
        S.op(POOL, lambda e: e.tensor_tensor(vn[:], vn[:], sgrows[:, 0, :], ALU.mult), reads=[B("vn"), B("sgrows")], writes=[B("vn")])
        S.op(POOL, lambda e: e.tensor_tensor(vnb[:], vn[:], sgrows[:, 1, :], ALU.add), reads=[B("vn"), B("sgrows")], writes=[B("vnb")])
        yield "b1"
        bankE, bbE = proj(GE)
        bankF, bbF = proj(GF)
        yield "gm"
        S.op(ACT, lambda e, bank=bankE: e.activation(out=sgl[:, par, 0:512], in_=bank[:, 0:512], func=AF.Silu), reads=[bbE], writes=[B("sgl_a%d" % par)])
        S.op(ACT, lambda e, bank=bankF: e.activation(out=sgl[:, par, 512:1024], in_=bank[:, 0:512], func=AF.Silu), reads=[bbF], writes=[B("sgl_s%d" % par)])
        for g in range(8):
            S.op(PE, lambda e, g=g: e.matmul(svb[:, g * 64:(g + 1) * 64], sgw[:, g, :], vnb[:, g * 64:(g + 1) * 64], start=True, stop=True),
                 reads=[B("sgw"), B("vnb")], writes=[B("svb")] if g == 0 else (), partial=[B("svb")] if g else ())
        S.op(POOL, lambda e: e.tensor_tensor(su[:], su[:], sgl[:, par, 512:1024], ALU.mult), reads=[B("su"), B("sgl_s%d" % par)], writes=[B("su")])
        S.op(DVE, lambda e: e.tensor_tensor(v3(sq, 64), v3(svb[:, 0:512], 64), bc_last(sgbias[:], 64), ALU.add),
             reads=[B("svb"), B("sgbias")], writes=[B("z")])
        S.op(DVE, lambda e: e.tensor_tensor(y[:, par, 512:1024], sq, su[:], ALU.mult), reads=[B("z"), B("su")], writes=[B("y_s%d" % par)])

    def attn(t, mid=None):
        par = t % 2
        mrow = 0 if t == 0 else 1
        batt = B("att")

        def stage1(hp):
            j, half = hp // 2, hp % 2
            sb_, bsb = sbk[hp % 2], bsbk[hp % 2]
            S.op(PE, lambda e: e.matmul(sb_[:, 0:384], qT[half * 64:(half + 1) * 64, par, j * 128:(j + 1) * 128],
                                        kT[half * 64:(half + 1) * 64, t:t + 3, :].rearrange("p a b -> p (a b)"), start=True, stop=False),
                 reads=[B("qT%d" % par), B("kT%d" % t), B("kT%d" % (t + 1)), B("kT%d" % (t + 2))], writes=[bsb])
            S.op(PE, lambda e: e.matmul(sb_[:, 0:384], C.idb[:], mask[:, mrow * 384:(mrow + 1) * 384], start=False, stop=True),
                 reads=[B("idb"), B("mask")], partial=[bsb])
            S.op(DVE, lambda e: e.tensor_reduce(att[:, 0, hp:hp + 1], sb_[:, 0:384], AX.X, ALU.max), reads=[bsb], writes=[B("amx%d" % hp)])
            S.op(DVE, lambda e: e.tensor_scalar(att[:, 1, hp:hp + 1], att[:, 0, hp:hp + 1], C.m0125_col[:, 0:1], nsink[:, hp:hp + 1], ALU.mult, ALU.min),
                 reads=[B("amx%d" % hp), B("nsink"), B("m0125_col")], writes=[B("anm%d" % hp)])
            S.op(ACT, lambda e: e.activation(out=P[:, hp % 2, :], in_=sb_[:, 0:384], func=AF.Exp, bias=att[:, 1, hp:hp + 1], scale=0.125,
                                             accum_out=att[:, 2, hp:hp + 1]),
                 reads=[bsb, B("anm%d" % hp), B("ars")], writes=[B("P%d" % (hp % 2)), B("ars%d" % hp)])

        def stage2a(hp):
            pt, bpt = next_ptr()
            for kt in range(3):
                S.op(PE, lambda e, kt=kt: e.transpose(pt[:, kt * 128:(kt + 1) * 128], P[:, hp % 2, kt * 128:(kt + 1) * 128], C.idb[:]),
                     reads=[B("P%d" % (hp % 2)), B("idb")], writes=[bpt] if kt == 0 else (), partial=[bpt] if kt else ())
            S.op(DVE, lambda e: e.tensor_copy(PT[:, hp % 2, :], pt[:, 0:384]), reads=[bpt], writes=[B("PT%d" % (hp % 2))])

        def stage2b(hp):
            half = hp % 2
            for kt in range(3):
                S.op(PE, lambda e, kt=kt: e.matmul(ob[:, hp * 64:(hp + 1) * 64], PT[:, hp % 2, kt * 128:(kt + 1) * 128],
                                                   V[:, t + kt, half * 64:(half + 1) * 64], start=(kt == 0), stop=(kt == 2)),
                     reads=[B("PT%d" % (hp % 2)), B("V%d" % (t + kt))],
                     writes=[B("ob")] if (hp == 0 and kt == 0) else (), partial=() if (hp == 0 and kt == 0) else [B("ob")])

        S.op(DVE, lambda e: e.memset(att[:, 2, :], 0.0), writes=[B("ars")] + [B("ars%d" % h) for h in range(8)])
        stage1(0)
        stage1(1)
        if mid is not None:
            mid()
        for hp in range(8):
            stage2a(hp)
            if hp + 2 < 8:
                stage1(hp + 2)
            stage2b(hp)
        anm = [B("anm%d" % h) for h in range(8)]
        ars = [B("ars%d" % h) for h in range(8)]
        S.op(DVE, lambda e: e.tensor_tensor(att[:, 3, :], att[:, 1, :], sink[:], ALU.add), reads=anm + [B("sink")], writes=[batt])
        S.op(ACT, lambda e: e.activation(out=att[:, 3, :], in_=att[:, 3, :], func=AF.Exp), reads=[batt], writes=[batt])
        S.op(DVE, lambda e: e.tensor_tensor(att[:, 4, :], att[:, 3, :], att[:, 2, :], ALU.add), reads=[batt] + ars, writes=[batt])
        S.op(DVE, lambda e: e.reciprocal(att[:, 5, :], att[:, 4, :]), reads=[batt], writes=[batt])
        S.op(DVE, lambda e: e.tensor_tensor(v3(yat[:], 64), v3(ob[:, 0:512], 64), bc_last(att[:, 5, :], 64), ALU.mult),
             reads=[B("ob"), batt], writes=[B("vn")])
        S.op(DVE, lambda e: e.tensor_tensor(y[:, par, 0:512], yat[:], sgl[:, par, 0:512], ALU.mult),
             reads=[B("vn"), B("sgl_a%d" % par)], writes=[B("y_a%d" % par)])

    def back(t):
        par = t % 2
        pt, bpt = next_ptr()
        for k in range(8):
            S.op(PE, lambda e, k=k: e.transpose(pt[:, k * 128:(k + 1) * 128], y[:, par, k * 128:(k + 1) * 128], C.idb[:]),
                 reads=[B("y_a%d" % par), B("y_s%d" % par), B("idb")], writes=[bpt] if k == 0 else (), partial=[bpt] if k else ())
        S.op(ACT, lambda e: e.activation(out=yT[:], in_=pt[:], func=AF.Copy), reads=[bpt], writes=[B("yT")])
        for cg in range(2):
            bank, bb = next_pb()
            for k in range(8):
                S.op(PE, lambda e, k=k, cg=cg, bank=bank: e.matmul(bank[:, 0:512], yT[:, k * 128:(k + 1) * 128], w_out[:, k, cg * 512:(cg + 1) * 512],
                                                        start=(k == 0), stop=(k == 7)),
                     reads=[B("yT"), B("l0w_out")], writes=[bb] if k == 0 else (), partial=[bb] if k else ())
            S.op(DVE, lambda e, cg=cg, bank=bank: e.scalar_tensor_tensor(z[:, cg * 512:(cg + 1) * 512], xres[:, t, cg * 512:(cg + 1) * 512], ALPHA,
                                                              bank[:, 0:512], ALU.mult, ALU.add),
                 reads=[B("x%d" % t), bb], writes=[B("z")] if cg == 0 else (), partial=[B("z")] if cg else ())
        yield "a"
        emit_ln_tile(C, "l0", z, B("z"), xres[:, t, :], B("x%d" % t), lnrows[:, 0, :], lnrows[:, 1, :], B("l0lnrows"))
        if out_dram is not None:
            S.op(SP, lambda e: e.dma_start(out=out_dram[t * 128:(t + 1) * 128, :], in_=xres[:, t, :]), reads=[B("x%d" % t)], dma=True)

    import os
    dbg = os.environ.get("L0_DBG", "")
    if dbg == "a":
        for t in range(NT):
            S.op(SP, lambda e, t=t: e.dma_start(out=out_dram[t * 128:(t + 1) * 128, :], in_=xres[:, t, :]), reads=[B("x%d" % t)], dma=True)
        return
    def step(g):
        try:
            next(g)
        except StopIteration:
            pass

    def finish(g):
        for _ in g:
            pass

    modulate(0)
    pend_back = None
    for t in range(NT + 1):
        if t + 2 < NT:
            S.op(SP, lambda e, t=t: e.dma_start(out=xres[:, t + 2, :], in_=x_dram[(t + 2) * 128:(t + 3) * 128, :]),
                 writes=[B("x%d" % (t + 2))], dma=True)
        f = front(t)
        step(f)
        if pend_back is not None:
            finish(pend_back)
            pend_back = None
        step(f)
        if t >= 1:
            attn(t - 1, mid=lambda f=f: step(f))
            finish(f)
            b = back(t - 1)
            step(b)
            pend_back = b
        else:
            finish(f)
    if pend_back is not None:
        finish(pend_back)
    if C.send1 is not None:
        S.op(SP, lambda e: e.dma_start(out=C.send1[0:1, :], in_=xres[127:128, 15, :]), reads=[B("x15")], partial=[B("send1")], dma=True)
        S.op(SP, lambda e: e.dma_start(out=C.send1[1:2, :], in_=xres[126:127, 15, :]), reads=[B("x15")], partial=[B("send1")], dma=True)
        S.op(POOL, lambda e: e.collective_compute("AllGather", ALU.bypass, replica_groups=[[0, 1], [2, 3], [4, 5], [6, 7]],
                                                 ins=[C.send1], outs=[C.recv1]), reads=[B("send1")], writes=[B("recv1")], cc=True)
    if dbg in ("b", "c"):
        for t in range(NT):
            S.op(SP, lambda e, t=t: e.dma_start(out=out_dram[t * 128:(t + 1) * 128, :], in_=xres[:, t, :]), reads=[B("x%d" % t)], dma=True)


def alloc_common(C):
    C.xres = C.sb("xres", [128, NT, 1024])
    C.wstage = C.sb("wstage", [128, 3, 512])
    C.ps_ptr = [C.ps("ptr%d" % i, [128, 1024], BF16) for i in range(2)]
    C.ps_f32 = [C.ps("pf%d" % i, [128, 512]) for i in range(6)]
    if not hasattr(C, "send1"):
        C.send1 = None
    C.bwstage = [C.B("wstage%d" % i) for i in range(3)]
    C.ring3 = [(C.wstage[:, i, :], [C.bwstage[i]]) for i in range(3)]
    C.ln_st6 = C.sb("ln_st6", [128, 12])
    C.m0125_col = C.sb("m0125_col", [128, 1])
    C.S.op(DVE, lambda e: e.memset(C.m0125_col[:], -0.125), writes=[C.B("m0125_col")])
    C.ln_mv = C.sb("ln_mv", [128, 2])
    C.ln_sd = C.sb("ln_sd", [128, 4])
    C.eps_col = C.sb("eps_col", [128, 1])
    C.one_col = C.sb("one_col", [128, 1])
    C.S.op(DVE, lambda e: e.memset(C.eps_col[:], LN_EPS), writes=[C.B("eps_col")])
    C.S.op(DVE, lambda e: e.memset(C.one_col[:], 1.0), writes=[C.B("one_col")])


def build_l0():
    nc = bass.Bass("TRN2", target_bir_lowering=False)
    with contextlib.ExitStack() as st:
        C = Ctx(nc, st)
        x_d = C.dram("x", [2176, 1024], F32, "ExternalInput")
        o_d = C.dram("out", [2048, 1024], F32, "ExternalOutput")
        alloc_common(C)
        emit_consts(C)
        emit_l0(C, x_d, o_d)
        print("sbuf remaining after l0 alloc:", nc.sbuf_bytes_remaining)
        C.S.emit()
    return nc


def rev_ap(ap2):
    n = ap2.shape[1]
    return bass.AP(ap2.tensor, ap2.offset + (n - 1), [list(ap2.ap[0]), [-1, n]])


def emit_l1(C, out_dram):
    S, nc = C.S, C.nc
    B = C.B
    xres = C.xres
    S.barrier()
    C.areset()
    RG = [[0, 1], [2, 3], [4, 5], [6, 7]]
    d_win = C.dram("l1_w_in", [1024, 2048], F32, "ExternalInput")
    d_w5 = C.dram("l1_w5", [128, 40], F32, "ExternalInput")
    d_cb = C.dram("l1_cb", [128, 8], F32, "ExternalInput")
    d_wg = C.dram("l1_wg", [128, 4096], F32, "ExternalInput")
    d_bg = C.dram("l1_bg", [128, 32], F32, "ExternalInput")
    d_lam = C.dram("l1_lam", [128, 16], F32, "ExternalInput")
    d_lng = C.dram("l1_ln_g", [128, 1024], F32, "ExternalInput")
    d_lnb = C.dram("l1_ln_b", [128, 1024], F32, "ExternalInput")
    d_sel = C.dram("sel", [128, 2], F32, "ExternalInput")
    hT = C.asb("l1_hT", [128, 8, 2056], BF16)
    w_out = hT.rearrange("p k t -> p (k t)")[:, 0:8192].rearrange("p (k c) -> p k c", c=1024)
    yT = C.asb("l1_yT", [128, 8, 2048], BF16)
    xc = C.asb("l1_xc", [128, 8, 2048], BF16)
    wg = C.asb("l1_wg", [128, 4, 8, 128], BF16)
    wt = C.asb("l1_wt", [128, 8, 128], BF16)
    xr = C.asb("l1_xr", [128, 2052], BF16)
    sg = xr[:, 0:2048]
    lnrows = xc[:, 4:6, :].rearrange("p a b -> p (a b)").bitcast(F32).rearrange("p (a b) -> p a b", a=2)
    dg = C.asb("l1_dg", [128, 10, 128], BF16)
    modrow = xc[:, 0:2, :].rearrange("p a b -> p (a b)").bitcast(F32).rearrange("p (a b) -> p a b", a=2)
    wk = C.asb("l1_wk", [128, 5120])
    WA = wk[:, 0:2048]
    WTI = wk[:, 2048:3072].bitcast(BF16)
    WVB = wk[:, 3072:4096].bitcast(BF16)
    WH = wk[:, 4096:5120].rearrange("p (a b) -> p a b", a=2)
    h32 = wk[:, 0:1024]
    hbf = wk[:, 1024:1536].bitcast(BF16)
    hb2 = wk[:, 2048:3072]
    z = h32
    gaterow = wk[:, 1024:2048]
    crep = wk[:, 0:1024].rearrange("p (k m) -> p k m", m=128)
    brow = wk[0:1, 4096:5120]
    hbg = C.asb("l1_hbg", [128, 32])
    hn = C.asb("l1_hn", [128, 16])
    w5 = C.asb("l1_w5", [128, 40])
    cb = C.asb("l1_cb", [128, 8])
    bg = C.asb("l1_bg", [128, 32])
    lam = C.asb("l1_lam", [128, 16])
    nsp8 = C.asb("l1_nsp8", [128, 16])
    sel = C.asb("l1_sel", [128, 2])
    carry = C.asb("l1_carry", [128, 8])
    cga = C.asb("l1_cga", [128, 8])
    cgb = C.asb("l1_cgb", [128, 8])
    stA = C.asb("l1_stA", [128, 8])
    stB = C.asb("l1_stB", [128, 8])
    ptr = C.ps_ptr
    pb = C.ps_f32[0:4]
    bptr = [B("ptr0"), B("ptr1")]
    bpb = [B("pb%d" % i) for i in range(4)]
    for b_ in bptr + bpb:
        b_.excl = True
    cnt = {"ptr": 0, "pb": 0}

    def next_ptr():
        i = cnt["ptr"] % 2
        cnt["ptr"] += 1
        return ptr[i], bptr[i]

    def next_pb():
        i = cnt["pb"] % 4
        cnt["pb"] += 1
        return pb[i], bpb[i]

    w4k = (C.wstage[:].rearrange("p a b -> p (a b)")[:, 0:1024], [C.bwstage[0], C.bwstage[1]])
    ring1 = [w4k] + [(yT[:, k, :].bitcast(F32), [B("yT%d" % k)]) for k in range(8)]
    bring1 = [(xc[:, 3, 0:1024], [B("xc3")]), (xc[:, 3, 1024:2048], [B("xc3")]), (xc[:, 2, 0:1024], [B("xc2")]), (xc[:, 2, 1024:2048], [B("xc2")])]
    crep_b = wk[:, 0:512].bitcast(BF16).rearrange("p (k m) -> p k m", m=128)
    for nm, dst, src in (("w5", w5, d_w5), ("cb", cb, d_cb), ("bg", bg, d_bg), ("lam", lam, d_lam), ("sel", sel, d_sel)):
        S.op(SP, lambda e, dst=dst, src=src: e.dma_start(out=dst, in_=src[:, :]), writes=[B("l1" + nm)], dma=True)
    emit_adaln2(C, "l1", [(pb[0], bpb[0]), (pb[1], bpb[1])],
                [(modrow[:, 0, :], B("modrow"), False), (modrow[:, 1, :], B("modrow"), True), (gaterow, B("t23"), False)],
                crep_b, B("t01"), brow, (0, 1), ring1, bring1, tag="a")
    wg2 = wg.rearrange("p m h j -> p (m h j)")
    for i in range(8):
        slot = i % 3
        sap, sbufs = ring1[(i + 5) % len(ring1)]
        S.op(SP, lambda e, i=i, sap=sap: e.dma_start(out=sap[:, 0:512], in_=d_wg[:, i * 512:(i + 1) * 512]), writes=sbufs, dma=True)
        S.op(DVE if i % 2 else ACT, (lambda e, i=i, sap=sap: e.tensor_copy(wg2[:, i * 512:(i + 1) * 512], sap[:, 0:512])) if i % 2 else
             (lambda e, i=i, sap=sap: e.activation(out=wg2[:, i * 512:(i + 1) * 512], in_=sap[:, 0:512], func=AF.Copy)),
             reads=sbufs, partial=[B("l1wg")])
    S.op(ACT, lambda e: e.activation(out=nsp8, in_=lam, func=AF.Exp, scale=-1.0), reads=[B("l1lam")], writes=[B("nsp8")])
    S.op(ACT, lambda e: e.activation(out=nsp8, in_=nsp8, func=AF.Ln, bias=C.one_col[:, 0:1], scale=1.0), reads=[B("nsp8"), B("one_col")], writes=[B("nsp8")])
    S.op(DVE, lambda e: e.tensor_scalar(nsp8, nsp8, -8.0, None, ALU.mult), reads=[B("nsp8")], writes=[B("nsp8")])
    S.op(DVE, lambda e: e.tensor_scalar(hn, nsp8, 0.5, None, ALU.mult), reads=[B("nsp8")], writes=[B("hn")])
    S.op(DVE, lambda e: e.tensor_scalar(hbg, bg, 0.5, None, ALU.mult), reads=[B("l1bg")], writes=[B("hbg")])
    S.op(DVE, lambda e: e.memset(xr[:, 0:2], 0.0), partial=[B("xr")])

    for t in range(NT + 1):
        if t < NT:
            xsrc, bxs = xres[:, t, :], [B("x%d" % t)]
        else:
            S.op(DVE, lambda e: e.memset(h32, 0.0), writes=[B("t01")])
            S.op(SP, lambda e: e.dma_start(out=h32[0:2, :], in_=C.recv1[0:2, :]), reads=[B("recv1")], partial=[B("t01")], dma=True)
            S.op(SP, lambda e: e.dma_start(out=hb2[0:2, :], in_=C.recv1[2:4, :]), reads=[B("recv1")], writes=[B("t34")], dma=True)
            S.op(DVE, lambda e: e.tensor_scalar(h32[0:2, :], h32[0:2, :], sel[0:2, 0:1], None, ALU.mult), reads=[B("t01"), B("l1sel")], partial=[B("t01")])
            S.op(DVE, lambda e: e.scalar_tensor_tensor(h32[0:2, :], hb2[0:2, :], sel[0:2, 1:2], h32[0:2, :], ALU.mult, ALU.add),
                 reads=[B("t01"), B("t34"), B("l1sel")], partial=[B("t01")])
            xsrc, bxs = h32, [B("t01")]
        meng = POOL if t % 3 == 0 else DVE
        S.op(meng, lambda e, xsrc=xsrc: e.tensor_tensor(h32, xsrc, modrow[:, 1, :], ALU.mult), reads=bxs + [B("modrow")], writes=[B("t01")])
        S.op(meng, lambda e: e.tensor_tensor(hbf, h32, modrow[:, 0, :], ALU.add), reads=[B("t01"), B("modrow")], writes=[B("t23")])
        pt, bpt = next_ptr()
        for k in range(8):
            S.op(PE, lambda e, k=k, pt=pt: e.transpose(pt[:, k * 128:(k + 1) * 128], hbf[:, k * 128:(k + 1) * 128], C.idb[:]),
                 reads=[B("t23"), B("idb")], writes=[bpt] if k == 0 else (), partial=[bpt] if k else ())
        ncol = 128 if t < NT else 8
        S.op(ACT, lambda e, t=t, pt=pt, ncol=ncol: e.activation(out=hT[:, :, t * 128:t * 128 + ncol],
                                                               in_=pt[:].rearrange("p (k m) -> p k m", m=128)[:, :, 0:ncol], func=AF.Copy),
             reads=[bpt], partial=[B("l1hT")])
    S.barrier()

    def load_wt(c0):
        wst = C.wstage[:].rearrange("p a b -> p (a b)")[:, 0:1024].rearrange("p (k m) -> p k m", m=128)
        S.op(SP, lambda e: e.dma_start(out=wst, in_=d_win.rearrange("(k p) c -> p k c", p=128)[:, :, c0:c0 + 128]),
             writes=[C.bwstage[0], C.bwstage[1]], dma=True)
        S.op(DVE, lambda e: e.tensor_copy(wt, wst), reads=[C.bwstage[0], C.bwstage[1]], writes=[B("l1wt")])

    def batch(d, ct, finish_chunk):
        qs = [0, 1, 2, 3] if d == 0 else [3, 2, 1, 0]
        ca = (2 * d) * 8 + ct
        ci = (2 * d + 1) * 8 + ct
        cl = d * 8 + ct
        bxc = B("xc%d" % ct)
        for n_, q in enumerate(qs):
            sl = slice(q * 512, (q + 1) * 512)
            bA, bI, bV = B("wA%d" % q), B("wTI%d" % q), B("wVB%d" % q)
            Ht, bHt = WH[:, n_ % 2, :], B("wH%d" % (n_ % 2))
            bk_r, bb_r = next_pb()
            bk_i, bb_i = next_pb()
            S.op(PE, lambda e, sl=sl, bk_r=bk_r: e.matmul(bk_r[:, 0:512], wg[:, 2 * d, ct, :], xc[:, ct, sl], start=True, stop=True),
                 reads=[B("l1wg"), bxc], writes=[bb_r])
            S.op(PE, lambda e, sl=sl, bk_i=bk_i: e.matmul(bk_i[:, 0:512], wg[:, 2 * d + 1, ct, :], xc[:, ct, sl], start=True, stop=True),
                 reads=[B("l1wg"), bxc], writes=[bb_i])
            S.op(ACT, lambda e, sl=sl, bk_r=bk_r: e.activation(out=WA[:, sl], in_=bk_r[:, 0:512], func=AF.Tanh, bias=hbg[:, ca:ca + 1], scale=0.5),
                 reads=[bb_r, B("hbg")], writes=[bA])
            S.op(ACT, lambda e, sl=sl, bk_i=bk_i: e.activation(out=WTI[:, sl], in_=bk_i[:, 0:512], func=AF.Tanh, bias=hbg[:, ci:ci + 1], scale=0.5),
                 reads=[bb_i, B("hbg")], writes=[bI])
            S.op(ACT, lambda e, sl=sl: e.activation(out=WA[:, sl], in_=WA[:, sl], func=AF.Exp, bias=hn[:, cl:cl + 1], scale=hn[:, cl:cl + 1]),
                 reads=[bA, B("hn")], writes=[bA])
            S.op(DVE, lambda e, sl=sl, Ht=Ht: e.tensor_tensor(Ht, WA[:, sl], WA[:, sl], ALU.mult), reads=[bA], writes=[bHt])
            S.op(DVE, lambda e, sl=sl, Ht=Ht: e.tensor_scalar(WVB[:, sl], Ht, -1.0, 1.0, ALU.mult, ALU.add), reads=[bHt], writes=[bV])
        yield "G"
        allV = [B("wVB%d" % q) for q in range(4)]
        allI = [B("wTI%d" % q) for q in range(4)]
        S.op(ACT, lambda e: e.activation(out=WVB, in_=WVB, func=AF.Sqrt, scale=0.25), reads=allV, writes=allV)
        S.op(DVE, lambda e: e.scalar_tensor_tensor(WVB, WTI, 1.0, WVB, ALU.add, ALU.mult), reads=allV + allI, writes=allV)
        S.op(DVE, lambda e: e.tensor_tensor(WVB, WVB, xc[:, ct, :], ALU.mult), reads=allV + [bxc], writes=allV)
        st = stA if d == 0 else stB
        bst = B("stA") if d == 0 else B("stB")
        for n_, q in enumerate(qs):
            sl = slice(q * 512, (q + 1) * 512)
            bA, bV = B("wA%d" % q), B("wVB%d" % q)
            H, bH = WH[:, n_ % 2, :], B("wH%d" % (n_ % 2))
            if d == 0:
                init = 0.0 if n_ == 0 else st[:, ct:ct + 1]
                S.op(DVE, lambda e, sl=sl, H=H, init=init: e.tensor_tensor_scan(H, WA[:, sl], WVB[:, sl], init, ALU.mult, ALU.add),
                     reads=[bA, bV, bst], writes=[bH])
                S.op(DVE, lambda e, H=H: e.tensor_copy(st[:, ct:ct + 1], H[:, 511:512]), reads=[bH], writes=[bst])
            else:
                init = carry[:, ct:ct + 1] if n_ == 0 else st[:, ct:ct + 1]
                S.op(DVE, lambda e, sl=sl, H=H, init=init: e.tensor_tensor_scan(rev_ap(H), rev_ap(WA[:, sl]), rev_ap(WVB[:, sl]), init, ALU.mult, ALU.add),
                     reads=[bA, bV, bst, B("l1carry")], writes=[bH])
                S.op(DVE, lambda e, H=H: e.tensor_copy(st[:, ct:ct + 1], H[:, 0:1]), reads=[bH], writes=[bst])
            finish_chunk(q, H, bH)

    def conv_stage(ct):
        for tg in range(5):
            n = 512 if tg < 4 else 8
            bank, bb = next_pb()
            for k in range(8):
                S.op(PE, lambda e, k=k, tg=tg, n=n, bank=bank: e.matmul(bank[:, 0:n], wt[:, k, :], hT[:, k, tg * 512:tg * 512 + n],
                                                                      start=(k == 0), stop=(k == 7)),
                     reads=[B("l1wt"), B("l1hT")], writes=[bb] if k == 0 else (), partial=[bb] if k else ())
            m = n if tg < 4 else 2
            S.op(DVE, lambda e, tg=tg, m=m, bank=bank: e.tensor_copy(xr[:, 2 + tg * 512:2 + tg * 512 + m], bank[:, 0:m]),
                 reads=[bb], partial=[B("xr")])
        if ct + 1 < 8:
            load_wt((ct + 1) * 128)
        par = ct % 2
        for j in range(5):
            S.op(DVE, lambda e, ct=ct, j=j, par=par: e.tensor_scalar(dg[:, par * 5 + j, :], C.idb[:], w5[:, ct * 5 + j:ct * 5 + j + 1], None, ALU.mult),
                 reads=[B("idb"), B("l1w5")], writes=[B("dg%d_%d" % (par, j))])
        for q in range(4):
            sl = slice(q * 512, (q + 1) * 512)
            bank, bb = next_pb()
            for j in range(5):
                S.op(PE, lambda e, j=j, q=q, par=par, bank=bank: e.matmul(bank[:, 0:512], dg[:, par * 5 + j, :], xr[:, q * 512 + j:q * 512 + j + 512],
                                                                        start=(j == 0), stop=(j == 4)),
                     reads=[B("dg%d_%d" % (par, j)), B("xr")], writes=[bb] if j == 0 else (), partial=[bb] if j else ())
            S.op(DVE, lambda e, ct=ct, sl=sl, bank=bank: e.tensor_scalar(xc[:, ct, sl], bank[:, 0:512], cb[:, ct:ct + 1], None, ALU.add),
                 reads=[bb, B("l1cb")], partial=[B("xc%d" % ct)])

    def drain(g):
        for _ in g:
            pass

    load_wt(0)
    conv_stage(0)
    for ct in range(8):
        def fin_a(q, H, bH, ct=ct):
            sl = slice(q * 512, (q + 1) * 512)
            S.op(DVE, lambda e: e.tensor_copy(yT[:, ct, sl], H), reads=[bH], partial=[B("yT%d" % ct)])
        g = batch(0, ct, fin_a)
        next(g)
        if ct + 1 < 8:
            conv_stage(ct + 1)
        drain(g)

    S.op(SP, lambda e: e.dma_start(out=C.send2[:, :], in_=stA), reads=[B("stA")], writes=[B("send2")], dma=True)
    S.op(POOL, lambda e: e.collective_compute("AllGather", ALU.bypass, replica_groups=RG, ins=[C.send2], outs=[C.recv2]),
         reads=[B("send2")], writes=[B("recv2")], cc=True)
    S.op(SP, lambda e: e.dma_start(out=cga, in_=C.recv2[0:128, :]), reads=[B("recv2")], writes=[B("cga")], dma=True)
    S.op(SP, lambda e: e.dma_start(out=cgb, in_=C.recv2[128:256, :]), reads=[B("recv2")], writes=[B("cgb")], dma=True)
    S.op(DVE, lambda e: e.tensor_scalar(carry, cga, sel[:, 0:1], None, ALU.mult), reads=[B("cga"), B("l1sel")], writes=[B("l1carry")])
    S.op(DVE, lambda e: e.scalar_tensor_tensor(carry, cgb, sel[:, 1:2], carry, ALU.mult, ALU.add), reads=[B("cgb"), B("l1carry"), B("l1sel")], writes=[B("l1carry")])

    def gate_chunk(ct, q):
        bank, bb = next_pb()
        for k in range(8):
            S.op(PE, lambda e, k=k, bank=bank: e.matmul(bank[:, 0:512], wt[:, k, :], hT[:, k, q * 512:(q + 1) * 512],
                                                       start=(k == 0), stop=(k == 7)),
                 reads=[B("l1wt"), B("l1hT")], writes=[bb] if k == 0 else (), partial=[bb] if k else ())
        S.op(ACT, lambda e, bank=bank: e.activation(out=sg[:, q * 512:(q + 1) * 512], in_=bank[:, 0:512], func=AF.Silu),
             reads=[bb], writes=[B("sg%d" % q)], partial=[B("xr")] if ct == 0 else ())

    load_wt(1024)
    for q in (3, 2, 1, 0):
        gate_chunk(0, q)
    for ct in range(8):
        if ct + 1 < 8:
            load_wt(1024 + (ct + 1) * 128)

        def fin_b(q, H, bH, ct=ct):
            sl = slice(q * 512, (q + 1) * 512)
            S.op(DVE, lambda e: e.tensor_tensor(H, H, yT[:, ct, sl], ALU.add), reads=[bH, B("yT%d" % ct)], writes=[bH])
            S.op(DVE, lambda e: e.tensor_tensor(yT[:, ct, sl], H, sg[:, sl], ALU.mult), reads=[bH, B("sg%d" % q)], partial=[B("yT%d" % ct)])
            if ct + 1 < 8:
                gate_chunk(ct + 1, q)
        drain(batch(1, ct, fin_b))

    S.barrier()
    ring2 = [w4k, (xc[:, 6, :].bitcast(F32), [B("xc6")]), (xc[:, 7, :].bitcast(F32), [B("xc7")])]
    emit_adaln2(C, "l1", [(pb[0], bpb[0]), (pb[1], bpb[1])],
                [None, None, (gaterow, B("t23"), False)], crep_b, B("t01"), brow, (2,), ring2, bring1, tag="b")
    emit_weight_bf16(C, "l1_w_out", w_out, B("l1hT"), 8, 1024, ring2, scale_row=gaterow, bscale=B("t23"), chunk=1024)
    S.op(SP, lambda e: e.dma_start(out=lnrows[:, 0, :], in_=d_lng[:, :]), writes=[B("xc4")], partial=[B("l1lnrows")], dma=True)
    S.op(SP, lambda e: e.dma_start(out=lnrows[:, 1, :], in_=d_lnb[:, :]), writes=[B("xc5")], partial=[B("l1lnrows")], dma=True)
    yTb = [B("yT%d" % k) for k in range(8)]
    S.barrier()
    zall = xc.rearrange("p a b -> p (a b)").bitcast(F32)
    lnsm = [(C.asb("l1_st6_%d" % i, [128, 12]), C.asb("l1_mv_%d" % i, [128, 2]), C.asb("l1_sd_%d" % i, [128, 4]), "_%d" % i) for i in range(3)]
    for t in range(NT):
        z = zall[:, (t % 3) * 1024:(t % 3 + 1) * 1024]
        bz = B("l1z%d" % (t % 3))
        for cg in range(2):
            bank, bb = next_pb()
            for k in range(8):
                S.op(PE, lambda e, k=k, cg=cg, t=t, bank=bank: e.matmul(bank[:, 0:512], yT[:, k, t * 128:(t + 1) * 128], w_out[:, k, cg * 512:(cg + 1) * 512],
                                                                       start=(k == 0), stop=(k == 7)),
                     reads=[yTb[k], B("l1hT")], writes=[bb] if k == 0 else (), partial=[bb] if k else ())
            S.op(DVE, lambda e, cg=cg, t=t, bank=bank, z=z: e.scalar_tensor_tensor(z[:, cg * 512:(cg + 1) * 512], xres[:, t, cg * 512:(cg + 1) * 512], ALPHA,
                                                                             bank[:, 0:512], ALU.mult, ALU.add),
                 reads=[B("x%d" % t), bb], writes=[bz] if cg == 0 else (), partial=[bz] if cg else ())
        emit_ln_tile(C, "l1", z, bz, xres[:, t, :], B("x%d" % t), lnrows[:, 0, :], lnrows[:, 1, :], B("l1lnrows"), small=lnsm[t % 3])
        S.op(SP, lambda e, t=t: e.dma_start(out=out_dram[t * 128:(t + 1) * 128, :], in_=xres[:, t, :]), reads=[B("x%d" % t)], dma=True)


def l1_inputs(c, inp):
    b, half = c // 2, c % 2
    dA, dB = (0, 1) if half == 0 else (1, 0)
    cw = inp["od_conv_w"][0]
    zero = np.zeros((1, 1024), np.float32)
    w5 = np.concatenate([cw, zero], 0) if half == 0 else np.concatenate([zero, cw[::-1]], 0)
    w5 = np.ascontiguousarray(w5.reshape(5, 8, 128).transpose(2, 1, 0)).reshape(128, 40)
    wa, wx = inp["od_w_a"][0], inp["od_w_x"][0]
    wg = np.stack([wa[dA], wx[dA], wa[dB], wx[dB]], 0)
    wg = np.ascontiguousarray(wg.transpose(2, 0, 1, 3)).reshape(128, 4096)
    ba, bx = inp["od_b_a"][0], inp["od_b_x"][0]
    bgm = np.stack([ba[dA], bx[dA], ba[dB], bx[dB]], 0).reshape(4, 8, 128)
    bgm = np.ascontiguousarray(bgm.transpose(2, 0, 1)).reshape(128, 32)
    lam = inp["od_lam"][0]
    lamm = np.stack([lam[dA], lam[dB]], 0).reshape(2, 8, 128)
    lamm = np.ascontiguousarray(lamm.transpose(2, 0, 1)).reshape(128, 16)
    return {
        "sel": rep(np.array([0.0, 1.0], np.float32) if half == 0 else np.array([1.0, 0.0], np.float32)),
        "l1_cT": np.ascontiguousarray(inp["c"][b].reshape(8, 128).T),
        "l1_ada_w": np.ascontiguousarray(inp["ada_w"][1]),
        "l1_ada_b": np.ascontiguousarray(inp["ada_b"][1][None, :]),
        "l1_w_in": np.ascontiguousarray(inp["od_w_in"][0]),
        "l1_w5": w5,
        "l1_cb": np.ascontiguousarray(inp["od_conv_b"][0].reshape(8, 128).T),
        "l1_wg": wg,
        "l1_bg": bgm,
        "l1_lam": lamm,
        "l1_w_out": np.ascontiguousarray(inp["od_w_out"][0]),
        "l1_ln_g": rep(inp["ln_g"][1]),
        "l1_ln_b": rep(inp["ln_b"][1]),
    }


def build_fused():
    nc = bass.Bass("TRN2", target_bir_lowering=False)
    with contextlib.ExitStack() as st:
        C = Ctx(nc, st)
        x_d = C.dram("x", [2176, 1024], F32, "ExternalInput")
        o_d = C.dram("out", [2048, 1024], F32, "ExternalOutput")
        C.send1 = nc.dram_tensor("send1", [2, 1024], F32).ap()
        C.recv1 = nc.dram_tensor("recv1", [4, 1024], F32).ap()
        C.send2 = nc.dram_tensor("send2", [128, 8], F32).ap()
        C.recv2 = nc.dram_tensor("recv2", [256, 8], F32).ap()
        alloc_common(C)
        emit_consts(C)
        C.make_arena(ARENA_WORDS)
        emit_l0(C, x_d, None)
        p0 = C.apeak
        emit_l1(C, o_d)
        print("arena words: l0 peak", p0, "overall peak", C.apeak, "of", C.asize, "sbuf remaining", nc.sbuf_bytes_remaining)
        C.S.emit()
    return nc


def rep(v, n=128):
    return np.ascontiguousarray(np.broadcast_to(np.asarray(v, np.float32)[None, :], (n, v.shape[0])))


def core_tokens(half):
    if half == 0:
        return np.arange(0, 2048), np.arange(2048, 2176)
    return np.arange(4095, 2047, -1), np.arange(2047, 1919, -1)


def l0_masks():
    qi = np.arange(128)[:, None]
    kj = np.arange(384)[None, :]
    valid = np.abs(kj - 128 - qi) <= 128
    m1 = np.where(valid, 0.0, NEG).astype(np.float32)
    m0 = np.where(valid & (kj >= 128), 0.0, NEG).astype(np.float32)
    return np.ascontiguousarray(np.concatenate([m0, m1], axis=1))


def l0_inputs(c, inp, x_override=None):
    b, half = c // 2, c % 2
    own, halo = core_tokens(half)
    idx = np.concatenate([own, halo])
    x = inp["x"][b][idx]
    pos = inp["positions"][b][idx].astype(np.int32).reshape(17, 128).T
    w_in = inp["ev_w_in"][0]
    qcols = np.concatenate([np.arange(h * 64, (h + 1) * 64) for h in PERM])
    cols = np.concatenate([qcols, np.arange(512, 1792), 1792 + qcols, np.arange(2304, 2816)])
    w_out = inp["ev_w_out"][0]
    rows = np.concatenate([qcols, np.arange(512, 1024)])
    sgw = inp["ev_sg_w"][0]
    sgb = inp["ev_sg_b"][0]
    if half == 1:
        sgw = sgw[:, ::-1, ::-1]
        sgb = sgb[:, ::-1]
    sgwT = np.ascontiguousarray(np.transpose(sgw, (2, 0, 1))).reshape(128, 1024)
    return {
        "x": np.ascontiguousarray(x),
        "ident": np.eye(128, dtype=np.float32),
        "l0_pos": np.ascontiguousarray(pos),
        "l0_cT": np.ascontiguousarray(inp["c"][b].reshape(8, 128).T),
        "l0_ada_w": np.ascontiguousarray(inp["ada_w"][0]),
        "l0_ada_b": np.ascontiguousarray(inp["ada_b"][0][None, :]),
        "l0_w_in": np.ascontiguousarray(w_in[:, cols]),
        "l0_w_out": np.ascontiguousarray(w_out[rows, :]),
        "l0_sink": rep(inp["ev_sink"][0][PERM]),
        "l0_sg_ln_g": rep(inp["ev_sg_ln_g"][0]),
        "l0_sg_ln_b": rep(inp["ev_sg_ln_b"][0]),
        "l0_sg_wT": sgwT,
        "l0_sg_b": np.ascontiguousarray(sgb.T),
        "l0_ln_g": rep(inp["ln_g"][0]),
        "l0_ln_b": rep(inp["ln_b"][0]),
        "l0_mask": l0_masks(),
    }


def run_l0(inp):
    nc = build_l0()
    in_maps = [l0_inputs(c, inp) for c in range(8)]
    res = run_bass_kernel_spmd(nc, in_maps, core_ids=list(range(8)))
    return [r["out"] for r in res.results]


def assemble(outs):
    full = np.zeros((4, 4096, 1024), np.float32)
    for c in range(8):
        b, half = c // 2, c % 2
        own, _ = core_tokens(half)
        full[b, own] = outs[c]
    return full


def fused_inputs(c, inp):
    d = l0_inputs(c, inp)
    d.update(l1_inputs(c, inp))
    return d


def kernel(**inputs):
    inp = {k: np.asarray(v) for k, v in inputs.items()}
    nc = build_fused()
    in_maps = [fused_inputs(c, inp) for c in range(8)]
    res = run_bass_kernel_spmd(nc, in_maps, core_ids=list(range(8)))
    return assemble([r["out"] for r in res.results])
```

```python
import contextlib
import numpy as np
import concourse.bass as bass
import concourse.mybir as mybir
from concourse.bass_utils import run_bass_kernel_spmd

F32 = mybir.dt.float32
BF16 = mybir.dt.bfloat16
I32 = mybir.dt.int32
ALU = mybir.AluOpType
AF = mybir.ActivationFunctionType
AX = mybir.AxisListType

PE, ACT, DVE, POOL, SP = "tensor", "scalar", "vector", "gpsimd", "sync"
ENGS = (PE, ACT, DVE, POOL, SP)

D = 1024
T = 2048
NT = 16
ALPHA = 4.0 ** 0.25
LN_EPS = 1e-5
PERM = [0, 4, 1, 5, 2, 6, 3, 7]
INV_FREQ = [float(np.float32(500000.0) ** np.float32(-i / 8.0)) for i in range(8)]
NEG = -30000.0
ARENA_WORDS = 34880


class Buf:
    __slots__ = ("name", "w", "r", "excl")

    def __init__(self, name):
        self.name = name
        self.w = {}
        self.r = {}
        self.excl = False


class Op:
    __slots__ = ("eng", "fn", "deps", "sig", "dma", "idx")

    def __init__(self, eng, fn):
        self.eng = eng
        self.fn = fn
        self.deps = {}
        self.sig = None
        self.dma = None
        self.idx = None


class Sched:
    def __init__(self, nc, n_dma_sems=8, n_cc_sems=2):
        self.nc = nc
        self.ops = {e: [] for e in ENGS}
        self.n_dma_sems = n_dma_sems
        self.n_cc = n_cc_sems
        nt = n_dma_sems + n_cc_sems
        self.sem_inc = [16] * n_dma_sems + [1] * n_cc_sems
        self.dma_val = [0] * nt
        self.dma_last = [None] * nt
        self.dma_rr = 0
        self.cc_rr = 0
        self.pending = {e: [] for e in ENGS}

    def barrier(self):
        toks = []
        for e in ENGS:
            if self.ops[e]:
                last = [o for o in self.ops[e] if o.dma is None]
                if last:
                    toks.append(("op", last[-1]))
        for t in self.dma_last[:self.n_dma_sems]:
            if t is not None:
                toks.append(t)
        for e in ENGS:
            self.pending[e] = list(toks)

    @staticmethod
    def _key(tok):
        return tok[1].eng if tok[0] == "op" else ("dma", tok[1])

    @staticmethod
    def _newer(a, b):
        if a is None:
            return b
        if a[0] == "op":
            return b if b[1].idx > a[1].idx else a
        return b if b[2] > a[2] else a

    def _add_dep(self, op, tok):
        if tok[0] == "op" and tok[1] is op:
            return
        k = self._key(tok)
        op.deps[k] = self._newer(op.deps.get(k), tok)

    def op(self, eng, fn, reads=(), writes=(), partial=(), dma=False, cc=False):
        o = Op(eng, fn)
        o.idx = len(self.ops[eng])
        had_pending = bool(self.pending[eng])
        if had_pending:
            for t in self.pending[eng]:
                self._add_dep(o, t)
            self.pending[eng] = []
        if cc:
            dma = True
            si = self.n_dma_sems + self.cc_rr
            self.cc_rr = (self.cc_rr + 1) % self.n_cc
        elif dma:
            si = self.dma_rr
            self.dma_rr = (self.dma_rr + 1) % self.n_dma_sems
        if dma:
            prev = self.dma_last[si]
            if prev is not None:
                self._add_dep(o, prev)
            self.dma_val[si] += self.sem_inc[si]
            o.dma = (si, self.dma_val[si])
            tok = ("dma", si, self.dma_val[si])
            self.dma_last[si] = tok
        else:
            tok = ("op", o)
        for b in reads:
            for t in b.w.values():
                self._add_dep(o, t)
            if b.excl:
                for kk, t in b.r.items():
                    if kk != eng:
                        self._add_dep(o, t)
        for b in list(writes) + list(partial):
            for t in b.w.values():
                self._add_dep(o, t)
            for t in b.r.values():
                self._add_dep(o, t)
        if not dma and not had_pending:
            raw = -1
            for b in reads:
                t = b.w.get(eng)
                if t is not None and t[0] == "op":
                    raw = max(raw, t[1].idx)
            if eng in o.deps:
                if raw < 0:
                    del o.deps[eng]
                else:
                    o.deps[eng] = ("op", self.ops[eng][raw])
        k = self._key(tok)
        for b in reads:
            b.r[k] = self._newer(b.r.get(k), tok)
        for b in writes:
            b.w = {k: tok}
            b.r = {}
        for b in partial:
            b.w[k] = self._newer(b.w.get(k), tok)
        self.ops[eng].append(o)
        return o

    def emit(self, final_eng=SP):
        nc = self.nc
        needed = set()
        for e in ENGS:
            for o in self.ops[e]:
                for t in o.deps.values():
                    if t[0] == "op":
                        needed.add(id(t[1]))
        for e in ENGS:
            c = 0
            for o in self.ops[e]:
                if o.dma is None and id(o) in needed:
                    c += 1
                    o.sig = c
        with contextlib.ExitStack() as st:
            esem = {e: st.enter_context(nc.semaphore("s_" + e)) for e in ENGS}
            dsem = [st.enter_context(nc.semaphore("d_%d" % i)) for i in range(self.n_dma_sems + self.n_cc)]
            block = st.enter_context(nc.Block())

            def run(eng_name, engine):
                known = {}
                for o in self.ops[eng_name]:
                    for k, t in o.deps.items():
                        if t[0] == "op":
                            sem, val = esem[t[1].eng], t[1].sig
                        else:
                            sem, val = dsem[t[1]], t[2]
                        if known.get(k, 0) >= val:
                            continue
                        known[k] = val
                        engine.wait_ge(sem, val)
                    ins = o.fn(engine)
                    if o.dma is not None:
                        ins.then_inc(dsem[o.dma[0]], self.sem_inc[o.dma[0]])
                    elif o.sig is not None:
                        ins.then_inc(esem[eng_name], 1)
                if eng_name == final_eng:
                    for i in range(self.n_dma_sems + self.n_cc):
                        if self.dma_val[i] > 0:
                            engine.wait_ge(dsem[i], self.dma_val[i])

            @block.tensor
            def _(e):
                run(PE, e)

            @block.scalar
            def _(e):
                run(ACT, e)

            @block.vector
            def _(e):
                run(DVE, e)

            @block.gpsimd
            def _(e):
                run(POOL, e)

            @block.sync
            def _(e):
                run(SP, e)


class Ctx:
    def __init__(self, nc, st):
        self.nc = nc
        self.st = st
        self.S = Sched(nc)
        self.bufs = {}
        self.arena = None
        self.aoff = 0
        self.asize = 0
        self.apeak = 0

    def make_arena(self, words):
        self.arena = self.sb("arena", [128, words])
        self.asize = words
        self.aoff = 0

    def areset(self):
        self.aoff = 0

    def asb(self, name, shape, dt=F32):
        if self.arena is None:
            return self.sb(name, shape, dt)[:]
        esz = {F32: 4, I32: 4, BF16: 2}[dt]
        n = 1
        for d_ in shape[1:]:
            n *= d_
        words = (n * esz + 3) // 4
        off = self.aoff
        self.aoff += words
        self.apeak = max(self.apeak, self.aoff)
        assert self.aoff <= self.asize, ("arena overflow", name, self.aoff, self.asize)
        ap = self.arena[:, off:off + words]
        if dt != F32:
            ap = ap.bitcast(dt)[:, 0:n]
        if shape[0] != 128:
            ap = ap[0:shape[0], :]
        if len(shape) == 3:
            ap = ap.rearrange("p (a b) -> p a b", b=shape[2])
        elif len(shape) == 4:
            ap = ap.rearrange("p (a b c) -> p a b c", b=shape[2], c=shape[3])
        return ap

    def sb(self, name, shape, dt=F32):
        return self.st.enter_context(self.nc.sbuf_tensor("sb_" + name, shape, dt))

    def ps(self, name, shape, dt=F32):
        return self.st.enter_context(self.nc.psum_tensor("ps_" + name, shape, dt))

    def B(self, name):
        b = self.bufs.get(name)
        if b is None:
            b = self.bufs[name] = Buf(name)
        return b

    def dram(self, name, shape, dt, kind):
        if not hasattr(self, "_dram"):
            self._dram = {}
        if name not in self._dram:
            self._dram[name] = self.nc.dram_tensor(name, list(shape), dt, kind=kind).ap()
        return self._dram[name]


def v3(ap, d):
    return ap.rearrange("p (h d) -> p h d", d=d)


def bc_mid(ap2, n):
    return ap2.unsqueeze(1).broadcast_to([ap2.shape[0], n, ap2.shape[1]])


def bc_last(ap2, n):
    return ap2.unsqueeze(2).broadcast_to([ap2.shape[0], ap2.shape[1], n])


def emit_consts(C):
    S = C.S
    C.id32 = C.sb("id32", [128, 128])
    C.idb = C.sb("idb", [128, 128], BF16)
    C.ones32 = C.sb("ones32", [128, 128])
    d_id = C.dram("ident", [128, 128], F32, "ExternalInput")
    S.op(SP, lambda e: e.dma_start(out=C.id32[:], in_=d_id[:, :]), writes=[C.B("id32")], dma=True)
    S.op(DVE, lambda e: e.tensor_copy(C.idb[:], C.id32[:]), reads=[C.B("id32")], writes=[C.B("idb")])
    S.op(DVE, lambda e: e.memset(C.ones32[:], 1.0), writes=[C.B("ones32")])


def emit_adaln(C, lname, pbanks, rows_out, crep, bcrep, brow_ap=None, cgs=range(6), tag="", ring=None, bring=None):
    S = C.S
    d_c = C.dram(lname + "_cT", [128, 8], F32, "ExternalInput")
    d_w = C.dram(lname + "_ada_w", [1024, 3072], F32, "ExternalInput")
    d_b = C.dram(lname + "_ada_b", [1, 3072], F32, "ExternalInput")
    cT = C.asb(lname + tag + "cT", [128, 8])
    brow = brow_ap if brow_ap is not None else C.asb(lname + tag + "brow", [1, 1024])
    bcT = C.B(lname + tag + "cT")
    bbrows = [C.B(lname + "brow0"), C.B(lname + "brow1")]
    S.op(SP, lambda e: e.dma_start(out=cT[:], in_=d_c[:, :]), writes=[bcT], dma=True)
    S.op(ACT, lambda e: e.activation(out=cT[:], in_=cT[:], func=AF.Silu), reads=[bcT], writes=[bcT])
    S.op(DVE, lambda e: e.tensor_copy(crep, bc_last(cT[:], 128)), reads=[bcT], writes=[bcrep])
    if ring is None:
        ring = C.ring3
    i = 0
    for cg in cgs:
        pb, bpb = pbanks[cg % 2]
        bbrow = bbrows[cg % 2]
        bsl = slice((cg % 2) * 512, (cg % 2 + 1) * 512)
        S.op(SP, lambda e, cg=cg, bsl=bsl: e.dma_start(out=brow[0:1, bsl], in_=d_b[0:1, cg * 512:(cg + 1) * 512]), writes=[bbrow], dma=True)
        for k in range(8):
            sap, sbufs = ring[i % len(ring)]
            i += 1
            S.op(SP, lambda e, k=k, cg=cg, sap=sap: e.dma_start(
                out=sap, in_=d_w[k * 128:(k + 1) * 128, cg * 512:(cg + 1) * 512]),
                writes=sbufs, dma=True)
            if bring is not None:
                bap, bbufs = bring[(i - 1) % len(bring)]
                ceng = (ACT, DVE, ACT, POOL)[i % 4]
                if ceng == ACT:
                    S.op(ACT, lambda e, sap=sap, bap=bap: e.activation(out=bap, in_=sap, func=AF.Copy), reads=sbufs, writes=bbufs)
                else:
                    S.op(ceng, lambda e, sap=sap, bap=bap: e.tensor_copy(bap, sap), reads=sbufs, writes=bbufs)
                S.op(PE, lambda e, k=k, bap=bap, pb=pb: e.matmul(pb[:, 0:512], crep[:, k, :], bap, start=(k == 0), stop=False),
                     reads=[bcrep] + bbufs, partial=[bpb] if k else (), writes=[bpb] if k == 0 else ())
                continue
            S.op(PE, lambda e, k=k, sap=sap, pb=pb: e.matmul(pb[:, 0:512], crep[:, k, :], sap,
                                                            start=(k == 0), stop=False),
                 reads=[bcrep] + sbufs, partial=[bpb] if k else (), writes=[bpb] if k == 0 else ())
        S.op(PE, lambda e, cg=cg, pb=pb, bsl=bsl: e.matmul(pb[:, 0:512], C.ones32[0:1, :], brow[0:1, bsl],
                                                  start=False, stop=True),
             reads=[C.B("ones32"), bbrow], partial=[bpb])
        dst, bdst, add_one = rows_out[cg // 2]
        half = cg % 2
        if add_one:
            S.op(ACT, lambda e, dst=dst, half=half, pb=pb: e.activation(
                out=dst[:, half * 512:(half + 1) * 512], in_=pb[:, 0:512], func=AF.Identity, bias=C.one_col[:, 0:1], scale=1.0),
                reads=[bpb, C.B("one_col")], partial=[bdst])
        else:
            S.op(ACT, lambda e, dst=dst, half=half, pb=pb: e.activation(
                out=dst[:, half * 512:(half + 1) * 512], in_=pb[:, 0:512], func=AF.Copy),
                reads=[bpb], partial=[bdst])


def emit_adaln2(C, lname, pbanks, rows_out, crep, bcrep, brow, rows, ring4k, bring2k, tag=""):
    S = C.S
    d_c = C.dram(lname + "_cT", [128, 8], F32, "ExternalInput")
    d_w = C.dram(lname + "_ada_w", [1024, 3072], F32, "ExternalInput")
    d_b = C.dram(lname + "_ada_b", [1, 3072], F32, "ExternalInput")
    cT = C.asb(lname + tag + "cT", [128, 8])
    bcT = C.B(lname + tag + "cT")
    bbrow = C.B(lname + tag + "brow")
    S.op(SP, lambda e: e.dma_start(out=cT[:], in_=d_c[:, :]), writes=[bcT], dma=True)
    S.op(ACT, lambda e: e.activation(out=cT[:], in_=cT[:], func=AF.Silu), reads=[bcT], writes=[bcT])
    S.op(DVE, lambda e: e.tensor_copy(crep, bc_last(cT[:], 128)), reads=[bcT], writes=[bcrep])
    i = 0
    for r in rows:
        S.op(SP, lambda e, r=r: e.dma_start(out=brow[0:1, 0:1024], in_=d_b[0:1, r * 1024:(r + 1) * 1024]), writes=[bbrow], dma=True)
        for k in range(8):
            sap, sbufs = ring4k[i % len(ring4k)]
            bap, bbufs = bring2k[i % len(bring2k)]
            S.op(SP, lambda e, k=k, r=r, sap=sap: e.dma_start(out=sap, in_=d_w[k * 128:(k + 1) * 128, r * 1024:(r + 1) * 1024]),
                 writes=sbufs, dma=True)
            if i % 2:
                S.op(ACT, lambda e, sap=sap, bap=bap: e.activation(out=bap, in_=sap, func=AF.Copy), reads=sbufs, writes=bbufs)
            else:
                S.op(DVE, lambda e, sap=sap, bap=bap: e.tensor_copy(bap, sap), reads=sbufs, writes=bbufs)
            i += 1
            for hh in range(2):
                pb, bpb = pbanks[hh]
                S.op(PE, lambda e, k=k, bap=bap, pb=pb, hh=hh: e.matmul(pb[:, 0:512], crep[:, k, :], bap[:, hh * 512:(hh + 1) * 512],
                                                                      start=(k == 0), stop=False),
                     reads=[bcrep] + bbufs, partial=[bpb] if k else (), writes=[bpb] if k == 0 else ())
        dst, bdst, add_one = rows_out[r]
        for hh in range(2):
            pb, bpb = pbanks[hh]
            S.op(PE, lambda e, pb=pb, hh=hh: e.matmul(pb[:, 0:512], C.ones32[0:1, :], brow[0:1, hh * 512:(hh + 1) * 512], start=False, stop=True),
                 reads=[C.B("ones32"), bbrow], partial=[bpb])
            if add_one:
                S.op(ACT, lambda e, dst=dst, hh=hh, pb=pb: e.activation(out=dst[:, hh * 512:(hh + 1) * 512], in_=pb[:, 0:512], func=AF.Identity,
                                                                      bias=C.one_col[:, 0:1], scale=1.0),
                     reads=[bpb, C.B("one_col")], partial=[bdst])
            else:
                S.op(ACT, lambda e, dst=dst, hh=hh, pb=pb: e.activation(out=dst[:, hh * 512:(hh + 1) * 512], in_=pb[:, 0:512], func=AF.Copy),
                     reads=[bpb], partial=[bdst])


def emit_weight_bf16(C, dname, dst, bdst, nk, ncols, ring, scale_row=None, bscale=None, chunk=512):
    S = C.S
    d_w = C.dram(dname, [nk * 128, ncols], F32, "ExternalInput")
    i = 0
    engs = [DVE, ACT] if scale_row is None else [DVE]
    for k in range(nk):
        for c0 in range(0, ncols, chunk):
            n = min(chunk, ncols - c0)
            sap, sbufs = ring[i % len(ring)]
            S.op(SP, lambda e, k=k, c0=c0, n=n, sap=sap: e.dma_start(
                out=sap[:, 0:n], in_=d_w[k * 128:(k + 1) * 128, c0:c0 + n]), writes=sbufs, dma=True)
            eng = engs[i % len(engs)]
            if scale_row is not None:
                S.op(eng, lambda e, k=k, c0=c0, n=n, sap=sap: e.tensor_tensor(
                    dst[:, k, c0:c0 + n], sap[:, 0:n], scale_row[:, c0:c0 + n], ALU.mult),
                    reads=sbufs + [bscale], partial=[bdst])
            elif eng == ACT:
                S.op(ACT, lambda e, k=k, c0=c0, n=n, sap=sap: e.activation(
                    out=dst[:, k, c0:c0 + n], in_=sap[:, 0:n], func=AF.Copy), reads=sbufs, partial=[bdst])
            else:
                S.op(eng, lambda e, k=k, c0=c0, n=n, sap=sap: e.tensor_copy(
                    dst[:, k, c0:c0 + n], sap[:, 0:n]), reads=sbufs, partial=[bdst])
            i += 1


def emit_ln_tile(C, pfx, zt, bz, xdst, bxdst, lng, lnb, brows, small=None):
    S = C.S
    if small is None:
        st6, mv, sd = C.ln_st6, C.ln_mv, C.ln_sd
        bst6, bmv, bsd = C.B("ln_st6"), C.B("ln_mv"), C.B("ln_sd")
    else:
        st6, mv, sd, sfx = small
        bst6, bmv, bsd = C.B("ln_st6" + sfx), C.B("ln_mv" + sfx), C.B("ln_sd" + sfx)
    for hh in range(2):
        S.op(DVE, lambda e, hh=hh: e.bn_stats(st6[:, hh * 6:(hh + 1) * 6], zt[:, hh * 512:(hh + 1) * 512]), reads=[bz],
             writes=[bst6] if hh == 0 else (), partial=[bst6] if hh else ())
    S.op(DVE, lambda e: e.bn_aggr(mv[:, 0:2], st6[:]), reads=[bst6], writes=[bmv])
    S.op(ACT, lambda e: e.activation(out=sd[:, 0:1], in_=mv[:, 1:2], func=AF.Sqrt, bias=C.eps_col[:, 0:1], scale=1.0),
         reads=[bmv, C.B("eps_col")], writes=[bsd])
    S.op(DVE, lambda e: e.reciprocal(sd[:, 1:2], sd[:, 0:1]), reads=[bsd], partial=[bsd])
    S.op(DVE, lambda e: e.scalar_tensor_tensor(sd[:, 2:3], mv[:, 0:1], -1.0, sd[:, 1:2], ALU.mult, ALU.mult),
         reads=[bmv, bsd], partial=[bsd])
    S.op(ACT, lambda e: e.activation(out=zt[:], in_=zt[:], func=AF.Identity, scale=sd[:, 1:2], bias=sd[:, 2:3]),
         reads=[bz, bsd], writes=[bz])
    S.op(POOL, lambda e: e.tensor_tensor(zt[:], zt[:], lng[:], ALU.mult), reads=[bz, brows], writes=[bz])
    S.op(POOL, lambda e: e.tensor_tensor(xdst, zt[:], lnb[:], ALU.add), reads=[bz, brows], writes=[bxdst])


def emit_l0(C, x_dram, out_dram):
    S, nc = C.S, C.nc
    B = C.B
    xres = C.xres
    d_pos = C.dram("l0_pos", [128, 17], I32, "ExternalInput")
    d_sink = C.dram("l0_sink", [128, 8], F32, "ExternalInput")
    d_sgg = C.dram("l0_sg_ln_g", [128, 512], F32, "ExternalInput")
    d_sgb = C.dram("l0_sg_ln_b", [128, 512], F32, "ExternalInput")
    d_sgw = C.dram("l0_sg_wT", [128, 1024], F32, "ExternalInput")
    d_sgbias = C.dram("l0_sg_b", [128, 8], F32, "ExternalInput")
    d_lng = C.dram("l0_ln_g", [128, 1024], F32, "ExternalInput")
    d_lnb = C.dram("l0_ln_b", [128, 1024], F32, "ExternalInput")
    d_mask = C.dram("l0_mask", [128, 768], F32, "ExternalInput")
    w_in = C.asb("l0_w_in", [128, 8, 2816], BF16)
    w_out = C.asb("l0_w_out", [128, 8, 1024], BF16)
    sgw = C.asb("l0_sgw", [128, 8, 128], BF16)
    modrow = C.asb("l0_modrow", [128, 2, 1024])
    lnrows = C.asb("l0_lnrows", [128, 2, 1024])
    sgrows = C.asb("l0_sgrows", [128, 2, 512])
    sgbias = C.asb("l0_sgbias", [128, 8])
    nsink = C.asb("l0_nsink", [128, 8])
    sink = C.asb("l0_sink", [128, 8])
    mask = C.asb("l0_mask", [128, 768], BF16)
    kT = C.asb("l0_kT", [128, 18, 128], BF16)
    V = C.asb("l0_V", [128, 18, 128], BF16)
    posi = C.asb("l0_posi", [128, 17], I32)
    posf = C.asb("l0_posf", [128, 17])
    invf = C.asb("l0_invf", [128, 8])
    ang = C.asb("l0_ang", [128, 17, 8])
    angk = C.asb("l0_angk", [128, 17, 8])
    angi = C.asb("l0_angi", [128, 17, 8], I32)
    C2 = C.asb("l0_C2", [128, 17, 16])
    S2 = C.asb("l0_S2", [128, 17, 16])
    h32 = C.asb("l0_h32", [128, 1024])
    mask32 = h32[:, 0:768]
    hbf = C.asb("l0_hbf", [128, 1024], BF16)
    hT = C.asb("l0_hT", [128, 1024], BF16)
    qkr = C.asb("l0_qkr", [128, 640], BF16)
    qT = C.asb("l0_qT", [128, 2, 512], BF16)
    su = C.asb("l0_su", [128, 512], BF16)
    vn = C.asb("l0_vn", [128, 512])
    vnb = C.asb("l0_vnb", [128, 512], BF16)
    st8 = C.asb("l0_st8", [128, 8, 8])
    sgl = C.asb("l0_sgl", [128, 2, 1024], BF16)
    y = C.asb("l0_y", [128, 2, 1024], BF16)
    yT = C.asb("l0_yT", [128, 1024], BF16)
    P = C.asb("l0_P", [128, 2, 384], BF16)
    PT = C.asb("l0_PT", [128, 2, 384], BF16)
    att = C.asb("l0_att", [128, 6, 8])
    z = C.asb("l0_z", [128, 1024])
    gaterow = z
    yat = vn
    rot = C.asb("l0_rot", [128, 2, 160])
    sq = z[:, 512:1024]
    ptr = C.ps_ptr
    pb = C.ps_f32[0:2]
    sbk = C.ps_f32[2:4]
    ob = C.ps_f32[4]
    svb = C.ps_f32[5]
    bptr = [B("ptr0"), B("ptr1")]
    bpb = [B("pb0"), B("pb1")]
    bsbk = [B("pb2"), B("pb3")]
    cnt = {"ptr": 0, "pb": 0}
    for bn in ("ptr0", "ptr1", "pb0", "pb1", "pb2", "pb3", "ob", "svb"):
        B(bn).excl = True

    def next_ptr():
        i = cnt["ptr"] % 2
        cnt["ptr"] += 1
        return ptr[i], bptr[i]

    def next_pb():
        i = cnt["pb"] % 2
        cnt["pb"] += 1
        return pb[i], bpb[i]

    w4k = (C.wstage[:].rearrange("p a b -> p (a b)")[:, 0:1024], [C.bwstage[0], C.bwstage[1]])
    ring0 = [w4k,
             (y.rearrange("p a b -> p (a b)").bitcast(F32), [B("y_a0"), B("y_s0"), B("y_a1"), B("y_s1")]),
             (sgl.rearrange("p a b -> p (a b)").bitcast(F32), [B("sgl_a0"), B("sgl_s0"), B("sgl_a1"), B("sgl_s1")])]
    bring0 = [(hT[:], [B("hT")]), (hbf[:], [B("hbf")]), (yT[:], [B("yT")]), (qT.rearrange("p a b -> p (a b)"), [B("qT0"), B("qT1")])]
    S.op(SP, lambda e: e.dma_start(out=xres[:, 0, :], in_=x_dram[0:128, :]), writes=[B("x0")], dma=True)
    S.op(SP, lambda e: e.dma_start(out=posi[:], in_=d_pos[:, :]), writes=[B("posi")], dma=True)
    S.op(SP, lambda e: e.dma_start(out=sink[:], in_=d_sink[:, :]), writes=[B("sink")], dma=True)
    S.op(SP, lambda e: e.dma_start(out=mask32, in_=d_mask[:, :]), writes=[B("h32")], dma=True)
    S.op(POOL, lambda e: e.tensor_copy(mask[:], mask32), reads=[B("h32")], writes=[B("mask")])
    S.op(SP, lambda e: e.dma_start(out=sgbias[:], in_=d_sgbias[:, :]), writes=[B("sgbias")], dma=True)
    crep_b = h32[:, 0:512].bitcast(BF16).rearrange("p (k m) -> p k m", m=128)
    brow0 = C.asb("l0_brow", [1, 1024])
    emit_adaln2(C, "l0", [(pb[0], bpb[0]), (pb[1], bpb[1])],
                [(modrow[:, 0, :], B("l0shift"), False), (modrow[:, 1, :], B("l0scale"), True), (gaterow[:], B("z"), False)],
                crep_b, B("h32"), brow0, (0, 1, 2), ring0, bring0)
    emit_weight_bf16(C, "l0_w_in", w_in, B("l0w_in"), 8, 2816, ring0, chunk=1024)
    emit_weight_bf16(C, "l0_w_out", w_out, B("l0w_out"), 8, 1024, ring0, scale_row=gaterow, bscale=B("z"), chunk=1024)
    S.op(SP, lambda e: e.dma_start(out=sgrows[:, 0, :], in_=d_sgg[:, :]), partial=[B("sgrows")], dma=True)
    S.op(SP, lambda e: e.dma_start(out=sgrows[:, 1, :], in_=d_sgb[:, :]), partial=[B("sgrows")], dma=True)
    S.op(SP, lambda e: e.dma_start(out=lnrows[:, 0, :], in_=d_lng[:, :]), partial=[B("l0lnrows")], dma=True)
    S.op(SP, lambda e: e.dma_start(out=lnrows[:, 1, :], in_=d_lnb[:, :]), partial=[B("l0lnrows")], dma=True)
    S.op(SP, lambda e: e.dma_start(out=xres[:, 1, :], in_=x_dram[128:256, :]), writes=[B("x1")], dma=True)
    for hh in range(2):
        slot = hh
        S.op(SP, lambda e, hh=hh, slot=slot: e.dma_start(out=C.wstage[:, slot, 0:512], in_=d_sgw[:, hh * 512:(hh + 1) * 512]),
             writes=[C.bwstage[slot]], dma=True)
        S.op(POOL, lambda e, hh=hh, slot=slot: e.tensor_copy(sgw[:, hh * 4:(hh + 1) * 4, :].rearrange("p g q -> p (g q)"), C.wstage[:, slot, 0:512]),
             reads=[C.bwstage[slot]], partial=[B("sgw")])
    S.op(POOL, lambda e: e.tensor_scalar(nsink[:], sink[:], -1.0, None, ALU.mult), reads=[B("sink")], writes=[B("nsink")])
    S.op(POOL, lambda e: e.memset(kT[:, 0, :], 0.0), partial=[B("kT0")])
    S.op(POOL, lambda e: e.memset(V[:, 0, :], 0.0), partial=[B("V0")])

    for f in range(8):
        S.op(POOL, lambda e, f=f: e.memset(invf[:, f:f + 1], INV_FREQ[f]), partial=[B("invf")])
    S.op(DVE, lambda e: e.tensor_copy(posf[:], posi[:]), reads=[B("posi")], writes=[B("posf")])
    S.op(DVE, lambda e: e.tensor_tensor(ang[:], bc_last(posf[:], 8), bc_mid(invf[:], 17), ALU.mult),
         reads=[B("posf"), B("invf")], writes=[B("ang")])
    TWO_PI = float(2 * np.pi)

    def sin_into(dst_ap, shift, negate, tag):
        bk, bi_, br = B("angk"), B("angi"), B("angr")
        S.op(DVE, lambda e: e.tensor_scalar(angk[:], ang[:], shift, 1.0 / TWO_PI, ALU.add, ALU.mult), reads=[B("ang")], writes=[bk])
        S.op(DVE, lambda e: e.tensor_copy(angi[:], angk[:]), reads=[bk], writes=[bi_])
        S.op(DVE, lambda e: e.tensor_copy(angk[:], angi[:]), reads=[bi_], writes=[bk])
        S.op(DVE, lambda e: e.scalar_tensor_tensor(angk[:], angk[:], -TWO_PI, ang[:], ALU.mult, ALU.add), reads=[bk, B("ang")], writes=[bk])
        S.op(DVE, lambda e: e.tensor_scalar(angk[:], angk[:], shift, float(np.pi), ALU.add, ALU.min), reads=[bk], writes=[bk])
        S.op(DVE, lambda e: e.tensor_scalar(angk[:], angk[:], float(-np.pi), None, ALU.max), reads=[bk], writes=[bk])
        S.op(ACT, lambda e: e.activation(out=dst_ap, in_=angk[:], func=AF.Sin, scale=(-1.0 if negate else 1.0)),
             reads=[bk], partial=[B(tag)])

    sin_into(C2[:, :, 0:8], float(np.pi / 2), False, "C2")
    sin_into(C2[:, :, 8:16], float(np.pi / 2), False, "C2")
    sin_into(S2[:, :, 0:8], 0.0, True, "S2")
    sin_into(S2[:, :, 8:16], 0.0, False, "S2")

    GA, GB, GC, GD, GE, GF = (0, 512), (512, 256), (768, 512), (1280, 512), (1792, 512), (2304, 512)

    def rotary(src_bank, nh, dst3, t, slot, bdst):
        s3 = v3(src_bank[:, 0:nh * 64], 64)
        ta = v3(rot[:, 0, 0:nh * 16], 16)
        tb = v3(rot[:, 1, 0:nh * 16], 16)
        br_ = B("rot")
        S.op(DVE, lambda e: e.tensor_tensor(ta, s3[:, :, 0:16], bc_mid(C2[:, t, :], nh), ALU.mult),
             reads=[slot, B("C2")], writes=[br_])
        S.op(DVE, lambda e: e.tensor_tensor(tb[:, :, 0:8], s3[:, :, 8:16], bc_mid(S2[:, t, 0:8], nh), ALU.mult),
             reads=[slot, B("S2")], partial=[br_])
        S.op(DVE, lambda e: e.tensor_tensor(tb[:, :, 8:16], s3[:, :, 0:8], bc_mid(S2[:, t, 8:16], nh), ALU.mult),
             reads=[slot, B("S2")], partial=[br_])
        S.op(DVE, lambda e: e.tensor_tensor(dst3[:, :, 0:16], ta, tb, ALU.add), reads=[br_], partial=[bdst])

    def modulate(t):
        if t == NT:
            S.op(SP, lambda e: e.dma_start(out=h32[:], in_=x_dram[2048:2176, :]), writes=[B("h32")], dma=True)
            xsrc, bxs = h32[:], B("h32")
        else:
            xsrc, bxs = xres[:, t, :], B("x%d" % t)
        S.op(POOL, lambda e: e.tensor_tensor(h32[:], xsrc, modrow[:, 1, :], ALU.mult), reads=[bxs, B("l0scale")], writes=[B("h32")])
        S.op(POOL, lambda e: e.tensor_tensor(hbf[:], h32[:], modrow[:, 0, :], ALU.add), reads=[B("h32"), B("l0shift")], writes=[B("hbf")])

    def front(t):
        par = t % 2
        last = (t == NT)
        pt, bpt = next_ptr()
        for k in range(8):
            S.op(PE, lambda e, k=k, pt=pt: e.transpose(pt[:, k * 128:(k + 1) * 128], hbf[:, k * 128:(k + 1) * 128], C.idb[:]),
                 reads=[B("hbf"), B("idb")], writes=[bpt] if k == 0 else (), partial=[bpt] if k else ())
        if t < NT:
            modulate(t + 1)
        S.op(ACT, lambda e, pt=pt: e.activation(out=hT[:], in_=pt[:], func=AF.Copy), reads=[bpt], writes=[B("hT")])
        yield "a"

        def proj(g):
            c0, n = g
            bank, bbank = next_pb()
            for k in range(8):
                S.op(PE, lambda e, k=k: e.matmul(bank[:, 0:n], hT[:, k * 128:(k + 1) * 128], w_in[:, k, c0:c0 + n],
                                                 start=(k == 0), stop=(k == 7)),
                     reads=[B("hT"), B("l0w_in")], writes=[bbank] if k == 0 else (), partial=[bbank] if k else ())
            return bank, bbank

        if not last:
            bank, bb = proj(GA)
            S.op(ACT, lambda e, bank=bank: e.activation(out=qkr[:, 0:512], in_=bank[:, 0:512], func=AF.Copy), reads=[bb], writes=[B("qkr_q")])
            rotary(bank, 8, v3(qkr[:, 0:512], 64), t, bb, B("qkr_q"))
        bank, bb = proj(GB)
        S.op(ACT, lambda e, bank=bank: e.activation(out=qkr[:, 512:640], in_=bank[:, 0:128], func=AF.Copy), reads=[bb], writes=[B("qkr_k")])
        S.op(ACT, lambda e, bank=bank: e.activation(out=V[:, t + 1, :], in_=bank[:, 128:256], func=AF.Copy), reads=[bb], writes=[B("V%d" % (t + 1))])
        rotary(bank, 2, v3(qkr[:, 512:640], 64), t, bb, B("qkr_k"))
        pt, bpt = next_ptr()
        first = True
        if not last:
            for j in range(4):
                S.op(PE, lambda e, j=j, pt=pt: e.transpose(pt[:, j * 128:(j + 1) * 128], qkr[:, j * 128:(j + 1) * 128], C.idb[:]),
                     reads=[B("qkr_q"), B("idb")], writes=[bpt] if first else (), partial=() if first else [bpt])
                first = False
        S.op(PE, lambda e, pt=pt: e.transpose(pt[:, 512:640], qkr[:, 512:640], C.idb[:]),
             reads=[B("qkr_k"), B("idb")], writes=[bpt] if first else (), partial=() if first else [bpt])
        if not last:
            S.op(DVE, lambda e, pt=pt: e.tensor_copy(qT[:, par, :], pt[:, 0:512]), reads=[bpt], writes=[B("qT%d" % par)])
        S.op(DVE, lambda e, pt=pt: e.tensor_copy(kT[:, t + 1, :], pt[:, 512:640]), reads=[bpt], writes=[B("kT%d" % (t + 1))])
        if last:
            return
        bank, bb = proj(GC)
        S.op(ACT, lambda e, bank=bank: e.activation(out=su[:], in_=bank[:, 0:512], func=AF.Copy), reads=[bb], writes=[B("su")])
        bank, bb = proj(GD)
        bst = B("st8")
        S.op(ACT, lambda e, bank=bank: e.activation(out=sq, in_=bank[:, 0:512], func=AF.Square), reads=[bb], writes=[B("z")])
        S.op(DVE, lambda e, bank=bank: e.tensor_reduce(st8[:, 0, :], v3(bank[:, 0:512], 64), AX.X, ALU.add), reads=[bb], writes=[bst])
        S.op(DVE, lambda e: e.tensor_reduce(st8[:, 1, :], v3(sq, 64), AX.X, ALU.add), reads=[B("z")], partial=[bst])
        S.op(DVE, lambda e: e.tensor_scalar(st8[:, 2, :], st8[:, 0, :], 1.0 / 64, None, ALU.mult), reads=[bst], partial=[bst])
        S.op(DVE, lambda e: e.tensor_tensor(st8[:, 3, :], st8[:, 2, :], st8[:, 2, :], ALU.mult), reads=[bst], partial=[bst])
        S.op(DVE, lambda e: e.scalar_tensor_tensor(st8[:, 4, :], st8[:, 1, :], 1.0 / 64, st8[:, 3, :], ALU.mult, ALU.subtract),
             reads=[bst], partial=[bst])
        S.op(ACT, lambda e: e.activation(out=st8[:, 5, :], in_=st8[:, 4, :], func=AF.Sqrt, bias=C.eps_col[:, 0:1], scale=1.0),
             reads=[bst, B("eps_col")], partial=[bst])
        S.op(DVE, lambda e: e.reciprocal(st8[:, 6, :], st8[:, 5, :]), reads=[bst], partial=[bst])
        S.op(DVE, lambda e, bank=bank: e.tensor_tensor(v3(vn[:], 64), v3(bank[:, 0:512], 64), bc_last(st8[:, 2, :], 64), ALU.subtract),
             reads=[bb, bst], writes=[B("vn")])
        S.op(DVE, lambda e: e.tensor_tensor(v3(vn[:], 64), v3(vn[:], 64), bc_last(st8[:, 6, :], 64), ALU.mult),
             reads=[B("vn"), bst], writes=[B("vn")])
        S.op(POOL, lambda e: e.tensor_tensor(vn[:], vn[:], sgrows[:, 0, :], ALU.mult), reads=[B("vn"), B("sgrows")], writes=[B("vn")])
        S.op(POOL, lambda e: e.tensor_tensor(vnb[:], vn[:], sgrows[:, 1, :], ALU.add), reads=[B("vn"), B("sgrows")], writes=[B("vnb")])
        yield "b1"
        bankE, bbE = proj(GE)
        bankF, bbF = proj(GF)
        yield "gm"
        S.op(ACT, lambda e, bank=bankE: e.activation(out=sgl[:, par, 0:512], in_=bank[:, 0:512], func=AF.Silu), reads=[bbE], writes=[B("sgl_a%d" % par)])
        S.op(ACT, lambda e, bank=bankF: e.activation(out=sgl[:, par, 512:1024], in_=bank[:, 0:512], func=AF.Silu), reads=[bbF], writes=[B("sgl_s%d" % par)])
        for g in range(8):
            S.op(PE, lambda e, g=g: e.matmul(svb[:, g * 64:(g + 1) * 64], sgw[:, g, :], vnb[:, g * 64:(g + 1) * 64], start=True, stop=True),
                 reads=[B("sgw"), B("vnb")], writes=[B("svb")] if g == 0 else (), partial=[B("svb")] if g else ())
        S.op(POOL, lambda e: e.tensor_tensor(su[:], su[:], sgl[:, par, 512:1024], ALU.mult), reads=[B("su"), B("sgl_s%d" % par)], writes=[B("su")])
        S.op(DVE, lambda e: e.tensor_tensor(v3(sq, 64), v3(svb[:, 0:512], 64), bc_last(sgbias[:], 64), ALU.add),
             reads=[B("svb"), B("sgbias")], writes=[B("z")])
        S.op(DVE, lambda e: e.tensor_tensor(y[:, par, 512:1024], sq, su[:], ALU.mult), reads=[B("z"), B("su")], writes=[B("y_s%d" % par)])

    def attn(t, mid=None):
        par = t % 2
        mrow = 0 if t == 0 else 1
        batt = B("att")

        def stage1(hp):
            j, half = hp // 2, hp % 2
            sb_, bsb = sbk[hp % 2], bsbk[hp % 2]
            S.op(PE, lambda e: e.matmul(sb_[:, 0:384], qT[half * 64:(half + 1) * 64, par, j * 128:(j + 1) * 128],
                                        kT[half * 64:(half + 1) * 64, t:t + 3, :].rearrange("p a b -> p (a b)"), start=True, stop=False),
                 reads=[B("qT%d" % par), B("kT%d" % t), B("kT%d" % (t + 1)), B("kT%d" % (t + 2))], writes=[bsb])
            S.op(PE, lambda e: e.matmul(sb_[:, 0:384], C.idb[:], mask[:, mrow * 384:(mrow + 1) * 384], start=False, stop=True),
                 reads=[B("idb"), B("mask")], partial=[bsb])
            S.op(DVE, lambda e: e.tensor_reduce(att[:, 0, hp:hp + 1], sb_[:, 0:384], AX.X, ALU.max), reads=[bsb], writes=[B("amx%d" % hp)])
            S.op(DVE, lambda e: e.tensor_scalar(att[:, 1, hp:hp + 1], att[:, 0, hp:hp + 1], C.m0125_col[:, 0:1], nsink[:, hp:hp + 1], ALU.mult, ALU.min),
                 reads=[B("amx%d" % hp), B("nsink"), B("m0125_col")], writes=[B("anm%d" % hp)])
            S.op(ACT, lambda e: e.activation(out=P[:, hp % 2, :], in_=sb_[:, 0:384], func=AF.Exp, bias=att[:, 1, hp:hp + 1], scale=0.125,
                                             accum_out=att[:, 2, hp:hp + 1]),
                 reads=[bsb, B("anm%d" % hp), B("ars")], writes=[B("P%d" % (hp % 2)), B("ars%d" % hp)])

        def stage2a(hp):
            pt, bpt = next_ptr()
            for kt in range(3):
                S.op(PE, lambda e, kt=kt: e.transpose(pt[:, kt * 128:(kt + 1) * 128], P[:, hp % 2, kt * 128:(kt + 1) * 128], C.idb[:]),
                     reads=[B("P%d" % (hp % 2)), B("idb")], writes=[bpt] if kt == 0 else (), partial=[bpt] if kt else ())
            S.op(DVE, lambda e: e.tensor_copy(PT[:, hp % 2, :], pt[:, 0:384]), reads=[bpt], writes=[B("PT%d" % (hp % 2))])

        def stage2b(hp):
            half = hp % 2
            for kt in range(3):
                S.op(PE, lambda e, kt=kt: e.matmul(ob[:, hp * 64:(hp + 1) * 64], PT[:, hp % 2, kt * 128:(kt + 1) * 128],
                                                   V[:, t + kt, half * 64:(half + 1) * 64], start=(kt == 0), stop=(kt == 2)),
                     reads=[B("PT%d" % (hp % 2)), B("V%d" % (t + kt))],
                     writes=[B("ob")] if (hp == 0 and kt == 0) else (), partial=() if (hp == 0 and kt == 0) else [B("ob")])

        S.op(DVE, lambda e: e.memset(att[:, 2, :], 0.0), writes=[B("ars")] + [B("ars%d" % h) for h in range(8)])
        stage1(0)
        stage1(1)
        if mid is not None:
            mid()
        for hp in range(8):
            stage2a(hp)
            if hp + 2 < 8:
                stage1(hp + 2)
            stage2b(hp)
        anm = [B("anm%d" % h) for h in range(8)]
        ars = [B("ars%d" % h) for h in range(8)]
        S.op(DVE, lambda e: e.tensor_tensor(att[:, 3, :], att[:, 1, :], sink[:], ALU.add), reads=anm + [B("sink")], writes=[batt])
        S.op(ACT, lambda e: e.activation(out=att[:, 3, :], in_=att[:, 3, :], func=AF.Exp), reads=[batt], writes=[batt])
        S.op(DVE, lambda e: e.tensor_tensor(att[:, 4, :], att[:, 3, :], att[:, 2, :], ALU.add), reads=[batt] + ars, writes=[batt])
        S.op(DVE, lambda e: e.reciprocal(att[:, 5, :], att[:, 4, :]), reads=[batt], writes=[batt])
        S.op(DVE, lambda e: e.tensor_tensor(v3(yat[:], 64), v3(ob[:, 0:512], 64), bc_last(att[:, 5, :], 64), ALU.mult),
             reads=[B("ob"), batt], writes=[B("vn")])
        S.op(DVE, lambda e: e.tensor_tensor(y[:, par, 0:512], yat[:], sgl[:, par, 0:512], ALU.mult),
             reads=[B("vn"), B("sgl_a%d" % par)], writes=[B("y_a%d" % par)])

    def back(t):
        par = t % 2
        pt, bpt = next_ptr()
        for k in range(8):
            S.op(PE, lambda e, k=k: e.transpose(pt[:, k * 128:(k + 1) * 128], y[:, par, k * 128:(k + 1) * 128], C.idb[:]),
                 reads=[B("y_a%d" % par), B("y_s%d" % par), B("idb")], writes=[bpt] if k == 0 else (), partial=[bpt] if k else ())
        S.op(ACT, lambda e: e.activation(out=yT[:], in_=pt[:], func=AF.Copy), reads=[bpt], writes=[B("yT")])
        for cg in range(2):
            bank, bb = next_pb()
            for k in range(8):
                S.op(PE, lambda e, k=k, cg=cg, bank=bank: e.matmul(bank[:, 0:512], yT[:, k * 128:(k + 1) * 128], w_out[:, k, cg * 512:(cg + 1) * 512],
                                                        start=(k == 0), stop=(k == 7)),
                     reads=[B("yT"), B("l0w_out")], writes=[bb] if k == 0 else (), partial=[bb] if k else ())
            S.op(DVE, lambda e, cg=cg, bank=bank: e.scalar_tensor_tensor(z[:, cg * 512:(cg + 1) * 512], xres[:, t, cg * 512:(cg + 1) * 512], ALPHA,
                                                              bank[:, 0:512], ALU.mult, ALU.add),
                 reads=[B("x%d" % t), bb], writes=[B("z")] if cg == 0 else (), partial=[B("z")] if cg else ())
        yield "a"
        emit_ln_tile(C, "l0", z, B("z"), xres[:, t, :], B("x%d" % t), lnrows[:, 0, :], lnrows[:, 1, :], B("l0lnrows"))
        if out_dram is not None:
            S.op(SP, lambda e: e.dma_start(out=out_dram[t * 128:(t + 1) * 128, :], in_=xres[:, t, :]), reads=[B("x%d" % t)], dma=True)

    import os
    dbg = os.environ.get("L0_DBG", "")
    if dbg == "a":
        for t in range(NT):
            S.op(SP, lambda e, t=t: e.dma_start(out=out_dram[t * 128:(t + 1) * 128, :], in_=xres[:, t, :]), reads=[B("x%d" % t)], dma=True)
        return
    def step(g):
        try:
            next(g)
        except StopIteration:
            pass

    def finish(g):
        for _ in g:
            pass

    modulate(0)
    pend_back = None
    for t in range(NT + 1):
        if t + 2 < NT:
            S.op(SP, lambda e, t=t: e.dma_start(out=xres[:, t + 2, :], in_=x_dram[(t + 2) * 128:(t + 3) * 128, :]),
                 writes=[B("x%d" % (t + 2))], dma=True)
        f = front(t)
        step(f)
        if pend_back is not None:
            finish(pend_back)
            pend_back = None
        step(f)
        if t >= 1:
            attn(t - 1, mid=lambda f=f: step(f))
            finish(f)
            b = back(t - 1)
            step(b)
            pend_back = b
        else:
            finish(f)
    if pend_back is not None:
        finish(pend_back)
    if C.send1 is not None:
        S.op(SP, lambda e: e.dma_start(out=C.send1[0:1, :], in_=xres[127:128, 15, :]), reads=[B("x15")], partial=[B("send1")], dma=True)
        S.op(SP, lambda e: e.dma_start(out=C.send1[1:2, :], in_=xres[126:127, 15, :]), reads=[B("x15")], partial=[B("send1")], dma=True)
        S.op(POOL, lambda e: e.collective_compute("AllGather", ALU.bypass, replica_groups=[[0, 1], [2, 3], [4, 5], [6, 7]],
                                                 ins=[C.send1], outs=[C.recv1]), reads=[B("send1")], writes=[B("recv1")], cc=True)
    if dbg in ("b", "c"):
        for t in range(NT):
            S.op(SP, lambda e, t=t: e.dma_start(out=out_dram[t * 128:(t + 1) * 128, :], in_=xres[:, t, :]), reads=[B("x%d" % t)], dma=True)


def alloc_common(C):
    C.xres = C.sb("xres", [128, NT, 1024])
    C.wstage = C.sb("wstage", [128, 3, 512])
    C.ps_ptr = [C.ps("ptr%d" % i, [128, 1024], BF16) for i in range(2)]
    C.ps_f32 = [C.ps("pf%d" % i, [128, 512]) for i in range(6)]
    if not hasattr(C, "send1"):
        C.send1 = None
    C.bwstage = [C.B("wstage%d" % i) for i in range(3)]
    C.ring3 = [(C.wstage[:, i, :], [C.bwstage[i]]) for i in range(3)]
    C.ln_st6 = C.sb("ln_st6", [128, 12])
    C.m0125_col = C.sb("m0125_col", [128, 1])
    C.S.op(DVE, lambda e: e.memset(C.m0125_col[:], -0.125), writes=[C.B("m0125_col")])
    C.ln_mv = C.sb("ln_mv", [128, 2])
    C.ln_sd = C.sb("ln_sd", [128, 4])
    C.eps_col = C.sb("eps_col", [128, 1])
    C.one_col = C.sb("one_col", [128, 1])
    C.S.op(DVE, lambda e: e.memset(C.eps_col[:], LN_EPS), writes=[C.B("eps_col")])
    C.S.op(DVE, lambda e: e.memset(C.one_col[:], 1.0), writes=[C.B("one_col")])


def build_l0():
    nc = bass.Bass("TRN2", target_bir_lowering=False)
    with contextlib.ExitStack() as st:
        C = Ctx(nc, st)
        x_d = C.dram("x", [2176, 1024], F32, "ExternalInput")
        o_d = C.dram("out", [2048, 1024], F32, "ExternalOutput")
        alloc_common(C)
        emit_consts(C)
        emit_l0(C, x_d, o_d)
        print("sbuf remaining after l0 alloc:", nc.sbuf_bytes_remaining)
        C.S.emit()
    return nc


def rev_ap(ap2):
    n = ap2.shape[1]
    return bass.AP(ap2.tensor, ap2.offset + (n - 1), [list(ap2.ap[0]), [-1, n]])


def emit_l1(C, out_dram):
    S, nc = C.S, C.nc
    B = C.B
    xres = C.xres
    S.barrier()
    C.areset()
    RG = [[0, 1], [2, 3], [4, 5], [6, 7]]
    d_win = C.dram("l1_w_in", [1024, 2048], F32, "ExternalInput")
    d_w5 = C.dram("l1_w5", [128, 40], F32, "ExternalInput")
    d_cb = C.dram("l1_cb", [128, 8], F32, "ExternalInput")
    d_wg = C.dram("l1_wg", [128, 4096], F32, "ExternalInput")
    d_bg = C.dram("l1_bg", [128, 32], F32, "ExternalInput")
    d_lam = C.dram("l1_lam", [128, 16], F32, "ExternalInput")
    d_lng = C.dram("l1_ln_g", [128, 1024], F32, "ExternalInput")
    d_lnb = C.dram("l1_ln_b", [128, 1024], F32, "ExternalInput")
    d_sel = C.dram("sel", [128, 2], F32, "ExternalInput")
    hT = C.asb("l1_hT", [128, 8, 2056], BF16)
    w_out = hT.rearrange("p k t -> p (k t)")[:, 0:8192].rearrange("p (k c) -> p k c", c=1024)
    yT = C.asb("l1_yT", [128, 8, 2048], BF16)
    xc = C.asb("l1_xc", [128, 8, 2048], BF16)
    wg = C.asb("l1_wg", [128, 4, 8, 128], BF16)
    wt = C.asb("l1_wt", [128, 8, 128], BF16)
    xr = C.asb("l1_xr", [128, 2052], BF16)
    sg = xr[:, 0:2048]
    lnrows = xc[:, 4:6, :].rearrange("p a b -> p (a b)").bitcast(F32).rearrange("p (a b) -> p a b", a=2)
    dg = C.asb("l1_dg", [128, 10, 128], BF16)
    modrow = xc[:, 0:2, :].rearrange("p a b -> p (a b)").bitcast(F32).rearrange("p (a b) -> p a b", a=2)
    wk = C.asb("l1_wk", [128, 5120])
    WA = wk[:, 0:2048]
    WTI = wk[:, 2048:3072].bitcast(BF16)
    WVB = wk[:, 3072:4096].bitcast(BF16)
    WH = wk[:, 4096:5120].rearrange("p (a b) -> p a b", a=2)
    h32 = wk[:, 0:1024]
    hbf = wk[:, 1024:1536].bitcast(BF16)
    hb2 = wk[:, 2048:3072]
    z = h32
    gaterow = wk[:, 1024:2048]
    crep = wk[:, 0:1024].rearrange("p (k m) -> p k m", m=128)
    brow = wk[0:1, 4096:5120]
    hbg = C.asb("l1_hbg", [128, 32])
    hn = C.asb("l1_hn", [128, 16])
    w5 = C.asb("l1_w5", [128, 40])
    cb = C.asb("l1_cb", [128, 8])
    bg = C.asb("l1_bg", [128, 32])
    lam = C.asb("l1_lam", [128, 16])
    nsp8 = C.asb("l1_nsp8", [128, 16])
    sel = C.asb("l1_sel", [128, 2])
    carry = C.asb("l1_carry", [128, 8])
    cga = C.asb("l1_cga", [128, 8])
    cgb = C.asb("l1_cgb", [128, 8])
    stA = C.asb("l1_stA", [128, 8])
    stB = C.asb("l1_stB", [128, 8])
    ptr = C.ps_ptr
    pb = C.ps_f32[0:4]
    bptr = [B("ptr0"), B("ptr1")]
    bpb = [B("pb%d" % i) for i in range(4)]
    for b_ in bptr + bpb:
        b_.excl = True
    cnt = {"ptr": 0, "pb": 0}

    def next_ptr():
        i = cnt["ptr"] % 2
        cnt["ptr"] += 1
        return ptr[i], bptr[i]

    def next_pb():
        i = cnt["pb"] % 4
        cnt["pb"] += 1
        return pb[i], bpb[i]

    w4k = (C.wstage[:].rearrange("p a b -> p (a b)")[:, 0:1024], [C.bwstage[0], C.bwstage[1]])
    ring1 = [w4k] + [(yT[:, k, :].bitcast(F32), [B("yT%d" % k)]) for k in range(8)]
    bring1 = [(xc[:, 3, 0:1024], [B("xc3")]), (xc[:, 3, 1024:2048], [B("xc3")]), (xc[:, 2, 0:1024], [B("xc2")]), (xc[:, 2, 1024:2048], [B("xc2")])]
    crep_b = wk[:, 0:512].bitcast(BF16).rearrange("p (k m) -> p k m", m=128)
    for nm, dst, src in (("w5", w5, d_w5), ("cb", cb, d_cb), ("bg", bg, d_bg), ("lam", lam, d_lam), ("sel", sel, d_sel)):
        S.op(SP, lambda e, dst=dst, src=src: e.dma_start(out=dst, in_=src[:, :]), writes=[B("l1" + nm)], dma=True)
    emit_adaln2(C, "l1", [(pb[0], bpb[0]), (pb[1], bpb[1])],
                [(modrow[:, 0, :], B("modrow"), False), (modrow[:, 1, :], B("modrow"), True), (gaterow, B("t23"), False)],
                crep_b, B("t01"), brow, (0, 1), ring1, bring1, tag="a")
    wg2 = wg.rearrange("p m h j -> p (m h j)")
    for i in range(8):
        slot = i % 3
        sap, sbufs = ring1[(i + 5) % len(ring1)]
        S.op(SP, lambda e, i=i, sap=sap: e.dma_start(out=sap[:, 0:512], in_=d_wg[:, i * 512:(i + 1) * 512]), writes=sbufs, dma=True)
        S.op(DVE if i % 2 else ACT, (lambda e, i=i, sap=sap: e.tensor_copy(wg2[:, i * 512:(i + 1) * 512], sap[:, 0:512])) if i % 2 else
             (lambda e, i=i, sap=sap: e.activation(out=wg2[:, i * 512:(i + 1) * 512], in_=sap[:, 0:512], func=AF.Copy)),
             reads=sbufs, partial=[B("l1wg")])
    S.op(ACT, lambda e: e.activation(out=nsp8, in_=lam, func=AF.Exp, scale=-1.0), reads=[B("l1lam")], writes=[B("nsp8")])
    S.op(ACT, lambda e: e.activation(out=nsp8, in_=nsp8, func=AF.Ln, bias=C.one_col[:, 0:1], scale=1.0), reads=[B("nsp8"), B("one_col")], writes=[B("nsp8")])
    S.op(DVE, lambda e: e.tensor_scalar(nsp8, nsp8, -8.0, None, ALU.mult), reads=[B("nsp8")], writes=[B("nsp8")])
    S.op(DVE, lambda e: e.tensor_scalar(hn, nsp8, 0.5, None, ALU.mult), reads=[B("nsp8")], writes=[B("hn")])
    S.op(DVE, lambda e: e.tensor_scalar(hbg, bg, 0.5, None, ALU.mult), reads=[B("l1bg")], writes=[B("hbg")])
    S.op(DVE, lambda e: e.memset(xr[:, 0:2], 0.0), partial=[B("xr")])

    for t in range(NT + 1):
        if t < NT:
            xsrc, bxs = xres[:, t, :], [B("x%d" % t)]
        else:
            S.op(DVE, lambda e: e.memset(h32, 0.0), writes=[B("t01")])
            S.op(SP, lambda e: e.dma_start(out=h32[0:2, :], in_=C.recv1[0:2, :]), reads=[B("recv1")], partial=[B("t01")], dma=True)
            S.op(SP, lambda e: e.dma_start(out=hb2[0:2, :], in_=C.recv1[2:4, :]), reads=[B("recv1")], writes=[B("t34")], dma=True)
            S.op(DVE, lambda e: e.tensor_scalar(h32[0:2, :], h32[0:2, :], sel[0:2, 0:1], None, ALU.mult), reads=[B("t01"), B("l1sel")], partial=[B("t01")])
            S.op(DVE, lambda e: e.scalar_tensor_tensor(h32[0:2, :], hb2[0:2, :], sel[0:2, 1:2], h32[0:2, :], ALU.mult, ALU.add),
                 reads=[B("t01"), B("t34"), B("l1sel")], partial=[B("t01")])
            xsrc, bxs = h32, [B("t01")]
        meng = POOL if t % 3 == 0 else DVE
        S.op(meng, lambda e, xsrc=xsrc: e.tensor_tensor(h32, xsrc, modrow[:, 1, :], ALU.mult), reads=bxs + [B("modrow")], writes=[B("t01")])
        S.op(meng, lambda e: e.tensor_tensor(hbf, h32, modrow[:, 0, :], ALU.add), reads=[B("t01"), B("modrow")], writes=[B("t23")])
        pt, bpt = next_ptr()
        for k in range(8):
            S.op(PE, lambda e, k=k, pt=pt: e.transpose(pt[:, k * 128:(k + 1) * 128], hbf[:, k * 128:(k + 1) * 128], C.idb[:]),
                 reads=[B("t23"), B("idb")], writes=[bpt] if k == 0 else (), partial=[bpt] if k else ())
        ncol = 128 if t < NT else 8
        S.op(ACT, lambda e, t=t, pt=pt, ncol=ncol: e.activation(out=hT[:, :, t * 128:t * 128 + ncol],
                                                               in_=pt[:].rearrange("p (k m) -> p k m", m=128)[:, :, 0:ncol], func=AF.Copy),
             reads=[bpt], partial=[B("l1hT")])
    S.barrier()

    def load_wt(c0):
        wst = C.wstage[:].rearrange("p a b -> p (a b)")[:, 0:1024].rearrange("p (k m) -> p k m", m=128)
        S.op(SP, lambda e: e.dma_start(out=wst, in_=d_win.rearrange("(k p) c -> p k c", p=128)[:, :, c0:c0 + 128]),
             writes=[C.bwstage[0], C.bwstage[1]], dma=True)
        S.op(DVE, lambda e: e.tensor_copy(wt, wst), reads=[C.bwstage[0], C.bwstage[1]], writes=[B("l1wt")])

    def batch(d, ct, finish_chunk):
        qs = [0, 1, 2, 3] if d == 0 else [3, 2, 1, 0]
        ca = (2 * d) * 8 + ct
        ci = (2 * d + 1) * 8 + ct
        cl = d * 8 + ct
        bxc = B("xc%d" % ct)
        for n_, q in enumerate(qs):
            sl = slice(q * 512, (q + 1) * 512)
            bA, bI, bV = B("wA%d" % q), B("wTI%d" % q), B("wVB%d" % q)
            Ht, bHt = WH[:, n_ % 2, :], B("wH%d" % (n_ % 2))
            bk_r, bb_r = next_pb()
            bk_i, bb_i = next_pb()
            S.op(PE, lambda e, sl=sl, bk_r=bk_r: e.matmul(bk_r[:, 0:512], wg[:, 2 * d, ct, :], xc[:, ct, sl], start=True, stop=True),
                 reads=[B("l1wg"), bxc], writes=[bb_r])
            S.op(PE, lambda e, sl=sl, bk_i=bk_i: e.matmul(bk_i[:, 0:512], wg[:, 2 * d + 1, ct, :], xc[:, ct, sl], start=True, stop=True),
                 reads=[B("l1wg"), bxc], writes=[bb_i])
            S.op(ACT, lambda e, sl=sl, bk_r=bk_r: e.activation(out=WA[:, sl], in_=bk_r[:, 0:512], func=AF.Tanh, bias=hbg[:, ca:ca + 1], scale=0.5),
                 reads=[bb_r, B("hbg")], writes=[bA])
            S.op(ACT, lambda e, sl=sl, bk_i=bk_i: e.activation(out=WTI[:, sl], in_=bk_i[:, 0:512], func=AF.Tanh, bias=hbg[:, ci:ci + 1], scale=0.5),
                 reads=[bb_i, B("hbg")], writes=[bI])
            S.op(ACT, lambda e, sl=sl: e.activation(out=WA[:, sl], in_=WA[:, sl], func=AF.Exp, bias=hn[:, cl:cl + 1], scale=hn[:, cl:cl + 1]),
                 reads=[bA, B("hn")], writes=[bA])
            S.op(DVE, lambda e, sl=sl, Ht=Ht: e.tensor_tensor(Ht, WA[:, sl], WA[:, sl], ALU.mult), reads=[bA], writes=[bHt])
            S.op(DVE, lambda e, sl=sl, Ht=Ht: e.tensor_scalar(WVB[:, sl], Ht, -1.0, 1.0, ALU.mult, ALU.add), reads=[bHt], writes=[bV])
        yield "G"
        allV = [B("wVB%d" % q) for q in range(4)]
        allI = [B("wTI%d" % q) for q in range(4)]
        S.op(ACT, lambda e: e.activation(out=WVB, in_=WVB, func=AF.Sqrt, scale=0.25), reads=allV, writes=allV)
        S.op(DVE, lambda e: e.scalar_tensor_tensor(WVB, WTI, 1.0, WVB, ALU.add, ALU.mult), reads=allV + allI, writes=allV)
        S.op(DVE, lambda e: e.tensor_tensor(WVB, WVB, xc[:, ct, :], ALU.mult), reads=allV + [bxc], writes=allV)
        st = stA if d == 0 else stB
        bst = B("stA") if d == 0 else B("stB")
        for n_, q in enumerate(qs):
            sl = slice(q * 512, (q + 1) * 512)
            bA, bV = B("wA%d" % q), B("wVB%d" % q)
            H, bH = WH[:, n_ % 2, :], B("wH%d" % (n_ % 2))
            if d == 0:
                init = 0.0 if n_ == 0 else st[:, ct:ct + 1]
                S.op(DVE, lambda e, sl=sl, H=H, init=init: e.tensor_tensor_scan(H, WA[:, sl], WVB[:, sl], init, ALU.mult, ALU.add),
                     reads=[bA, bV, bst], writes=[bH])
                S.op(DVE, lambda e, H=H: e.tensor_copy(st[:, ct:ct + 1], H[:, 511:512]), reads=[bH], writes=[bst])
            else:
                init = carry[:, ct:ct + 1] if n_ == 0 else st[:, ct:ct + 1]
                S.op(DVE, lambda e, sl=sl, H=H, init=init: e.tensor_tensor_scan(rev_ap(H), rev_ap(WA[:, sl]), rev_ap(WVB[:, sl]), init, ALU.mult, ALU.add),
                     reads=[bA, bV, bst, B("l1carry")], writes=[bH])
                S.op(DVE, lambda e, H=H: e.tensor_copy(st[:, ct:ct + 1], H[:, 0:1]), reads=[bH], writes=[bst])
            finish_chunk(q, H, bH)

    def conv_stage(ct):
        for tg in range(5):
            n = 512 if tg < 4 else 8
            bank, bb = next_pb()
            for k in range(8):
                S.op(PE, lambda e, k=k, tg=tg, n=n, bank=bank: e.matmul(bank[:, 0:n], wt[:, k, :], hT[:, k, tg * 512:tg * 512 + n],
                                                                      start=(k == 0), stop=(k == 7)),
                     reads=[B("l1wt"), B("l1hT")], writes=[bb] if k == 0 else (), partial=[bb] if k else ())
            m = n if tg < 4 else 2
            S.op(DVE, lambda e, tg=tg, m=m, bank=bank: e.tensor_copy(xr[:, 2 + tg * 512:2 + tg * 512 + m], bank[:, 0:m]),
                 reads=[bb], partial=[B("xr")])
        if ct + 1 < 8:
            load_wt((ct + 1) * 128)
        par = ct % 2
        for j in range(5):
            S.op(ACT, lambda e, ct=ct, j=j, par=par: e.activation(out=dg[:, par * 5 + j, :], in_=C.idb[:], func=AF.Copy, scale=w5[:, ct * 5 + j:ct * 5 + j + 1]),
                 reads=[B("idb"), B("l1w5")], writes=[B("dg%d_%d" % (par, j))])
        for q in range(4):
            sl = slice(q * 512, (q + 1) * 512)
            bank, bb = next_pb()
            for j in range(5):
                S.op(PE, lambda e, j=j, q=q, par=par, bank=bank: e.matmul(bank[:, 0:512], dg[:, par * 5 + j, :], xr[:, q * 512 + j:q * 512 + j + 512],
                                                                        start=(j == 0), stop=(j == 4)),
                     reads=[B("dg%d_%d" % (par, j)), B("xr")], writes=[bb] if j == 0 else (), partial=[bb] if j else ())
            S.op(DVE, lambda e, ct=ct, sl=sl, bank=bank: e.tensor_scalar(xc[:, ct, sl], bank[:, 0:512], cb[:, ct:ct + 1], None, ALU.add),
                 reads=[bb, B("l1cb")], partial=[B("xc%d" % ct)])

    def drain(g):
        for _ in g:
            pass

    load_wt(0)
    conv_stage(0)
    for ct in range(8):
        def fin_a(q, H, bH, ct=ct):
            sl = slice(q * 512, (q + 1) * 512)
            S.op(DVE, lambda e: e.tensor_copy(yT[:, ct, sl], H), reads=[bH], partial=[B("yT%d" % ct)])
        g = batch(0, ct, fin_a)
        next(g)
        if ct + 1 < 8:
            conv_stage(ct + 1)
        drain(g)

    S.op(SP, lambda e: e.dma_start(out=C.send2[:, :], in_=stA), reads=[B("stA")], writes=[B("send2")], dma=True)
    S.op(POOL, lambda e: e.collective_compute("AllGather", ALU.bypass, replica_groups=RG, ins=[C.send2], outs=[C.recv2]),
         reads=[B("send2")], writes=[B("recv2")], cc=True)
    S.op(SP, lambda e: e.dma_start(out=cga, in_=C.recv2[0:128, :]), reads=[B("recv2")], writes=[B("cga")], dma=True)
    S.op(SP, lambda e: e.dma_start(out=cgb, in_=C.recv2[128:256, :]), reads=[B("recv2")], writes=[B("cgb")], dma=True)
    S.op(DVE, lambda e: e.tensor_scalar(carry, cga, sel[:, 0:1], None, ALU.mult), reads=[B("cga"), B("l1sel")], writes=[B("l1carry")])
    S.op(DVE, lambda e: e.scalar_tensor_tensor(carry, cgb, sel[:, 1:2], carry, ALU.mult, ALU.add), reads=[B("cgb"), B("l1carry"), B("l1sel")], writes=[B("l1carry")])

    def gate_chunk(ct, q):
        bank, bb = next_pb()
        for k in range(8):
            S.op(PE, lambda e, k=k, bank=bank: e.matmul(bank[:, 0:512], wt[:, k, :], hT[:, k, q * 512:(q + 1) * 512],
                                                       start=(k == 0), stop=(k == 7)),
                 reads=[B("l1wt"), B("l1hT")], writes=[bb] if k == 0 else (), partial=[bb] if k else ())
        S.op(ACT, lambda e, bank=bank: e.activation(out=sg[:, q * 512:(q + 1) * 512], in_=bank[:, 0:512], func=AF.Silu),
             reads=[bb], writes=[B("sg%d" % q)], partial=[B("xr")] if ct == 0 else ())

    load_wt(1024)
    for q in (3, 2, 1, 0):
        gate_chunk(0, q)
    for ct in range(8):
        if ct + 1 < 8:
            load_wt(1024 + (ct + 1) * 128)

        def fin_b(q, H, bH, ct=ct):
            sl = slice(q * 512, (q + 1) * 512)
            S.op(DVE, lambda e: e.tensor_tensor(H, H, yT[:, ct, sl], ALU.add), reads=[bH, B("yT%d" % ct)], writes=[bH])
            S.op(DVE, lambda e: e.tensor_tensor(yT[:, ct, sl], H, sg[:, sl], ALU.mult), reads=[bH, B("sg%d" % q)], partial=[B("yT%d" % ct)])
            if ct + 1 < 8:
                gate_chunk(ct + 1, q)
        drain(batch(1, ct, fin_b))

    S.barrier()
    ring2 = [w4k, (xc[:, 6, :].bitcast(F32), [B("xc6")]), (xc[:, 7, :].bitcast(F32), [B("xc7")])]
    emit_adaln2(C, "l1", [(pb[0], bpb[0]), (pb[1], bpb[1])],
                [None, None, (gaterow, B("t23"), False)], crep_b, B("t01"), brow, (2,), ring2, bring1, tag="b")
    emit_weight_bf16(C, "l1_w_out", w_out, B("l1hT"), 8, 1024, ring2, scale_row=gaterow, bscale=B("t23"), chunk=1024)
    S.op(SP, lambda e: e.dma_start(out=lnrows[:, 0, :], in_=d_lng[:, :]), writes=[B("xc4")], partial=[B("l1lnrows")], dma=True)
    S.op(SP, lambda e: e.dma_start(out=lnrows[:, 1, :], in_=d_lnb[:, :]), writes=[B("xc5")], partial=[B("l1lnrows")], dma=True)
    yTb = [B("yT%d" % k) for k in range(8)]
    S.barrier()
    zall = xc.rearrange("p a b -> p (a b)").bitcast(F32)
    lnsm = [(C.asb("l1_st6_%d" % i, [128, 12]), C.asb("l1_mv_%d" % i, [128, 2]), C.asb("l1_sd_%d" % i, [128, 4]), "_%d" % i) for i in range(3)]
    for t in range(NT):
        z = zall[:, (t % 3) * 1024:(t % 3 + 1) * 1024]
        bz = B("l1z%d" % (t % 3))
        for cg in range(2):
            bank, bb = next_pb()
            for k in range(8):
                S.op(PE, lambda e, k=k, cg=cg, t=t, bank=bank: e.matmul(bank[:, 0:512], yT[:, k, t * 128:(t + 1) * 128], w_out[:, k, cg * 512:(cg + 1) * 512],
                                                                       start=(k == 0), stop=(k == 7)),
                     reads=[yTb[k], B("l1hT")], writes=[bb] if k == 0 else (), partial=[bb] if k else ())
            S.op(DVE, lambda e, cg=cg, t=t, bank=bank, z=z: e.scalar_tensor_tensor(z[:, cg * 512:(cg + 1) * 512], xres[:, t, cg * 512:(cg + 1) * 512], ALPHA,
                                                                             bank[:, 0:512], ALU.mult, ALU.add),
                 reads=[B("x%d" % t), bb], writes=[bz] if cg == 0 else (), partial=[bz] if cg else ())
        emit_ln_tile(C, "l1", z, bz, xres[:, t, :], B("x%d" % t), lnrows[:, 0, :], lnrows[:, 1, :], B("l1lnrows"), small=lnsm[t % 3])
        S.op(SP, lambda e, t=t: e.dma_start(out=out_dram[t * 128:(t + 1) * 128, :], in_=xres[:, t, :]), reads=[B("x%d" % t)], dma=True)


def l1_inputs(c, inp):
    b, half = c // 2, c % 2
    dA, dB = (0, 1) if half == 0 else (1, 0)
    cw = inp["od_conv_w"][0]
    zero = np.zeros((1, 1024), np.float32)
    w5 = np.concatenate([cw, zero], 0) if half == 0 else np.concatenate([zero, cw[::-1]], 0)
    w5 = np.ascontiguousarray(w5.reshape(5, 8, 128).transpose(2, 1, 0)).reshape(128, 40)
    wa, wx = inp["od_w_a"][0], inp["od_w_x"][0]
    wg = np.stack([wa[dA], wx[dA], wa[dB], wx[dB]], 0)
    wg = np.ascontiguousarray(wg.transpose(2, 0, 1, 3)).reshape(128, 4096)
    ba, bx = inp["od_b_a"][0], inp["od_b_x"][0]
    bgm = np.stack([ba[dA], bx[dA], ba[dB], bx[dB]], 0).reshape(4, 8, 128)
    bgm = np.ascontiguousarray(bgm.transpose(2, 0, 1)).reshape(128, 32)
    lam = inp["od_lam"][0]
    lamm = np.stack([lam[dA], lam[dB]], 0).reshape(2, 8, 128)
    lamm = np.ascontiguousarray(lamm.transpose(2, 0, 1)).reshape(128, 16)
    return {
        "sel": rep(np.array([0.0, 1.0], np.float32) if half == 0 else np.array([1.0, 0.0], np.float32)),
        "l1_cT": np.ascontiguousarray(inp["c"][b].reshape(8, 128).T),
        "l1_ada_w": np.ascontiguousarray(inp["ada_w"][1]),
        "l1_ada_b": np.ascontiguousarray(inp["ada_b"][1][None, :]),
        "l1_w_in": np.ascontiguousarray(inp["od_w_in"][0]),
        "l1_w5": w5,
        "l1_cb": np.ascontiguousarray(inp["od_conv_b"][0].reshape(8, 128).T),
        "l1_wg": wg,
        "l1_bg": bgm,
        "l1_lam": lamm,
        "l1_w_out": np.ascontiguousarray(inp["od_w_out"][0]),
        "l1_ln_g": rep(inp["ln_g"][1]),
        "l1_ln_b": rep(inp["ln_b"][1]),
    }


def build_fused():
    nc = bass.Bass("TRN2", target_bir_lowering=False)
    with contextlib.ExitStack() as st:
        C = Ctx(nc, st)
        x_d = C.dram("x", [2176, 1024], F32, "ExternalInput")
        o_d = C.dram("out", [2048, 1024], F32, "ExternalOutput")
        C.send1 = nc.dram_tensor("send1", [2, 1024], F32).ap()
        C.recv1 = nc.dram_tensor("recv1", [4, 1024], F32).ap()
        C.send2 = nc.dram_tensor("send2", [128, 8], F32).ap()
        C.recv2 = nc.dram_tensor("recv2", [256, 8], F32).ap()
        alloc_common(C)
        emit_consts(C)
        C.make_arena(ARENA_WORDS)
        emit_l0(C, x_d, None)
        p0 = C.apeak
        emit_l1(C, o_d)
        print("arena words: l0 peak", p0, "overall peak", C.apeak, "of", C.asize, "sbuf remaining", nc.sbuf_bytes_remaining)
        C.S.emit()
    return nc


def rep(v, n=128):
    return np.ascontiguousarray(np.broadcast_to(np.asarray(v, np.float32)[None, :], (n, v.shape[0])))


def core_tokens(half):
    if half == 0:
        return np.arange(0, 2048), np.arange(2048, 2176)
    return np.arange(4095, 2047, -1), np.arange(2047, 1919, -1)


def l0_masks():
    qi = np.arange(128)[:, None]
    kj = np.arange(384)[None, :]
    valid = np.abs(kj - 128 - qi) <= 128
    m1 = np.where(valid, 0.0, NEG).astype(np.float32)
    m0 = np.where(valid & (kj >= 128), 0.0, NEG).astype(np.float32)
    return np.ascontiguousarray(np.concatenate([m0, m1], axis=1))


def l0_inputs(c, inp, x_override=None):
    b, half = c // 2, c % 2
    own, halo = core_tokens(half)
    idx = np.concatenate([own, halo])
    x = inp["x"][b][idx]
    pos = inp["positions"][b][idx].astype(np.int32).reshape(17, 128).T
    w_in = inp["ev_w_in"][0]
    qcols = np.concatenate([np.arange(h * 64, (h + 1) * 64) for h in PERM])
    cols = np.concatenate([qcols, np.arange(512, 1792), 1792 + qcols, np.arange(2304, 2816)])
    w_out = inp["ev_w_out"][0]
    rows = np.concatenate([qcols, np.arange(512, 1024)])
    sgw = inp["ev_sg_w"][0]
    sgb = inp["ev_sg_b"][0]
    if half == 1:
        sgw = sgw[:, ::-1, ::-1]
        sgb = sgb[:, ::-1]
    sgwT = np.ascontiguousarray(np.transpose(sgw, (2, 0, 1))).reshape(128, 1024)
    return {
        "x": np.ascontiguousarray(x),
        "ident": np.eye(128, dtype=np.float32),
        "l0_pos": np.ascontiguousarray(pos),
        "l0_cT": np.ascontiguousarray(inp["c"][b].reshape(8, 128).T),
        "l0_ada_w": np.ascontiguousarray(inp["ada_w"][0]),
        "l0_ada_b": np.ascontiguousarray(inp["ada_b"][0][None, :]),
        "l0_w_in": np.ascontiguousarray(w_in[:, cols]),
        "l0_w_out": np.ascontiguousarray(w_out[rows, :]),
        "l0_sink": rep(inp["ev_sink"][0][PERM]),
        "l0_sg_ln_g": rep(inp["ev_sg_ln_g"][0]),
        "l0_sg_ln_b": rep(inp["ev_sg_ln_b"][0]),
        "l0_sg_wT": sgwT,
        "l0_sg_b": np.ascontiguousarray(sgb.T),
        "l0_ln_g": rep(inp["ln_g"][0]),
        "l0_ln_b": rep(inp["ln_b"][0]),
        "l0_mask": l0_masks(),
    }


def run_l0(inp):
    nc = build_l0()
    in_maps = [l0_inputs(c, inp) for c in range(8)]
    res = run_bass_kernel_spmd(nc, in_maps, core_ids=list(range(8)))
    return [r["out"] for r in res.results]


def assemble(outs):
    full = np.zeros((4, 4096, 1024), np.float32)
    for c in range(8):
        b, half = c // 2, c % 2
        own, _ = core_tokens(half)
        full[b, own] = outs[c]
    return full


def fused_inputs(c, inp):
    d = l0_inputs(c, inp)
    d.update(l1_inputs(c, inp))
    return d


def kernel(**inputs):
    inp = {k: np.asarray(v) for k, v in inputs.items()}
    nc = build_fused()
    in_maps = [fused_inputs(c, inp) for c in range(8)]
    res = run_bass_kernel_spmd(nc, in_maps, core_ids=list(range(8)))
    return assemble([r["out"] for r in res.results])
```

```python
import contextlib
import numpy as np
import concourse.bass as bass
import concourse.mybir as mybir
from concourse.bass_utils import run_bass_kernel_spmd

F32 = mybir.dt.float32
BF16 = mybir.dt.bfloat16
I32 = mybir.dt.int32
ALU = mybir.AluOpType
AF = mybir.ActivationFunctionType
AX = mybir.AxisListType

PE, ACT, DVE, POOL, SP = "tensor", "scalar", "vector", "gpsimd", "sync"
ENGS = (PE, ACT, DVE, POOL, SP)

D = 1024
T = 2048
NT = 16
ALPHA = 4.0 ** 0.25
LN_EPS = 1e-5
PERM = [0, 4, 1, 5, 2, 6, 3, 7]
INV_FREQ = [float(np.float32(500000.0) ** np.float32(-i / 8.0)) for i in range(8)]
NEG = -30000.0
ARENA_WORDS = 34880


class Buf:
    __slots__ = ("name", "w", "r", "excl")

    def __init__(self, name):
        self.name = name
        self.w = {}
        self.r = {}
        self.excl = False


class Op:
    __slots__ = ("eng", "fn", "deps", "sig", "dma", "idx")

    def __init__(self, eng, fn):
        self.eng = eng
        self.fn = fn
        self.deps = {}
        self.sig = None
        self.dma = None
        self.idx = None


class Sched:
    def __init__(self, nc, n_dma_sems=8, n_cc_sems=2):
        self.nc = nc
        self.ops = {e: [] for e in ENGS}
        self.n_dma_sems = n_dma_sems
        self.n_cc = n_cc_sems
        nt = n_dma_sems + n_cc_sems
        self.sem_inc = [16] * n_dma_sems + [1] * n_cc_sems
        self.dma_val = [0] * nt
        self.dma_last = [None] * nt
        self.dma_rr = 0
        self.cc_rr = 0
        self.pending = {e: [] for e in ENGS}

    def barrier(self):
        toks = []
        for e in ENGS:
            if self.ops[e]:
                last = [o for o in self.ops[e] if o.dma is None]
                if last:
                    toks.append(("op", last[-1]))
        for t in self.dma_last[:self.n_dma_sems]:
            if t is not None:
                toks.append(t)
        for e in ENGS:
            self.pending[e] = list(toks)

    @staticmethod
    def _key(tok):
        return tok[1].eng if tok[0] == "op" else ("dma", tok[1])

    @staticmethod
    def _newer(a, b):
        if a is None:
            return b
        if a[0] == "op":
            return b if b[1].idx > a[1].idx else a
        return b if b[2] > a[2] else a

    def _add_dep(self, op, tok):
        if tok[0] == "op" and tok[1] is op:
            return
        k = self._key(tok)
        op.deps[k] = self._newer(op.deps.get(k), tok)

    def op(self, eng, fn, reads=(), writes=(), partial=(), dma=False, cc=False):
        o = Op(eng, fn)
        o.idx = len(self.ops[eng])
        had_pending = bool(self.pending[eng])
        if had_pending:
            for t in self.pending[eng]:
                self._add_dep(o, t)
            self.pending[eng] = []
        if cc:
            dma = True
            si = self.n_dma_sems + self.cc_rr
            self.cc_rr = (self.cc_rr + 1) % self.n_cc
        elif dma:
            si = self.dma_rr
            self.dma_rr = (self.dma_rr + 1) % self.n_dma_sems
        if dma:
            prev = self.dma_last[si]
            if prev is not None:
                self._add_dep(o, prev)
            self.dma_val[si] += self.sem_inc[si]
            o.dma = (si, self.dma_val[si])
            tok = ("dma", si, self.dma_val[si])
            self.dma_last[si] = tok
        else:
            tok = ("op", o)
        for b in reads:
            for t in b.w.values():
                self._add_dep(o, t)
            if b.excl:
                for kk, t in b.r.items():
                    if kk != eng:
                        self._add_dep(o, t)
        for b in list(writes) + list(partial):
            for t in b.w.values():
                self._add_dep(o, t)
            for t in b.r.values():
                self._add_dep(o, t)
        if not dma and not had_pending:
            raw = -1
            for b in reads:
                t = b.w.get(eng)
                if t is not None and t[0] == "op":
                    raw = max(raw, t[1].idx)
            if eng in o.deps:
                if raw < 0:
                    del o.deps[eng]
                else:
                    o.deps[eng] = ("op", self.ops[eng][raw])
        k = self._key(tok)
        for b in reads:
            b.r[k] = self._newer(b.r.get(k), tok)
        for b in writes:
            b.w = {k: tok}
            b.r = {}
        for b in partial:
            b.w[k] = self._newer(b.w.get(k), tok)
        self.ops[eng].append(o)
        return o

    def emit(self, final_eng=SP):
        nc = self.nc
        needed = set()
        for e in ENGS:
            for o in self.ops[e]:
                for t in o.deps.values():
                    if t[0] == "op":
                        needed.add(id(t[1]))
        for e in ENGS:
            c = 0
            for o in self.ops[e]:
                if o.dma is None and id(o) in needed:
                    c += 1
                    o.sig = c
        with contextlib.ExitStack() as st:
            esem = {e: st.enter_context(nc.semaphore("s_" + e)) for e in ENGS}
            dsem = [st.enter_context(nc.semaphore("d_%d" % i)) for i in range(self.n_dma_sems + self.n_cc)]
            block = st.enter_context(nc.Block())

            def run(eng_name, engine):
                known = {}
                for o in self.ops[eng_name]:
                    for k, t in o.deps.items():
                        if t[0] == "op":
                            sem, val = esem[t[1].eng], t[1].sig
                        else:
                            sem, val = dsem[t[1]], t[2]
                        if known.get(k, 0) >= val:
                            continue
                        known[k] = val
                        engine.wait_ge(sem, val)
                    ins = o.fn(engine)
                    if o.dma is not None:
                        ins.then_inc(dsem[o.dma[0]], self.sem_inc[o.dma[0]])
                    elif o.sig is not None:
                        ins.then_inc(esem[eng_name], 1)
                if eng_name == final_eng:
                    for i in range(self.n_dma_sems + self.n_cc):
                        if self.dma_val[i] > 0:
                            engine.wait_ge(dsem[i], self.dma_val[i])

            @block.tensor
            def _(e):
                run(PE, e)

            @block.scalar
            def _(e):
                run(ACT, e)

            @block.vector
            def _(e):
                run(DVE, e)

            @block.gpsimd
            def _(e):
                run(POOL, e)

            @block.sync
            def _(e):
                run(SP, e)


class Ctx:
    def __init__(self, nc, st):
        self.nc = nc
        self.st = st
        self.S = Sched(nc)
        self.bufs = {}
        self.arena = None
        self.aoff = 0
        self.asize = 0
        self.apeak = 0

    def make_arena(self, words):
        self.arena = self.sb("arena", [128, words])
        self.asize = words
        self.aoff = 0

    def areset(self):
        self.aoff = 0

    def asb(self, name, shape, dt=F32):
        if self.arena is None:
            return self.sb(name, shape, dt)[:]
        esz = {F32: 4, I32: 4, BF16: 2}[dt]
        n = 1
        for d_ in shape[1:]:
            n *= d_
        words = (n * esz + 3) // 4
        off = self.aoff
        self.aoff += words
        self.apeak = max(self.apeak, self.aoff)
        assert self.aoff <= self.asize, ("arena overflow", name, self.aoff, self.asize)
        ap = self.arena[:, off:off + words]
        if dt != F32:
            ap = ap.bitcast(dt)[:, 0:n]
        if shape[0] != 128:
            ap = ap[0:shape[0], :]
        if len(shape) == 3:
            ap = ap.rearrange("p (a b) -> p a b", b=shape[2])
        elif len(shape) == 4:
            ap = ap.rearrange("p (a b c) -> p a b c", b=shape[2], c=shape[3])
        return ap

    def sb(self, name, shape, dt=F32):
        return self.st.enter_context(self.nc.sbuf_tensor("sb_" + name, shape, dt))

    def ps(self, name, shape, dt=F32):
        return self.st.enter_context(self.nc.psum_tensor("ps_" + name, shape, dt))

    def B(self, name):
        b = self.bufs.get(name)
        if b is None:
            b = self.bufs[name] = Buf(name)
        return b

    def dram(self, name, shape, dt, kind):
        if not hasattr(self, "_dram"):
            self._dram = {}
        if name not in self._dram:
            self._dram[name] = self.nc.dram_tensor(name, list(shape), dt, kind=kind).ap()
        return self._dram[name]


def v3(ap, d):
    return ap.rearrange("p (h d) -> p h d", d=d)


def bc_mid(ap2, n):
    return ap2.unsqueeze(1).broadcast_to([ap2.shape[0], n, ap2.shape[1]])


def bc_last(ap2, n):
    return ap2.unsqueeze(2).broadcast_to([ap2.shape[0], ap2.shape[1], n])


def emit_consts(C):
    S = C.S
    C.id32 = C.sb("id32", [128, 128])
    C.idb = C.sb("idb", [128, 128], BF16)
    C.ones32 = C.sb("ones32", [128, 128])
    d_id = C.dram("ident", [128, 128], F32, "ExternalInput")
    S.op(SP, lambda e: e.dma_start(out=C.id32[:], in_=d_id[:, :]), writes=[C.B("id32")], dma=True)
    S.op(DVE, lambda e: e.tensor_copy(C.idb[:], C.id32[:]), reads=[C.B("id32")], writes=[C.B("idb")])
    S.op(DVE, lambda e: e.memset(C.ones32[:], 1.0), writes=[C.B("ones32")])


def emit_adaln(C, lname, pbanks, rows_out, crep, bcrep, brow_ap=None, cgs=range(6), tag="", ring=None, bring=None):
    S = C.S
    d_c = C.dram(lname + "_cT", [128, 8], F32, "ExternalInput")
    d_w = C.dram(lname + "_ada_w", [1024, 3072], F32, "ExternalInput")
    d_b = C.dram(lname + "_ada_b", [1, 3072], F32, "ExternalInput")
    cT = C.asb(lname + tag + "cT", [128, 8])
    brow = brow_ap if brow_ap is not None else C.asb(lname + tag + "brow", [1, 1024])
    bcT = C.B(lname + tag + "cT")
    bbrows = [C.B(lname + "brow0"), C.B(lname + "brow1")]
    S.op(SP, lambda e: e.dma_start(out=cT[:], in_=d_c[:, :]), writes=[bcT], dma=True)
    S.op(ACT, lambda e: e.activation(out=cT[:], in_=cT[:], func=AF.Silu), reads=[bcT], writes=[bcT])
    S.op(DVE, lambda e: e.tensor_copy(crep, bc_last(cT[:], 128)), reads=[bcT], writes=[bcrep])
    if ring is None:
        ring = C.ring3
    i = 0
    for cg in cgs:
        pb, bpb = pbanks[cg % 2]
        bbrow = bbrows[cg % 2]
        bsl = slice((cg % 2) * 512, (cg % 2 + 1) * 512)
        S.op(SP, lambda e, cg=cg, bsl=bsl: e.dma_start(out=brow[0:1, bsl], in_=d_b[0:1, cg * 512:(cg + 1) * 512]), writes=[bbrow], dma=True)
        for k in range(8):
            sap, sbufs = ring[i % len(ring)]
            i += 1
            S.op(SP, lambda e, k=k, cg=cg, sap=sap: e.dma_start(
                out=sap, in_=d_w[k * 128:(k + 1) * 128, cg * 512:(cg + 1) * 512]),
                writes=sbufs, dma=True)
            if bring is not None:
                bap, bbufs = bring[(i - 1) % len(bring)]
                ceng = (ACT, DVE, ACT, POOL)[i % 4]
                if ceng == ACT:
                    S.op(ACT, lambda e, sap=sap, bap=bap: e.activation(out=bap, in_=sap, func=AF.Copy), reads=sbufs, writes=bbufs)
                else:
                    S.op(ceng, lambda e, sap=sap, bap=bap: e.tensor_copy(bap, sap), reads=sbufs, writes=bbufs)
                S.op(PE, lambda e, k=k, bap=bap, pb=pb: e.matmul(pb[:, 0:512], crep[:, k, :], bap, start=(k == 0), stop=False),
                     reads=[bcrep] + bbufs, partial=[bpb] if k else (), writes=[bpb] if k == 0 else ())
                continue
            S.op(PE, lambda e, k=k, sap=sap, pb=pb: e.matmul(pb[:, 0:512], crep[:, k, :], sap,
                                                            start=(k == 0), stop=False),
                 reads=[bcrep] + sbufs, partial=[bpb] if k else (), writes=[bpb] if k == 0 else ())
        S.op(PE, lambda e, cg=cg, pb=pb, bsl=bsl: e.matmul(pb[:, 0:512], C.ones32[0:1, :], brow[0:1, bsl],
                                                  start=False, stop=True),
             reads=[C.B("ones32"), bbrow], partial=[bpb])
        dst, bdst, add_one = rows_out[cg // 2]
        half = cg % 2
        if add_one:
            S.op(ACT, lambda e, dst=dst, half=half, pb=pb: e.activation(
                out=dst[:, half * 512:(half + 1) * 512], in_=pb[:, 0:512], func=AF.Identity, bias=C.one_col[:, 0:1], scale=1.0),
                reads=[bpb, C.B("one_col")], partial=[bdst])
        else:
            S.op(ACT, lambda e, dst=dst, half=half, pb=pb: e.activation(
                out=dst[:, half * 512:(half + 1) * 512], in_=pb[:, 0:512], func=AF.Copy),
                reads=[bpb], partial=[bdst])


def emit_adaln2(C, lname, pbanks, rows_out, crep, bcrep, brow, rows, ring4k, bring2k, tag=""):
    S = C.S
    d_c = C.dram(lname + "_cT", [128, 8], F32, "ExternalInput")
    d_w = C.dram(lname + "_ada_w", [1024, 3072], F32, "ExternalInput")
    d_b = C.dram(lname + "_ada_b", [1, 3072], F32, "ExternalInput")
    cT = C.asb(lname + tag + "cT", [128, 8])
    bcT = C.B(lname + tag + "cT")
    bbrow = C.B(lname + tag + "brow")
    S.op(SP, lambda e: e.dma_start(out=cT[:], in_=d_c[:, :]), writes=[bcT], dma=True)
    S.op(ACT, lambda e: e.activation(out=cT[:], in_=cT[:], func=AF.Silu), reads=[bcT], writes=[bcT])
    S.op(DVE, lambda e: e.tensor_copy(crep, bc_last(cT[:], 128)), reads=[bcT], writes=[bcrep])
    i = 0
    for r in rows:
        S.op(SP, lambda e, r=r: e.dma_start(out=brow[0:1, 0:1024], in_=d_b[0:1, r * 1024:(r + 1) * 1024]), writes=[bbrow], dma=True)
        for k in range(8):
            sap, sbufs = ring4k[i % len(ring4k)]
            bap, bbufs = bring2k[i % len(bring2k)]
            S.op(SP, lambda e, k=k, r=r, sap=sap: e.dma_start(out=sap, in_=d_w[k * 128:(k + 1) * 128, r * 1024:(r + 1) * 1024]),
                 writes=sbufs, dma=True)
            if i % 2:
                S.op(ACT, lambda e, sap=sap, bap=bap: e.activation(out=bap, in_=sap, func=AF.Copy), reads=sbufs, writes=bbufs)
            else:
                S.op(DVE, lambda e, sap=sap, bap=bap: e.tensor_copy(bap, sap), reads=sbufs, writes=bbufs)
            i += 1
            for hh in range(2):
                pb, bpb = pbanks[hh]
                S.op(PE, lambda e, k=k, bap=bap, pb=pb, hh=hh: e.matmul(pb[:, 0:512], crep[:, k, :], bap[:, hh * 512:(hh + 1) * 512],
                                                                      start=(k == 0), stop=False),
                     reads=[bcrep] + bbufs, partial=[bpb] if k else (), writes=[bpb] if k == 0 else ())
        dst, bdst, add_one = rows_out[r]
        for hh in range(2):
            pb, bpb = pbanks[hh]
            S.op(PE, lambda e, pb=pb, hh=hh: e.matmul(pb[:, 0:512], C.ones32[0:1, :], brow[0:1, hh * 512:(hh + 1) * 512], start=False, stop=True),
                 reads=[C.B("ones32"), bbrow], partial=[bpb])
            if add_one:
                S.op(ACT, lambda e, dst=dst, hh=hh, pb=pb: e.activation(out=dst[:, hh * 512:(hh + 1) * 512], in_=pb[:, 0:512], func=AF.Identity,
                                                                      bias=C.one_col[:, 0:1], scale=1.0),
                     reads=[bpb, C.B("one_col")], partial=[bdst])
            else:
                S.op(ACT, lambda e, dst=dst, hh=hh, pb=pb: e.activation(out=dst[:, hh * 512:(hh + 1) * 512], in_=pb[:, 0:512], func=AF.Copy),
                     reads=[bpb], partial=[bdst])


def emit_weight_bf16(C, dname, dst, bdst, nk, ncols, ring, scale_row=None, bscale=None, chunk=512):
    S = C.S
    d_w = C.dram(dname, [nk * 128, ncols], F32, "ExternalInput")
    i = 0
    engs = [DVE, ACT] if scale_row is None else [DVE]
    for k in range(nk):
        for c0 in range(0, ncols, chunk):
            n = min(chunk, ncols - c0)
            sap, sbufs = ring[i % len(ring)]
            S.op(SP, lambda e, k=k, c0=c0, n=n, sap=sap: e.dma_start(
                out=sap[:, 0:n], in_=d_w[k * 128:(k + 1) * 128, c0:c0 + n]), writes=sbufs, dma=True)
            eng = engs[i % len(engs)]
            if scale_row is not None:
                S.op(eng, lambda e, k=k, c0=c0, n=n, sap=sap: e.tensor_tensor(
                    dst[:, k, c0:c0 + n], sap[:, 0:n], scale_row[:, c0:c0 + n], ALU.mult),
                    reads=sbufs + [bscale], partial=[bdst])
            elif eng == ACT:
                S.op(ACT, lambda e, k=k, c0=c0, n=n, sap=sap: e.activation(
                    out=dst[:, k, c0:c0 + n], in_=sap[:, 0:n], func=AF.Copy), reads=sbufs, partial=[bdst])
            else:
                S.op(eng, lambda e, k=k, c0=c0, n=n, sap=sap: e.tensor_copy(
                    dst[:, k, c0:c0 + n], sap[:, 0:n]), reads=sbufs, partial=[bdst])
            i += 1


def emit_ln_tile(C, pfx, zt, bz, xdst, bxdst, lng, lnb, brows, small=None):
    S = C.S
    if small is None:
        st6, mv, sd = C.ln_st6, C.ln_mv, C.ln_sd
        bst6, bmv, bsd = C.B("ln_st6"), C.B("ln_mv"), C.B("ln_sd")
    else:
        st6, mv, sd, sfx = small
        bst6, bmv, bsd = C.B("ln_st6" + sfx), C.B("ln_mv" + sfx), C.B("ln_sd" + sfx)
    for hh in range(2):
        S.op(DVE, lambda e, hh=hh: e.bn_stats(st6[:, hh * 6:(hh + 1) * 6], zt[:, hh * 512:(hh + 1) * 512]), reads=[bz],
             writes=[bst6] if hh == 0 else (), partial=[bst6] if hh else ())
    S.op(DVE, lambda e: e.bn_aggr(mv[:, 0:2], st6[:]), reads=[bst6], writes=[bmv])
    S.op(ACT, lambda e: e.activation(out=sd[:, 0:1], in_=mv[:, 1:2], func=AF.Sqrt, bias=C.eps_col[:, 0:1], scale=1.0),
         reads=[bmv, C.B("eps_col")], writes=[bsd])
    S.op(DVE, lambda e: e.reciprocal(sd[:, 1:2], sd[:, 0:1]), reads=[bsd], partial=[bsd])
    S.op(DVE, lambda e: e.scalar_tensor_tensor(sd[:, 2:3], mv[:, 0:1], -1.0, sd[:, 1:2], ALU.mult, ALU.mult),
         reads=[bmv, bsd], partial=[bsd])
    S.op(ACT, lambda e: e.activation(out=zt[:], in_=zt[:], func=AF.Identity, scale=sd[:, 1:2], bias=sd[:, 2:3]),
         reads=[bz, bsd], writes=[bz])
    S.op(POOL, lambda e: e.tensor_tensor(zt[:], zt[:], lng[:], ALU.mult), reads=[bz, brows], writes=[bz])
    S.op(POOL, lambda e: e.tensor_tensor(xdst, zt[:], lnb[:], ALU.add), reads=[bz, brows], writes=[bxdst])


def emit_l0(C, x_dram, out_dram):
    S, nc = C.S, C.nc
    B = C.B
    xres = C.xres
    d_pos = C.dram("l0_pos", [128, 17], I32, "ExternalInput")
    d_sink = C.dram("l0_sink", [128, 8], F32, "ExternalInput")
    d_sgg = C.dram("l0_sg_ln_g", [128, 512], F32, "ExternalInput")
    d_sgb = C.dram("l0_sg_ln_b", [128, 512], F32, "ExternalInput")
    d_sgw = C.dram("l0_sg_wT", [128, 1024], F32, "ExternalInput")
    d_sgbias = C.dram("l0_sg_b", [128, 8], F32, "ExternalInput")
    d_lng = C.dram("l0_ln_g", [128, 1024], F32, "ExternalInput")
    d_lnb = C.dram("l0_ln_b", [128, 1024], F32, "ExternalInput")
    d_mask = C.dram("l0_mask", [128, 768], F32, "ExternalInput")
    w_in = C.asb("l0_w_in", [128, 8, 2816], BF16)
    w_out = C.asb("l0_w_out", [128, 8, 1024], BF16)
    sgw = C.asb("l0_sgw", [128, 8, 128], BF16)
    modrow = C.asb("l0_modrow", [128, 2, 1024])
    lnrows = C.asb("l0_lnrows", [128, 2, 1024])
    sgrows = C.asb("l0_sgrows", [128, 2, 512])
    sgbias = C.asb("l0_sgbias", [128, 8])
    nsink = C.asb("l0_nsink", [128, 8])
    sink = C.asb("l0_sink", [128, 8])
    mask = C.asb("l0_mask", [128, 768], BF16)
    kT = C.asb("l0_kT", [128, 18, 128], BF16)
    V = C.asb("l0_V", [128, 18, 128], BF16)
    posi = C.asb("l0_posi", [128, 17], I32)
    posf = C.asb("l0_posf", [128, 17])
    invf = C.asb("l0_invf", [128, 8])
    ang = C.asb("l0_ang", [128, 17, 8])
    angk = C.asb("l0_angk", [128, 17, 8])
    angi = C.asb("l0_angi", [128, 17, 8], I32)
    C2 = C.asb("l0_C2", [128, 17, 16])
    S2 = C.asb("l0_S2", [128, 17, 16])
    h32 = C.asb("l0_h32", [128, 1024])
    mask32 = h32[:, 0:768]
    hbf = C.asb("l0_hbf", [128, 1024], BF16)
    hT = C.asb("l0_hT", [128, 1024], BF16)
    qkr = C.asb("l0_qkr", [128, 640], BF16)
    qT = C.asb("l0_qT", [128, 2, 512], BF16)
    su = C.asb("l0_su", [128, 512], BF16)
    vn = C.asb("l0_vn", [128, 512])
    vnb = C.asb("l0_vnb", [128, 512], BF16)
    st8 = C.asb("l0_st8", [128, 8, 8])
    sgl = C.asb("l0_sgl", [128, 2, 1024], BF16)
    y = C.asb("l0_y", [128, 2, 1024], BF16)
    yT = C.asb("l0_yT", [128, 1024], BF16)
    P = C.asb("l0_P", [128, 2, 384], BF16)
    PT = C.asb("l0_PT", [128, 2, 384], BF16)
    att = C.asb("l0_att", [128, 6, 8])
    z = C.asb("l0_z", [128, 1024])
    gaterow = z
    yat = vn
    rot = C.asb("l0_rot", [128, 2, 160])
    sq = z[:, 512:1024]
    ptr = C.ps_ptr
    pb = C.ps_f32[0:2]
    sbk = C.ps_f32[2:4]
    ob = C.ps_f32[4]
    svb = C.ps_f32[5]
    bptr = [B("ptr0"), B("ptr1")]
    bpb = [B("pb0"), B("pb1")]
    bsbk = [B("pb2"), B("pb3")]
    cnt = {"ptr": 0, "pb": 0}
    for bn in ("ptr0", "ptr1", "pb0", "pb1", "pb2", "pb3", "ob", "svb"):
        B(bn).excl = True

    def next_ptr():
        i = cnt["ptr"] % 2
        cnt["ptr"] += 1
        return ptr[i], bptr[i]

    def next_pb():
        i = cnt["pb"] % 2
        cnt["pb"] += 1
        return pb[i], bpb[i]

    w4k = (C.wstage[:].rearrange("p a b -> p (a b)")[:, 0:1024], [C.bwstage[0], C.bwstage[1]])
    ring0 = [w4k,
             (y.rearrange("p a b -> p (a b)").bitcast(F32), [B("y_a0"), B("y_s0"), B("y_a1"), B("y_s1")]),
             (sgl.rearrange("p a b -> p (a b)").bitcast(F32), [B("sgl_a0"), B("sgl_s0"), B("sgl_a1"), B("sgl_s1")])]
    bring0 = [(hT[:], [B("hT")]), (hbf[:], [B("hbf")]), (yT[:], [B("yT")]), (qT.rearrange("p a b -> p (a b)"), [B("qT0"), B("qT1")])]
    S.op(SP, lambda e: e.dma_start(out=xres[:, 0, :], in_=x_dram[0:128, :]), writes=[B("x0")], dma=True)
    S.op(SP, lambda e: e.dma_start(out=posi[:], in_=d_pos[:, :]), writes=[B("posi")], dma=True)
    S.op(SP, lambda e: e.dma_start(out=sink[:], in_=d_sink[:, :]), writes=[B("sink")], dma=True)
    S.op(SP, lambda e: e.dma_start(out=mask32, in_=d_mask[:, :]), writes=[B("h32")], dma=True)
    S.op(POOL, lambda e: e.tensor_copy(mask[:], mask32), reads=[B("h32")], writes=[B("mask")])
    S.op(SP, lambda e: e.dma_start(out=sgbias[:], in_=d_sgbias[:, :]), writes=[B("sgbias")], dma=True)
    crep_b = h32[:, 0:512].bitcast(BF16).rearrange("p (k m) -> p k m", m=128)
    brow0 = C.asb("l0_brow", [1, 1024])
    emit_adaln2(C, "l0", [(pb[0], bpb[0]), (pb[1], bpb[1])],
                [(modrow[:, 0, :], B("l0shift"), False), (modrow[:, 1, :], B("l0scale"), True), (gaterow[:], B("z"), False)],
                crep_b, B("h32"), brow0, (0, 1, 2), ring0, bring0)
    emit_weight_bf16(C, "l0_w_in", w_in, B("l0w_in"), 8, 2816, ring0, chunk=1024)
    emit_weight_bf16(C, "l0_w_out", w_out, B("l0w_out"), 8, 1024, ring0, scale_row=gaterow, bscale=B("z"), chunk=1024)
    S.op(SP, lambda e: e.dma_start(out=sgrows[:, 0, :], in_=d_sgg[:, :]), partial=[B("sgrows")], dma=True)
    S.op(SP, lambda e: e.dma_start(out=sgrows[:, 1, :], in_=d_sgb[:, :]), partial=[B("sgrows")], dma=True)
    S.op(SP, lambda e: e.dma_start(out=lnrows[:, 0, :], in_=d_lng[:, :]), partial=[B("l0lnrows")], dma=True)
    S.op(SP, lambda e: e.dma_start(out=lnrows[:, 1, :], in_=d_lnb[:, :]), partial=[B("l0lnrows")], dma=True)
    S.op(SP, lambda e: e.dma_start(out=xres[:, 1, :], in_=x_dram[128:256, :]), writes=[B("x1")], dma=True)
    for hh in range(2):
        slot = hh
        S.op(SP, lambda e, hh=hh, slot=slot: e.dma_start(out=C.wstage[:, slot, 0:512], in_=d_sgw[:, hh * 512:(hh + 1) * 512]),
             writes=[C.bwstage[slot]], dma=True)
        S.op(POOL, lambda e, hh=hh, slot=slot: e.tensor_copy(sgw[:, hh * 4:(hh + 1) * 4, :].rearrange("p g q -> p (g q)"), C.wstage[:, slot, 0:512]),
             reads=[C.bwstage[slot]], partial=[B("sgw")])
    S.op(POOL, lambda e: e.tensor_scalar(nsink[:], sink[:], -1.0, None, ALU.mult), reads=[B("sink")], writes=[B("nsink")])
    S.op(POOL, lambda e: e.memset(kT[:, 0, :], 0.0), partial=[B("kT0")])
    S.op(POOL, lambda e: e.memset(V[:, 0, :], 0.0), partial=[B("V0")])

    for f in range(8):
        S.op(POOL, lambda e, f=f: e.memset(invf[:, f:f + 1], INV_FREQ[f]), partial=[B("invf")])
    S.op(DVE, lambda e: e.tensor_copy(posf[:], posi[:]), reads=[B("posi")], writes=[B("posf")])
    S.op(DVE, lambda e: e.tensor_tensor(ang[:], bc_last(posf[:], 8), bc_mid(invf[:], 17), ALU.mult),
         reads=[B("posf"), B("invf")], writes=[B("ang")])
    TWO_PI = float(2 * np.pi)

    def sin_into(dst_ap, shift, negate, tag):
        bk, bi_, br = B("angk"), B("angi"), B("angr")
        S.op(DVE, lambda e: e.tensor_scalar(angk[:], ang[:], shift, 1.0 / TWO_PI, ALU.add, ALU.mult), reads=[B("ang")], writes=[bk])
        S.op(DVE, lambda e: e.tensor_copy(angi[:], angk[:]), reads=[bk], writes=[bi_])
        S.op(DVE, lambda e: e.tensor_copy(angk[:], angi[:]), reads=[bi_], writes=[bk])
        S.op(DVE, lambda e: e.scalar_tensor_tensor(angk[:], angk[:], -TWO_PI, ang[:], ALU.mult, ALU.add), reads=[bk, B("ang")], writes=[bk])
        S.op(DVE, lambda e: e.tensor_scalar(angk[:], angk[:], shift, float(np.pi), ALU.add, ALU.min), reads=[bk], writes=[bk])
        S.op(DVE, lambda e: e.tensor_scalar(angk[:], angk[:], float(-np.pi), None, ALU.max), reads=[bk], writes=[bk])
        S.op(ACT, lambda e: e.activation(out=dst_ap, in_=angk[:], func=AF.Sin, scale=(-1.0 if negate else 1.0)),
             reads=[bk], partial=[B(tag)])

    sin_into(C2[:, :, 0:8], float(np.pi / 2), False, "C2")
    sin_into(C2[:, :, 8:16], float(np.pi / 2), False, "C2")
    sin_into(S2[:, :, 0:8], 0.0, True, "S2")
    sin_into(S2[:, :, 8:16], 0.0, False, "S2")

    GA, GB, GC, GD, GE, GF = (0, 512), (512, 256), (768, 512), (1280, 512), (1792, 512), (2304, 512)

    def rotary(src_bank, nh, dst3, t, slot, bdst):
        s3 = v3(src_bank[:, 0:nh * 64], 64)
        ta = v3(rot[:, 0, 0:nh * 16], 16)
        tb = v3(rot[:, 1, 0:nh * 16], 16)
        br_ = B("rot")
        S.op(DVE, lambda e: e.tensor_tensor(ta, s3[:, :, 0:16], bc_mid(C2[:, t, :], nh), ALU.mult),
             reads=[slot, B("C2")], writes=[br_])
        S.op(DVE, lambda e: e.tensor_tensor(tb[:, :, 0:8], s3[:, :, 8:16], bc_mid(S2[:, t, 0:8], nh), ALU.mult),
             reads=[slot, B("S2")], partial=[br_])
        S.op(DVE, lambda e: e.tensor_tensor(tb[:, :, 8:16], s3[:, :, 0:8], bc_mid(S2[:, t, 8:16], nh), ALU.mult),
             reads=[slot, B("S2")], partial=[br_])
        S.op(DVE, lambda e: e.tensor_tensor(dst3[:, :, 0:16], ta, tb, ALU.add), reads=[br_], partial=[bdst])

    def modulate(t):
        if t == NT:
            S.op(SP, lambda e: e.dma_start(out=h32[:], in_=x_dram[2048:2176, :]), writes=[B("h32")], dma=True)
            xsrc, bxs = h32[:], B("h32")
        else:
            xsrc, bxs = xres[:, t, :], B("x%d" % t)
        S.op(POOL, lambda e: e.tensor_tensor(h32[:], xsrc, modrow[:, 1, :], ALU.mult), reads=[bxs, B("l0scale")], writes=[B("h32")])
        S.op(POOL, lambda e: e.tensor_tensor(hbf[:], h32[:], modrow[:, 0, :], ALU.add), reads=[B("h32"), B("l0shift")], writes=[B("hbf")])

    def front(t):
        par = t % 2
        last = (t == NT)
        pt, bpt = next_ptr()
        for k in range(8):
            S.op(PE, lambda e, k=k, pt=pt: e.transpose(pt[:, k * 128:(k + 1) * 128], hbf[:, k * 128:(k + 1) * 128], C.idb[:]),
                 reads=[B("hbf"), B("idb")], writes=[bpt] if k == 0 else (), partial=[bpt] if k else ())
        if t < NT:
            modulate(t + 1)
        S.op(ACT, lambda e, pt=pt: e.activation(out=hT[:], in_=pt[:], func=AF.Copy), reads=[bpt], writes=[B("hT")])
        yield "a"

        def proj(g):
            c0, n = g
            bank, bbank = next_pb()
            for k in range(8):
                S.op(PE, lambda e, k=k: e.matmul(bank[:, 0:n], hT[:, k * 128:(k + 1) * 128], w_in[:, k, c0:c0 + n],
                                                 start=(k == 0), stop=(k == 7)),
                     reads=[B("hT"), B("l0w_in")], writes=[bbank] if k == 0 else (), partial=[bbank] if k else ())
            return bank, bbank

        if not last:
            bank, bb = proj(GA)
            S.op(ACT, lambda e, bank=bank: e.activation(out=qkr[:, 0:512], in_=bank[:, 0:512], func=AF.Copy), reads=[bb], writes=[B("qkr_q")])
            rotary(bank, 8, v3(qkr[:, 0:512], 64), t, bb, B("qkr_q"))
        bank, bb = proj(GB)
        S.op(ACT, lambda e, bank=bank: e.activation(out=qkr[:, 512:640], in_=bank[:, 0:128], func=AF.Copy), reads=[bb], writes=[B("qkr_k")])
        S.op(ACT, lambda e, bank=bank: e.activation(out=V[:, t + 1, :], in_=bank[:, 128:256], func=AF.Copy), reads=[bb], writes=[B("V%d" % (t + 1))])
        rotary(bank, 2, v3(qkr[:, 512:640], 64), t, bb, B("qkr_k"))
        pt, bpt = next_ptr()
        first = True
        if not last:
            for j in range(4):
                S.op(PE, lambda e, j=j, pt=pt: e.transpose(pt[:, j * 128:(j + 1) * 128], qkr[:, j * 128:(j + 1) * 128], C.idb[:]),
                     reads=[B("qkr_q"), B("idb")], writes=[bpt] if first else (), partial=() if first else [bpt])
                first = False
        S.op(PE, lambda e, pt=pt: e.transpose(pt[:, 512:640], qkr[:, 512:640], C.idb[:]),
             reads=[B("qkr_k"), B("idb")], writes=[bpt] if first else (), partial=() if first else [bpt])
        if not last:
            S.op(DVE, lambda e, pt=pt: e.tensor_copy(qT[:, par, :], pt[:, 0:512]), reads=[bpt], writes=[B("qT%d" % par)])
        S.op(DVE, lambda e, pt=pt: e.tensor_copy(kT[:, t + 1, :], pt[:, 512:640]), reads=[bpt], writes=[B("kT%d" % (t + 1))])
        if last:
            return
        bank, bb = proj(GC)
        S.op(ACT, lambda e, bank=bank: e.activation(out=su[:], in_=bank[:, 0:512], func=AF.Copy), reads=[bb], writes=[B("su")])
        bank, bb = proj(GD)
        bst = B("st8")
        S.op(ACT, lambda e, bank=bank: e.activation(out=sq, in_=bank[:, 0:512], func=AF.Square), reads=[bb], writes=[B("z")])
        S.op(DVE, lambda e, bank=bank: e.tensor_reduce(st8[:, 0, :], v3(bank[:, 0:512], 64), AX.X, ALU.add), reads=[bb], writes=[bst])
        S.op(DVE, lambda e: e.tensor_reduce(st8[:, 1, :], v3(sq, 64), AX.X, ALU.add), reads=[B("z")], partial=[bst])
        S.op(DVE, lambda e: e.tensor_scalar(st8[:, 2, :], st8[:, 0, :], 1.0 / 64, None, ALU.mult), reads=[bst], partial=[bst])
        S.op(DVE, lambda e: e.tensor_tensor(st8[:, 3, :], st8[:, 2, :], st8[:, 2, :], ALU.mult), reads=[bst], partial=[bst])
        S.op(DVE, lambda e: e.scalar_tensor_tensor(st8[:, 4, :], st8[:, 1, :], 1.0 / 64, st8[:, 3, :], ALU.mult, ALU.subtract),
             reads=[bst], partial=[bst])
        S.op(ACT, lambda e: e.activation(out=st8[:, 5, :], in_=st8[:, 4, :], func=AF.Sqrt, bias=C.eps_col[:, 0:1], scale=1.0),
             reads=[bst, B("eps_col")], partial=[bst])
        S.op(DVE, lambda e: e.reciprocal(st8[:, 6, :], st8[:, 5, :]), reads=[bst], partial=[bst])
        S.op(DVE, lambda e, bank=bank: e.tensor_tensor(v3(vn[:], 64), v3(bank[:, 0:512], 64), bc_last(st8[:, 2, :], 64), ALU.subtract),
             reads=[bb, bst], writes=[B("vn")])
        S.op(DVE, lambda e: e.tensor_tensor(v3(vn[:], 64), v3(vn[:], 64), bc_last(st8[:, 6, :], 64), ALU.mult),
             reads=[B("vn"), bst], writes=[B("vn")])
        S.op(POOL, lambda e: e.tensor_tensor(vn[:], vn[:], sgrows[:, 0, :], ALU.mult), reads=[B("vn"), B("sgrows")], writes=[B("vn")])
        S.op(POOL, lambda e: e.tensor_tensor(vnb[:], vn[:], sgrows[:, 1, :], ALU.add), reads=[B("vn"), B("sgrows")], writes=[B("vnb")])
        yield "b1"
        bankE, bbE = proj(GE)
        bankF, bbF = proj(GF)
        yield "gm"
        S.op(ACT, lambda e, bank=bankE: e.activation(out=sgl[:, par, 0:512], in_=bank[:, 0:512], func=AF.Silu), reads=[bbE], writes=[B("sgl_a%d" % par)])
        S.op(ACT, lambda e, bank=bankF: e.activation(out=sgl[:, par, 512:1024], in_=bank[:, 0:512], func=AF.Silu), reads=[bbF], writes=[B("sgl_s%d" % par)])
        for g in range(8):
            S.op(PE, lambda e, g=g: e.matmul(svb[:, g * 64:(g + 1) * 64], sgw[:, g, :], vnb[:, g * 64:(g + 1) * 64], start=True, stop=True),
                 reads=[B("sgw"), B("vnb")], writes=[B("svb")] if g == 0 else (), partial=[B("svb")] if g else ())
        S.op(POOL, lambda e: e.tensor_tensor(su[:], su[:], sgl[:, par, 512:1024], ALU.mult), reads=[B("su"), B("sgl_s%d" % par)], writes=[B("su")])
        S.op(DVE, lambda e: e.tensor_tensor(v3(sq, 64), v3(svb[:, 0:512], 64), bc_last(sgbias[:], 64), ALU.add),
             reads=[B("svb"), B("sgbias")], writes=[B("z")])
        S.op(DVE, lambda e: e.tensor_tensor(y[:, par, 512:1024], sq, su[:], ALU.mult), reads=[B("z"), B("su")], writes=[B("y_s%d" % par)])

    def attn(t, mid=None):
        par = t % 2
        mrow = 0 if t == 0 else 1
        batt = B("att")

        def stage1(hp):
            j, half = hp // 2, hp % 2
            sb_, bsb = sbk[hp % 2], bsbk[hp % 2]
            S.op(PE, lambda e: e.matmul(sb_[:, 0:384], qT[half * 64:(half + 1) * 64, par, j * 128:(j + 1) * 128],
                                        kT[half * 64:(half + 1) * 64, t:t + 3, :].rearrange("p a b -> p (a b)"), start=True, stop=False),
                 reads=[B("qT%d" % par), B("kT%d" % t), B("kT%d" % (t + 1)), B("kT%d" % (t + 2))], writes=[bsb])
            S.op(PE, lambda e: e.matmul(sb_[:, 0:384], C.idb[:], mask[:, mrow * 384:(mrow + 1) * 384], start=False, stop=True),
                 reads=[B("idb"), B("mask")], partial=[bsb])
            S.op(DVE, lambda e: e.tensor_reduce(att[:, 0, hp:hp + 1], sb_[:, 0:384], AX.X, ALU.max), reads=[bsb], writes=[B("amx%d" % hp)])
            S.op(DVE, lambda e: e.tensor_scalar(att[:, 1, hp:hp + 1], att[:, 0, hp:hp + 1], C.m0125_col[:, 0:1], nsink[:, hp:hp + 1], ALU.mult, ALU.min),
                 reads=[B("amx%d" % hp), B("nsink"), B("m0125_col")], writes=[B("anm%d" % hp)])
            S.op(ACT, lambda e: e.activation(out=P[:, hp % 2, :], in_=sb_[:, 0:384], func=AF.Exp, bias=att[:, 1, hp:hp + 1], scale=0.125,
                                             accum_out=att[:, 2, hp:hp + 1]),
                 reads=[bsb, B("anm%d" % hp), B("ars")], writes=[B("P%d" % (hp % 2)), B("ars%d" % hp)])

        def stage2a(hp):
            pt, bpt = next_ptr()
            for kt in range(3):
                S.op(PE, lambda e, kt=kt: e.transpose(pt[:, kt * 128:(kt + 1) * 128], P[:, hp % 2, kt * 128:(kt + 1) * 128], C.idb[:]),
                     reads=[B("P%d" % (hp % 2)), B("idb")], writes=[bpt] if kt == 0 else (), partial=[bpt] if kt else ())
            S.op(DVE, lambda e: e.tensor_copy(PT[:, hp % 2, :], pt[:, 0:384]), reads=[bpt], writes=[B("PT%d" % (hp % 2))])

        def stage2b(hp):
            half = hp % 2
            for kt in range(3):
                S.op(PE, lambda e, kt=kt: e.matmul(ob[:, hp * 64:(hp + 1) * 64], PT[:, hp % 2, kt * 128:(kt + 1) * 128],
                                                   V[:, t + kt, half * 64:(half + 1) * 64], start=(kt == 0), stop=(kt == 2)),
                     reads=[B("PT%d" % (hp % 2)), B("V%d" % (t + kt))],
                     writes=[B("ob")] if (hp == 0 and kt == 0) else (), partial=() if (hp == 0 and kt == 0) else [B("ob")])

        S.op(DVE, lambda e: e.memset(att[:, 2, :], 0.0), writes=[B("ars")] + [B("ars%d" % h) for h in range(8)])
        stage1(0)
        stage1(1)
        if mid is not None:
            mid()
        for hp in range(8):
            stage2a(hp)
            if hp + 2 < 8:
                stage1(hp + 2)
            stage2b(hp)
        anm = [B("anm%d" % h) for h in range(8)]
        ars = [B("ars%d" % h) for h in range(8)]
        S.op(DVE, lambda e: e.tensor_tensor(att[:, 3, :], att[:, 1, :], sink[:], ALU.add), reads=anm + [B("sink")], writes=[batt])
        S.op(ACT, lambda e: e.activation(out=att[:, 3, :], in_=att[:, 3, :], func=AF.Exp), reads=[batt], writes=[batt])
        S.op(DVE, lambda e: e.tensor_tensor(att[:, 4, :], att[:, 3, :], att[:, 2, :], ALU.add), reads=[batt] + ars, writes=[batt])
        S.op(DVE, lambda e: e.reciprocal(att[:, 5, :], att[:, 4, :]), reads=[batt], writes=[batt])
        S.op(DVE, lambda e: e.tensor_tensor(v3(yat[:], 64), v3(ob[:, 0:512], 64), bc_last(att[:, 5, :], 64), ALU.mult),
             reads=[B("ob"), batt], writes=[B("vn")])
        S.op(DVE, lambda e: e.tensor_tensor(y[:, par, 0:512], yat[:], sgl[:, par, 0:512], ALU.mult),
             reads=[B("vn"), B("sgl_a%d" % par)], writes=[B("y_a%d" % par)])

    def back(t):
        par = t % 2
        pt, bpt = next_ptr()
        for k in range(8):
            S.op(PE, lambda e, k=k: e.transpose(pt[:, k * 128:(k + 1) * 128], y[:, par, k * 128:(k + 1) * 128], C.idb[:]),
                 reads=[B("y_a%d" % par), B("y_s%d" % par), B("idb")], writes=[bpt] if k == 0 else (), partial=[bpt] if k else ())
        S.op(ACT, lambda e: e.activation(out=yT[:], in_=pt[:], func=AF.Copy), reads=[bpt], writes=[B("yT")])
        for cg in range(2):
            bank, bb = next_pb()
            for k in range(8):
                S.op(PE, lambda e, k=k, cg=cg, bank=bank: e.matmul(bank[:, 0:512], yT[:, k * 128:(k + 1) * 128], w_out[:, k, cg * 512:(cg + 1) * 512],
                                                        start=(k == 0), stop=(k == 7)),
                     reads=[B("yT"), B("l0w_out")], writes=[bb] if k == 0 else (), partial=[bb] if k else ())
            S.op(DVE, lambda e, cg=cg, bank=bank: e.scalar_tensor_tensor(z[:, cg * 512:(cg + 1) * 512], xres[:, t, cg * 512:(cg + 1) * 512], ALPHA,
                                                              bank[:, 0:512], ALU.mult, ALU.add),
                 reads=[B("x%d" % t), bb], writes=[B("z")] if cg == 0 else (), partial=[B("z")] if cg else ())
        yield "a"
        emit_ln_tile(C, "l0", z, B("z"), xres[:, t, :], B("x%d" % t), lnrows[:, 0, :], lnrows[:, 1, :], B("l0lnrows"))
        if out_dram is not None:
            S.op(SP, lambda e: e.dma_start(out=out_dram[t * 128:(t + 1) * 128, :], in_=xres[:, t, :]), reads=[B("x%d" % t)], dma=True)

    import os
    dbg = os.environ.get("L0_DBG", "")
    if dbg == "a":
        for t in range(NT):
            S.op(SP, lambda e, t=t: e.dma_start(out=out_dram[t * 128:(t + 1) * 128, :], in_=xres[:, t, :]), reads=[B("x%d" % t)], dma=True)
        return
    def step(g):
        try:
            next(g)
        except StopIteration:
            pass

    def finish(g):
        for _ in g:
            pass

    modulate(0)
    pend_back = None
    for t in range(NT + 1):
        if t + 2 < NT:
            S.op(SP, lambda e, t=t: e.dma_start(out=xres[:, t + 2, :], in_=x_dram[(t + 2) * 128:(t + 3) * 128, :]),
                 writes=[B("x%d" % (t + 2))], dma=True)
        f = front(t)
        step(f)
        if pend_back is not None:
            finish(pend_back)
            pend_back = None
        step(f)
        if t >= 1:
            attn(t - 1, mid=lambda f=f: step(f))
            finish(f)
            b = back(t - 1)
            step(b)
            pend_back = b
        else:
            finish(f)
    if pend_back is not None:
        finish(pend_back)
    if C.send1 is not None:
        S.op(SP, lambda e: e.dma_start(out=C.send1[0:1, :], in_=xres[127:128, 15, :]), reads=[B("x15")], partial=[B("send1")], dma=True)
        S.op(SP, lambda e: e.dma_start(out=C.send1[1:2, :], in_=xres[126:127, 15, :]), reads=[B("x15")], partial=[B("send1")], dma=True)
        S.op(POOL, lambda e: e.collective_compute("AllGather", ALU.bypass, replica_groups=[[0, 1], [2, 3], [4, 5], [6, 7]],
                                                 ins=[C.send1], outs=[C.recv1]), reads=[B("send1")], writes=[B("recv1")], cc=True)
    if dbg in ("b", "c"):
        for t in range(NT):
            S.op(SP, lambda e, t=t: e.dma_start(out=out_dram[t * 128:(t + 1) * 128, :], in_=xres[:, t, :]), reads=[B("x%d" % t)], dma=True)


def alloc_common(C):
    C.xres = C.sb("xres", [128, NT, 1024])
    C.wstage = C.sb("wstage", [128, 3, 512])
    C.ps_ptr = [C.ps("ptr%d" % i, [128, 1024], BF16) for i in range(2)]
    C.ps_f32 = [C.ps("pf%d" % i, [128, 512]) for i in range(6)]
    if not hasattr(C, "send1"):
        C.send1 = None
    C.bwstage = [C.B("wstage%d" % i) for i in range(3)]
    C.ring3 = [(C.wstage[:, i, :], [C.bwstage[i]]) for i in range(3)]
    C.ln_st6 = C.sb("ln_st6", [128, 12])
    C.m0125_col = C.sb("m0125_col", [128, 1])
    C.S.op(DVE, lambda e: e.memset(C.m0125_col[:], -0.125), writes=[C.B("m0125_col")])
    C.ln_mv = C.sb("ln_mv", [128, 2])
    C.ln_sd = C.sb("ln_sd", [128, 4])
    C.eps_col = C.sb("eps_col", [128, 1])
    C.one_col = C.sb("one_col", [128, 1])
    C.S.op(DVE, lambda e: e.memset(C.eps_col[:], LN_EPS), writes=[C.B("eps_col")])
    C.S.op(DVE, lambda e: e.memset(C.one_col[:], 1.0), writes=[C.B("one_col")])


def build_l0():
    nc = bass.Bass("TRN2", target_bir_lowering=False)
    with contextlib.ExitStack() as st:
        C = Ctx(nc, st)
        x_d = C.dram("x", [2176, 1024], F32, "ExternalInput")
        o_d = C.dram("out", [2048, 1024], F32, "ExternalOutput")
        alloc_common(C)
        emit_consts(C)
        emit_l0(C, x_d, o_d)
        print("sbuf remaining after l0 alloc:", nc.sbuf_bytes_remaining)
        C.S.emit()
    return nc


def rev_ap(ap2):
    n = ap2.shape[1]
    return bass.AP(ap2.tensor, ap2.offset + (n - 1), [list(ap2.ap[0]), [-1, n]])


def emit_l1(C, out_dram):
    S, nc = C.S, C.nc
    B = C.B
    xres = C.xres
    S.barrier()
    C.areset()
    RG = [[0, 1], [2, 3], [4, 5], [6, 7]]
    d_win = C.dram("l1_w_in", [1024, 2048], F32, "ExternalInput")
    d_w5 = C.dram("l1_w5", [128, 40], F32, "ExternalInput")
    d_cb = C.dram("l1_cb", [128, 8], F32, "ExternalInput")
    d_wg = C.dram("l1_wg", [128, 4096], F32, "ExternalInput")
    d_bg = C.dram("l1_bg", [128, 32], F32, "ExternalInput")
    d_lam = C.dram("l1_lam", [128, 16], F32, "ExternalInput")
    d_lng = C.dram("l1_ln_g", [128, 1024], F32, "ExternalInput")
    d_lnb = C.dram("l1_ln_b", [128, 1024], F32, "ExternalInput")
    d_sel = C.dram("sel", [128, 2], F32, "ExternalInput")
    hT = C.asb("l1_hT", [128, 8, 2056], BF16)
    w_out = hT.rearrange("p k t -> p (k t)")[:, 0:8192].rearrange("p (k c) -> p k c", c=1024)
    yT = C.asb("l1_yT", [128, 8, 2048], BF16)
    xc = C.asb("l1_xc", [128, 8, 2048], BF16)
    wg = C.asb("l1_wg", [128, 4, 8, 128], BF16)
    wt = C.asb("l1_wt", [128, 8, 128], BF16)
    xr = C.asb("l1_xr", [128, 2052], BF16)
    sg = xr[:, 0:2048]
    lnrows = xc[:, 4:6, :].rearrange("p a b -> p (a b)").bitcast(F32).rearrange("p (a b) -> p a b", a=2)
    dg = C.asb("l1_dg", [128, 10, 128], BF16)
    modrow = xc[:, 0:2, :].rearrange("p a b -> p (a b)").bitcast(F32).rearrange("p (a b) -> p a b", a=2)
    wk = C.asb("l1_wk", [128, 5120])
    WA = wk[:, 0:2048]
    WTI = wk[:, 2048:3072].bitcast(BF16)
    WVB = wk[:, 3072:4096].bitcast(BF16)
    WH = wk[:, 4096:5120].rearrange("p (a b) -> p a b", a=2)
    h32 = wk[:, 0:1024]
    hbf = wk[:, 1024:1536].bitcast(BF16)
    hb2 = wk[:, 2048:3072]
    z = h32
    gaterow = wk[:, 1024:2048]
    crep = wk[:, 0:1024].rearrange("p (k m) -> p k m", m=128)
    brow = wk[0:1, 4096:5120]
    hbg = C.asb("l1_hbg", [128, 32])
    hn = C.asb("l1_hn", [128, 16])
    w5 = C.asb("l1_w5", [128, 40])
    cb = C.asb("l1_cb", [128, 8])
    bg = C.asb("l1_bg", [128, 32])
    lam = C.asb("l1_lam", [128, 16])
    nsp8 = C.asb("l1_nsp8", [128, 16])
    sel = C.asb("l1_sel", [128, 2])
    carry = C.asb("l1_carry", [128, 8])
    cga = C.asb("l1_cga", [128, 8])
    cgb = C.asb("l1_cgb", [128, 8])
    stA = C.asb("l1_stA", [128, 8])
    stB = C.asb("l1_stB", [128, 8])
    ptr = C.ps_ptr
    pb = C.ps_f32[0:6]
    bptr = [B("ptr0"), B("ptr1")]
    bpb = [B("pb%d" % i) for i in range(6)]
    for b_ in bptr + bpb:
        b_.excl = True
    cnt = {"ptr": 0, "pb": 0}

    def next_ptr():
        i = cnt["ptr"] % 2
        cnt["ptr"] += 1
        return ptr[i], bptr[i]

    def next_pb():
        i = cnt["pb"] % 6
        cnt["pb"] += 1
        return pb[i], bpb[i]

    w4k = (C.wstage[:].rearrange("p a b -> p (a b)")[:, 0:1024], [C.bwstage[0], C.bwstage[1]])
    ring1 = [w4k] + [(yT[:, k, :].bitcast(F32), [B("yT%d" % k)]) for k in range(8)]
    bring1 = [(xc[:, 3, 0:1024], [B("xc3")]), (xc[:, 3, 1024:2048], [B("xc3")]), (xc[:, 2, 0:1024], [B("xc2")]), (xc[:, 2, 1024:2048], [B("xc2")])]
    crep_b = wk[:, 0:512].bitcast(BF16).rearrange("p (k m) -> p k m", m=128)
    for nm, dst, src in (("w5", w5, d_w5), ("cb", cb, d_cb), ("bg", bg, d_bg), ("lam", lam, d_lam), ("sel", sel, d_sel)):
        S.op(SP, lambda e, dst=dst, src=src: e.dma_start(out=dst, in_=src[:, :]), writes=[B("l1" + nm)], dma=True)
    emit_adaln2(C, "l1", [(pb[0], bpb[0]), (pb[1], bpb[1])],
                [(modrow[:, 0, :], B("modrow"), False), (modrow[:, 1, :], B("modrow"), True), (gaterow, B("t23"), False)],
                crep_b, B("t01"), brow, (0, 1), ring1, bring1, tag="a")
    wg2 = wg.rearrange("p m h j -> p (m h j)")
    for i in range(8):
        slot = i % 3
        sap, sbufs = ring1[(i + 5) % len(ring1)]
        S.op(SP, lambda e, i=i, sap=sap: e.dma_start(out=sap[:, 0:512], in_=d_wg[:, i * 512:(i + 1) * 512]), writes=sbufs, dma=True)
        S.op(DVE if i % 2 else ACT, (lambda e, i=i, sap=sap: e.tensor_copy(wg2[:, i * 512:(i + 1) * 512], sap[:, 0:512])) if i % 2 else
             (lambda e, i=i, sap=sap: e.activation(out=wg2[:, i * 512:(i + 1) * 512], in_=sap[:, 0:512], func=AF.Copy)),
             reads=sbufs, partial=[B("l1wg")])
    S.op(ACT, lambda e: e.activation(out=nsp8, in_=lam, func=AF.Exp, scale=-1.0), reads=[B("l1lam")], writes=[B("nsp8")])
    S.op(ACT, lambda e: e.activation(out=nsp8, in_=nsp8, func=AF.Ln, bias=C.one_col[:, 0:1], scale=1.0), reads=[B("nsp8"), B("one_col")], writes=[B("nsp8")])
    S.op(DVE, lambda e: e.tensor_scalar(nsp8, nsp8, -8.0, None, ALU.mult), reads=[B("nsp8")], writes=[B("nsp8")])
    S.op(DVE, lambda e: e.tensor_scalar(hn, nsp8, 0.5, None, ALU.mult), reads=[B("nsp8")], writes=[B("hn")])
    S.op(DVE, lambda e: e.tensor_scalar(hbg, bg, 0.5, None, ALU.mult), reads=[B("l1bg")], writes=[B("hbg")])
    S.op(DVE, lambda e: e.memset(xr[:, 0:2], 0.0), partial=[B("xr")])

    for t in range(NT + 1):
        if t < NT:
            xsrc, bxs = xres[:, t, :], [B("x%d" % t)]
        else:
            S.op(DVE, lambda e: e.memset(h32, 0.0), writes=[B("t01")])
            S.op(SP, lambda e: e.dma_start(out=h32[0:2, :], in_=C.recv1[0:2, :]), reads=[B("recv1")], partial=[B("t01")], dma=True)
            S.op(SP, lambda e: e.dma_start(out=hb2[0:2, :], in_=C.recv1[2:4, :]), reads=[B("recv1")], writes=[B("t34")], dma=True)
            S.op(DVE, lambda e: e.tensor_scalar(h32[0:2, :], h32[0:2, :], sel[0:2, 0:1], None, ALU.mult), reads=[B("t01"), B("l1sel")], partial=[B("t01")])
            S.op(DVE, lambda e: e.scalar_tensor_tensor(h32[0:2, :], hb2[0:2, :], sel[0:2, 1:2], h32[0:2, :], ALU.mult, ALU.add),
                 reads=[B("t01"), B("t34"), B("l1sel")], partial=[B("t01")])
            xsrc, bxs = h32, [B("t01")]
        meng = POOL if t % 3 == 0 else DVE
        S.op(meng, lambda e, xsrc=xsrc: e.tensor_tensor(h32, xsrc, modrow[:, 1, :], ALU.mult), reads=bxs + [B("modrow")], writes=[B("t01")])
        S.op(meng, lambda e: e.tensor_tensor(hbf, h32, modrow[:, 0, :], ALU.add), reads=[B("t01"), B("modrow")], writes=[B("t23")])
        pt, bpt = next_ptr()
        for k in range(8):
            S.op(PE, lambda e, k=k, pt=pt: e.transpose(pt[:, k * 128:(k + 1) * 128], hbf[:, k * 128:(k + 1) * 128], C.idb[:]),
                 reads=[B("t23"), B("idb")], writes=[bpt] if k == 0 else (), partial=[bpt] if k else ())
        ncol = 128 if t < NT else 8
        S.op(ACT, lambda e, t=t, pt=pt, ncol=ncol: e.activation(out=hT[:, :, t * 128:t * 128 + ncol],
                                                               in_=pt[:].rearrange("p (k m) -> p k m", m=128)[:, :, 0:ncol], func=AF.Copy),
             reads=[bpt], partial=[B("l1hT")])
    S.barrier()

    def load_wt(c0):
        wst = C.wstage[:].rearrange("p a b -> p (a b)")[:, 0:1024].rearrange("p (k m) -> p k m", m=128)
        S.op(SP, lambda e: e.dma_start(out=wst, in_=d_win.rearrange("(k p) c -> p k c", p=128)[:, :, c0:c0 + 128]),
             writes=[C.bwstage[0], C.bwstage[1]], dma=True)
        S.op(DVE, lambda e: e.tensor_copy(wt, wst), reads=[C.bwstage[0], C.bwstage[1]], writes=[B("l1wt")])

    def batch(d, ct, finish_chunk):
        qs = [0, 1, 2, 3] if d == 0 else [3, 2, 1, 0]
        ca = (2 * d) * 8 + ct
        ci = (2 * d + 1) * 8 + ct
        cl = d * 8 + ct
        bxc = B("xc%d" % ct)
        for n_, q in enumerate(qs):
            sl = slice(q * 512, (q + 1) * 512)
            bA, bI, bV = B("wA%d" % q), B("wTI%d" % q), B("wVB%d" % q)
            Ht, bHt = WH[:, n_ % 2, :], B("wH%d" % (n_ % 2))
            bk_r, bb_r = next_pb()
            bk_i, bb_i = next_pb()
            S.op(PE, lambda e, sl=sl, bk_r=bk_r: e.matmul(bk_r[:, 0:512], wg[:, 2 * d, ct, :], xc[:, ct, sl], start=True, stop=True),
                 reads=[B("l1wg"), bxc], writes=[bb_r])
            S.op(PE, lambda e, sl=sl, bk_i=bk_i: e.matmul(bk_i[:, 0:512], wg[:, 2 * d + 1, ct, :], xc[:, ct, sl], start=True, stop=True),
                 reads=[B("l1wg"), bxc], writes=[bb_i])
            S.op(ACT, lambda e, sl=sl, bk_r=bk_r: e.activation(out=WA[:, sl], in_=bk_r[:, 0:512], func=AF.Tanh, bias=hbg[:, ca:ca + 1], scale=0.5),
                 reads=[bb_r, B("hbg")], writes=[bA])
            S.op(ACT, lambda e, sl=sl, bk_i=bk_i: e.activation(out=WTI[:, sl], in_=bk_i[:, 0:512], func=AF.Tanh, bias=hbg[:, ci:ci + 1], scale=0.5),
                 reads=[bb_i, B("hbg")], writes=[bI])
            S.op(ACT, lambda e, sl=sl: e.activation(out=WA[:, sl], in_=WA[:, sl], func=AF.Exp, bias=hn[:, cl:cl + 1], scale=hn[:, cl:cl + 1]),
                 reads=[bA, B("hn")], writes=[bA])
            S.op(DVE, lambda e, sl=sl, Ht=Ht: e.tensor_tensor(Ht, WA[:, sl], WA[:, sl], ALU.mult), reads=[bA], writes=[bHt])
            S.op(DVE, lambda e, sl=sl, Ht=Ht: e.tensor_scalar(WVB[:, sl], Ht, -1.0, 1.0, ALU.mult, ALU.add), reads=[bHt], writes=[bV])
        yield "G"
        allV = [B("wVB%d" % q) for q in range(4)]
        allI = [B("wTI%d" % q) for q in range(4)]
        S.op(ACT, lambda e: e.activation(out=WVB, in_=WVB, func=AF.Sqrt, scale=0.25), reads=allV, writes=allV)
        S.op(DVE, lambda e: e.scalar_tensor_tensor(WVB, WTI, 1.0, WVB, ALU.add, ALU.mult), reads=allV + allI, writes=allV)
        S.op(DVE, lambda e: e.tensor_tensor(WVB, WVB, xc[:, ct, :], ALU.mult), reads=allV + [bxc], writes=allV)
        st = stA if d == 0 else stB
        bst = B("stA") if d == 0 else B("stB")
        for n_, q in enumerate(qs):
            sl = slice(q * 512, (q + 1) * 512)
            bA, bV = B("wA%d" % q), B("wVB%d" % q)
            H, bH = WH[:, n_ % 2, :], B("wH%d" % (n_ % 2))
            if d == 0:
                init = 0.0 if n_ == 0 else st[:, ct:ct + 1]
                S.op(DVE, lambda e, sl=sl, H=H, init=init: e.tensor_tensor_scan(H, WA[:, sl], WVB[:, sl], init, ALU.mult, ALU.add),
                     reads=[bA, bV, bst], writes=[bH])
                S.op(DVE, lambda e, H=H: e.tensor_copy(st[:, ct:ct + 1], H[:, 511:512]), reads=[bH], writes=[bst])
            else:
                init = carry[:, ct:ct + 1] if n_ == 0 else st[:, ct:ct + 1]
                S.op(DVE, lambda e, sl=sl, H=H, init=init: e.tensor_tensor_scan(rev_ap(H), rev_ap(WA[:, sl]), rev_ap(WVB[:, sl]), init, ALU.mult, ALU.add),
                     reads=[bA, bV, bst, B("l1carry")], writes=[bH])
                S.op(DVE, lambda e, H=H: e.tensor_copy(st[:, ct:ct + 1], H[:, 0:1]), reads=[bH], writes=[bst])
            finish_chunk(q, H, bH)

    def conv_stage(ct):
        for tg in range(5):
            n = 512 if tg < 4 else 8
            bank, bb = next_pb()
            for k in range(8):
                S.op(PE, lambda e, k=k, tg=tg, n=n, bank=bank: e.matmul(bank[:, 0:n], wt[:, k, :], hT[:, k, tg * 512:tg * 512 + n],
                                                                      start=(k == 0), stop=(k == 7)),
                     reads=[B("l1wt"), B("l1hT")], writes=[bb] if k == 0 else (), partial=[bb] if k else ())
            m = n if tg < 4 else 2
            S.op(DVE, lambda e, tg=tg, m=m, bank=bank: e.tensor_copy(xr[:, 2 + tg * 512:2 + tg * 512 + m], bank[:, 0:m]),
                 reads=[bb], partial=[B("xr")])
        if ct + 1 < 8:
            load_wt((ct + 1) * 128)
        par = ct % 2
        for j in range(5):
            S.op(ACT, lambda e, ct=ct, j=j, par=par: e.activation(out=dg[:, par * 5 + j, :], in_=C.idb[:], func=AF.Copy, scale=w5[:, ct * 5 + j:ct * 5 + j + 1]),
                 reads=[B("idb"), B("l1w5")], writes=[B("dg%d_%d" % (par, j))])
        for q in range(4):
            sl = slice(q * 512, (q + 1) * 512)
            bank, bb = next_pb()
            for j in range(5):
                S.op(PE, lambda e, j=j, q=q, par=par, bank=bank: e.matmul(bank[:, 0:512], dg[:, par * 5 + j, :], xr[:, q * 512 + j:q * 512 + j + 512],
                                                                        start=(j == 0), stop=(j == 4)),
                     reads=[B("dg%d_%d" % (par, j)), B("xr")], writes=[bb] if j == 0 else (), partial=[bb] if j else ())
            S.op(DVE, lambda e, ct=ct, sl=sl, bank=bank: e.tensor_scalar(xc[:, ct, sl], bank[:, 0:512], cb[:, ct:ct + 1], None, ALU.add),
                 reads=[bb, B("l1cb")], partial=[B("xc%d" % ct)])

    def drain(g):
        for _ in g:
            pass

    load_wt(0)
    conv_stage(0)
    for ct in range(8):
        def fin_a(q, H, bH, ct=ct):
            sl = slice(q * 512, (q + 1) * 512)
            S.op(DVE, lambda e: e.tensor_copy(yT[:, ct, sl], H), reads=[bH], partial=[B("yT%d" % ct)])
        g = batch(0, ct, fin_a)
        next(g)
        if ct + 1 < 8:
            conv_stage(ct + 1)
        drain(g)

    S.op(SP, lambda e: e.dma_start(out=C.send2[:, :], in_=stA), reads=[B("stA")], writes=[B("send2")], dma=True)
    S.op(POOL, lambda e: e.collective_compute("AllGather", ALU.bypass, replica_groups=RG, ins=[C.send2], outs=[C.recv2]),
         reads=[B("send2")], writes=[B("recv2")], cc=True)
    S.op(SP, lambda e: e.dma_start(out=cga, in_=C.recv2[0:128, :]), reads=[B("recv2")], writes=[B("cga")], dma=True)
    S.op(SP, lambda e: e.dma_start(out=cgb, in_=C.recv2[128:256, :]), reads=[B("recv2")], writes=[B("cgb")], dma=True)
    S.op(DVE, lambda e: e.tensor_scalar(carry, cga, sel[:, 0:1], None, ALU.mult), reads=[B("cga"), B("l1sel")], writes=[B("l1carry")])
    S.op(DVE, lambda e: e.scalar_tensor_tensor(carry, cgb, sel[:, 1:2], carry, ALU.mult, ALU.add), reads=[B("cgb"), B("l1carry"), B("l1sel")], writes=[B("l1carry")])

    def gate_chunk(ct, q):
        bank, bb = next_pb()
        for k in range(8):
            S.op(PE, lambda e, k=k, bank=bank: e.matmul(bank[:, 0:512], wt[:, k, :], hT[:, k, q * 512:(q + 1) * 512],
                                                       start=(k == 0), stop=(k == 7)),
                 reads=[B("l1wt"), B("l1hT")], writes=[bb] if k == 0 else (), partial=[bb] if k else ())
        S.op(ACT, lambda e, bank=bank: e.activation(out=sg[:, q * 512:(q + 1) * 512], in_=bank[:, 0:512], func=AF.Silu),
             reads=[bb], writes=[B("sg%d" % q)], partial=[B("xr")] if ct == 0 else ())

    load_wt(1024)
    for q in (3, 2, 1, 0):
        gate_chunk(0, q)
    for ct in range(8):
        if ct + 1 < 8:
            load_wt(1024 + (ct + 1) * 128)

        def fin_b(q, H, bH, ct=ct):
            sl = slice(q * 512, (q + 1) * 512)
            S.op(DVE, lambda e: e.tensor_tensor(H, H, yT[:, ct, sl], ALU.add), reads=[bH, B("yT%d" % ct)], writes=[bH])
            S.op(DVE, lambda e: e.tensor_tensor(yT[:, ct, sl], H, sg[:, sl], ALU.mult), reads=[bH, B("sg%d" % q)], partial=[B("yT%d" % ct)])
            if ct + 1 < 8:
                gate_chunk(ct + 1, q)
        drain(batch(1, ct, fin_b))

    S.barrier()
    ring2 = [w4k, (xc[:, 6, :].bitcast(F32), [B("xc6")]), (xc[:, 7, :].bitcast(F32), [B("xc7")])]
    emit_adaln2(C, "l1", [(pb[0], bpb[0]), (pb[1], bpb[1])],
                [None, None, (gaterow, B("t23"), False)], crep_b, B("t01"), brow, (2,), ring2, bring1, tag="b")
    emit_weight_bf16(C, "l1_w_out", w_out, B("l1hT"), 8, 1024, ring2, scale_row=gaterow, bscale=B("t23"), chunk=1024)
    S.op(SP, lambda e: e.dma_start(out=lnrows[:, 0, :], in_=d_lng[:, :]), writes=[B("xc4")], partial=[B("l1lnrows")], dma=True)
    S.op(SP, lambda e: e.dma_start(out=lnrows[:, 1, :], in_=d_lnb[:, :]), writes=[B("xc5")], partial=[B("l1lnrows")], dma=True)
    yTb = [B("yT%d" % k) for k in range(8)]
    S.barrier()
    zall = xc.rearrange("p a b -> p (a b)").bitcast(F32)
    lnsm = [(C.asb("l1_st6_%d" % i, [128, 12]), C.asb("l1_mv_%d" % i, [128, 2]), C.asb("l1_sd_%d" % i, [128, 4]), "_%d" % i) for i in range(3)]
    for t in range(NT):
        z = zall[:, (t % 3) * 1024:(t % 3 + 1) * 1024]
        bz = B("l1z%d" % (t % 3))
        for cg in range(2):
            bank, bb = next_pb()
            for k in range(8):
                S.op(PE, lambda e, k=k, cg=cg, t=t, bank=bank: e.matmul(bank[:, 0:512], yT[:, k, t * 128:(t + 1) * 128], w_out[:, k, cg * 512:(cg + 1) * 512],
                                                                       start=(k == 0), stop=(k == 7)),
                     reads=[yTb[k], B("l1hT")], writes=[bb] if k == 0 else (), partial=[bb] if k else ())
            S.op(DVE, lambda e, cg=cg, t=t, bank=bank, z=z: e.scalar_tensor_tensor(z[:, cg * 512:(cg + 1) * 512], xres[:, t, cg * 512:(cg + 1) * 512], ALPHA,
                                                                             bank[:, 0:512], ALU.mult, ALU.add),
                 reads=[B("x%d" % t), bb], writes=[bz] if cg == 0 else (), partial=[bz] if cg else ())
        emit_ln_tile(C, "l1", z, bz, xres[:, t, :], B("x%d" % t), lnrows[:, 0, :], lnrows[:, 1, :], B("l1lnrows"), small=lnsm[t % 3])
        S.op(SP, lambda e, t=t: e.dma_start(out=out_dram[t * 128:(t + 1) * 128, :], in_=xres[:, t, :]), reads=[B("x%d" % t)], dma=True)


def l1_inputs(c, inp):
    b, half = c // 2, c % 2
    dA, dB = (0, 1) if half == 0 else (1, 0)
    cw = inp["od_conv_w"][0]
    zero = np.zeros((1, 1024), np.float32)
    w5 = np.concatenate([cw, zero], 0) if half == 0 else np.concatenate([zero, cw[::-1]], 0)
    w5 = np.ascontiguousarray(w5.reshape(5, 8, 128).transpose(2, 1, 0)).reshape(128, 40)
    wa, wx = inp["od_w_a"][0], inp["od_w_x"][0]
    wg = np.stack([wa[dA], wx[dA], wa[dB], wx[dB]], 0)
    wg = np.ascontiguousarray(wg.transpose(2, 0, 1, 3)).reshape(128, 4096)
    ba, bx = inp["od_b_a"][0], inp["od_b_x"][0]
    bgm = np.stack([ba[dA], bx[dA], ba[dB], bx[dB]], 0).reshape(4, 8, 128)
    bgm = np.ascontiguousarray(bgm.transpose(2, 0, 1)).reshape(128, 32)
    lam = inp["od_lam"][0]
    lamm = np.stack([lam[dA], lam[dB]], 0).reshape(2, 8, 128)
    lamm = np.ascontiguousarray(lamm.transpose(2, 0, 1)).reshape(128, 16)
    return {
        "sel": rep(np.array([0.0, 1.0], np.float32) if half == 0 else np.array([1.0, 0.0], np.float32)),
        "l1_cT": np.ascontiguousarray(inp["c"][b].reshape(8, 128).T),
        "l1_ada_w": np.ascontiguousarray(inp["ada_w"][1]),
        "l1_ada_b": np.ascontiguousarray(inp["ada_b"][1][None, :]),
        "l1_w_in": np.ascontiguousarray(inp["od_w_in"][0]),
        "l1_w5": w5,
        "l1_cb": np.ascontiguousarray(inp["od_conv_b"][0].reshape(8, 128).T),
        "l1_wg": wg,
        "l1_bg": bgm,
        "l1_lam": lamm,
        "l1_w_out": np.ascontiguousarray(inp["od_w_out"][0]),
        "l1_ln_g": rep(inp["ln_g"][1]),
        "l1_ln_b": rep(inp["ln_b"][1]),
    }


def build_fused():
    nc = bass.Bass("TRN2", target_bir_lowering=False)
    with contextlib.ExitStack() as st:
        C = Ctx(nc, st)
        x_d = C.dram("x", [2176, 1024], F32, "ExternalInput")
        o_d = C.dram("out", [2048, 1024], F32, "ExternalOutput")
        C.send1 = nc.dram_tensor("send1", [2, 1024], F32).ap()
        C.recv1 = nc.dram_tensor("recv1", [4, 1024], F32).ap()
        C.send2 = nc.dram_tensor("send2", [128, 8], F32).ap()
        C.recv2 = nc.dram_tensor("recv2", [256, 8], F32).ap()
        alloc_common(C)
        emit_consts(C)
        C.make_arena(ARENA_WORDS)
        emit_l0(C, x_d, None)
        p0 = C.apeak
        emit_l1(C, o_d)
        print("arena words: l0 peak", p0, "overall peak", C.apeak, "of", C.asize, "sbuf remaining", nc.sbuf_bytes_remaining)
        C.S.emit()
    return nc


def rep(v, n=128):
    return np.ascontiguousarray(np.broadcast_to(np.asarray(v, np.float32)[None, :], (n, v.shape[0])))


def core_tokens(half):
    if half == 0:
        return np.arange(0, 2048), np.arange(2048, 2176)
    return np.arange(4095, 2047, -1), np.arange(2047, 1919, -1)


def l0_masks():
    qi = np.arange(128)[:, None]
    kj = np.arange(384)[None, :]
    valid = np.abs(kj - 128 - qi) <= 128
    m1 = np.where(valid, 0.0, NEG).astype(np.float32)
    m0 = np.where(valid & (kj >= 128), 0.0, NEG).astype(np.float32)
    return np.ascontiguousarray(np.concatenate([m0, m1], axis=1))


def l0_inputs(c, inp, x_override=None):
    b, half = c // 2, c % 2
    own, halo = core_tokens(half)
    idx = np.concatenate([own, halo])
    x = inp["x"][b][idx]
    pos = inp["positions"][b][idx].astype(np.int32).reshape(17, 128).T
    w_in = inp["ev_w_in"][0]
    qcols = np.concatenate([np.arange(h * 64, (h + 1) * 64) for h in PERM])
    cols = np.concatenate([qcols, np.arange(512, 1792), 1792 + qcols, np.arange(2304, 2816)])
    w_out = inp["ev_w_out"][0]
    rows = np.concatenate([qcols, np.arange(512, 1024)])
    sgw = inp["ev_sg_w"][0]
    sgb = inp["ev_sg_b"][0]
    if half == 1:
        sgw = sgw[:, ::-1, ::-1]
        sgb = sgb[:, ::-1]
    sgwT = np.ascontiguousarray(np.transpose(sgw, (2, 0, 1))).reshape(128, 1024)
    return {
        "x": np.ascontiguousarray(x),
        "ident": np.eye(128, dtype=np.float32),
        "l0_pos": np.ascontiguousarray(pos),
        "l0_cT": np.ascontiguousarray(inp["c"][b].reshape(8, 128).T),
        "l0_ada_w": np.ascontiguousarray(inp["ada_w"][0]),
        "l0_ada_b": np.ascontiguousarray(inp["ada_b"][0][None, :]),
        "l0_w_in": np.ascontiguousarray(w_in[:, cols]),
        "l0_w_out": np.ascontiguousarray(w_out[rows, :]),
        "l0_sink": rep(inp["ev_sink"][0][PERM]),
        "l0_sg_ln_g": rep(inp["ev_sg_ln_g"][0]),
        "l0_sg_ln_b": rep(inp["ev_sg_ln_b"][0]),
        "l0_sg_wT": sgwT,
        "l0_sg_b": np.ascontiguousarray(sgb.T),
        "l0_ln_g": rep(inp["ln_g"][0]),
        "l0_ln_b": rep(inp["ln_b"][0]),
        "l0_mask": l0_masks(),
    }


def run_l0(inp):
    nc = build_l0()
    in_maps = [l0_inputs(c, inp) for c in range(8)]
    res = run_bass_kernel_spmd(nc, in_maps, core_ids=list(range(8)))
    return [r["out"] for r in res.results]


def assemble(outs):
    full = np.zeros((4, 4096, 1024), np.float32)
    for c in range(8):
        b, half = c // 2, c % 2
        own, _ = core_tokens(half)
        full[b, own] = outs[c]
    return full


def fused_inputs(c, inp):
    d = l0_inputs(c, inp)
    d.update(l1_inputs(c, inp))
    return d


def kernel(**inputs):
    inp = {k: np.asarray(v) for k, v in inputs.items()}
    nc = build_fused()
    in_maps = [fused_inputs(c, inp) for c in range(8)]
    res = run_bass_kernel_spmd(nc, in_maps, core_ids=list(range(8)))
    return assemble([r["out"] for r in res.results])
```

```python
import contextlib
import numpy as np
import concourse.bass as bass
import concourse.mybir as mybir
from concourse.bass_utils import run_bass_kernel_spmd

F32 = mybir.dt.float32
BF16 = mybir.dt.bfloat16
I32 = mybir.dt.int32
ALU = mybir.AluOpType
AF = mybir.ActivationFunctionType
AX = mybir.AxisListType

PE, ACT, DVE, POOL, SP = "tensor", "scalar", "vector", "gpsimd", "sync"
ENGS = (PE, ACT, DVE, POOL, SP)

D = 1024
T = 2048
NT = 16
ALPHA = 4.0 ** 0.25
LN_EPS = 1e-5
PERM = [0, 4, 1, 5, 2, 6, 3, 7]
INV_FREQ = [float(np.float32(500000.0) ** np.float32(-i / 8.0)) for i in range(8)]
NEG = -30000.0
ARENA_WORDS = 34880


class Buf:
    __slots__ = ("name", "w", "r", "excl")

    def __init__(self, name):
        self.name = name
        self.w = {}
        self.r = {}
        self.excl = False


class Op:
    __slots__ = ("eng", "fn", "deps", "sig", "dma", "idx")

    def __init__(self, eng, fn):
        self.eng = eng
        self.fn = fn
        self.deps = {}
        self.sig = None
        self.dma = None
        self.idx = None


class Sched:
    def __init__(self, nc, n_dma_sems=8, n_cc_sems=2):
        self.nc = nc
        self.ops = {e: [] for e in ENGS}
        self.n_dma_sems = n_dma_sems
        self.n_cc = n_cc_sems
        nt = n_dma_sems + n_cc_sems
        self.sem_inc = [16] * n_dma_sems + [1] * n_cc_sems
        self.dma_val = [0] * nt
        self.dma_last = [None] * nt
        self.dma_rr = 0
        self.cc_rr = 0
        self.pending = {e: [] for e in ENGS}

    def barrier(self):
        toks = []
        for e in ENGS:
            if self.ops[e]:
                last = [o for o in self.ops[e] if o.dma is None]
                if last:
                    toks.append(("op", last[-1]))
        for t in self.dma_last[:self.n_dma_sems]:
            if t is not None:
                toks.append(t)
        for e in ENGS:
            self.pending[e] = list(toks)

    @staticmethod
    def _key(tok):
        return tok[1].eng if tok[0] == "op" else ("dma", tok[1])

    @staticmethod
    def _newer(a, b):
        if a is None:
            return b
        if a[0] == "op":
            return b if b[1].idx > a[1].idx else a
        return b if b[2] > a[2] else a

    def _add_dep(self, op, tok):
        if tok[0] == "op" and tok[1] is op:
            return
        k = self._key(tok)
        op.deps[k] = self._newer(op.deps.get(k), tok)

    def op(self, eng, fn, reads=(), writes=(), partial=(), dma=False, cc=False):
        o = Op(eng, fn)
        o.idx = len(self.ops[eng])
        had_pending = bool(self.pending[eng])
        if had_pending:
            for t in self.pending[eng]:
                self._add_dep(o, t)
            self.pending[eng] = []
        if cc:
            dma = True
            si = self.n_dma_sems + self.cc_rr
            self.cc_rr = (self.cc_rr + 1) % self.n_cc
        elif dma:
            si = self.dma_rr
            self.dma_rr = (self.dma_rr + 1) % self.n_dma_sems
        if dma:
            prev = self.dma_last[si]
            if prev is not None:
                self._add_dep(o, prev)
            self.dma_val[si] += self.sem_inc[si]
            o.dma = (si, self.dma_val[si])
            tok = ("dma", si, self.dma_val[si])
            self.dma_last[si] = tok
        else:
            tok = ("op", o)
        for b in reads:
            for t in b.w.values():
                self._add_dep(o, t)
            if b.excl:
                for kk, t in b.r.items():
                    if kk != eng:
                        self._add_dep(o, t)
        for b in list(writes) + list(partial):
            for t in b.w.values():
                self._add_dep(o, t)
            for t in b.r.values():
                self._add_dep(o, t)
        if not dma and not had_pending:
            raw = -1
            for b in reads:
                t = b.w.get(eng)
                if t is not None and t[0] == "op":
                    raw = max(raw, t[1].idx)
            if eng in o.deps:
                if raw < 0:
                    del o.deps[eng]
                else:
                    o.deps[eng] = ("op", self.ops[eng][raw])
        k = self._key(tok)
        for b in reads:
            b.r[k] = self._newer(b.r.get(k), tok)
        for b in writes:
            b.w = {k: tok}
            b.r = {}
        for b in partial:
            b.w[k] = self._newer(b.w.get(k), tok)
        self.ops[eng].append(o)
        return o

    def emit(self, final_eng=SP):
        nc = self.nc
        needed = set()
        for e in ENGS:
            for o in self.ops[e]:
                for t in o.deps.values():
                    if t[0] == "op":
                        needed.add(id(t[1]))
        for e in ENGS:
            c = 0
            for o in self.ops[e]:
                if o.dma is None and id(o) in needed:
                    c += 1
                    o.sig = c
        with contextlib.ExitStack() as st:
            esem = {e: st.enter_context(nc.semaphore("s_" + e)) for e in ENGS}
            dsem = [st.enter_context(nc.semaphore("d_%d" % i)) for i in range(self.n_dma_sems + self.n_cc)]
            block = st.enter_context(nc.Block())

            def run(eng_name, engine):
                known = {}
                for o in self.ops[eng_name]:
                    for k, t in o.deps.items():
                        if t[0] == "op":
                            sem, val = esem[t[1].eng], t[1].sig
                        else:
                            sem, val = dsem[t[1]], t[2]
                        if known.get(k, 0) >= val:
                            continue
                        known[k] = val
                        engine.wait_ge(sem, val)
                    ins = o.fn(engine)
                    if o.dma is not None:
                        ins.then_inc(dsem[o.dma[0]], self.sem_inc[o.dma[0]])
                    elif o.sig is not None:
                        ins.then_inc(esem[eng_name], 1)
                if eng_name == final_eng:
                    for i in range(self.n_dma_sems + self.n_cc):
                        if self.dma_val[i] > 0:
                            engine.wait_ge(dsem[i], self.dma_val[i])

            @block.tensor
            def _(e):
                run(PE, e)

            @block.scalar
            def _(e):
                run(ACT, e)

            @block.vector
            def _(e):
                run(DVE, e)

            @block.gpsimd
            def _(e):
                run(POOL, e)

            @block.sync
            def _(e):
                run(SP, e)


class Ctx:
    def __init__(self, nc, st):
        self.nc = nc
        self.st = st
        self.S = Sched(nc)
        self.bufs = {}
        self.arena = None
        self.aoff = 0
        self.asize = 0
        self.apeak = 0

    def make_arena(self, words):
        self.arena = self.sb("arena", [128, words])
        self.asize = words
        self.aoff = 0

    def areset(self):
        self.aoff = 0

    def asb(self, name, shape, dt=F32):
        if self.arena is None:
            return self.sb(name, shape, dt)[:]
        esz = {F32: 4, I32: 4, BF16: 2}[dt]
        n = 1
        for d_ in shape[1:]:
            n *= d_
        words = (n * esz + 3) // 4
        off = self.aoff
        self.aoff += words
        self.apeak = max(self.apeak, self.aoff)
        assert self.aoff <= self.asize, ("arena overflow", name, self.aoff, self.asize)
        ap = self.arena[:, off:off + words]
        if dt != F32:
            ap = ap.bitcast(dt)[:, 0:n]
        if shape[0] != 128:
            ap = ap[0:shape[0], :]
        if len(shape) == 3:
            ap = ap.rearrange("p (a b) -> p a b", b=shape[2])
        elif len(shape) == 4:
            ap = ap.rearrange("p (a b c) -> p a b c", b=shape[2], c=shape[3])
        return ap

    def sb(self, name, shape, dt=F32):
        return self.st.enter_context(self.nc.sbuf_tensor("sb_" + name, shape, dt))

    def ps(self, name, shape, dt=F32):
        return self.st.enter_context(self.nc.psum_tensor("ps_" + name, shape, dt))

    def B(self, name):
        b = self.bufs.get(name)
        if b is None:
            b = self.bufs[name] = Buf(name)
        return b

    def dram(self, name, shape, dt, kind):
        if not hasattr(self, "_dram"):
            self._dram = {}
        if name not in self._dram:
            self._dram[name] = self.nc.dram_tensor(name, list(shape), dt, kind=kind).ap()
        return self._dram[name]


def v3(ap, d):
    return ap.rearrange("p (h d) -> p h d", d=d)


def bc_mid(ap2, n):
    return ap2.unsqueeze(1).broadcast_to([ap2.shape[0], n, ap2.shape[1]])


def bc_last(ap2, n):
    return ap2.unsqueeze(2).broadcast_to([ap2.shape[0], ap2.shape[1], n])


def emit_consts(C):
    S = C.S
    C.id32 = C.sb("id32", [128, 128])
    C.idb = C.sb("idb", [128, 128], BF16)
    C.ones32 = C.sb("ones32", [128, 128])
    d_id = C.dram("ident", [128, 128], F32, "ExternalInput")
    S.op(SP, lambda e: e.dma_start(out=C.id32[:], in_=d_id[:, :]), writes=[C.B("id32")], dma=True)
    S.op(DVE, lambda e: e.tensor_copy(C.idb[:], C.id32[:]), reads=[C.B("id32")], writes=[C.B("idb")])
    S.op(DVE, lambda e: e.memset(C.ones32[:], 1.0), writes=[C.B("ones32")])


def emit_adaln(C, lname, pbanks, rows_out, crep, bcrep, brow_ap=None, cgs=range(6), tag="", ring=None, bring=None):
    S = C.S
    d_c = C.dram(lname + "_cT", [128, 8], F32, "ExternalInput")
    d_w = C.dram(lname + "_ada_w", [1024, 3072], F32, "ExternalInput")
    d_b = C.dram(lname + "_ada_b", [1, 3072], F32, "ExternalInput")
    cT = C.asb(lname + tag + "cT", [128, 8])
    brow = brow_ap if brow_ap is not None else C.asb(lname + tag + "brow", [1, 1024])
    bcT = C.B(lname + tag + "cT")
    bbrows = [C.B(lname + "brow0"), C.B(lname + "brow1")]
    S.op(SP, lambda e: e.dma_start(out=cT[:], in_=d_c[:, :]), writes=[bcT], dma=True)
    S.op(ACT, lambda e: e.activation(out=cT[:], in_=cT[:], func=AF.Silu), reads=[bcT], writes=[bcT])
    S.op(DVE, lambda e: e.tensor_copy(crep, bc_last(cT[:], 128)), reads=[bcT], writes=[bcrep])
    if ring is None:
        ring = C.ring3
    i = 0
    for cg in cgs:
        pb, bpb = pbanks[cg % 2]
        bbrow = bbrows[cg % 2]
        bsl = slice((cg % 2) * 512, (cg % 2 + 1) * 512)
        S.op(SP, lambda e, cg=cg, bsl=bsl: e.dma_start(out=brow[0:1, bsl], in_=d_b[0:1, cg * 512:(cg + 1) * 512]), writes=[bbrow], dma=True)
        for k in range(8):
            sap, sbufs = ring[i % len(ring)]
            i += 1
            S.op(SP, lambda e, k=k, cg=cg, sap=sap: e.dma_start(
                out=sap, in_=d_w[k * 128:(k + 1) * 128, cg * 512:(cg + 1) * 512]),
                writes=sbufs, dma=True)
            if bring is not None:
                bap, bbufs = bring[(i - 1) % len(bring)]
                ceng = (ACT, DVE, ACT, POOL)[i % 4]
                if ceng == ACT:
                    S.op(ACT, lambda e, sap=sap, bap=bap: e.activation(out=bap, in_=sap, func=AF.Copy), reads=sbufs, writes=bbufs)
                else:
                    S.op(ceng, lambda e, sap=sap, bap=bap: e.tensor_copy(bap, sap), reads=sbufs, writes=bbufs)
                S.op(PE, lambda e, k=k, bap=bap, pb=pb: e.matmul(pb[:, 0:512], crep[:, k, :], bap, start=(k == 0), stop=False),
                     reads=[bcrep] + bbufs, partial=[bpb] if k else (), writes=[bpb] if k == 0 else ())
                continue
            S.op(PE, lambda e, k=k, sap=sap, pb=pb: e.matmul(pb[:, 0:512], crep[:, k, :], sap,
                                                            start=(k == 0), stop=False),
                 reads=[bcrep] + sbufs, partial=[bpb] if k else (), writes=[bpb] if k == 0 else ())
        S.op(PE, lambda e, cg=cg, pb=pb, bsl=bsl: e.matmul(pb[:, 0:512], C.ones32[0:1, :], brow[0:1, bsl],
                                                  start=False, stop=True),
             reads=[C.B("ones32"), bbrow], partial=[bpb])
        dst, bdst, add_one = rows_out[cg // 2]
        half = cg % 2
        if add_one:
            S.op(ACT, lambda e, dst=dst, half=half, pb=pb: e.activation(
                out=dst[:, half * 512:(half + 1) * 512], in_=pb[:, 0:512], func=AF.Identity, bias=C.one_col[:, 0:1], scale=1.0),
                reads=[bpb, C.B("one_col")], partial=[bdst])
        else:
            S.op(ACT, lambda e, dst=dst, half=half, pb=pb: e.activation(
                out=dst[:, half * 512:(half + 1) * 512], in_=pb[:, 0:512], func=AF.Copy),
                reads=[bpb], partial=[bdst])


def emit_adaln2(C, lname, pbanks, rows_out, crep, bcrep, brow, rows, ring4k, bring2k, tag=""):
    S = C.S
    d_c = C.dram(lname + "_cT", [128, 8], F32, "ExternalInput")
    d_w = C.dram(lname + "_ada_w", [1024, 3072], F32, "ExternalInput")
    d_b = C.dram(lname + "_ada_b", [1, 3072], F32, "ExternalInput")
    cT = C.asb(lname + tag + "cT", [128, 8])
    bcT = C.B(lname + tag + "cT")
    bbrow = C.B(lname + tag + "brow")
    S.op(SP, lambda e: e.dma_start(out=cT[:], in_=d_c[:, :]), writes=[bcT], dma=True)
    S.op(ACT, lambda e: e.activation(out=cT[:], in_=cT[:], func=AF.Silu), reads=[bcT], writes=[bcT])
    S.op(DVE, lambda e: e.tensor_copy(crep, bc_last(cT[:], 128)), reads=[bcT], writes=[bcrep])
    i = 0
    for r in rows:
        S.op(SP, lambda e, r=r: e.dma_start(out=brow[0:1, 0:1024], in_=d_b[0:1, r * 1024:(r + 1) * 1024]), writes=[bbrow], dma=True)
        for k in range(8):
            sap, sbufs = ring4k[i % len(ring4k)]
            bap, bbufs = bring2k[i % len(bring2k)]
            S.op(SP, lambda e, k=k, r=r, sap=sap: e.dma_start(out=sap, in_=d_w[k * 128:(k + 1) * 128, r * 1024:(r + 1) * 1024]),
                 writes=sbufs, dma=True)
            if i % 2:
                S.op(ACT, lambda e, sap=sap, bap=bap: e.activation(out=bap, in_=sap, func=AF.Copy), reads=sbufs, writes=bbufs)
            else:
                S.op(DVE, lambda e, sap=sap, bap=bap: e.tensor_copy(bap, sap), reads=sbufs, writes=bbufs)
            i += 1
            for hh in range(2):
                pb, bpb = pbanks[hh]
                S.op(PE, lambda e, k=k, bap=bap, pb=pb, hh=hh: e.matmul(pb[:, 0:512], crep[:, k, :], bap[:, hh * 512:(hh + 1) * 512],
                                                                      start=(k == 0), stop=False),
                     reads=[bcrep] + bbufs, partial=[bpb] if k else (), writes=[bpb] if k == 0 else ())
        dst, bdst, add_one = rows_out[r]
        for hh in range(2):
            pb, bpb = pbanks[hh]
            S.op(PE, lambda e, pb=pb, hh=hh: e.matmul(pb[:, 0:512], C.ones32[0:1, :], brow[0:1, hh * 512:(hh + 1) * 512], start=False, stop=True),
                 reads=[C.B("ones32"), bbrow], partial=[bpb])
            if add_one:
                S.op(ACT, lambda e, dst=dst, hh=hh, pb=pb: e.activation(out=dst[:, hh * 512:(hh + 1) * 512], in_=pb[:, 0:512], func=AF.Identity,
                                                                      bias=C.one_col[:, 0:1], scale=1.0),
                     reads=[bpb, C.B("one_col")], partial=[bdst])
            else:
                S.op(ACT, lambda e, dst=dst, hh=hh, pb=pb: e.activation(out=dst[:, hh * 512:(hh + 1) * 512], in_=pb[:, 0:512], func=AF.Copy),
                     reads=[bpb], partial=[bdst])


def emit_weight_bf16(C, dname, dst, bdst, nk, ncols, ring, scale_row=None, bscale=None, chunk=512):
    S = C.S
    d_w = C.dram(dname, [nk * 128, ncols], F32, "ExternalInput")
    i = 0
    engs = [DVE, ACT] if scale_row is None else [DVE]
    for k in range(nk):
        for c0 in range(0, ncols, chunk):
            n = min(chunk, ncols - c0)
            sap, sbufs = ring[i % len(ring)]
            S.op(SP, lambda e, k=k, c0=c0, n=n, sap=sap: e.dma_start(
                out=sap[:, 0:n], in_=d_w[k * 128:(k + 1) * 128, c0:c0 + n]), writes=sbufs, dma=True)
            eng = engs[i % len(engs)]
            if scale_row is not None:
                S.op(eng, lambda e, k=k, c0=c0, n=n, sap=sap: e.tensor_tensor(
                    dst[:, k, c0:c0 + n], sap[:, 0:n], scale_row[:, c0:c0 + n], ALU.mult),
                    reads=sbufs + [bscale], partial=[bdst])
            elif eng == ACT:
                S.op(ACT, lambda e, k=k, c0=c0, n=n, sap=sap: e.activation(
                    out=dst[:, k, c0:c0 + n], in_=sap[:, 0:n], func=AF.Copy), reads=sbufs, partial=[bdst])
            else:
                S.op(eng, lambda e, k=k, c0=c0, n=n, sap=sap: e.tensor_copy(
                    dst[:, k, c0:c0 + n], sap[:, 0:n]), reads=sbufs, partial=[bdst])
            i += 1


def emit_ln_tile(C, pfx, zt, bz, xdst, bxdst, lng, lnb, brows, small=None):
    S = C.S
    if small is None:
        st6, mv, sd = C.ln_st6, C.ln_mv, C.ln_sd
        bst6, bmv, bsd = C.B("ln_st6"), C.B("ln_mv"), C.B("ln_sd")
    else:
        st6, mv, sd, sfx = small
        bst6, bmv, bsd = C.B("ln_st6" + sfx), C.B("ln_mv" + sfx), C.B("ln_sd" + sfx)
    for hh in range(2):
        S.op(DVE, lambda e, hh=hh: e.bn_stats(st6[:, hh * 6:(hh + 1) * 6], zt[:, hh * 512:(hh + 1) * 512]), reads=[bz],
             writes=[bst6] if hh == 0 else (), partial=[bst6] if hh else ())
    S.op(DVE, lambda e: e.bn_aggr(mv[:, 0:2], st6[:]), reads=[bst6], writes=[bmv])
    S.op(ACT, lambda e: e.activation(out=sd[:, 0:1], in_=mv[:, 1:2], func=AF.Sqrt, bias=C.eps_col[:, 0:1], scale=1.0),
         reads=[bmv, C.B("eps_col")], writes=[bsd])
    S.op(DVE, lambda e: e.reciprocal(sd[:, 1:2], sd[:, 0:1]), reads=[bsd], partial=[bsd])
    S.op(DVE, lambda e: e.scalar_tensor_tensor(sd[:, 2:3], mv[:, 0:1], -1.0, sd[:, 1:2], ALU.mult, ALU.mult),
         reads=[bmv, bsd], partial=[bsd])
    S.op(ACT, lambda e: e.activation(out=zt[:], in_=zt[:], func=AF.Identity, scale=sd[:, 1:2], bias=sd[:, 2:3]),
         reads=[bz, bsd], writes=[bz])
    S.op(POOL, lambda e: e.tensor_tensor(zt[:], zt[:], lng[:], ALU.mult), reads=[bz, brows], writes=[bz])
    S.op(POOL, lambda e: e.tensor_tensor(xdst, zt[:], lnb[:], ALU.add), reads=[bz, brows], writes=[bxdst])


def emit_l0(C, x_dram, out_dram):
    S, nc = C.S, C.nc
    B = C.B
    xres = C.xres
    d_pos = C.dram("l0_pos", [128, 17], I32, "ExternalInput")
    d_sink = C.dram("l0_sink", [128, 8], F32, "ExternalInput")
    d_sgg = C.dram("l0_sg_ln_g", [128, 512], F32, "ExternalInput")
    d_sgb = C.dram("l0_sg_ln_b", [128, 512], F32, "ExternalInput")
    d_sgw = C.dram("l0_sg_wT", [128, 1024], F32, "ExternalInput")
    d_sgbias = C.dram("l0_sg_b", [128, 8], F32, "ExternalInput")
    d_lng = C.dram("l0_ln_g", [128, 1024], F32, "ExternalInput")
    d_lnb = C.dram("l0_ln_b", [128, 1024], F32, "ExternalInput")
    d_mask = C.dram("l0_mask", [128, 768], F32, "ExternalInput")
    w_in = C.asb("l0_w_in", [128, 8, 2816], BF16)
    w_out = C.asb("l0_w_out", [128, 8, 1024], BF16)
    sgw = C.asb("l0_sgw", [128, 8, 128], BF16)
    modrow = C.asb("l0_modrow", [128, 2, 1024])
    lnrows = C.asb("l0_lnrows", [128, 2, 1024])
    sgrows = C.asb("l0_sgrows", [128, 2, 512])
    sgbias = C.asb("l0_sgbias", [128, 8])
    nsink = C.asb("l0_nsink", [128, 8])
    sink = C.asb("l0_sink", [128, 8])
    mask = C.asb("l0_mask", [128, 768], BF16)
    kT = C.asb("l0_kT", [128, 18, 128], BF16)
    V = C.asb("l0_V", [128, 18, 128], BF16)
    posi = C.asb("l0_posi", [128, 17], I32)
    posf = C.asb("l0_posf", [128, 17])
    invf = C.asb("l0_invf", [128, 8])
    ang = C.asb("l0_ang", [128, 17, 8])
    angk = C.asb("l0_angk", [128, 17, 8])
    angi = C.asb("l0_angi", [128, 17, 8], I32)
    C2 = C.asb("l0_C2", [128, 17, 16])
    S2 = C.asb("l0_S2", [128, 17, 16])
    h32 = C.asb("l0_h32", [128, 1024])
    mask32 = h32[:, 0:768]
    hbf = C.asb("l0_hbf", [128, 1024], BF16)
    hT = C.asb("l0_hT", [128, 1024], BF16)
    qkr = C.asb("l0_qkr", [128, 640], BF16)
    qT = C.asb("l0_qT", [128, 2, 512], BF16)
    su = C.asb("l0_su", [128, 512], BF16)
    vn = C.asb("l0_vn", [128, 512])
    vnb = C.asb("l0_vnb", [128, 512], BF16)
    st8 = C.asb("l0_st8", [128, 8, 8])
    sgl = C.asb("l0_sgl", [128, 2, 1024], BF16)
    y = C.asb("l0_y", [128, 2, 1024], BF16)
    yT = C.asb("l0_yT", [128, 1024], BF16)
    P = C.asb("l0_P", [128, 2, 384], BF16)
    PT = C.asb("l0_PT", [128, 2, 384], BF16)
    att = C.asb("l0_att", [128, 6, 8])
    z = C.asb("l0_z", [128, 1024])
    gaterow = z
    yat = vn
    rot = C.asb("l0_rot", [128, 2, 160])
    sq = z[:, 512:1024]
    ptr = C.ps_ptr
    pb = C.ps_f32[0:2]
    sbk = C.ps_f32[2:4]
    ob = C.ps_f32[4]
    svb = C.ps_f32[5]
    bptr = [B("ptr0"), B("ptr1")]
    bpb = [B("pb0"), B("pb1")]
    bsbk = [B("pb2"), B("pb3")]
    cnt = {"ptr": 0, "pb": 0}
    for bn in ("ptr0", "ptr1", "pb0", "pb1", "pb2", "pb3", "ob", "svb"):
        B(bn).excl = True

    def next_ptr():
        i = cnt["ptr"] % 2
        cnt["ptr"] += 1
        return ptr[i], bptr[i]

    def next_pb():
        i = cnt["pb"] % 2
        cnt["pb"] += 1
        return pb[i], bpb[i]

    w4k = (C.wstage[:].rearrange("p a b -> p (a b)")[:, 0:1024], [C.bwstage[0], C.bwstage[1]])
    ring0 = [w4k,
             (y.rearrange("p a b -> p (a b)").bitcast(F32), [B("y_a0"), B("y_s0"), B("y_a1"), B("y_s1")]),
             (sgl.rearrange("p a b -> p (a b)").bitcast(F32), [B("sgl_a0"), B("sgl_s0"), B("sgl_a1"), B("sgl_s1")])]
    bring0 = [(hT[:], [B("hT")]), (hbf[:], [B("hbf")]), (yT[:], [B("yT")]), (qT.rearrange("p a b -> p (a b)"), [B("qT0"), B("qT1")])]
    S.op(SP, lambda e: e.dma_start(out=xres[:, 0, :], in_=x_dram[0:128, :]), writes=[B("x0")], dma=True)
    S.op(SP, lambda e: e.dma_start(out=posi[:], in_=d_pos[:, :]), writes=[B("posi")], dma=True)
    S.op(SP, lambda e: e.dma_start(out=sink[:], in_=d_sink[:, :]), writes=[B("sink")], dma=True)
    S.op(SP, lambda e: e.dma_start(out=mask32, in_=d_mask[:, :]), writes=[B("h32")], dma=True)
    S.op(POOL, lambda e: e.tensor_copy(mask[:], mask32), reads=[B("h32")], writes=[B("mask")])
    S.op(SP, lambda e: e.dma_start(out=sgbias[:], in_=d_sgbias[:, :]), writes=[B("sgbias")], dma=True)
    crep_b = h32[:, 0:512].bitcast(BF16).rearrange("p (k m) -> p k m", m=128)
    brow0 = C.asb("l0_brow", [1, 1024])
    emit_adaln2(C, "l0", [(pb[0], bpb[0]), (pb[1], bpb[1])],
                [(modrow[:, 0, :], B("l0shift"), False), (modrow[:, 1, :], B("l0scale"), True), (gaterow[:], B("z"), False)],
                crep_b, B("h32"), brow0, (0, 1, 2), ring0, bring0)
    emit_weight_bf16(C, "l0_w_in", w_in, B("l0w_in"), 8, 2816, ring0, chunk=1024)
    emit_weight_bf16(C, "l0_w_out", w_out, B("l0w_out"), 8, 1024, ring0, scale_row=gaterow, bscale=B("z"), chunk=1024)
    S.op(SP, lambda e: e.dma_start(out=sgrows[:, 0, :], in_=d_sgg[:, :]), partial=[B("sgrows")], dma=True)
    S.op(SP, lambda e: e.dma_start(out=sgrows[:, 1, :], in_=d_sgb[:, :]), partial=[B("sgrows")], dma=True)
    S.op(SP, lambda e: e.dma_start(out=lnrows[:, 0, :], in_=d_lng[:, :]), partial=[B("l0lnrows")], dma=True)
    S.op(SP, lambda e: e.dma_start(out=lnrows[:, 1, :], in_=d_lnb[:, :]), partial=[B("l0lnrows")], dma=True)
    S.op(SP, lambda e: e.dma_start(out=xres[:, 1, :], in_=x_dram[128:256, :]), writes=[B("x1")], dma=True)
    for hh in range(2):
        slot = hh
        S.op(SP, lambda e, hh=hh, slot=slot: e.dma_start(out=C.wstage[:, slot, 0:512], in_=d_sgw[:, hh * 512:(hh + 1) * 512]),
             writes=[C.bwstage[slot]], dma=True)
        S.op(POOL, lambda e, hh=hh, slot=slot: e.tensor_copy(sgw[:, hh * 4:(hh + 1) * 4, :].rearrange("p g q -> p (g q)"), C.wstage[:, slot, 0:512]),
             reads=[C.bwstage[slot]], partial=[B("sgw")])
    S.op(POOL, lambda e: e.tensor_scalar(nsink[:], sink[:], -1.0, None, ALU.mult), reads=[B("sink")], writes=[B("nsink")])
    S.op(POOL, lambda e: e.memset(kT[:, 0, :], 0.0), partial=[B("kT0")])
    S.op(POOL, lambda e: e.memset(V[:, 0, :], 0.0), partial=[B("V0")])

    for f in range(8):
        S.op(POOL, lambda e, f=f: e.memset(invf[:, f:f + 1], INV_FREQ[f]), partial=[B("invf")])
    S.op(DVE, lambda e: e.tensor_copy(posf[:], posi[:]), reads=[B("posi")], writes=[B("posf")])
    S.op(DVE, lambda e: e.tensor_tensor(ang[:], bc_last(posf[:], 8), bc_mid(invf[:], 17), ALU.mult),
         reads=[B("posf"), B("invf")], writes=[B("ang")])
    TWO_PI = float(2 * np.pi)

    def sin_into(dst_ap, shift, negate, tag):
        bk, bi_, br = B("angk"), B("angi"), B("angr")
        S.op(DVE, lambda e: e.tensor_scalar(angk[:], ang[:], shift, 1.0 / TWO_PI, ALU.add, ALU.mult), reads=[B("ang")], writes=[bk])
        S.op(DVE, lambda e: e.tensor_copy(angi[:], angk[:]), reads=[bk], writes=[bi_])
        S.op(DVE, lambda e: e.tensor_copy(angk[:], angi[:]), reads=[bi_], writes=[bk])
        S.op(DVE, lambda e: e.scalar_tensor_tensor(angk[:], angk[:], -TWO_PI, ang[:], ALU.mult, ALU.add), reads=[bk, B("ang")], writes=[bk])
        S.op(DVE, lambda e: e.tensor_scalar(angk[:], angk[:], shift, float(np.pi), ALU.add, ALU.min), reads=[bk], writes=[bk])
        S.op(DVE, lambda e: e.tensor_scalar(angk[:], angk[:], float(-np.pi), None, ALU.max), reads=[bk], writes=[bk])
        S.op(ACT, lambda e: e.activation(out=dst_ap, in_=angk[:], func=AF.Sin, scale=(-1.0 if negate else 1.0)),
             reads=[bk], partial=[B(tag)])

    sin_into(C2[:, :, 0:8], float(np.pi / 2), False, "C2")
    sin_into(C2[:, :, 8:16], float(np.pi / 2), False, "C2")
    sin_into(S2[:, :, 0:8], 0.0, True, "S2")
    sin_into(S2[:, :, 8:16], 0.0, False, "S2")

    GA, GB, GC, GD, GE, GF = (0, 512), (512, 256), (768, 512), (1280, 512), (1792, 512), (2304, 512)

    def rotary(src_bank, nh, dst3, t, slot, bdst):
        s3 = v3(src_bank[:, 0:nh * 64], 64)
        ta = v3(rot[:, 0, 0:nh * 16], 16)
        tb = v3(rot[:, 1, 0:nh * 16], 16)
        br_ = B("rot")
        S.op(DVE, lambda e: e.tensor_tensor(ta, s3[:, :, 0:16], bc_mid(C2[:, t, :], nh), ALU.mult),
             reads=[slot, B("C2")], writes=[br_])
        S.op(DVE, lambda e: e.tensor_tensor(tb[:, :, 0:8], s3[:, :, 8:16], bc_mid(S2[:, t, 0:8], nh), ALU.mult),
             reads=[slot, B("S2")], partial=[br_])
        S.op(DVE, lambda e: e.tensor_tensor(tb[:, :, 8:16], s3[:, :, 0:8], bc_mid(S2[:, t, 8:16], nh), ALU.mult),
             reads=[slot, B("S2")], partial=[br_])
        S.op(DVE, lambda e: e.tensor_tensor(dst3[:, :, 0:16], ta, tb, ALU.add), reads=[br_], partial=[bdst])

    def modulate(t):
        if t == NT:
            S.op(SP, lambda e: e.dma_start(out=h32[:], in_=x_dram[2048:2176, :]), writes=[B("h32")], dma=True)
            xsrc, bxs = h32[:], B("h32")
        else:
            xsrc, bxs = xres[:, t, :], B("x%d" % t)
        S.op(POOL, lambda e: e.tensor_tensor(h32[:], xsrc, modrow[:, 1, :], ALU.mult), reads=[bxs, B("l0scale")], writes=[B("h32")])
        S.op(POOL, lambda e: e.tensor_tensor(hbf[:], h32[:], modrow[:, 0, :], ALU.add), reads=[B("h32"), B("l0shift")], writes=[B("hbf")])

    def front(t):
        par = t % 2
        last = (t == NT)
        pt, bpt = next_ptr()
        for k in range(8):
            S.op(PE, lambda e, k=k, pt=pt: e.transpose(pt[:, k * 128:(k + 1) * 128], hbf[:, k * 128:(k + 1) * 128], C.idb[:]),
                 reads=[B("hbf"), B("idb")], writes=[bpt] if k == 0 else (), partial=[bpt] if k else ())
        if t < NT:
            modulate(t + 1)
        S.op(ACT, lambda e, pt=pt: e.activation(out=hT[:], in_=pt[:], func=AF.Copy), reads=[bpt], writes=[B("hT")])
        yield "a"

        def proj(g):
            c0, n = g
            bank, bbank = next_pb()
            for k in range(8):
                S.op(PE, lambda e, k=k: e.matmul(bank[:, 0:n], hT[:, k * 128:(k + 1) * 128], w_in[:, k, c0:c0 + n],
                                                 start=(k == 0), stop=(k == 7)),
                     reads=[B("hT"), B("l0w_in")], writes=[bbank] if k == 0 else (), partial=[bbank] if k else ())
            return bank, bbank

        if not last:
            bank, bb = proj(GA)
            S.op(ACT, lambda e, bank=bank: e.activation(out=qkr[:, 0:512], in_=bank[:, 0:512], func=AF.Copy), reads=[bb], writes=[B("qkr_q")])
            rotary(bank, 8, v3(qkr[:, 0:512], 64), t, bb, B("qkr_q"))
        bank, bb = proj(GB)
        S.op(ACT, lambda e, bank=bank: e.activation(out=qkr[:, 512:640], in_=bank[:, 0:128], func=AF.Copy), reads=[bb], writes=[B("qkr_k")])
        S.op(ACT, lambda e, bank=bank: e.activation(out=V[:, t + 1, :], in_=bank[:, 128:256], func=AF.Copy), reads=[bb], writes=[B("V%d" % (t + 1))])
        rotary(bank, 2, v3(qkr[:, 512:640], 64), t, bb, B("qkr_k"))
        pt, bpt = next_ptr()
        first = True
        if not last:
            for j in range(4):
                S.op(PE, lambda e, j=j, pt=pt: e.transpose(pt[:, j * 128:(j + 1) * 128], qkr[:, j * 128:(j + 1) * 128], C.idb[:]),
                     reads=[B("qkr_q"), B("idb")], writes=[bpt] if first else (), partial=() if first else [bpt])
                first = False
        S.op(PE, lambda e, pt=pt: e.transpose(pt[:, 512:640], qkr[:, 512:640], C.idb[:]),
             reads=[B("qkr_k"), B("idb")], writes=[bpt] if first else (), partial=() if first else [bpt])
        if not last:
            S.op(DVE, lambda e, pt=pt: e.tensor_copy(qT[:, par, :], pt[:, 0:512]), reads=[bpt], writes=[B("qT%d" % par)])
        S.op(DVE, lambda e, pt=pt: e.tensor_copy(kT[:, t + 1, :], pt[:, 512:640]), reads=[bpt], writes=[B("kT%d" % (t + 1))])
        if last:
            return
        bank, bb = proj(GC)
        S.op(ACT, lambda e, bank=bank: e.activation(out=su[:], in_=bank[:, 0:512], func=AF.Copy), reads=[bb], writes=[B("su")])
        bank, bb = proj(GD)
        bst = B("st8")
        S.op(ACT, lambda e, bank=bank: e.activation(out=sq, in_=bank[:, 0:512], func=AF.Square), reads=[bb], writes=[B("z")])
        S.op(DVE, lambda e, bank=bank: e.tensor_reduce(st8[:, 0, :], v3(bank[:, 0:512], 64), AX.X, ALU.add), reads=[bb], writes=[bst])
        S.op(DVE, lambda e: e.tensor_reduce(st8[:, 1, :], v3(sq, 64), AX.X, ALU.add), reads=[B("z")], partial=[bst])
        S.op(DVE, lambda e: e.tensor_scalar(st8[:, 2, :], st8[:, 0, :], 1.0 / 64, None, ALU.mult), reads=[bst], partial=[bst])
        S.op(DVE, lambda e: e.tensor_tensor(st8[:, 3, :], st8[:, 2, :], st8[:, 2, :], ALU.mult), reads=[bst], partial=[bst])
        S.op(DVE, lambda e: e.scalar_tensor_tensor(st8[:, 4, :], st8[:, 1, :], 1.0 / 64, st8[:, 3, :], ALU.mult, ALU.subtract),
             reads=[bst], partial=[bst])
        S.op(ACT, lambda e: e.activation(out=st8[:, 5, :], in_=st8[:, 4, :], func=AF.Sqrt, bias=C.eps_col[:, 0:1], scale=1.0),
             reads=[bst, B("eps_col")], partial=[bst])
        S.op(DVE, lambda e: e.reciprocal(st8[:, 6, :], st8[:, 5, :]), reads=[bst], partial=[bst])
        S.op(DVE, lambda e, bank=bank: e.tensor_tensor(v3(vn[:], 64), v3(bank[:, 0:512], 64), bc_last(st8[:, 2, :], 64), ALU.subtract),
             reads=[bb, bst], writes=[B("vn")])
        S.op(DVE, lambda e: e.tensor_tensor(v3(vn[:], 64), v3(vn[:], 64), bc_last(st8[:, 6, :], 64), ALU.mult),
             reads=[B("vn"), bst], writes=[B("vn")])
        S.op(POOL, lambda e: e.tensor_tensor(vn[:], vn[:], sgrows[:, 0, :], ALU.mult), reads=[B("vn"), B("sgrows")], writes=[B("vn")])
        S.op(POOL, lambda e: e.tensor_tensor(vnb[:], vn[:], sgrows[:, 1, :], ALU.add), reads=[B("vn"), B("sgrows")], writes=[B("vnb")])
        yield "b1"
        bankE, bbE = proj(GE)
        bankF, bbF = proj(GF)
        yield "gm"
        S.op(ACT, lambda e, bank=bankE: e.activation(out=sgl[:, par, 0:512], in_=bank[:, 0:512], func=AF.Silu), reads=[bbE], writes=[B("sgl_a%d" % par)])
        S.op(ACT, lambda e, bank=bankF: e.activation(out=sgl[:, par, 512:1024], in_=bank[:, 0:512], func=AF.Silu), reads=[bbF], writes=[B("sgl_s%d" % par)])
        for g in range(8):
            S.op(PE, lambda e, g=g: e.matmul(svb[:, g * 64:(g + 1) * 64], sgw[:, g, :], vnb[:, g * 64:(g + 1) * 64], start=True, stop=True),
                 reads=[B("sgw"), B("vnb")], writes=[B("svb")] if g == 0 else (), partial=[B("svb")] if g else ())
        S.op(POOL, lambda e: e.tensor_tensor(su[:], su[:], sgl[:, par, 512:1024], ALU.mult), reads=[B("su"), B("sgl_s%d" % par)], writes=[B("su")])
        S.op(DVE, lambda e: e.tensor_tensor(v3(sq, 64), v3(svb[:, 0:512], 64), bc_last(sgbias[:], 64), ALU.add),
             reads=[B("svb"), B("sgbias")], writes=[B("z")])
        S.op(DVE, lambda e: e.tensor_tensor(y[:, par, 512:1024], sq, su[:], ALU.mult), reads=[B("z"), B("su")], writes=[B("y_s%d" % par)])

    def attn(t, mid=None):
        par = t % 2
        mrow = 0 if t == 0 else 1
        batt = B("att")

        def stage1(hp):
            j, half = hp // 2, hp % 2
            sb_, bsb = sbk[hp % 2], bsbk[hp % 2]
            S.op(PE, lambda e: e.matmul(sb_[:, 0:384], qT[half * 64:(half + 1) * 64, par, j * 128:(j + 1) * 128],
                                        kT[half * 64:(half + 1) * 64, t:t + 3, :].rearrange("p a b -> p (a b)"), start=True, stop=False),
                 reads=[B("qT%d" % par), B("kT%d" % t), B("kT%d" % (t + 1)), B("kT%d" % (t + 2))], writes=[bsb])
            S.op(PE, lambda e: e.matmul(sb_[:, 0:384], C.idb[:], mask[:, mrow * 384:(mrow + 1) * 384], start=False, stop=True),
                 reads=[B("idb"), B("mask")], partial=[bsb])
            S.op(DVE, lambda e: e.tensor_reduce(att[:, 0, hp:hp + 1], sb_[:, 0:384], AX.X, ALU.max), reads=[bsb], writes=[B("amx%d" % hp)])
            S.op(DVE, lambda e: e.tensor_scalar(att[:, 1, hp:hp + 1], att[:, 0, hp:hp + 1], C.m0125_col[:, 0:1], nsink[:, hp:hp + 1], ALU.mult, ALU.min),
                 reads=[B("amx%d" % hp), B("nsink"), B("m0125_col")], writes=[B("anm%d" % hp)])
            S.op(ACT, lambda e: e.activation(out=P[:, hp % 2, :], in_=sb_[:, 0:384], func=AF.Exp, bias=att[:, 1, hp:hp + 1], scale=0.125,
                                             accum_out=att[:, 2, hp:hp + 1]),
                 reads=[bsb, B("anm%d" % hp), B("ars")], writes=[B("P%d" % (hp % 2)), B("ars%d" % hp)])

        def stage2a(hp):
            pt, bpt = next_ptr()
            for kt in range(3):
                S.op(PE, lambda e, kt=kt: e.transpose(pt[:, kt * 128:(kt + 1) * 128], P[:, hp % 2, kt * 128:(kt + 1) * 128], C.idb[:]),
                     reads=[B("P%d" % (hp % 2)), B("idb")], writes=[bpt] if kt == 0 else (), partial=[bpt] if kt else ())
            S.op(DVE, lambda e: e.tensor_copy(PT[:, hp % 2, :], pt[:, 0:384]), reads=[bpt], writes=[B("PT%d" % (hp % 2))])

        def stage2b(hp):
            half = hp % 2
            for kt in range(3):
                S.op(PE, lambda e, kt=kt: e.matmul(ob[:, hp * 64:(hp + 1) * 64], PT[:, hp % 2, kt * 128:(kt + 1) * 128],
                                                   V[:, t + kt, half * 64:(half + 1) * 64], start=(kt == 0), stop=(kt == 2)),
                     reads=[B("PT%d" % (hp % 2)), B("V%d" % (t + kt))],
                     writes=[B("ob")] if (hp == 0 and kt == 0) else (), partial=() if (hp == 0 and kt == 0) else [B("ob")])

        S.op(DVE, lambda e: e.memset(att[:, 2, :], 0.0), writes=[B("ars")] + [B("ars%d" % h) for h in range(8)])
        stage1(0)
        stage1(1)
        if mid is not None:
            mid()
        for hp in range(8):
            stage2a(hp)
            if hp + 2 < 8:
                stage1(hp + 2)
            stage2b(hp)
        anm = [B("anm%d" % h) for h in range(8)]
        ars = [B("ars%d" % h) for h in range(8)]
        S.op(DVE, lambda e: e.tensor_tensor(att[:, 3, :], att[:, 1, :], sink[:], ALU.add), reads=anm + [B("sink")], writes=[batt])
        S.op(ACT, lambda e: e.activation(out=att[:, 3, :], in_=att[:, 3, :], func=AF.Exp), reads=[batt], writes=[batt])
        S.op(DVE, lambda e: e.tensor_tensor(att[:, 4, :], att[:, 3, :], att[:, 2, :], ALU.add), reads=[batt] + ars, writes=[batt])
        S.op(DVE, lambda e: e.reciprocal(att[:, 5, :], att[:, 4, :]), reads=[batt], writes=[batt])
        S.op(DVE, lambda e: e.tensor_tensor(v3(yat[:], 64), v3(ob[:, 0:512], 64), bc_last(att[:, 5, :], 64), ALU.mult),
             reads=[B("ob"), batt], writes=[B("vn")])
        S.op(DVE, lambda e: e.tensor_tensor(y[:, par, 0:512], yat[:], sgl[:, par, 0:512], ALU.mult),
             reads=[B("vn"), B("sgl_a%d" % par)], writes=[B("y_a%d" % par)])

    def back(t):
        par = t % 2
        pt, bpt = next_ptr()
        for k in range(8):
            S.op(PE, lambda e, k=k: e.transpose(pt[:, k * 128:(k + 1) * 128], y[:, par, k * 128:(k + 1) * 128], C.idb[:]),
                 reads=[B("y_a%d" % par), B("y_s%d" % par), B("idb")], writes=[bpt] if k == 0 else (), partial=[bpt] if k else ())
        S.op(ACT, lambda e: e.activation(out=yT[:], in_=pt[:], func=AF.Copy), reads=[bpt], writes=[B("yT")])
        for cg in range(2):
            bank, bb = next_pb()
            for k in range(8):
                S.op(PE, lambda e, k=k, cg=cg, bank=bank: e.matmul(bank[:, 0:512], yT[:, k * 128:(k + 1) * 128], w_out[:, k, cg * 512:(cg + 1) * 512],
                                                        start=(k == 0), stop=(k == 7)),
                     reads=[B("yT"), B("l0w_out")], writes=[bb] if k == 0 else (), partial=[bb] if k else ())
            S.op(DVE, lambda e, cg=cg, bank=bank: e.scalar_tensor_tensor(z[:, cg * 512:(cg + 1) * 512], xres[:, t, cg * 512:(cg + 1) * 512], ALPHA,
                                                              bank[:, 0:512], ALU.mult, ALU.add),
                 reads=[B("x%d" % t), bb], writes=[B("z")] if cg == 0 else (), partial=[B("z")] if cg else ())
        yield "a"
        emit_ln_tile(C, "l0", z, B("z"), xres[:, t, :], B("x%d" % t), lnrows[:, 0, :], lnrows[:, 1, :], B("l0lnrows"))
        if out_dram is not None:
            S.op(SP, lambda e: e.dma_start(out=out_dram[t * 128:(t + 1) * 128, :], in_=xres[:, t, :]), reads=[B("x%d" % t)], dma=True)

    import os
    dbg = os.environ.get("L0_DBG", "")
    if dbg == "a":
        for t in range(NT):
            S.op(SP, lambda e, t=t: e.dma_start(out=out_dram[t * 128:(t + 1) * 128, :], in_=xres[:, t, :]), reads=[B("x%d" % t)], dma=True)
        return
    def step(g):
        try:
            next(g)
        except StopIteration:
            pass

    def finish(g):
        for _ in g:
            pass

    modulate(0)
    pend_back = None
    for t in range(NT + 1):
        if t + 2 < NT:
            S.op(SP, lambda e, t=t: e.dma_start(out=xres[:, t + 2, :], in_=x_dram[(t + 2) * 128:(t + 3) * 128, :]),
                 writes=[B("x%d" % (t + 2))], dma=True)
        f = front(t)
        step(f)
        if pend_back is not None:
            finish(pend_back)
            pend_back = None
        step(f)
        if t >= 1:
            attn(t - 1, mid=lambda f=f: step(f))
            finish(f)
            b = back(t - 1)
            step(b)
            pend_back = b
        else:
            finish(f)
    if pend_back is not None:
        finish(pend_back)
    if C.send1 is not None:
        S.op(SP, lambda e: e.dma_start(out=C.send1[0:1, :], in_=xres[127:128, 15, :]), reads=[B("x15")], partial=[B("send1")], dma=True)
        S.op(SP, lambda e: e.dma_start(out=C.send1[1:2, :], in_=xres[126:127, 15, :]), reads=[B("x15")], partial=[B("send1")], dma=True)
        S.op(POOL, lambda e: e.collective_compute("AllGather", ALU.bypass, replica_groups=[[0, 1], [2, 3], [4, 5], [6, 7]],
                                                 ins=[C.send1], outs=[C.recv1]), reads=[B("send1")], writes=[B("recv1")], cc=True)
    if dbg in ("b", "c"):
        for t in range(NT):
            S.op(SP, lambda e, t=t: e.dma_start(out=out_dram[t * 128:(t + 1) * 128, :], in_=xres[:, t, :]), reads=[B("x%d" % t)], dma=True)


def alloc_common(C):
    C.xres = C.sb("xres", [128, NT, 1024])
    C.wstage = C.sb("wstage", [128, 3, 512])
    C.ps_ptr = [C.ps("ptr%d" % i, [128, 1024], BF16) for i in range(2)]
    C.ps_f32 = [C.ps("pf%d" % i, [128, 512]) for i in range(6)]
    if not hasattr(C, "send1"):
        C.send1 = None
    C.bwstage = [C.B("wstage%d" % i) for i in range(3)]
    C.ring3 = [(C.wstage[:, i, :], [C.bwstage[i]]) for i in range(3)]
    C.ln_st6 = C.sb("ln_st6", [128, 12])
    C.m0125_col = C.sb("m0125_col", [128, 1])
    C.S.op(DVE, lambda e: e.memset(C.m0125_col[:], -0.125), writes=[C.B("m0125_col")])
    C.ln_mv = C.sb("ln_mv", [128, 2])
    C.ln_sd = C.sb("ln_sd", [128, 4])
    C.eps_col = C.sb("eps_col", [128, 1])
    C.one_col = C.sb("one_col", [128, 1])
    C.S.op(DVE, lambda e: e.memset(C.eps_col[:], LN_EPS), writes=[C.B("eps_col")])
    C.S.op(DVE, lambda e: e.memset(C.one_col[:], 1.0), writes=[C.B("one_col")])


def build_l0():
    nc = bass.Bass("TRN2", target_bir_lowering=False)
    with contextlib.ExitStack() as st:
        C = Ctx(nc, st)
        x_d = C.dram("x", [2176, 1024], F32, "ExternalInput")
        o_d = C.dram("out", [2048, 1024], F32, "ExternalOutput")
        alloc_common(C)
        emit_consts(C)
        emit_l0(C, x_d, o_d)
        print("sbuf remaining after l0 alloc:", nc.sbuf_bytes_remaining)
        C.S.emit()
    return nc


def rev_ap(ap2):
    n = ap2.shape[1]
    return bass.AP(ap2.tensor, ap2.offset + (n - 1), [list(ap2.ap[0]), [-1, n]])


def emit_l1(C, out_dram):
    S, nc = C.S, C.nc
    B = C.B
    xres = C.xres
    S.barrier()
    C.areset()
    RG = [[0, 1], [2, 3], [4, 5], [6, 7]]
    d_win = C.dram("l1_w_in", [1024, 2048], F32, "ExternalInput")
    d_w5 = C.dram("l1_w5", [128, 40], F32, "ExternalInput")
    d_cb = C.dram("l1_cb", [128, 8], F32, "ExternalInput")
    d_wg = C.dram("l1_wg", [128, 4096], F32, "ExternalInput")
    d_bg = C.dram("l1_bg", [128, 32], F32, "ExternalInput")
    d_lam = C.dram("l1_lam", [128, 16], F32, "ExternalInput")
    d_lng = C.dram("l1_ln_g", [128, 1024], F32, "ExternalInput")
    d_lnb = C.dram("l1_ln_b", [128, 1024], F32, "ExternalInput")
    d_sel = C.dram("sel", [128, 2], F32, "ExternalInput")
    hT = C.asb("l1_hT", [128, 8, 2056], BF16)
    w_out = hT.rearrange("p k t -> p (k t)")[:, 0:8192].rearrange("p (k c) -> p k c", c=1024)
    yT = C.asb("l1_yT", [128, 8, 2048], BF16)
    xc = C.asb("l1_xc", [128, 8, 2048], BF16)
    wg = C.asb("l1_wg", [128, 4, 8, 128], BF16)
    wt = C.asb("l1_wt", [128, 8, 128], BF16)
    xr = C.asb("l1_xr", [128, 2052], BF16)
    sg = xr[:, 0:2048]
    lnrows = xc[:, 4:6, :].rearrange("p a b -> p (a b)").bitcast(F32).rearrange("p (a b) -> p a b", a=2)
    dg = C.asb("l1_dg", [128, 10, 128], BF16)
    modrow = xc[:, 0:2, :].rearrange("p a b -> p (a b)").bitcast(F32).rearrange("p (a b) -> p a b", a=2)
    wk = C.asb("l1_wk", [128, 5120])
    WA = wk[:, 0:2048]
    WTI = wk[:, 2048:3072].bitcast(BF16)
    WVB = wk[:, 3072:4096].bitcast(BF16)
    WH = wk[:, 4096:5120].rearrange("p (a b) -> p a b", a=2)
    h32 = wk[:, 0:1024]
    hbf = wk[:, 1024:1536].bitcast(BF16)
    hb2 = wk[:, 2048:3072]
    z = h32
    gaterow = wk[:, 1024:2048]
    crep = wk[:, 0:1024].rearrange("p (k m) -> p k m", m=128)
    brow = wk[0:1, 4096:5120]
    hbg = C.asb("l1_hbg", [128, 32])
    hn = C.asb("l1_hn", [128, 16])
    w5 = C.asb("l1_w5", [128, 40])
    cb = C.asb("l1_cb", [128, 8])
    bg = C.asb("l1_bg", [128, 32])
    lam = C.asb("l1_lam", [128, 16])
    nsp8 = C.asb("l1_nsp8", [128, 16])
    sel = C.asb("l1_sel", [128, 2])
    carry = C.asb("l1_carry", [128, 8])
    cga = C.asb("l1_cga", [128, 8])
    cgb = C.asb("l1_cgb", [128, 8])
    stA = C.asb("l1_stA", [128, 8])
    stB = C.asb("l1_stB", [128, 8])
    ptr = C.ps_ptr
    pb = C.ps_f32[0:6]
    bptr = [B("ptr0"), B("ptr1")]
    bpb = [B("pb%d" % i) for i in range(6)]
    for b_ in bptr + bpb:
        b_.excl = True
    cnt = {"ptr": 0, "pb": 0}

    def next_ptr():
        i = cnt["ptr"] % 2
        cnt["ptr"] += 1
        return ptr[i], bptr[i]

    pb8 = list(pb) + [ptr[0][:].bitcast(F32), ptr[1][:].bitcast(F32)]
    bpb8 = list(bpb) + bptr

    def next_pb():
        i = cnt["pb"] % 8
        cnt["pb"] += 1
        return pb8[i], bpb8[i]

    w4k = (C.wstage[:].rearrange("p a b -> p (a b)")[:, 0:1024], [C.bwstage[0], C.bwstage[1]])
    ring1 = [w4k] + [(yT[:, k, :].bitcast(F32), [B("yT%d" % k)]) for k in range(8)]
    bring1 = [(xc[:, 3, 0:1024], [B("xc3")]), (xc[:, 3, 1024:2048], [B("xc3")]), (xc[:, 2, 0:1024], [B("xc2")]), (xc[:, 2, 1024:2048], [B("xc2")])]
    crep_b = wk[:, 0:512].bitcast(BF16).rearrange("p (k m) -> p k m", m=128)
    for nm, dst, src in (("w5", w5, d_w5), ("cb", cb, d_cb), ("bg", bg, d_bg), ("lam", lam, d_lam), ("sel", sel, d_sel)):
        S.op(SP, lambda e, dst=dst, src=src: e.dma_start(out=dst, in_=src[:, :]), writes=[B("l1" + nm)], dma=True)
    emit_adaln2(C, "l1", [(pb[0], bpb[0]), (pb[1], bpb[1])],
                [(modrow[:, 0, :], B("modrow"), False), (modrow[:, 1, :], B("modrow"), True), (gaterow, B("t23"), False)],
                crep_b, B("t01"), brow, (0, 1), ring1, bring1, tag="a")
    wg2 = wg.rearrange("p m h j -> p (m h j)")
    for i in range(8):
        slot = i % 3
        sap, sbufs = ring1[(i + 5) % len(ring1)]
        S.op(SP, lambda e, i=i, sap=sap: e.dma_start(out=sap[:, 0:512], in_=d_wg[:, i * 512:(i + 1) * 512]), writes=sbufs, dma=True)
        S.op(DVE if i % 2 else ACT, (lambda e, i=i, sap=sap: e.tensor_copy(wg2[:, i * 512:(i + 1) * 512], sap[:, 0:512])) if i % 2 else
             (lambda e, i=i, sap=sap: e.activation(out=wg2[:, i * 512:(i + 1) * 512], in_=sap[:, 0:512], func=AF.Copy)),
             reads=sbufs, partial=[B("l1wg")])
    S.op(ACT, lambda e: e.activation(out=nsp8, in_=lam, func=AF.Exp, scale=-1.0), reads=[B("l1lam")], writes=[B("nsp8")])
    S.op(ACT, lambda e: e.activation(out=nsp8, in_=nsp8, func=AF.Ln, bias=C.one_col[:, 0:1], scale=1.0), reads=[B("nsp8"), B("one_col")], writes=[B("nsp8")])
    S.op(DVE, lambda e: e.tensor_scalar(nsp8, nsp8, -8.0, None, ALU.mult), reads=[B("nsp8")], writes=[B("nsp8")])
    S.op(DVE, lambda e: e.tensor_scalar(hn, nsp8, 0.5, None, ALU.mult), reads=[B("nsp8")], writes=[B("hn")])
    S.op(DVE, lambda e: e.tensor_scalar(hbg, bg, 0.5, None, ALU.mult), reads=[B("l1bg")], writes=[B("hbg")])
    S.op(DVE, lambda e: e.memset(xr[:, 0:2], 0.0), partial=[B("xr")])

    for t in range(NT + 1):
        if t < NT:
            xsrc, bxs = xres[:, t, :], [B("x%d" % t)]
        else:
            S.op(DVE, lambda e: e.memset(h32, 0.0), writes=[B("t01")])
            S.op(SP, lambda e: e.dma_start(out=h32[0:2, :], in_=C.recv1[0:2, :]), reads=[B("recv1")], partial=[B("t01")], dma=True)
            S.op(SP, lambda e: e.dma_start(out=hb2[0:2, :], in_=C.recv1[2:4, :]), reads=[B("recv1")], writes=[B("t34")], dma=True)
            S.op(DVE, lambda e: e.tensor_scalar(h32[0:2, :], h32[0:2, :], sel[0:2, 0:1], None, ALU.mult), reads=[B("t01"), B("l1sel")], partial=[B("t01")])
            S.op(DVE, lambda e: e.scalar_tensor_tensor(h32[0:2, :], hb2[0:2, :], sel[0:2, 1:2], h32[0:2, :], ALU.mult, ALU.add),
                 reads=[B("t01"), B("t34"), B("l1sel")], partial=[B("t01")])
            xsrc, bxs = h32, [B("t01")]
        meng = POOL if t % 3 == 0 else DVE
        S.op(meng, lambda e, xsrc=xsrc: e.tensor_tensor(h32, xsrc, modrow[:, 1, :], ALU.mult), reads=bxs + [B("modrow")], writes=[B("t01")])
        S.op(meng, lambda e: e.tensor_tensor(hbf, h32, modrow[:, 0, :], ALU.add), reads=[B("t01"), B("modrow")], writes=[B("t23")])
        pt, bpt = next_ptr()
        for k in range(8):
            S.op(PE, lambda e, k=k, pt=pt: e.transpose(pt[:, k * 128:(k + 1) * 128], hbf[:, k * 128:(k + 1) * 128], C.idb[:]),
                 reads=[B("t23"), B("idb")], writes=[bpt] if k == 0 else (), partial=[bpt] if k else ())
        ncol = 128 if t < NT else 8
        S.op(ACT, lambda e, t=t, pt=pt, ncol=ncol: e.activation(out=hT[:, :, t * 128:t * 128 + ncol],
                                                               in_=pt[:].rearrange("p (k m) -> p k m", m=128)[:, :, 0:ncol], func=AF.Copy),
             reads=[bpt], partial=[B("l1hT")])
    S.barrier()

    def load_wt(c0):
        wst = C.wstage[:].rearrange("p a b -> p (a b)")[:, 0:1024].rearrange("p (k m) -> p k m", m=128)
        S.op(SP, lambda e: e.dma_start(out=wst, in_=d_win.rearrange("(k p) c -> p k c", p=128)[:, :, c0:c0 + 128]),
             writes=[C.bwstage[0], C.bwstage[1]], dma=True)
        S.op(DVE, lambda e: e.tensor_copy(wt, wst), reads=[C.bwstage[0], C.bwstage[1]], writes=[B("l1wt")])

    def batch(d, ct, finish_chunk):
        qs = [0, 1, 2, 3] if d == 0 else [3, 2, 1, 0]
        ca = (2 * d) * 8 + ct
        ci = (2 * d + 1) * 8 + ct
        cl = d * 8 + ct
        bxc = B("xc%d" % ct)
        for n_, q in enumerate(qs):
            sl = slice(q * 512, (q + 1) * 512)
            bA, bI, bV = B("wA%d" % q), B("wTI%d" % q), B("wVB%d" % q)
            Ht, bHt = WH[:, n_ % 2, :], B("wH%d" % (n_ % 2))
            bk_r, bb_r = next_pb()
            bk_i, bb_i = next_pb()
            S.op(PE, lambda e, sl=sl, bk_r=bk_r: e.matmul(bk_r[:, 0:512], wg[:, 2 * d, ct, :], xc[:, ct, sl], start=True, stop=True),
                 reads=[B("l1wg"), bxc], writes=[bb_r])
            S.op(PE, lambda e, sl=sl, bk_i=bk_i: e.matmul(bk_i[:, 0:512], wg[:, 2 * d + 1, ct, :], xc[:, ct, sl], start=True, stop=True),
                 reads=[B("l1wg"), bxc], writes=[bb_i])
            S.op(ACT, lambda e, sl=sl, bk_r=bk_r: e.activation(out=WA[:, sl], in_=bk_r[:, 0:512], func=AF.Tanh, bias=hbg[:, ca:ca + 1], scale=0.5),
                 reads=[bb_r, B("hbg")], writes=[bA])
            S.op(ACT, lambda e, sl=sl, bk_i=bk_i: e.activation(out=WTI[:, sl], in_=bk_i[:, 0:512], func=AF.Tanh, bias=hbg[:, ci:ci + 1], scale=0.5),
                 reads=[bb_i, B("hbg")], writes=[bI])
            S.op(ACT, lambda e, sl=sl: e.activation(out=WA[:, sl], in_=WA[:, sl], func=AF.Exp, bias=hn[:, cl:cl + 1], scale=hn[:, cl:cl + 1]),
                 reads=[bA, B("hn")], writes=[bA])
            S.op(DVE, lambda e, sl=sl, Ht=Ht: e.tensor_tensor(Ht, WA[:, sl], WA[:, sl], ALU.mult), reads=[bA], writes=[bHt])
            S.op(DVE, lambda e, sl=sl, Ht=Ht: e.tensor_scalar(WVB[:, sl], Ht, -1.0, 1.0, ALU.mult, ALU.add), reads=[bHt], writes=[bV])
        yield "G"
        allV = [B("wVB%d" % q) for q in range(4)]
        allI = [B("wTI%d" % q) for q in range(4)]
        S.op(ACT, lambda e: e.activation(out=WVB, in_=WVB, func=AF.Sqrt, scale=0.25), reads=allV, writes=allV)
        S.op(DVE, lambda e: e.scalar_tensor_tensor(WVB, WTI, 1.0, WVB, ALU.add, ALU.mult), reads=allV + allI, writes=allV)
        S.op(DVE, lambda e: e.tensor_tensor(WVB, WVB, xc[:, ct, :], ALU.mult), reads=allV + [bxc], writes=allV)
        st = stA if d == 0 else stB
        bst = B("stA") if d == 0 else B("stB")
        for n_, q in enumerate(qs):
            sl = slice(q * 512, (q + 1) * 512)
            bA, bV = B("wA%d" % q), B("wVB%d" % q)
            H, bH = WH[:, n_ % 2, :], B("wH%d" % (n_ % 2))
            if d == 0:
                init = 0.0 if n_ == 0 else st[:, ct:ct + 1]
                S.op(DVE, lambda e, sl=sl, H=H, init=init: e.tensor_tensor_scan(H, WA[:, sl], WVB[:, sl], init, ALU.mult, ALU.add),
                     reads=[bA, bV, bst], writes=[bH])
                S.op(DVE, lambda e, H=H: e.tensor_copy(st[:, ct:ct + 1], H[:, 511:512]), reads=[bH], writes=[bst])
            else:
                init = carry[:, ct:ct + 1] if n_ == 0 else st[:, ct:ct + 1]
                S.op(DVE, lambda e, sl=sl, H=H, init=init: e.tensor_tensor_scan(rev_ap(H), rev_ap(WA[:, sl]), rev_ap(WVB[:, sl]), init, ALU.mult, ALU.add),
                     reads=[bA, bV, bst, B("l1carry")], writes=[bH])
                S.op(DVE, lambda e, H=H: e.tensor_copy(st[:, ct:ct + 1], H[:, 0:1]), reads=[bH], writes=[bst])
            finish_chunk(q, H, bH)

    def conv_stage(ct):
        for tg in range(5):
            n = 512 if tg < 4 else 8
            bank, bb = next_pb()
            for k in range(8):
                S.op(PE, lambda e, k=k, tg=tg, n=n, bank=bank: e.matmul(bank[:, 0:n], wt[:, k, :], hT[:, k, tg * 512:tg * 512 + n],
                                                                      start=(k == 0), stop=(k == 7)),
                     reads=[B("l1wt"), B("l1hT")], writes=[bb] if k == 0 else (), partial=[bb] if k else ())
            m = n if tg < 4 else 2
            S.op(DVE, lambda e, tg=tg, m=m, bank=bank: e.tensor_copy(xr[:, 2 + tg * 512:2 + tg * 512 + m], bank[:, 0:m]),
                 reads=[bb], partial=[B("xr")])
        if ct + 1 < 8:
            load_wt((ct + 1) * 128)
        par = ct % 2
        for j in range(5):
            S.op(ACT, lambda e, ct=ct, j=j, par=par: e.activation(out=dg[:, par * 5 + j, :], in_=C.idb[:], func=AF.Copy, scale=w5[:, ct * 5 + j:ct * 5 + j + 1]),
                 reads=[B("idb"), B("l1w5")], writes=[B("dg%d_%d" % (par, j))])
        for q in range(4):
            sl = slice(q * 512, (q + 1) * 512)
            bank, bb = next_pb()
            for j in range(5):
                S.op(PE, lambda e, j=j, q=q, par=par, bank=bank: e.matmul(bank[:, 0:512], dg[:, par * 5 + j, :], xr[:, q * 512 + j:q * 512 + j + 512],
                                                                        start=(j == 0), stop=(j == 4)),
                     reads=[B("dg%d_%d" % (par, j)), B("xr")], writes=[bb] if j == 0 else (), partial=[bb] if j else ())
            S.op(DVE, lambda e, ct=ct, sl=sl, bank=bank: e.tensor_scalar(xc[:, ct, sl], bank[:, 0:512], cb[:, ct:ct + 1], None, ALU.add),
                 reads=[bb, B("l1cb")], partial=[B("xc%d" % ct)])

    def drain(g):
        for _ in g:
            pass

    load_wt(0)
    conv_stage(0)
    for ct in range(8):
        def fin_a(q, H, bH, ct=ct):
            sl = slice(q * 512, (q + 1) * 512)
            S.op(DVE, lambda e: e.tensor_copy(yT[:, ct, sl], H), reads=[bH], partial=[B("yT%d" % ct)])
        g = batch(0, ct, fin_a)
        next(g)
        if ct + 1 < 8:
            conv_stage(ct + 1)
        drain(g)

    S.op(SP, lambda e: e.dma_start(out=C.send2[:, :], in_=stA), reads=[B("stA")], writes=[B("send2")], dma=True)
    S.op(POOL, lambda e: e.collective_compute("AllGather", ALU.bypass, replica_groups=RG, ins=[C.send2], outs=[C.recv2]),
         reads=[B("send2")], writes=[B("recv2")], cc=True)
    S.op(SP, lambda e: e.dma_start(out=cga, in_=C.recv2[0:128, :]), reads=[B("recv2")], writes=[B("cga")], dma=True)
    S.op(SP, lambda e: e.dma_start(out=cgb, in_=C.recv2[128:256, :]), reads=[B("recv2")], writes=[B("cgb")], dma=True)
    S.op(DVE, lambda e: e.tensor_scalar(carry, cga, sel[:, 0:1], None, ALU.mult), reads=[B("cga"), B("l1sel")], writes=[B("l1carry")])
    S.op(DVE, lambda e: e.scalar_tensor_tensor(carry, cgb, sel[:, 1:2], carry, ALU.mult, ALU.add), reads=[B("cgb"), B("l1carry"), B("l1sel")], writes=[B("l1carry")])

    def gate_chunk(ct, q):
        bank, bb = next_pb()
        for k in range(8):
            S.op(PE, lambda e, k=k, bank=bank: e.matmul(bank[:, 0:512], wt[:, k, :], hT[:, k, q * 512:(q + 1) * 512],
                                                       start=(k == 0), stop=(k == 7)),
                 reads=[B("l1wt"), B("l1hT")], writes=[bb] if k == 0 else (), partial=[bb] if k else ())
        S.op(ACT, lambda e, bank=bank: e.activation(out=sg[:, q * 512:(q + 1) * 512], in_=bank[:, 0:512], func=AF.Silu),
             reads=[bb], writes=[B("sg%d" % q)], partial=[B("xr")] if ct == 0 else ())

    load_wt(1024)
    for q in (3, 2, 1, 0):
        gate_chunk(0, q)
    for ct in range(8):
        if ct + 1 < 8:
            load_wt(1024 + (ct + 1) * 128)

        def fin_b(q, H, bH, ct=ct):
            sl = slice(q * 512, (q + 1) * 512)
            S.op(DVE, lambda e: e.tensor_tensor(H, H, yT[:, ct, sl], ALU.add), reads=[bH, B("yT%d" % ct)], writes=[bH])
            S.op(DVE, lambda e: e.tensor_tensor(yT[:, ct, sl], H, sg[:, sl], ALU.mult), reads=[bH, B("sg%d" % q)], partial=[B("yT%d" % ct)])
            if ct + 1 < 8:
                gate_chunk(ct + 1, q)
        drain(batch(1, ct, fin_b))

    S.barrier()
    ring2 = [w4k, (xc[:, 6, :].bitcast(F32), [B("xc6")]), (xc[:, 7, :].bitcast(F32), [B("xc7")])]
    emit_adaln2(C, "l1", [(pb[0], bpb[0]), (pb[1], bpb[1])],
                [None, None, (gaterow, B("t23"), False)], crep_b, B("t01"), brow, (2,), ring2, bring1, tag="b")
    emit_weight_bf16(C, "l1_w_out", w_out, B("l1hT"), 8, 1024, ring2, scale_row=gaterow, bscale=B("t23"), chunk=1024)
    S.op(SP, lambda e: e.dma_start(out=lnrows[:, 0, :], in_=d_lng[:, :]), writes=[B("xc4")], partial=[B("l1lnrows")], dma=True)
    S.op(SP, lambda e: e.dma_start(out=lnrows[:, 1, :], in_=d_lnb[:, :]), writes=[B("xc5")], partial=[B("l1lnrows")], dma=True)
    yTb = [B("yT%d" % k) for k in range(8)]
    S.barrier()
    zall = xc.rearrange("p a b -> p (a b)").bitcast(F32)
    lnsm = [(C.asb("l1_st6_%d" % i, [128, 12]), C.asb("l1_mv_%d" % i, [128, 2]), C.asb("l1_sd_%d" % i, [128, 4]), "_%d" % i) for i in range(3)]
    for t in range(NT):
        z = zall[:, (t % 3) * 1024:(t % 3 + 1) * 1024]
        bz = B("l1z%d" % (t % 3))
        for cg in range(2):
            bank, bb = next_pb()
            for k in range(8):
                S.op(PE, lambda e, k=k, cg=cg, t=t, bank=bank: e.matmul(bank[:, 0:512], yT[:, k, t * 128:(t + 1) * 128], w_out[:, k, cg * 512:(cg + 1) * 512],
                                                                       start=(k == 0), stop=(k == 7)),
                     reads=[yTb[k], B("l1hT")], writes=[bb] if k == 0 else (), partial=[bb] if k else ())
            S.op(DVE, lambda e, cg=cg, t=t, bank=bank, z=z: e.scalar_tensor_tensor(z[:, cg * 512:(cg + 1) * 512], xres[:, t, cg * 512:(cg + 1) * 512], ALPHA,
                                                                             bank[:, 0:512], ALU.mult, ALU.add),
                 reads=[B("x%d" % t), bb], writes=[bz] if cg == 0 else (), partial=[bz] if cg else ())
        emit_ln_tile(C, "l1", z, bz, xres[:, t, :], B("x%d" % t), lnrows[:, 0, :], lnrows[:, 1, :], B("l1lnrows"), small=lnsm[t % 3])
        S.op(SP, lambda e, t=t: e.dma_start(out=out_dram[t * 128:(t + 1) * 128, :], in_=xres[:, t, :]), reads=[B("x%d" % t)], dma=True)


def l1_inputs(c, inp):
    b, half = c // 2, c % 2
    dA, dB = (0, 1) if half == 0 else (1, 0)
    cw = inp["od_conv_w"][0]
    zero = np.zeros((1, 1024), np.float32)
    w5 = np.concatenate([cw, zero], 0) if half == 0 else np.concatenate([zero, cw[::-1]], 0)
    w5 = np.ascontiguousarray(w5.reshape(5, 8, 128).transpose(2, 1, 0)).reshape(128, 40)
    wa, wx = inp["od_w_a"][0], inp["od_w_x"][0]
    wg = np.stack([wa[dA], wx[dA], wa[dB], wx[dB]], 0)
    wg = np.ascontiguousarray(wg.transpose(2, 0, 1, 3)).reshape(128, 4096)
    ba, bx = inp["od_b_a"][0], inp["od_b_x"][0]
    bgm = np.stack([ba[dA], bx[dA], ba[dB], bx[dB]], 0).reshape(4, 8, 128)
    bgm = np.ascontiguousarray(bgm.transpose(2, 0, 1)).reshape(128, 32)
    lam = inp["od_lam"][0]
    lamm = np.stack([lam[dA], lam[dB]], 0).reshape(2, 8, 128)
    lamm = np.ascontiguousarray(lamm.transpose(2, 0, 1)).reshape(128, 16)
    return {
        "sel": rep(np.array([0.0, 1.0], np.float32) if half == 0 else np.array([1.0, 0.0], np.float32)),
        "l1_cT": np.ascontiguousarray(inp["c"][b].reshape(8, 128).T),
        "l1_ada_w": np.ascontiguousarray(inp["ada_w"][1]),
        "l1_ada_b": np.ascontiguousarray(inp["ada_b"][1][None, :]),
        "l1_w_in": np.ascontiguousarray(inp["od_w_in"][0]),
        "l1_w5": w5,
        "l1_cb": np.ascontiguousarray(inp["od_conv_b"][0].reshape(8, 128).T),
        "l1_wg": wg,
        "l1_bg": bgm,
        "l1_lam": lamm,
        "l1_w_out": np.ascontiguousarray(inp["od_w_out"][0]),
        "l1_ln_g": rep(inp["ln_g"][1]),
        "l1_ln_b": rep(inp["ln_b"][1]),
    }


def build_fused():
    nc = bass.Bass("TRN2", target_bir_lowering=False)
    with contextlib.ExitStack() as st:
        C = Ctx(nc, st)
        x_d = C.dram("x", [2176, 1024], F32, "ExternalInput")
        o_d = C.dram("out", [2048, 1024], F32, "ExternalOutput")
        C.send1 = nc.dram_tensor("send1", [2, 1024], F32).ap()
        C.recv1 = nc.dram_tensor("recv1", [4, 1024], F32).ap()
        C.send2 = nc.dram_tensor("send2", [128, 8], F32).ap()
        C.recv2 = nc.dram_tensor("recv2", [256, 8], F32).ap()
        alloc_common(C)
        emit_consts(C)
        C.make_arena(ARENA_WORDS)
        emit_l0(C, x_d, None)
        p0 = C.apeak
        emit_l1(C, o_d)
        print("arena words: l0 peak", p0, "overall peak", C.apeak, "of", C.asize, "sbuf remaining", nc.sbuf_bytes_remaining)
        C.S.emit()
    return nc


def rep(v, n=128):
    return np.ascontiguousarray(np.broadcast_to(np.asarray(v, np.float32)[None, :], (n, v.shape[0])))


def core_tokens(half):
    if half == 0:
        return np.arange(0, 2048), np.arange(2048, 2176)
    return np.arange(4095, 2047, -1), np.arange(2047, 1919, -1)


def l0_masks():
    qi = np.arange(128)[:, None]
    kj = np.arange(384)[None, :]
    valid = np.abs(kj - 128 - qi) <= 128
    m1 = np.where(valid, 0.0, NEG).astype(np.float32)
    m0 = np.where(valid & (kj >= 128), 0.0, NEG).astype(np.float32)
    return np.ascontiguousarray(np.concatenate([m0, m1], axis=1))


def l0_inputs(c, inp, x_override=None):
    b, half = c // 2, c % 2
    own, halo = core_tokens(half)
    idx = np.concatenate([own, halo])
    x = inp["x"][b][idx]
    pos = inp["positions"][b][idx].astype(np.int32).reshape(17, 128).T
    w_in = inp["ev_w_in"][0]
    qcols = np.concatenate([np.arange(h * 64, (h + 1) * 64) for h in PERM])
    cols = np.concatenate([qcols, np.arange(512, 1792), 1792 + qcols, np.arange(2304, 2816)])
    w_out = inp["ev_w_out"][0]
    rows = np.concatenate([qcols, np.arange(512, 1024)])
    sgw = inp["ev_sg_w"][0]
    sgb = inp["ev_sg_b"][0]
    if half == 1:
        sgw = sgw[:, ::-1, ::-1]
        sgb = sgb[:, ::-1]
    sgwT = np.ascontiguousarray(np.transpose(sgw, (2, 0, 1))).reshape(128, 1024)
    return {
        "x": np.ascontiguousarray(x),
        "ident": np.eye(128, dtype=np.float32),
        "l0_pos": np.ascontiguousarray(pos),
        "l0_cT": np.ascontiguousarray(inp["c"][b].reshape(8, 128).T),
        "l0_ada_w": np.ascontiguousarray(inp["ada_w"][0]),
        "l0_ada_b": np.ascontiguousarray(inp["ada_b"][0][None, :]),
        "l0_w_in": np.ascontiguousarray(w_in[:, cols]),
        "l0_w_out": np.ascontiguousarray(w_out[rows, :]),
        "l0_sink": rep(inp["ev_sink"][0][PERM]),
        "l0_sg_ln_g": rep(inp["ev_sg_ln_g"][0]),
        "l0_sg_ln_b": rep(inp["ev_sg_ln_b"][0]),
        "l0_sg_wT": sgwT,
        "l0_sg_b": np.ascontiguousarray(sgb.T),
        "l0_ln_g": rep(inp["ln_g"][0]),
        "l0_ln_b": rep(inp["ln_b"][0]),
        "l0_mask": l0_masks(),
    }


def run_l0(inp):
    nc = build_l0()
    in_maps = [l0_inputs(c, inp) for c in range(8)]
    res = run_bass_kernel_spmd(nc, in_maps, core_ids=list(range(8)))
    return [r["out"] for r in res.results]


def assemble(outs):
    full = np.zeros((4, 4096, 1024), np.float32)
    for c in range(8):
        b, half = c // 2, c % 2
        own, _ = core_tokens(half)
        full[b, own] = outs[c]
    return full


def fused_inputs(c, inp):
    d = l0_inputs(c, inp)
    d.update(l1_inputs(c, inp))
    return d


def kernel(**inputs):
    inp = {k: np.asarray(v) for k, v in inputs.items()}
    nc = build_fused()
    in_maps = [fused_inputs(c, inp) for c in range(8)]
    res = run_bass_kernel_spmd(nc, in_maps, core_ids=list(range(8)))
    return assemble([r["out"] for r in res.results])
```
